# Optimizing a Trainium2 kernel written in Bass

```python
import math
import jax
import jax.numpy as jnp
from jax import lax
import numpy as np

D_MODEL = 1024
BATCH = 32
SEQ = 256
DEPTH = 2
DEC_BATCH = 8
DEC_SEQ = 1024
PAST_LEN = 256

GRID_W = 64
N_AB = (DEPTH + 1) // 2
N_CD = DEPTH // 2
N_MOD = 9
D_FF = 2816
NORM_EPS = 1e-6
SHORT_CONV = 3
A_HEAD = 64
A_HEADS = 8
A_WIDTH = A_HEADS * A_HEAD
A_DECAY_LORA = 64
A_ICLR_LORA = 64
A_GATE_LORA = 128
A_GN_EPS = 64e-5
A_SPLITS = (A_WIDTH, 2 * A_WIDTH, 3 * A_WIDTH, 3 * A_WIDTH + 2 * A_DECAY_LORA, 3 * A_WIDTH + 2 * A_DECAY_LORA + 2 * A_ICLR_LORA)
A_COLS = A_SPLITS[-1] + A_GATE_LORA
B_HEADS = 4
B_DK = 128
B_DV = 128
B_WIDTH = B_HEADS * B_DV
B_CHUNK = 64
B_COLS = 4 * B_WIDTH + 4 * B_HEADS
AB_COLS = A_COLS + B_COLS
C_HEADS = 8
C_Q_RANK = 384
C_KV_RANK = 256
C_NOPE = 64
C_ROPE = 32
C_V = 64
C_WIDTH = C_HEADS * C_V
C_COLS = C_Q_RANK + C_KV_RANK + C_ROPE
ROPE_BASE = 10000.0
Q_BLOCK = 128
D_WIDTH = 512
D_ORDER = 2
D_BANDS = 16
D_EMB = 1 + 2 * D_BANDS
D_FILTER_HIDDEN = 64
D_DECAY_TARGET = 1e-2
D_FAST_PCT = 0.3
D_SLOW_PCT = 1.5
CD_COLS = C_COLS + 3 * D_WIDTH

kernel_name = 'hybrid_prefix_diffusion_step'


def rmsnorm(x, g):
    xf = x.astype(jnp.float32)
    y = xf * lax.rsqrt(jnp.mean(xf * xf, axis=-1, keepdims=True) + NORM_EPS)
    return (y * g.astype(jnp.float32)).astype(x.dtype)


def l2norm(x):
    xf = x.astype(jnp.float32)
    return (xf * lax.rsqrt(jnp.sum(xf * xf, axis=-1, keepdims=True) + 1e-6)).astype(x.dtype)


def swiglu(x, w_gu, w_down):
    g, u = jnp.split(x @ w_gu, 2, axis=-1)
    return (jax.nn.silu(g) * u) @ w_down


def adaln(cvec, w_mod, b_mod):
    m = jax.nn.silu(cvec) @ w_mod + b_mod
    return m.reshape(m.shape[0], 1, N_MOD, D_MODEL)


def pre(x, g, m, j):
    return rmsnorm(x, g) * (1.0 + m[:, :, 3 * j + 1]) + m[:, :, 3 * j]


def neighbours(p):
    z = jnp.zeros_like(p[:, :1])
    return jnp.concatenate([z, p[:, :-1]], axis=1), jnp.concatenate([p[:, 1:], z], axis=1)


def centred_shift(p):
    prev, nxt = neighbours(p)
    return 0.5 * (prev + nxt)


def conv3(p, w):
    prev, nxt = neighbours(p)
    return w[0] * prev + w[1] * p + w[2] * nxt


def wkv7_scan(r, w, k, v, kk, a, S0):
    def step(S, xs):
        r_t, w_t, k_t, v_t, kk_t, a_t = xs
        sk = jnp.einsum('bhvk,bhk->bhv', S, kk_t)
        S = (S * w_t[:, :, None, :] - sk[..., None] * (kk_t * a_t)[:, :, None, :]
             + v_t[..., None] * k_t[:, :, None, :])
        return S, jnp.einsum('bhvk,bhk->bhv', S, r_t)
    xs = tuple(jnp.swapaxes(t, 0, 1) for t in (r, w, k, v, kk, a))
    S, y = lax.scan(step, S0, xs)
    return jnp.swapaxes(y, 0, 1), S


def rwkv7_mix(p, S0, mu, w0, w_up, a0, a_up, g_up, k_k, k_a, r_k, ln_w, ln_b):
    B, L, _ = p.shape
    f32 = jnp.float32
    p = p + mu * (centred_shift(p) - p)
    r, k, v, wd, ad, gd = jnp.split(p, A_SPLITS, axis=-1)
    wd = wd.reshape(B, L, 2, A_DECAY_LORA)
    ad = ad.reshape(B, L, 2, A_ICLR_LORA)
    w_log = -jax.nn.softplus(-(w0 + jnp.einsum('bldr,drc->bldc', jnp.tanh(wd), w_up))) - 0.5
    decay = jnp.exp(-jnp.exp(w_log.astype(f32)))
    a = jax.nn.sigmoid(a0 + jnp.einsum('bldr,drc->bldc', ad, a_up))
    g = jax.nn.sigmoid(gd) @ g_up
    k_dir = k[:, :, None, :] * (1.0 + (a - 1.0) * k_a)
    hd = lambda t: t.reshape(t.shape[:-1] + (A_HEADS, A_HEAD)).astype(f32)
    fl = lambda t: jnp.flip(t, axis=1)
    kk = l2norm(hd(k * k_k))
    rh, vh = hd(r), hd(v)
    S0 = S0.astype(f32)
    y_f, S_f = wkv7_scan(rh, hd(decay[:, :, 0]), hd(k_dir[:, :, 0]), vh, kk, hd(a[:, :, 0]), S0[:, 0])
    y_b, S_b = wkv7_scan(fl(rh), fl(hd(decay[:, :, 1])), fl(hd(k_dir[:, :, 1])), fl(vh), fl(kk),
                         fl(hd(a[:, :, 1])), S0[:, 1])
    y = y_f + fl(y_b)
    mean = jnp.mean(y, axis=-1, keepdims=True)
    var = jnp.mean(jnp.square(y - mean), axis=-1, keepdims=True)
    y = ((y - mean) * lax.rsqrt(var + A_GN_EPS)).reshape(B, L, A_WIDTH) * ln_w + ln_b
    bonus = jnp.sum(rh * hd(jnp.mean(k_dir, axis=2)) * r_k, axis=-1, keepdims=True) * vh
    out = (y + bonus.reshape(B, L, A_WIDTH)) * g
    return out.astype(p.dtype), jnp.stack([S_f, S_b], axis=1).astype(p.dtype)


def gated_delta_chunked(q, k, v, g, beta, S0):
    B, L, H, DK = q.shape
    DV = v.shape[-1]
    C = math.gcd(L, B_CHUNK)
    N = L // C
    to_chunks = lambda t: jnp.moveaxis(t.reshape(B, N, C, H, -1), (1, 3), (0, 2))
    qc, kc, vc = to_chunks(q), to_chunks(k), to_chunks(v)
    gc = to_chunks(g[..., None])[..., 0]
    bc = to_chunks(beta[..., None])[..., 0]
    G = jnp.cumsum(gc, axis=-1)
    causal = jnp.tril(jnp.ones((C, C), dtype=bool))
    strict = jnp.tril(jnp.ones((C, C), dtype=bool), -1)
    diff = G[..., :, None] - G[..., None, :]
    gamma = jnp.where(causal, jnp.exp(jnp.where(causal, diff, 0.0)), 0.0)
    kb = kc * bc[..., None]
    lower = jnp.where(strict, jnp.einsum('nbhid,nbhjd->nbhij', kb, kc) * gamma, 0.0)
    eye = jnp.eye(C, dtype=q.dtype)
    T = lax.linalg.triangular_solve(eye + lower, jnp.broadcast_to(eye, lower.shape),
                                    left_side=True, lower=True, unit_diagonal=True)
    u = jnp.einsum('nbhij,nbhjv->nbhiv', T, vc * bc[..., None])
    w = jnp.einsum('nbhij,nbhjk->nbhik', T, kb * jnp.exp(G)[..., None])
    qk = jnp.einsum('nbhid,nbhjd->nbhij', qc, kc) * gamma
    q_in = qc * jnp.exp(G)[..., None]
    G_last = G[..., -1]
    k_out = kc * jnp.exp(G_last[..., None] - G)[..., None]

    def step(S, xs):
        u_i, w_i, qk_i, q_i, k_i, gl_i = xs
        v_new = u_i - jnp.einsum('bhck,bhkv->bhcv', w_i, S)
        o = jnp.einsum('bhck,bhkv->bhcv', q_i, S) + jnp.einsum('bhij,bhjv->bhiv', qk_i, v_new)
        S = S * jnp.exp(gl_i)[..., None, None] + jnp.einsum('bhck,bhcv->bhkv', k_i, v_new)
        return S, o
    S, o = lax.scan(step, S0, (u, w, qk, q_in, k_out, G_last))
    o = jnp.moveaxis(o, (0, 2), (1, 3)).reshape(B, L, H, DV)
    return o, S


def gdn_mix(p, S0, conv_w, a_log, dt_bias, norm_w):
    B, L, _ = p.shape
    f32 = jnp.float32
    qkv, z, ab = jnp.split(p, [3 * B_WIDTH, 4 * B_WIDTH], axis=-1)
    q, k, v = jnp.split(jax.nn.silu(conv3(qkv, conv_w)), 3, axis=-1)
    hd = lambda t, d: t.reshape(B, L, B_HEADS, d).astype(f32)
    fl = lambda t: jnp.flip(t, axis=1)
    q = l2norm(hd(q, B_DK)) * (B_DK ** -0.5)
    k = l2norm(hd(k, B_DK))
    v = hd(v, B_DV)
    ab = ab.reshape(B, L, 2, 2, B_HEADS).astype(f32)
    g = -jnp.exp(a_log.astype(f32)) * jax.nn.softplus(ab[:, :, :, 0] + dt_bias.astype(f32))
    beta = jax.nn.sigmoid(ab[:, :, :, 1])
    S0 = S0.astype(f32)
    o_f, S_f = gated_delta_chunked(q, k, v, g[:, :, 0], beta[:, :, 0], S0[:, 0])
    o_b, S_b = gated_delta_chunked(fl(q), fl(k), fl(v), fl(g[:, :, 1]), fl(beta[:, :, 1]), S0[:, 1])
    o = rmsnorm(o_f + fl(o_b), norm_w) * jax.nn.silu(hd(z, B_DV))
    return o.reshape(B, L, B_WIDTH).astype(p.dtype), jnp.stack([S_f, S_b], axis=1).astype(p.dtype)


def axial_rope(L):
    rows = L // GRID_W
    row = jnp.repeat(jnp.arange(rows), GRID_W).astype(jnp.float32)
    col = (jnp.arange(L) % GRID_W).astype(jnp.float32)
    n = C_ROPE // 4
    inv = ROPE_BASE ** (-jnp.arange(n, dtype=jnp.float32) / n)
    ang = jnp.concatenate([row[:, None] * inv, col[:, None] * inv], axis=-1)
    return jnp.cos(ang), jnp.sin(ang)


def apply_rope(x, cos, sin):
    cos = cos.astype(x.dtype)
    sin = sin.astype(x.dtype)
    x1, x2 = x[..., 0::2], x[..., 1::2]
    return jnp.stack([x1 * cos - x2 * sin, x1 * sin + x2 * cos], axis=-1).reshape(x.shape)


def mla_split(p, q_norm, w_uq, kv_norm):
    B, L, _ = p.shape
    pq, pkv, kpe = jnp.split(p, [C_Q_RANK, C_Q_RANK + C_KV_RANK], axis=-1)
    q = (rmsnorm(pq, q_norm) @ w_uq).reshape(B, L, C_HEADS, C_NOPE + C_ROPE)
    return q[..., :C_NOPE], q[..., C_NOPE:], rmsnorm(pkv, kv_norm), kpe


def mla_expand(ckv, w_ukv):
    B, L, _ = ckv.shape
    kv = (ckv @ w_ukv).reshape(B, L, C_HEADS, C_NOPE + C_V)
    return kv[..., :C_NOPE], kv[..., C_NOPE:]


def mla_attend(q_nope, q_pe, k_nope, k_pe, v):
    B, Lq = q_nope.shape[:2]
    qb = math.gcd(Lq, Q_BLOCK)
    nb = Lq // qb
    blocks = lambda t: jnp.swapaxes(t.reshape((B, nb, qb) + t.shape[2:]), 0, 1)
    scale = (C_NOPE + C_ROPE) ** -0.5

    def one_block(qs):
        qn, qp = qs
        s = jnp.einsum('bqhd,bkhd->bhqk', qn, k_nope) + jnp.einsum('bqhr,bkr->bhqk', qp, k_pe)
        prob = jax.nn.softmax(s.astype(jnp.float32) * scale, axis=-1).astype(v.dtype)
        return jnp.einsum('bhqk,bkhd->bqhd', prob, v)
    o = lax.map(one_block, (blocks(q_nope), blocks(q_pe)))
    return jnp.swapaxes(o, 0, 1).reshape(B, Lq, C_WIDTH)


def hyena_filters(L, w1, b1, w2, b2, w3, freq):
    t = jnp.arange(L, dtype=jnp.float32)
    t_norm = t / max(L - 1, 1)
    bands = jnp.linspace(1e-4, D_BANDS - 1, D_BANDS, dtype=jnp.float32)
    ang = (2.0 * math.pi / L) * t[:, None] * bands
    feats = jnp.concatenate([t_norm[:, None], jnp.cos(ang), -jnp.sin(ang)], axis=-1).astype(w1.dtype)
    h = jnp.sin(freq[0] * (feats @ w1 + b1))
    h = jnp.sin(freq[1] * (h @ w2 + b2))
    h = (h @ w3).reshape(L, D_ORDER, 2, D_WIDTH)
    min_decay = math.log(D_DECAY_TARGET) / D_SLOW_PCT
    max_decay = math.log(D_DECAY_TARGET) / D_FAST_PCT
    deltas = jnp.abs(jnp.linspace(min_decay, max_decay, D_WIDTH, dtype=jnp.float32))
    window = jnp.exp(-t_norm[:, None] * deltas)
    return h * window[:, None, None, :].astype(h.dtype)


def bidir_fftconv(z, h_f, h_b):
    B, L, C = z.shape
    filt2 = jnp.concatenate([h_f, jnp.zeros((1, C), h_f.dtype), h_b[:0:-1]], axis=0).astype(jnp.float32)
    zf = jnp.fft.rfft(z.astype(jnp.float32), n=2 * L, axis=1)
    hf = jnp.fft.rfft(filt2, n=2 * L, axis=0)
    return jnp.fft.irfft(zf * hf[None], n=2 * L, axis=1)[:, :L].astype(z.dtype)


def hyena_mix(p, conv_w, w1, b1, w2, b2, w3, freq, bias):
    L = p.shape[1]
    v, x1, x2 = jnp.split(conv3(p, conv_w), 3, axis=-1)
    h = hyena_filters(L, w1, b1, w2, b2, w3, freq)
    z = v
    for n, gate in enumerate((x1, x2)):
        z = gate * (bidir_fftconv(z, h[:, n, 0], h[:, n, 1]) + bias[n] * z)
    return z


def setup_inputs(seed: int = 0) -> dict:
    key = jax.random.key(seed)
    ks = iter(jax.random.split(key, 64))
    f32 = jnp.float32

    def nrm(shape, s):
        return jax.random.normal(next(ks), shape, f32) * s

    def gain(shape, base=1.0):
        return base + nrm(shape, 0.05)

    def unif(shape, lo, hi):
        return jax.random.uniform(next(ks), shape, f32, lo, hi)

    D, F = D_MODEL, D_FF
    dt = jnp.exp(unif((N_AB, 2, B_HEADS), math.log(1e-3), math.log(1e-1)))
    return {
        'x_prompt': nrm((BATCH, SEQ, D), 1.0),
        'x_sample': nrm((DEC_BATCH, DEC_SEQ, D), 1.0),
        'state_rwkv': nrm((DEC_BATCH, N_AB, 2, A_HEADS, A_HEAD, A_HEAD), 1.0),
        'state_gdn': nrm((DEC_BATCH, N_AB, 2, B_HEADS, B_DK, B_DV), 0.3),
        'cache_ckv': nrm((DEC_BATCH, N_CD, PAST_LEN, C_KV_RANK), 1.0),
        'cache_kpe': nrm((DEC_BATCH, N_CD, PAST_LEN, C_ROPE), 1.0),
        'c': nrm((DEC_BATCH, D), 1.0),
        'c_ctx': nrm((D,), 1.0),
        'w_mod': nrm((DEPTH, D, N_MOD * D), 0.02),
        'b_mod': nrm((DEPTH, N_MOD * D), 0.02),
        'norm_ffn1': gain((DEPTH, D)),
        'norm_mix': gain((DEPTH, D)),
        'norm_ffn2': gain((DEPTH, D)),
        'ffn1_w_gu': nrm((DEPTH, D, 2 * F), D ** -0.5),
        'ffn1_w_down': nrm((DEPTH, F, D), F ** -0.5),
        'ffn2_w_gu': nrm((DEPTH, D, 2 * F), D ** -0.5),
        'ffn2_w_down': nrm((DEPTH, F, D), F ** -0.5),
        'w_out': nrm((DEPTH, D, D), D ** -0.5),
        'ab_w_in': nrm((N_AB, D, AB_COLS), D ** -0.5),
        'rwkv_mu': unif((N_AB, A_COLS), 0.0, 1.0),
        'rwkv_w0': unif((N_AB, 2, A_WIDTH), -6.0, 1.0),
        'rwkv_w_up': nrm((N_AB, 2, A_DECAY_LORA, A_WIDTH), A_DECAY_LORA ** -0.5),
        'rwkv_a0': nrm((N_AB, 2, A_WIDTH), 0.5),
        'rwkv_a_up': nrm((N_AB, 2, A_ICLR_LORA, A_WIDTH), A_ICLR_LORA ** -0.5),
        'rwkv_g_up': nrm((N_AB, A_GATE_LORA, A_WIDTH), A_GATE_LORA ** -0.5),
        'rwkv_k_k': gain((N_AB, A_WIDTH), 0.85),
        'rwkv_k_a': gain((N_AB, A_WIDTH)),
        'rwkv_r_k': nrm((N_AB, A_HEADS, A_HEAD), 0.1),
        'rwkv_ln_w': gain((N_AB, A_WIDTH)),
        'rwkv_ln_b': nrm((N_AB, A_WIDTH), 0.02),
        'gdn_conv': nrm((N_AB, SHORT_CONV, 3 * B_WIDTH), SHORT_CONV ** -0.5),
        'gdn_a_log': jnp.log(unif((N_AB, 2, B_HEADS), 1.0, 16.0)),
        'gdn_dt_bias': dt + jnp.log(-jnp.expm1(-dt)),
        'gdn_norm': gain((N_AB, B_DV)),
        'cd_w_in': nrm((N_CD, D, CD_COLS), D ** -0.5),
        'mla_q_norm': gain((N_CD, C_Q_RANK)),
        'mla_w_uq': nrm((N_CD, C_Q_RANK, C_HEADS * (C_NOPE + C_ROPE)), C_Q_RANK ** -0.5),
        'mla_kv_norm': gain((N_CD, C_KV_RANK)),
        'mla_w_ukv': nrm((N_CD, C_KV_RANK, C_HEADS * (C_NOPE + C_V)), C_KV_RANK ** -0.5),
        'hy_conv': nrm((N_CD, SHORT_CONV, 3 * D_WIDTH), SHORT_CONV ** -0.5),
        'hy_w1': nrm((N_CD, D_EMB, D_FILTER_HIDDEN), D_EMB ** -0.5),
        'hy_b1': nrm((N_CD, D_FILTER_HIDDEN), 0.1),
        'hy_w2': nrm((N_CD, D_FILTER_HIDDEN, D_FILTER_HIDDEN), D_FILTER_HIDDEN ** -0.5),
        'hy_b2': nrm((N_CD, D_FILTER_HIDDEN), 0.1),
        'hy_w3': nrm((N_CD, D_FILTER_HIDDEN, D_ORDER * 2 * D_WIDTH), 0.01),
        'hy_freq': gain((N_CD, 2, D_FILTER_HIDDEN)),
        'hy_bias': nrm((N_CD, D_ORDER, D_WIDTH), 0.5),
        'final_norm': gain((D,)),
    }


def reference(x_prompt, x_sample, state_rwkv, state_gdn, cache_ckv, cache_kpe, c, c_ctx,
              w_mod, b_mod, norm_ffn1, norm_mix, norm_ffn2, ffn1_w_gu, ffn1_w_down, ffn2_w_gu, ffn2_w_down, w_out,
              ab_w_in, rwkv_mu, rwkv_w0, rwkv_w_up, rwkv_a0, rwkv_a_up, rwkv_g_up, rwkv_k_k, rwkv_k_a, rwkv_r_k,
              rwkv_ln_w, rwkv_ln_b, gdn_conv, gdn_a_log, gdn_dt_bias, gdn_norm,
              cd_w_in, mla_q_norm, mla_w_uq, mla_kv_norm, mla_w_ukv,
              hy_conv, hy_w1, hy_b1, hy_w2, hy_b2, hy_w3, hy_freq, hy_bias, final_norm):
    xp, xs = x_prompt, x_sample
    Bp = xp.shape[0]
    rwkv_new, gdn_new, ckv_new, kpe_new = [], [], [], []
    for i in range(DEPTH):
        j = i // 2
        mp = adaln(c_ctx[None], w_mod[i], b_mod[i])
        ms = adaln(c, w_mod[i], b_mod[i])
        xp = xp + 0.5 * mp[:, :, 2] * swiglu(pre(xp, norm_ffn1[i], mp, 0), ffn1_w_gu[i], ffn1_w_down[i])
        xs = xs + 0.5 * ms[:, :, 2] * swiglu(pre(xs, norm_ffn1[i], ms, 0), ffn1_w_gu[i], ffn1_w_down[i])
        hp = pre(xp, norm_mix[i], mp, 1)
        hs = pre(xs, norm_mix[i], ms, 1)
        if i % 2 == 0:
            pp = hp @ ab_w_in[j]
            ps = hs @ ab_w_in[j]
            rw = (rwkv_mu[j], rwkv_w0[j], rwkv_w_up[j], rwkv_a0[j], rwkv_a_up[j], rwkv_g_up[j],
                  rwkv_k_k[j], rwkv_k_a[j], rwkv_r_k[j], rwkv_ln_w[j], rwkv_ln_b[j])
            gd = (gdn_conv[j], gdn_a_log[j], gdn_dt_bias[j], gdn_norm[j])
            a_p, s_a = rwkv7_mix(pp[..., :A_COLS], jnp.zeros((Bp, 2, A_HEADS, A_HEAD, A_HEAD), xp.dtype), *rw)
            b_p, s_b = gdn_mix(pp[..., A_COLS:], jnp.zeros((Bp, 2, B_HEADS, B_DK, B_DV), xp.dtype), *gd)
            a_s, _ = rwkv7_mix(ps[..., :A_COLS], state_rwkv[:, j], *rw)
            b_s, _ = gdn_mix(ps[..., A_COLS:], state_gdn[:, j], *gd)
            mix_p = jnp.concatenate([a_p, b_p], axis=-1)
            mix_s = jnp.concatenate([a_s, b_s], axis=-1)
            rwkv_new.append(s_a)
            gdn_new.append(s_b)
        else:
            pp = hp @ cd_w_in[j]
            ps = hs @ cd_w_in[j]
            qn, qp, ckv, kpe = mla_split(pp[..., :C_COLS], mla_q_norm[j], mla_w_uq[j], mla_kv_norm[j])
            kn, vv = mla_expand(ckv, mla_w_ukv[j])
            c_p = mla_attend(qn, qp, kn, kpe, vv)
            cos, sin = axial_rope(xs.shape[1])
            qn_s, qp_s, ckv_s, kpe_s = mla_split(ps[..., :C_COLS], mla_q_norm[j], mla_w_uq[j], mla_kv_norm[j])
            kn_s, v_s = mla_expand(jnp.concatenate([ckv_s, cache_ckv[:, j]], axis=1), mla_w_ukv[j])
            kpe_all = jnp.concatenate([apply_rope(kpe_s, cos, sin), cache_kpe[:, j]], axis=1)
            c_s = mla_attend(qn_s, apply_rope(qp_s, cos[:, None], sin[:, None]), kn_s, kpe_all, v_s)
            hy = (hy_conv[j], hy_w1[j], hy_b1[j], hy_w2[j], hy_b2[j], hy_w3[j], hy_freq[j], hy_bias[j])
            d_p = hyena_mix(pp[..., C_COLS:], *hy)
            d_s = hyena_mix(ps[..., C_COLS:], *hy)
            mix_p = jnp.concatenate([c_p, d_p], axis=-1)
            mix_s = jnp.concatenate([c_s, d_s], axis=-1)
            ckv_new.append(ckv)
            kpe_new.append(kpe)
        xp = xp + mp[:, :, 5] * (mix_p @ w_out[i])
        xs = xs + ms[:, :, 5] * (mix_s @ w_out[i])
        xp = xp + 0.5 * mp[:, :, 8] * swiglu(pre(xp, norm_ffn2[i], mp, 2), ffn2_w_gu[i], ffn2_w_down[i])
        xs = xs + 0.5 * ms[:, :, 8] * swiglu(pre(xs, norm_ffn2[i], ms, 2), ffn2_w_gu[i], ffn2_w_down[i])
    y_prompt = rmsnorm(xp, final_norm)
    y_sample = rmsnorm(xs, final_norm)
    new_state_rwkv = jnp.stack(rwkv_new, axis=1)
    new_state_gdn = jnp.stack(gdn_new, axis=1)
    new_cache_ckv = jnp.stack(ckv_new, axis=1)
    new_cache_kpe = jnp.stack(kpe_new, axis=1)
    return (y_prompt, y_sample, new_state_rwkv, new_state_gdn, new_cache_ckv, new_cache_kpe)
```

```python
import contextlib
import numpy as np
import concourse.bass as bass
import concourse.mybir as mybir
from concourse.bass_utils import run_bass_kernel_spmd

F32 = mybir.dt.float32
BF16 = mybir.dt.bfloat16
F32R = mybir.dt.float32r
AF = mybir.ActivationFunctionType
ALU = mybir.AluOpType

NCORE = 8
D = 1024
DFF = 2816
TOK = 2048
HALF = 1024
EPS = 1e-6


class KB:
    def __init__(self, nc, es, n_dma_sems=24):
        self.nc = nc
        self.es = es
        self.eng = {"pe": nc.tensor, "act": nc.scalar, "dve": nc.vector, "pool": nc.gpsimd, "sp": nc.sync}
        self.sem = {k: es.enter_context(nc.semaphore("sem_" + k)) for k in self.eng}
        self.cnt = {k: 0 for k in self.eng}
        self.seen = {k: {} for k in self.eng}
        self.dsem = [es.enter_context(nc.semaphore("dsem%d" % i)) for i in range(n_dma_sems)]
        self.dcnt = [0] * n_dma_sems
        self.drr = {"sp": 0, "pool": 0, "act": 0}
        self.dpool = {"sp": list(range(0, n_dma_sems // 2)), "act": list(range(0, n_dma_sems // 2)),
                      "pool": list(range(n_dma_sems // 2, n_dma_sems))}
        self.recs = {}
        self.semobj = {}
        for k in self.eng:
            self.semobj[k] = self.sem[k]
        for i, s in enumerate(self.dsem):
            self.semobj[("d", i)] = s
        self.ninst = 0
        self.rt = set()

    def R(self, ap):
        if ap is not None and not isinstance(ap, (int, float)) and ap.tensor.name in self.rt:
            return ap.bitcast(F32R)
        return ap

    @staticmethod
    def box(ap):
        t = ap.tensor
        dims = list(ap.ap)
        if type(t).__name__.startswith("DRam"):
            f0 = ap.offset
            f1 = f0 + sum((c - 1) * abs(s) for s, c in dims) + 1
            return (t.name, 0, 1, f0, f1)
        ps, pc = dims[0]
        if ps == 0:
            ps = 1 << 40
        if type(t).__name__.startswith("PSum"):
            return (t.name, 0, 128, 0, 1 << 30)
        p0 = ap.offset // ps
        f0 = ap.offset % ps
        f1 = f0 + sum((c - 1) * abs(s) for s, c in dims[1:]) + 1
        return (t.name, p0, p0 + pc, f0, f1)

    def _deps(self, b, write, deps, eng=None):
        name, p0, p1, f0, f1 = b
        lst = self.recs.get(name)
        if not lst:
            return
        psum = f1 == (1 << 30)
        for r in lst:
            if r[0] < p1 and p0 < r[1] and r[2] < f1 and f0 < r[3]:
                if write or r[6] or (psum and r[7] != eng):
                    k = r[4]
                    if deps.get(k, 0) < r[5]:
                        deps[k] = r[5]

    def _record(self, b, write, semkey, val, eng):
        name, p0, p1, f0, f1 = b
        lst = self.recs.setdefault(name, [])
        if write:
            lst[:] = [r for r in lst if not (p0 <= r[0] and r[1] <= p1 and f0 <= r[2] and r[3] <= f1)]
        else:
            lst[:] = [r for r in lst if not ((not r[6]) and r[7] == eng and r[4] == semkey
                                             and p0 <= r[0] and r[1] <= p1 and f0 <= r[2] and r[3] <= f1)]
        lst.append((p0, p1, f0, f1, semkey, val, write, eng))

    def _wait(self, e, deps):
        seen = self.seen[e]
        for k, v in deps.items():
            if e == "pe" and k == "pe":
                continue
            if seen.get(k, 0) < v:
                self.eng[e].wait_ge(self.semobj[k], v)
                seen[k] = v

    def emit(self, e, fn, outs, ins):
        deps = {}
        ob = [self.box(a) for a in outs]
        ib = [self.box(a) for a in ins if a is not None and not isinstance(a, (int, float))]
        for b in ib:
            self._deps(b, False, deps, e)
        for b in ob:
            self._deps(b, True, deps, e)
        self._wait(e, deps)
        inst = fn()
        self.cnt[e] += 1
        inst.then_inc(self.sem[e], 1)
        v = self.cnt[e]
        for b in ib:
            self._record(b, False, e, v, e)
        for b in ob:
            self._record(b, True, e, v, e)
        self.ninst += 1
        return inst

    def dma(self, q, out, in_):
        deps = {}
        ob = self.box(out)
        ib = self.box(in_)
        self._deps(ib, False, deps)
        self._deps(ob, True, deps)
        pl = self.dpool[q]
        i = pl[self.drr[q] % len(pl)]
        self.drr[q] += 1
        k = ("d", i)
        if self.dcnt[i] > 0:
            deps[k] = max(deps.get(k, 0), self.dcnt[i])
        self._wait(q, deps)
        inst = self.eng[q].dma_start(out=out, in_=in_)
        self.dcnt[i] += 16
        inst.then_inc(self.dsem[i], 16)
        self._record(ib, False, k, self.dcnt[i], "dma")
        self._record(ob, True, k, self.dcnt[i], "dma")
        self.ninst += 1

    def barrier(self, engines=None):
        engines = engines or list(self.eng)
        for e in engines:
            deps = {o: self.cnt[o] for o in self.eng if o != e and self.cnt[o] > 0}
            for i, c in enumerate(self.dcnt):
                if c > 0:
                    deps[("d", i)] = c
            self._wait(e, deps)

    def mm(self, out, lhsT, rhs, start=True, stop=True):
        nc = self.nc
        if lhsT.tensor.name in self.rt and rhs.tensor.name in self.rt:
            lhsT, rhs = lhsT.bitcast(F32R), rhs.bitcast(F32R)
        return self.emit("pe", lambda: nc.tensor.matmul(out, lhsT, rhs, start=start, stop=stop), [out], [lhsT, rhs])

    def tr(self, out, in_, ident):
        nc = self.nc
        return self.emit("pe", lambda: nc.tensor.transpose(out, in_, ident), [out], [in_, ident])

    def act(self, out, in_, func, bias=None, scale=1.0):
        nc = self.nc
        kw = {}
        if bias is not None:
            kw["bias"] = bias
        ins = [in_]
        if bias is not None and not isinstance(bias, (int, float)):
            ins.append(bias)
        if not isinstance(scale, (int, float)):
            ins.append(scale)
        out = self.R(out)
        return self.emit("act", lambda: nc.scalar.activation(out=out, in_=in_, func=func, scale=scale, **kw), [out], ins)

    def tt(self, e, out, in0, in1, op):
        eng = self.eng[e]
        out = self.R(out)
        return self.emit(e, lambda: eng.tensor_tensor(out=out, in0=in0, in1=in1, op=op), [out], [in0, in1])

    def ts(self, e, out, in0, s1, op0, s2=None, op1=None):
        eng = self.eng[e]
        out = self.R(out)
        ins = [in0] + [s for s in (s1, s2) if s is not None and not isinstance(s, (int, float))]
        if op1 is None:
            return self.emit(e, lambda: eng.tensor_scalar(out=out, in0=in0, scalar1=s1, scalar2=None, op0=op0), [out], ins)
        return self.emit(e, lambda: eng.tensor_scalar(out=out, in0=in0, scalar1=s1, scalar2=s2, op0=op0, op1=op1), [out], ins)

    def stt(self, out, in0, scalar, in1, op0, op1):
        nc = self.nc
        out = self.R(out)
        ins = [in0, in1] + ([scalar] if not isinstance(scalar, (int, float)) else [])
        return self.emit("dve", lambda: nc.vector.scalar_tensor_tensor(out=out, in0=in0, scalar=scalar, in1=in1, op0=op0, op1=op1), [out], ins)

    def copy(self, e, out, in_):
        eng = self.eng[e]
        out = self.R(out)
        if e == "act":
            return self.emit(e, lambda: eng.copy(out=out, in_=in_), [out], [in_])
        return self.emit(e, lambda: eng.tensor_copy(out=out, in_=in_), [out], [in_])

    def recip(self, out, in_):
        nc = self.nc
        return self.emit("dve", lambda: nc.vector.reciprocal(out=out, in_=in_), [out], [in_])

    def memset(self, e, out, val):
        eng = self.eng[e]
        if out.tensor.name in self.rt:
            return self.ts("dve", out, self.ones_ap, float(val), ALU.mult)
        return self.emit(e, lambda: eng.memset(out, val), [out], [])


def fm(v, nchunk):
    return np.ascontiguousarray(np.asarray(v, np.float32).reshape(nchunk, 128).T)


def rows(v):
    v = np.asarray(v, np.float32).reshape(1, -1)
    return np.ascontiguousarray(np.repeat(v, 128, axis=0))


class Pack:
    def __init__(self):
        self.cols = []
        self.off = {}
        self.n = 0

    def add(self, name, arr):
        arr = np.asarray(arr, np.float32)
        if arr.shape[0] < 128:
            arr = np.concatenate([arr, np.zeros((128 - arr.shape[0],) + arr.shape[1:], np.float32)], 0)
        arr = arr.reshape(128, -1)
        self.off[name] = (self.n, arr.shape[1])
        self.n += arr.shape[1]
        self.cols.append(arr)

    def build(self):
        return np.ascontiguousarray(np.concatenate(self.cols, 1))


def pack_params(inp, core):
    P = Pack()
    c = inp["c"][core]
    cc = np.stack([fm(inp["c_ctx"], 8), fm(c, 8)], axis=2)
    P.add("cT", cc)
    for i in range(2):
        P.add("bmod%d" % i, fm(inp["b_mod"][i], 72))
        P.add("nf1_%d" % i, fm(inp["norm_ffn1"][i], 8))
        P.add("nmx_%d" % i, fm(inp["norm_mix"][i], 8))
        P.add("nf2_%d" % i, fm(inp["norm_ffn2"][i], 8))
    P.add("fnorm", fm(inp["final_norm"], 8))
    P.add("gconv", np.stack([fm(inp["gdn_conv"][0][t], 12) for t in range(3)], axis=1))
    P.add("gnorm", np.asarray(inp["gdn_norm"][0], np.float32).reshape(128, 1))
    P.add("galog", rows(inp["gdn_a_log"][0]))
    P.add("gdtb", rows(inp["gdn_dt_bias"][0]))
    P.add("rmu", fm(inp["rwkv_mu"][0], 15))
    P.add("rw0", np.stack([fm(inp["rwkv_w0"][0][d], 4) for d in range(2)], axis=1))
    P.add("ra0", np.stack([fm(inp["rwkv_a0"][0][d], 4) for d in range(2)], axis=1))
    P.add("rkk", fm(inp["rwkv_k_k"][0], 4))
    P.add("rka", fm(inp["rwkv_k_a"][0], 4))
    P.add("rrk", fm(inp["rwkv_r_k"][0].reshape(-1), 4))
    P.add("rlnw", fm(inp["rwkv_ln_w"][0], 4))
    P.add("rlnb", fm(inp["rwkv_ln_b"][0], 4))
    P.add("mqn", fm(inp["mla_q_norm"][0], 3))
    P.add("mkvn", fm(inp["mla_kv_norm"][0], 2))
    P.add("hconv", np.stack([fm(inp["hy_conv"][0][t], 12) for t in range(3)], axis=1))
    P.add("hbias", np.stack([fm(inp["hy_bias"][0][n], 4) for n in range(2)], axis=1))
    P.add("hb1", np.asarray(inp["hy_b1"][0], np.float32).reshape(64, 1))
    P.add("hb2", np.asarray(inp["hy_b2"][0], np.float32).reshape(64, 1))
    P.add("hfreq", np.ascontiguousarray(np.asarray(inp["hy_freq"][0], np.float32).T))
    nz0 = np.ones((128, 1), np.float32)
    nz0[0, 0] = 0.0
    P.add("nz0", nz0)
    w1p = np.zeros((128, 64), np.float32)
    w1p[:33] = np.asarray(inp["hy_w1"][0], np.float32)
    P.add("hw1", w1p)
    P.add("hw2", np.asarray(inp["hy_w2"][0], np.float32))
    hm = np.zeros((128, 2), np.float32)
    hm[:64, 0] = 1.0
    hm[64:, 1] = 1.0
    P.add("hmask", hm)
    rm = np.ones((128, 1024), np.float32)
    rm[:, ::128] = 0.0
    P.add("rmask", rm)
    return P


def make_consts():
    p = np.arange(128)[:, None]
    f = np.arange(128)[None, :]
    ident = (p == f).astype(np.float32)
    le = (p <= f).astype(np.float32)
    lt = (p < f).astype(np.float32)
    ge = (p >= f).astype(np.float32)
    gt = (p > f).astype(np.float32)
    BIG = 30000.0
    bo = ((p // 64) == (f // 64)).astype(np.float32)
    jpad = np.zeros((128, 128), np.float32)
    for i in range(16):
        jpad[64 + 2 * i + 1, 64 + 2 * i] = -1.0
        jpad[64 + 2 * i, 64 + 2 * i + 1] = 1.0
    onesE = np.zeros((128, 128), np.float32)
    onesE[:, :64] = 1.0
    onesO = np.zeros((128, 128), np.float32)
    onesO[:, 64:] = 1.0
    b32 = ((p // 32) == (f // 32)).astype(np.float32)
    m1 = (((p // 64) == (f // 64)) & ((p // 32) != (f // 32))).astype(np.float32)
    m2 = ((p // 64) != (f // 64)).astype(np.float32)
    names = ["ident", "le", "lt", "ge", "gt", "pos_gt", "pos_lt", "neg_le", "neg_ge", "bo", "jpad", "onesE", "onesO", "b32", "m1", "m2"]
    arrs = [ident, le, lt, ge, gt, BIG * (1 - gt), BIG * (1 - lt), -BIG * (1 - le), -BIG * (1 - ge), bo, jpad, onesE, onesO, b32, m1, m2]
    return names, np.ascontiguousarray(np.concatenate(arrs, 1).astype(np.float32))


def rope_tables():
    L = 1024
    row = np.repeat(np.arange(L // 64), 64).astype(np.float32)
    col = (np.arange(L) % 64).astype(np.float32)
    n = 8
    inv = (10000.0 ** (-np.arange(n, dtype=np.float32) / n)).astype(np.float32)
    ang = np.concatenate([row[:, None] * inv, col[:, None] * inv], axis=-1)
    cs = np.zeros((2, 128, L), np.float32)
    for r in range(32):
        cs[0, 64 + r] = np.cos(ang[:, r // 2])
        cs[1, 64 + r] = np.sin(ang[:, r // 2])
    return cs


def hyena_consts(L):
    import ml_dtypes
    t = np.arange(L, dtype=np.float64)
    w = 2.0 * np.pi * (t + 0.5) / (2 * L)
    ph = np.outer(t, w)
    Cm = np.cos(ph)
    Sm = np.sin(ph)
    dft = np.stack([Cm, Sm, Cm.T, Sm.T]).astype(np.float32).astype(ml_dtypes.bfloat16)
    t32 = np.arange(L, dtype=np.float32)
    t_norm = t32 / max(L - 1, 1)
    bands = np.linspace(1e-4, 16 - 1, 16, dtype=np.float32)
    ang = (np.float32(2.0 * np.pi / L) * t32[:, None] * bands).astype(np.float32)
    feats = np.concatenate([t_norm[:, None], np.cos(ang), -np.sin(ang)], axis=-1).astype(np.float32)
    min_decay = np.log(1e-2) / 1.5
    max_decay = np.log(1e-2) / 0.3
    deltas = np.abs(np.linspace(min_decay, max_decay, 512, dtype=np.float32))
    window = np.exp(-t_norm[:, None] * deltas).astype(np.float32)
    featsT = np.zeros((128, L), np.float32)
    featsT[:33] = feats.T
    return dft, featsT, window


A_COLS = 1920
SEQS = {0: [(0, 256), (256, 256), (512, 256), (768, 256)], 1: [(0, 1024)]}


def build_program(offs, NP, test=None, stage=99):
    nc = bass.Bass("TRN2", target_bir_lowering=False)
    dr = {}

    def din(name, shape, dt=F32):
        dr[name] = nc.dram_tensor(name, list(shape), dt, kind="ExternalInput").ap()
        return dr[name]

    def dout(name, shape, dt=F32):
        dr[name] = nc.dram_tensor(name, list(shape), dt, kind="ExternalOutput").ap()
        return dr[name]

    cnames, carr = make_consts()
    xT = din("xT", [D, TOK])
    prm_d = din("prm", [128, NP])
    cst_d = din("cst", [128, carr.shape[1]])
    w_mod = din("w_mod", [2, D, 9 * D])
    ffn_gu = [din("ffn1_w_gu", [2, D, 2 * DFF]), din("ffn2_w_gu", [2, D, 2 * DFF])]
    ffn_dn = [din("ffn1_w_down", [2, DFF, D]), din("ffn2_w_down", [2, DFF, D])]
    w_out_d = din("w_out", [2, D, D])
    ab_w_in = din("ab_w_in", [1, D, 3984])
    sgdn_in = din("sgdn_in", [2, 4, 128, 128])
    srwkv_in = din("srwkv_in", [2, 8, 64, 64])
    rwkv_w_up = din("rwkv_w_up", [1, 2, 64, 512])
    rwkv_a_up = din("rwkv_a_up", [1, 2, 64, 512])
    rwkv_g_up = din("rwkv_g_up", [1, 128, 512])
    srwkv_out = dout("srwkv_out", [4, 2, 8, 64, 64])
    cd_w_in = din("cd_w_in", [1, D, 2208])
    mla_w_uq = din("mla_w_uq", [1, 384, 768])
    mla_w_ukv = din("mla_w_ukv", [1, 256, 1024])
    cckvT = din("cckvT", [256, 256])
    ckpeT = din("ckpeT", [32, 256])
    ropecs = din("ropecs", [2, 128, 1024])
    hy_w1 = din("hy_w1", [1, 33, 64])
    hy_w2 = din("hy_w2", [1, 64, 64])
    hy_w3 = din("hy_w3", [1, 64, 2048])
    dftd = {256: din("dft256", [4, 256, 256], BF16), 1024: din("dft1024", [4, 1024, 1024], BF16)}
    featd = {256: din("feat256", [128, 256]), 1024: din("feat1024", [128, 1024])}
    wind = {256: din("win256", [256, 512]), 1024: din("win1024", [1024, 512])}
    ckv_out = dout("ckv_out", [256, 1024])
    kpe_out = dout("kpe_out", [32, 1024])
    yT = dout("yT", [D, TOK])
    sgdn_out = dout("sgdn_out", [4, 2, 4, 128, 128])

    es = contextlib.ExitStack()
    with es:
        kb = KB(nc, es)

        uid = [0]

        def sbx(stack, name, shape, dt=F32, r=False):
            uid[0] += 1
            nm = "%s_%d" % (name, uid[0])
            if r:
                kb.rt.add(nm)
            return stack.enter_context(nc.sbuf_tensor(nm, list(shape), dt))

        def sb(name, shape, dt=F32):
            return sbx(es, name, shape, dt)

        X = sb("X", [128, 8, TOK])
        prm = sb("prm_sb", [128, NP])
        cst = sb("cst_sb", [128, carr.shape[1]])
        ones = sb("ones", [128, 128])
        modT = sb("modT", [128, 2, 2, 72])
        cs = sb("cs", [128, 8, 2], BF16)
        sq = [sb("sq%d" % i, [128, 512]) for i in range(2)]
        rstd = sb("rstd", [128, HALF])
        tmpf = [sb("tmpf%d" % i, [128, 512]) for i in range(2)]
        coef = sb("coef", [128, 64])
        pbank = [es.enter_context(nc.psum_tensor("pb%d" % i, [128, 512], F32)) for i in range(8)]

        def P(name):
            o, n = offs[name]
            return prm[:, o:o + n]

        def C(name):
            i = cnames.index(name)
            return cst[:, i * 128:(i + 1) * 128]

        slot = [0, 0, 0]

        def pq(ch=0):
            s = slot[ch]
            slot[ch] = (s + 1) % 4
            return pbank[2 * ch][:, s * 128:(s + 1) * 128]

        kb.dma("sp", prm[:], prm_d[:, :])
        kb.dma("sp", cst[:], cst_d[:, :])
        for j in range(8):
            kb.dma("sp", X[:, j, :], xT[j * 128:(j + 1) * 128, :])
        kb.memset("dve", ones[:], 1.0)
        kb.ones_ap = ones[:]
        ident = C("ident")

        with contextlib.ExitStack() as ph:
            wm = [sbx(ph, "wm%d" % i, [128, 8, 1024], BF16) for i in range(2)]
            cT = P("cT")
            kb.act(cs[:].rearrange("p a b -> p (a b)"), cT, AF.Silu)
            for L in range(2):
                pm = pbank[7]
                pmv = pm[:, 0:144].rearrange("p (a b) -> p a b", b=2)
                for n in range(9):
                    wt = wm[n % 2]
                    kb.dma("pool", wt[:], w_mod[L, :, n * 1024:(n + 1) * 1024].rearrange("(k p) c -> p k c", p=128))
                    for j in range(8):
                        for k in range(8):
                            kb.mm(pmv[:, n * 8 + j, :], wt[:, k, j * 128:(j + 1) * 128], cs[:, k, :], start=(k == 0), stop=(k == 7))
                for v in range(2):
                    kb.tt("dve", modT[:, L, v, :], pmv[:, :, v], P("bmod%d" % L), ALU.add)
            kb.barrier()

        def rms_stats(hf):
            for nt in range(2):
                t0 = hf * HALF + nt * 512
                pst = pbank[6]
                for j in range(8):
                    s = sq[j % 2]
                    kb.act(s[:], X[:, j, t0:t0 + 512], AF.Square)
                    kb.mm(pst[:], ones[:], s[:], start=(j == 0), stop=(j == 7))
                kb.act(rstd[:, nt * 512:(nt + 1) * 512], pst[:], AF.Sqrt, bias=EPS, scale=1.0 / D)
            kb.recip(rstd[:], rstd[:])

        def make_coefs(L, hf, sub, gname):
            m = modT[:, L, hf, :]
            kb.stt(coef[:, 0:8], m[:, (3 * sub + 1) * 8:(3 * sub + 2) * 8], 1.0, P(gname), ALU.add, ALU.mult)
            kb.copy("dve", coef[:, 8:16], m[:, (3 * sub) * 8:(3 * sub + 1) * 8])
            kb.ts("dve", coef[:, 16:24], m[:, (3 * sub + 2) * 8:(3 * sub + 3) * 8], 0.5 if sub != 1 else 1.0, ALU.mult)

        def make_H(Hb, hf):
            for j in range(8):
                for nt in range(2):
                    t0 = hf * HALF + nt * 512
                    tf = tmpf[(j * 2 + nt) % 2]
                    kb.stt(tf[:], X[:, j, t0:t0 + 512], coef[:, j:j + 1], rstd[:, nt * 512:(nt + 1) * 512], ALU.mult, ALU.mult)
                    kb.act(Hb[:, j, nt * 512:(nt + 1) * 512], tf[:], AF.Identity, bias=coef[:, 8 + j:9 + j], scale=1.0)

        gu_groups = [(0, 3), (3, 3), (6, 3), (9, 2)]

        def ffn_phase(L, which):
            with contextlib.ExitStack() as ph:
                Hb = sbx(ph, "Hb", [128, 8, HALF], BF16)
                hh = sbx(ph, "hh", [128, 11, HALF], BF16)
                wgu = [sbx(ph, "wgu%d" % i, [128, 8, 2, 384], BF16) for i in range(2)]
                wdn = sbx(ph, "wdn", [128, 11, D], BF16)
                sgt = [sbx(ph, "sgt%d" % i, [128, 512], BF16) for i in range(2)]
                sub = 0 if which == 0 else 2
                wg = ffn_gu[which]
                wd = ffn_dn[which]
                gi = 0
                for hf in range(2):
                    rms_stats(hf)
                    make_coefs(L, hf, sub, ("nf1_%d" if which == 0 else "nf2_%d") % L)
                    make_H(Hb, hf)
                    for fh in range(2):
                        kb.dma("pool", wdn[:], wd[L, fh * 1408:(fh + 1) * 1408, :].rearrange("(j p) c -> p j c", p=128))
                        for (g0, gn) in gu_groups:
                            wt = wgu[gi % 2]
                            gi += 1
                            c0 = fh * 1408 + g0 * 128
                            for gu in range(2):
                                kb.dma("pool", wt[:, :, gu, 0:gn * 128],
                                       wg[L, :, gu * DFF + c0: gu * DFF + c0 + gn * 128].rearrange("(k p) c -> p k c", p=128))
                            for jj in range(gn):
                                for nt in range(2):
                                    pg = pbank[(2 * (jj * 2 + nt)) % 4]
                                    pu = pbank[(2 * (jj * 2 + nt)) % 4 + 1]
                                    for k in range(8):
                                        kb.mm(pg[:], wt[:, k, 0, jj * 128:(jj + 1) * 128], Hb[:, k, nt * 512:(nt + 1) * 512], start=(k == 0), stop=(k == 7))
                                    for k in range(8):
                                        kb.mm(pu[:], wt[:, k, 1, jj * 128:(jj + 1) * 128], Hb[:, k, nt * 512:(nt + 1) * 512], start=(k == 0), stop=(k == 7))
                                    sg = sgt[(jj * 2 + nt) % 2]
                                    kb.act(sg[:], pg[:], AF.Silu)
                                    kb.tt("dve", hh[:, g0 + jj, nt * 512:(nt + 1) * 512], sg[:], pu[:], ALU.mult)
                        for m in range(8):
                            for nt in range(2):
                                po = pbank[4 + (m * 2 + nt) % 2]
                                for j in range(11):
                                    kb.mm(po[:], wdn[:, j, m * 128:(m + 1) * 128], hh[:, j, nt * 512:(nt + 1) * 512], start=(j == 0), stop=(j == 10))
                                t0 = hf * HALF + nt * 512
                                kb.stt(X[:, m, t0:t0 + 512], po[:], coef[:, 16 + m:17 + m], X[:, m, t0:t0 + 512], ALU.mult, ALU.add)
                kb.barrier()

        def interleave(gens):
            gens = list(gens)
            while gens:
                for g in list(gens):
                    try:
                        next(g)
                    except StopIteration:
                        gens.remove(g)

        hslot = [0, 0, 0]

        def pq2(ch=0):
            k = hslot[ch]
            hslot[ch] = (k + 1) % 2
            return pbank[2 * ch + 1][:, k * 256:(k + 1) * 256]

        def tri_inv_T(ph_tiles, N, NT):
            T = ph_tiles
            ch = T["ch"]
            Nd, Pa, Pb, Mt, Tun, PXa, PXb = T["Nd"], T["Pa"], T["Pb"], T["Mt"], T["Tun"], T["PXa"], T["PXb"]
            kb.tt("dve", Nd[:], N[:], C("b32"), ALU.mult)
            kb.tt("dve", PXa[:, 0:128], NT[:], C("b32"), ALU.mult)
            kb.copy("dve", PXa[:, 128:256], ident)
            yield
            Pc, Pn = Nd, Pa
            PXc, PXn = PXa, PXb
            for k in range(1, 5):
                px = pq2(ch)
                kb.mm(px, Pc[:], PXc[:])
                pp = pq(ch)
                kb.mm(pp, PXc[:, 0:128], Pc[:])
                if k < 4:
                    kb.copy("act", PXn[:, 0:128], px[:, 0:128])
                kb.tt("dve", PXn[:, 128:256], px[:, 128:256], PXc[:, 128:256], ALU.add)
                kb.copy("act", Pn[:], pp)
                yield
                Pc = Pn
                Pn = Pb if Pn is Pa else Pa
                PXc, PXn = PXn, PXc
            Xa = PXc[:, 128:256]
            pf = pq(ch)
            kb.mm(pf, Pc[:], Xa)
            kb.tt("dve", Xa, pf, Xa, ALU.add)
            yield
            for mname in ("m1", "m2"):
                kb.tt("dve", Nd[:], N[:], C(mname), ALU.mult)
                pm = pq(ch)
                kb.mm(pm, Nd[:], Xa)
                kb.copy("act", Mt[:], pm)
                ptr = pq(ch)
                kb.tr(ptr, Xa, ident)
                kb.copy("act", Tun[:], ptr)
                yield
                pw = pq(ch)
                kb.mm(pw, Tun[:], Mt[:])
                kb.tt("dve", Xa, pw, Xa, ALU.add)
                yield
            T["X"] = Xa

        def inv_tile_set(fr, pfx, ch=0):
            d = {n: fr(pfx + n, [128, 128]) for n in ["Nd", "Pa", "Pb", "Mt", "Tun"]}
            d["ch"] = ch
            d["PXa"] = fr(pfx + "PXa", [128, 256])
            d["PXb"] = fr(pfx + "PXb", [128, 256])
            return d

        def gdn(hf, Hb, wo_apply):
            with contextlib.ExitStack() as ph:
                f = lambda name, shape, dt=F32, r=False: sbx(ph, "g_" + name, shape, dt, r)
                fr = lambda name, shape: f(name, shape, F32, True)
                wb = [f("wb%d" % i, [128, 8, 4, 128], BF16) for i in range(1)]
                wab = f("wab", [128, 8, 16], BF16)
                praw = f("praw", [128, HALF])
                cv = f("cv", [128, HALF])
                qF, kF, vF, zF = fr("qF", [128, HALF]), fr("kF", [128, HALF]), f("vF", [128, HALF]), f("zF", [128, HALF])
                oacc = f("oacc", [128, HALF])
                mch = f("mch", [128, HALF], BF16)
                rn = f("rn", [128, HALF])
                abt = f("abt", [128, 8, 16])
                gsb = f("gsb", [128, 2, 8, 4])
                bsb = f("bsb", [128, 2, 8, 4])
                nbsb = f("nbsb", [128, 2, 8, 4])
                Gsb = f("Gsb", [128, 2, 8, 4])
                Gtot = f("Gtot", [128, 2, 8, 4])
                eG = f("eG", [128, 2, 8, 4])
                beG = f("beG", [128, 2, 8, 4])
                eGL = f("eGL", [128, 2, 8, 4])
                eGlast = f("eGlast", [128, 2, 8, 4])
                ea = f("ea", [128, 8])
                tsm = f("tsm", [128, 2, 8, 4])
                kT = f("kT", [128, 8, 128])
                vT = f("vT", [128, 8, 128])
                st_qkT = f("st_qkT", [128, 8, 128], BF16)
                st_u = f("st_u", [128, 8, 128])
                st_nwT = f("st_nwT", [128, 8, 128], BF16)
                st_qin = f("st_qin", [128, 8, 128], BF16)
                st_kout = f("st_kout", [128, 8, 128], BF16)
                S = f("S", [128, 128])
                Sb = f("Sb", [128, 128], BF16)
                NCH = 3
                t_vnew = f("t_vnew", [128, 128], BF16)
                chT = []
                for ci in range(NCH):
                    T = {n: f("c%d_%s" % (ci, n), [128, 128]) for n in ["diag", "d1", "d2", "E1", "E2", "eGr"]}
                    T.update({n: fr("c%d_%s" % (ci, n), [128, 128]) for n in ["N", "NT", "vb", "kbg"]})
                    T["inv"] = inv_tile_set(fr, "c%d_iv" % ci, ci)
                    T["ch"] = ci
                    chT.append(T)
                c_ab = A_COLS + 2048
                kb.dma("pool", wab[:], ab_w_in[0, :, c_ab:c_ab + 16].rearrange("(k p) c -> p k c", p=128))
                pab = pbank[6]
                for tt in range(8):
                    for k in range(8):
                        kb.mm(pab[:, tt * 16:(tt + 1) * 16], Hb[:, k, tt * 128:(tt + 1) * 128], wab[:, k, :], start=(k == 0), stop=(k == 7))
                kb.copy("dve", abt[:].rearrange("p a b -> p (a b)"), pab[:, 0:128])
                abv = abt[:].rearrange("p t (d a h) -> p t d a h", d=2, a=2)
                kb.act(ea[:], P("galog"), AF.Exp)
                for d in range(2):
                    dtb = P("gdtb")[:, d * 4:(d + 1) * 4]
                    for tt in range(8):
                        kb.tt("dve", tsm[:, d, tt, :], abv[:, tt, d, 0, :], dtb, ALU.add)
                        kb.copy("dve", bsb[:, d, tt, :], abv[:, tt, d, 1, :])
                g2 = lambda t: t[:].rearrange("p d t h -> p (d t h)")
                kb.act(g2(tsm), g2(tsm), AF.Exp)
                kb.act(g2(tsm), g2(tsm), AF.Ln, bias=1.0)
                for d in range(2):
                    for tt in range(8):
                        kb.stt(gsb[:, d, tt, :], tsm[:, d, tt, :], -1.0, ea[:, d * 4:(d + 1) * 4], ALU.mult, ALU.mult)
                kb.act(g2(bsb), g2(bsb), AF.Sigmoid)
                kb.ts("dve", g2(nbsb), g2(bsb), -1.0, ALU.mult)
                pG = pbank[6]
                g3 = lambda t, d: t[:, d, :, :].rearrange("p t h -> p (t h)")
                kb.mm(pG[:, 0:32], C("le"), g3(gsb, 0))
                kb.mm(pG[:, 32:64], C("ge"), g3(gsb, 1))
                kb.mm(pG[:, 64:128], ones[:], g2(gsb))
                kb.copy("dve", g2(Gsb), pG[:, 0:64])
                kb.copy("dve", g2(Gtot), pG[:, 64:128])
                kb.act(g2(eG), g2(Gsb), AF.Exp)
                kb.tt("dve", g2(beG), g2(eG), g2(bsb), ALU.mult)
                kb.tt("dve", g2(eGL), g2(Gtot), g2(Gsb), ALU.subtract)
                kb.act(g2(eGL), g2(eGL), AF.Exp)
                kb.act(g2(eGlast), g2(Gtot), AF.Exp)

                for h in range(4):
                    wt = wb[0]
                    for qi in range(4):
                        c0 = A_COLS + qi * 512 + h * 128
                        kb.dma("pool", wt[:, :, qi, :], ab_w_in[0, :, c0:c0 + 128].rearrange("(k p) c -> p k c", p=128))
                    gc = P("gconv").rearrange("p (t c) -> p t c", t=3)
                    for qi, dst in enumerate([qF, kF, vF, zF]):
                        for nt in range(2):
                            pp = pbank[nt]
                            for k in range(8):
                                kb.mm(pp[:], wt[:, k, qi, :], Hb[:, k, nt * 512:(nt + 1) * 512], start=(k == 0), stop=(k == 7))
                            if qi == 3:
                                kb.act(zF[:, nt * 512:(nt + 1) * 512], pp[:], AF.Silu)
                            else:
                                kb.copy("act", praw[:, nt * 512:(nt + 1) * 512], pp[:])
                        if qi == 3:
                            continue
                        cc = qi * 4 + h
                        kb.ts("dve", cv[:], praw[:], gc[:, 1, cc:cc + 1], ALU.mult)
                        for (s0, Ls) in SEQS[hf]:
                            kb.stt(cv[:, s0 + 1:s0 + Ls], praw[:, s0:s0 + Ls - 1], gc[:, 0, cc:cc + 1], cv[:, s0 + 1:s0 + Ls], ALU.mult, ALU.add)
                            kb.stt(cv[:, s0:s0 + Ls - 1], praw[:, s0 + 1:s0 + Ls], gc[:, 2, cc:cc + 1], cv[:, s0:s0 + Ls - 1], ALU.mult, ALU.add)
                        kb.act(dst[:], cv[:], AF.Silu)
                        if qi < 2:
                            for nt in range(2):
                                s = sq[nt]
                                kb.act(s[:], dst[:, nt * 512:(nt + 1) * 512], AF.Square)
                                pst = pbank[2 + nt]
                                kb.mm(pst[:], ones[:], s[:])
                                kb.act(rn[:, nt * 512:(nt + 1) * 512], pst[:], AF.Sqrt, bias=1e-6, scale=1.0)
                            kb.recip(rn[:], rn[:])
                            kb.stt(dst[:], dst[:], (128.0 ** -0.5) if qi == 0 else 1.0, rn[:], ALU.mult, ALU.mult)
                    for tt in range(8):
                        p1 = pq(0)
                        kb.tr(p1, kF[:, tt * 128:(tt + 1) * 128], ident)
                        kb.copy("act", kT[:, tt, :], p1)
                        p2 = pq(1)
                        kb.tr(p2, vF[:, tt * 128:(tt + 1) * 128], ident)
                        kb.copy("act", vT[:, tt, :], p2)
                    for d in range(2):
                        posm = C("pos_gt") if d == 0 else C("pos_lt")
                        negm = C("neg_le") if d == 0 else C("neg_ge")
                        def pre_tile(tt, T, d=d, h=h, posm=posm, negm=negm):
                            tsl = slice(tt * 128, (tt + 1) * 128)
                            col = lambda t: t[:, d, tt, h:h + 1]
                            kb.ts("dve", T["diag"][:], ident, col(Gsb), ALU.mult)
                            prb = pq(T["ch"])
                            kb.mm(prb, ones[:], T["diag"][:])
                            yield
                            kb.stt(T["d1"][:], prb, col(Gsb), posm, ALU.subtract, ALU.add)
                            kb.act(T["E1"][:], T["d1"][:], AF.Exp, scale=-1.0)
                            kb.stt(T["d2"][:], prb, col(Gsb), negm, ALU.subtract, ALU.add)
                            kb.act(T["E2"][:], T["d2"][:], AF.Exp)
                            kb.act(T["eGr"][:], prb, AF.Exp)
                            pkk = pq(T["ch"])
                            kb.mm(pkk, kF[:, tsl], kF[:, tsl])
                            yield
                            kb.stt(T["N"][:], pkk, col(nbsb), T["E1"][:], ALU.mult, ALU.mult)
                            pnt = pq(T["ch"])
                            kb.tr(pnt, T["N"][:], ident)
                            kb.copy("act", T["NT"][:], pnt)
                            pqk = pq(T["ch"])
                            kb.mm(pqk, kF[:, tsl], qF[:, tsl])
                            kb.tt("dve", st_qkT[:, tt, :], pqk, T["E2"][:], ALU.mult)
                            kb.ts("dve", T["vb"][:], vT[:, tt, :], col(bsb), ALU.mult)
                            kb.ts("dve", T["kbg"][:], kT[:, tt, :], col(beG), ALU.mult)
                            kb.ts("dve", st_kout[:, tt, :], kT[:, tt, :], col(eGL), ALU.mult)
                            kb.tt("dve", st_qin[:, tt, :], qF[:, tsl], T["eGr"][:], ALU.mult)
                            yield
                            yield from tri_inv_T(T["inv"], T["N"], T["NT"])
                            Xi = T["inv"]["X"]
                            pu_ = pq(T["ch"])
                            kb.mm(pu_, Xi, T["vb"][:])
                            kb.copy("act", st_u[:, tt, :], pu_)
                            pw_ = pq(T["ch"])
                            kb.mm(pw_, T["kbg"][:], Xi)
                            kb.ts("dve", st_nwT[:, tt, :], pw_, -1.0, ALU.mult)
                            yield

                        for t0_ in range(0, 8, NCH):
                            interleave([pre_tile(tt, chT[ci]) for ci, tt in enumerate(range(t0_, min(8, t0_ + NCH)))])
                        for si, (s0, Ls) in enumerate(SEQS[hf]):
                            tiles = list(range(s0 // 128, (s0 + Ls) // 128))
                            if d == 1:
                                tiles = tiles[::-1]
                            if hf == 0:
                                kb.memset("dve", S[:], 0.0)
                            else:
                                kb.dma("sp", S[:], sgdn_in[d, h, :, :])
                            kb.copy("dve", Sb[:], S[:])
                            for tt in tiles:
                                tsl = slice(tt * 128, (tt + 1) * 128)
                                pv = pq(0)
                                kb.mm(pv, st_nwT[:, tt, :], Sb[:])
                                kb.tt("dve", t_vnew[:], pv, st_u[:, tt, :], ALU.add)
                                po_ = pq(1)
                                kb.mm(po_, Sb[:], st_qin[:, tt, :], start=True, stop=False)
                                kb.mm(po_, t_vnew[:], st_qkT[:, tt, :], start=False, stop=True)
                                if d == 0:
                                    kb.copy("act", oacc[:, tsl], po_)
                                else:
                                    kb.tt("dve", oacc[:, tsl], po_, oacc[:, tsl], ALU.add)
                                ps_ = pq(2)
                                kb.mm(ps_, st_kout[:, tt, :], t_vnew[:])
                                kb.stt(S[:], S[:], eGlast[:, d, tt, h:h + 1], ps_, ALU.mult, ALU.add)
                                kb.copy("act", Sb[:], S[:])
                            if hf == 0:
                                kb.dma("sp", sgdn_out[si, d, h, :, :], S[:])
                    for nt in range(2):
                        s = sq[nt]
                        kb.act(s[:], oacc[:, nt * 512:(nt + 1) * 512], AF.Square)
                        pst = pbank[2 + nt]
                        kb.mm(pst[:], ones[:], s[:])
                        kb.act(rn[:, nt * 512:(nt + 1) * 512], pst[:], AF.Sqrt, bias=EPS, scale=1.0 / 128)
                    kb.recip(rn[:], rn[:])
                    kb.stt(oacc[:], oacc[:], P("gnorm"), rn[:], ALU.mult, ALU.mult)
                    kb.tt("dve", mch[:], oacc[:], zF[:], ALU.mult)
                    wo_apply(4 + h, mch)
                kb.barrier()

        def rwkv(hf, Hb, wo_apply):
            with contextlib.ExitStack() as ph:
                f = lambda name, shape, dt=F32, r=False: sbx(ph, "r_" + name, shape, dt, r)
                fr = lambda name, shape: f(name, shape, F32, True)
                wch = [f("wch%d" % i, [128, 8, 128], BF16) for i in range(2)]
                wup, aup, gup = f("wup", [128, 512], BF16), f("aup", [128, 512], BF16), f("gup", [128, 512], BF16)
                twd, tad, sgd = f("twd", [128, HALF], BF16), f("tad", [128, HALF], BF16), f("sgd", [128, HALF], BF16)
                T1 = f("T1", [128, HALF])
                rF, kF0, vF, kkF = f("rF", [128, HALF]), f("kF0", [128, HALF]), f("vF", [128, HALF]), f("kkF", [128, HALF])
                asum, yacc = f("asum", [128, HALF]), f("yacc", [128, HALF])
                lw, G, kd, bF = f("lw", [128, HALF]), f("G", [128, HALF]), f("kd", [128, HALF]), f("bF", [128, HALF])
                mch = f("mch", [128, HALF], BF16)
                omm, hmu, omka = f("omm", [128, 15]), f("hmu", [128, 15]), f("omka", [128, 4])
                Mp = f("Mp", [128, 128])
                Mpb = f("Mpb", [128, 128], BF16)
                pC = f("pC", [128, 1])
                nm = ["eX", "Rt", "Ct", "kinv", "binv", "khat", "bhat", "Z"]
                bset = {"Rt", "Ct", "kinv", "binv", "Z"}
                tl = {n: f("t_" + n, [128, 128], BF16 if n in bset else F32) for n in nm}
                hd = [{n: f("h%d_%s" % (h, n), [128, 128], BF16) for n in ["X", "BmT", "QKT", "QBT", "vT", "khT", "bhT", "nU"]} for h in range(2)]
                hT = []
                for h_ in range(2):
                    T = {n: fr("p%d_%s" % (h_, n), [128, 128]) for n in ["N", "NT"]}
                    T.update({n: f("p%d_%s" % (h_, n), [128, 128], BF16) for n in ["Ctm", "Rtm"]})
                    T["msk"] = [f("p%d_msk%d" % (h_, i), [128, 128]) for i in range(3)]
                    T["inv"] = inv_tile_set(fr, "p%d_iv" % h_, h_)
                    hT.append(T)
                hmask = P("hmask")
                mu = P("rmu")
                kb.ts("dve", omm[:], mu, -1.0, ALU.mult, 1.0, ALU.add)
                kb.ts("dve", hmu[:], mu, 0.5, ALU.mult)
                kb.ts("dve", omka[:], P("rka"), -1.0, ALU.mult, 1.0, ALU.add)
                for d in range(2):
                    kb.dma("pool", wup[d * 64:(d + 1) * 64, :], rwkv_w_up[0, d, :, :])
                    kb.dma("pool", aup[d * 64:(d + 1) * 64, :], rwkv_a_up[0, d, :, :])
                kb.dma("pool", gup[:], rwkv_g_up[0, :, :])
                wi = [0]

                def project_shift(chunk, dst, func=None):
                    wt = wch[wi[0] % 2]
                    wi[0] += 1
                    kb.dma("pool", wt[:], ab_w_in[0, :, chunk * 128:(chunk + 1) * 128].rearrange("(k p) c -> p k c", p=128))
                    for nt in range(2):
                        pp = pbank[nt]
                        for k in range(8):
                            kb.mm(pp[:], wt[:, k, :], Hb[:, k, nt * 512:(nt + 1) * 512], start=(k == 0), stop=(k == 7))
                        kb.copy("act", T1[:, nt * 512:(nt + 1) * 512], pp[:])
                    tgt = dst if func is None else lw
                    kb.ts("dve", tgt[:], T1[:], omm[:, chunk:chunk + 1], ALU.mult)
                    for (s0, Ls) in SEQS[hf]:
                        kb.stt(tgt[:, s0 + 1:s0 + Ls], T1[:, s0:s0 + Ls - 1], hmu[:, chunk:chunk + 1], tgt[:, s0 + 1:s0 + Ls], ALU.mult, ALU.add)
                        kb.stt(tgt[:, s0:s0 + Ls - 1], T1[:, s0 + 1:s0 + Ls], hmu[:, chunk:chunk + 1], tgt[:, s0:s0 + Ls - 1], ALU.mult, ALU.add)
                    if func is not None:
                        kb.act(dst[:], tgt[:], func)

                project_shift(12, twd, AF.Tanh)
                project_shift(13, tad, AF.Identity)
                project_shift(14, sgd, AF.Sigmoid)
                bo = C("bo")
                for c in range(4):
                    project_shift(c, rF)
                    project_shift(4 + c, kF0)
                    project_shift(8 + c, vF)
                    kb.ts("dve", kkF[:], kF0[:], P("rkk")[:, c:c + 1], ALU.mult)
                    for nt in range(2):
                        sl = slice(nt * 512, (nt + 1) * 512)
                        kb.act(sq[nt][:], kkF[:, sl], AF.Square)
                        pst = pbank[2 + nt]
                        kb.mm(pst[:], bo, sq[nt][:])
                        kb.act(T1[:, sl], pst[:], AF.Sqrt, bias=1e-6, scale=1.0)
                    kb.recip(T1[:], T1[:])
                    kb.tt("dve", kkF[:], kkF[:], T1[:], ALU.mult)
                    for d in range(2):
                        m_ts = C("gt") if d == 0 else C("lt")
                        m_st = C("lt") if d == 0 else C("gt")
                        m_in = C("le") if d == 0 else C("ge")
                        rows = slice(d * 64, (d + 1) * 64)
                        for nt in range(2):
                            sl = slice(nt * 512, (nt + 1) * 512)
                            pw = pbank[nt]
                            kb.mm(pw[:], wup[rows, c * 128:(c + 1) * 128], twd[rows, sl])
                            kb.act(lw[:, sl], pw[:], AF.Sigmoid, bias=P("rw0").rearrange("p (d c) -> p d c", d=2)[:, d, c:c + 1])
                            pa = pbank[2 + nt]
                            kb.mm(pa[:], aup[rows, c * 128:(c + 1) * 128], tad[rows, sl])
                            kb.act(T1[:, sl], pa[:], AF.Sigmoid, bias=P("ra0").rearrange("p (d c) -> p d c", d=2)[:, d, c:c + 1])
                        kb.ts("dve", lw[:], lw[:], -0.6065306597126334, ALU.mult)
                        if d == 0:
                            kb.copy("dve", asum[:], T1[:])
                        else:
                            kb.tt("dve", asum[:], asum[:], T1[:], ALU.add)
                        kb.tt("dve", bF[:], kkF[:], T1[:], ALU.mult)
                        kb.ts("dve", T1[:], T1[:], P("rka")[:, c:c + 1], ALU.mult, omka[:, c:c + 1], ALU.add)
                        kb.tt("dve", kd[:], kF0[:], T1[:], ALU.mult)
                        nc_ = nc
                        kb.emit("dve", lambda: nc_.vector.tensor_tensor_scan(out=G[:], data0=P("rmask"), data1=lw[:], initial=0.0, op0=ALU.mult, op1=ALU.add),
                                [G[:]], [P("rmask"), lw[:]])
                        if d == 1:
                            for tt in range(8):
                                tsl = slice(tt * 128, (tt + 1) * 128)
                                kb.stt(T1[:, tsl], G[:, tsl], -1.0, lw[:, tsl], ALU.mult, ALU.add)
                                kb.ts("dve", T1[:, tsl], T1[:, tsl], G[:, tt * 128 + 127:tt * 128 + 128], ALU.add)
                            kb.copy("dve", G[:], T1[:])
                        kb.tt("dve", lw[:], G[:], lw[:], ALU.subtract)
                        for h in range(2):
                            kb.memset("dve", hd[h]["nU"][:], 0.0)
                        for si, (s0, Ls) in enumerate(SEQS[hf]):
                            tiles = list(range(s0 // 128, (s0 + Ls) // 128))
                            if d == 1:
                                tiles = tiles[::-1]
                            kb.memset("dve", Mp[:], 0.0)
                            if hf == 1:
                                for h in range(2):
                                    kb.dma("sp", Mp[h * 64:(h + 1) * 64, h * 64:(h + 1) * 64], srwkv_in[d, 2 * c + h, :, :])
                            kb.copy("dve", Mpb[:], Mp[:])
                            for tt in tiles:
                                tsl = slice(tt * 128, (tt + 1) * 128)
                                e_end = tt * 128 + (127 if d == 0 else 0)
                                gtot = G[:, e_end:e_end + 1]
                                kb.act(tl["eX"][:], G[:, tsl], AF.Exp)
                                kb.tt("dve", tl["Rt"][:], rF[:, tsl], tl["eX"][:], ALU.mult)
                                kb.act(tl["eX"][:], lw[:, tsl], AF.Exp)
                                kb.tt("dve", tl["Ct"][:], kkF[:, tsl], tl["eX"][:], ALU.mult)
                                kb.act(tl["eX"][:], G[:, tsl], AF.Exp, scale=-1.0)
                                kb.tt("dve", tl["kinv"][:], kd[:, tsl], tl["eX"][:], ALU.mult)
                                kb.tt("dve", tl["binv"][:], bF[:, tsl], tl["eX"][:], ALU.mult)
                                kb.act(tl["eX"][:], G[:, tsl], AF.Exp, bias=gtot, scale=-1.0)
                                kb.tt("dve", tl["khat"][:], kd[:, tsl], tl["eX"][:], ALU.mult)
                                kb.tt("dve", tl["bhat"][:], bF[:, tsl], tl["eX"][:], ALU.mult)
                                kb.act(pC[:], gtot, AF.Exp)
                                def pre_head(h, tsl=tsl):
                                    H = hd[h]
                                    T = hT[h]
                                    hm = hmask[:, h:h + 1]
                                    kb.ts("dve", T["Ctm"][:], tl["Ct"][:], hm, ALU.mult)
                                    kb.ts("dve", T["Rtm"][:], tl["Rt"][:], hm, ALU.mult)
                                    p_ = pq(h)
                                    kb.mm(p_, T["Ctm"][:], tl["binv"][:])
                                    kb.stt(T["N"][:], p_, -1.0, m_ts, ALU.mult, ALU.mult)
                                    p_ = pq(h)
                                    kb.mm(p_, tl["binv"][:], T["Ctm"][:])
                                    kb.stt(T["NT"][:], p_, -1.0, m_st, ALU.mult, ALU.mult)
                                    yield
                                    p_ = pq(h)
                                    kb.mm(p_, tl["kinv"][:], T["Ctm"][:])
                                    kb.tt("dve", H["BmT"][:], p_, m_st, ALU.mult)
                                    p_ = pq(h)
                                    kb.mm(p_, tl["kinv"][:], T["Rtm"][:])
                                    kb.tt("dve", H["QKT"][:], p_, m_in, ALU.mult)
                                    p_ = pq(h)
                                    kb.mm(p_, tl["binv"][:], T["Rtm"][:])
                                    kb.tt("dve", H["QBT"][:], p_, m_in, ALU.mult)
                                    yield
                                    for mi, (src, dstn) in enumerate([(vF[:, tsl], "vT"), (tl["khat"][:], "khT"), (tl["bhat"][:], "bhT")]):
                                        kb.ts("dve", T["msk"][mi][:], src, hm, ALU.mult)
                                        p_ = pq(h)
                                        kb.tr(p_, T["msk"][mi][:], ident)
                                        kb.copy("act", H[dstn][:], p_)
                                    yield
                                    yield from tri_inv_T(T["inv"], T["N"], T["NT"])
                                    kb.copy("act", H["X"][:], T["inv"]["X"])
                                    yield

                                interleave([pre_head(0), pre_head(1)])
                                pz = pq(2)
                                kb.mm(pz, tl["Ct"][:], Mpb[:], start=True, stop=False)
                                kb.mm(pz, hd[0]["BmT"][:], hd[0]["vT"][:], start=False, stop=False)
                                kb.mm(pz, hd[1]["BmT"][:], hd[1]["vT"][:], start=False, stop=True)
                                kb.copy("act", tl["Z"][:], pz)
                                for h in range(2):
                                    p_ = pq(2)
                                    kb.mm(p_, hd[h]["X"][:], tl["Z"][:])
                                    kb.ts("dve", hd[h]["nU"][:, h * 64:(h + 1) * 64], p_[:, h * 64:(h + 1) * 64], -1.0, ALU.mult)
                                py = pbank[6][:, 0:128]
                                kb.mm(py, Mpb[:], tl["Rt"][:], start=True, stop=False)
                                for h in range(2):
                                    kb.mm(py, hd[h]["vT"][:], hd[h]["QKT"][:], start=False, stop=False)
                                    kb.mm(py, hd[h]["nU"][:], hd[h]["QBT"][:], start=False, stop=(h == 1))
                                if d == 0:
                                    kb.copy("act", yacc[:, tsl], py)
                                else:
                                    kb.tt("dve", yacc[:, tsl], py, yacc[:, tsl], ALU.add)
                                pm_ = pbank[7][:, 0:128]
                                for h in range(2):
                                    kb.mm(pm_, hd[h]["khT"][:], hd[h]["vT"][:], start=(h == 0), stop=False)
                                    kb.mm(pm_, hd[h]["bhT"][:], hd[h]["nU"][:], start=False, stop=(h == 1))
                                kb.stt(Mp[:], Mp[:], pC[:, 0:1], pm_, ALU.mult, ALU.add)
                                kb.copy("act", Mpb[:], Mp[:])
                            if hf == 0:
                                for h in range(2):
                                    kb.dma("sp", srwkv_out[si, d, 2 * c + h, :, :], Mp[h * 64:(h + 1) * 64, h * 64:(h + 1) * 64])
                    for nt in range(2):
                        sl = slice(nt * 512, (nt + 1) * 512)
                        pst = pbank[nt]
                        kb.mm(pst[:], bo, yacc[:, sl])
                        kb.stt(yacc[:, sl], pst[:], -1.0 / 64, yacc[:, sl], ALU.mult, ALU.add)
                        kb.act(sq[nt][:], yacc[:, sl], AF.Square)
                        pv = pbank[2 + nt]
                        kb.mm(pv[:], bo, sq[nt][:])
                        kb.act(T1[:, sl], pv[:], AF.Sqrt, bias=64e-5, scale=1.0 / 64)
                    kb.recip(T1[:], T1[:])
                    kb.stt(yacc[:], yacc[:], P("rlnw")[:, c:c + 1], T1[:], ALU.mult, ALU.mult)
                    kb.ts("dve", yacc[:], yacc[:], P("rlnb")[:, c:c + 1], ALU.add)
                    kb.ts("dve", asum[:], asum[:], 0.5, ALU.mult)
                    kb.ts("dve", asum[:], asum[:], P("rka")[:, c:c + 1], ALU.mult, omka[:, c:c + 1], ALU.add)
                    kb.tt("dve", asum[:], asum[:], kF0[:], ALU.mult)
                    kb.stt(asum[:], asum[:], P("rrk")[:, c:c + 1], rF[:], ALU.mult, ALU.mult)
                    for nt in range(2):
                        sl = slice(nt * 512, (nt + 1) * 512)
                        pb_ = pbank[nt]
                        kb.mm(pb_[:], bo, asum[:, sl])
                        kb.tt("dve", T1[:, sl], pb_[:], vF[:, sl], ALU.mult)
                        kb.tt("dve", yacc[:, sl], yacc[:, sl], T1[:, sl], ALU.add)
                        pg_ = pbank[2 + nt]
                        kb.mm(pg_[:], gup[:, c * 128:(c + 1) * 128], sgd[:, sl])
                        kb.tt("dve", mch[:, sl], pg_[:], yacc[:, sl], ALU.mult)
                    wo_apply(c, mch)
                kb.barrier()

        def mla(hf, Hb, wo_apply):
            NK = 1024 if hf == 0 else 1280
            with contextlib.ExitStack() as ph:
                f = lambda name, shape, dt=F32: sbx(ph, "m_" + name, shape, dt)
                wch = [f("wch%d" % i, [128, 8, 128], BF16) for i in range(2)]
                wuq = f("wuq", [128, 3, 768], BF16)
                wukv = f("wukv", [128, 2, 1024], BF16)
                pqF = f("pqF", [128, 3, HALF])
                qnF = f("qnF", [128, 3, HALF], BF16)
                ckvF = f("ckvF", [128, 2, HALF])
                ckvB = f("ckvB", [128, 2, 1280], BF16)
                kpe96 = f("kpe96", [128, 1280])
                kpeB = f("kpeB", [128, 1280], BF16)
                rq = f("rq", [128, HALF])
                Ve = f("Ve", [128, 10, 128], BF16)
                Vo = f("Vo", [128, 10, 128], BF16)
                qTf = f("qTf", [128, HALF])
                qTb = [f("qTb%d" % e, [128, HALF], BF16) for e in range(2)]
                KTb = [f("KTb%d" % e, [128, 1280], BF16) for e in range(2)]
                PT = [f("PT%d" % i, [128, 512], BF16) for i in range(2)]
                oE = f("oE", [128, 128], BF16)
                oO = f("oO", [128, 128], BF16)
                rs_ = f("rs", [128, 512])
                mch = f("mch", [128, HALF], BF16)
                t1, t2 = f("t1", [128, 512]), f("t2", [128, 512])
                cosT, sinT = f("cosT", [128, HALF]), f("sinT", [128, HALF])
                kb.copy("dve", oE[:], C("onesE"))
                kb.copy("dve", oO[:], C("onesO"))
                kb.memset("dve", kpe96[:], 0.0)
                kb.memset("dve", qTf[:], 0.0)
                kb.dma("pool", wuq[:], mla_w_uq[0].rearrange("(k p) c -> p k c", p=128))
                kb.dma("pool", wukv[:], mla_w_ukv[0].rearrange("(k p) c -> p k c", p=128))
                if hf == 1:
                    kb.dma("sp", cosT[:], ropecs[0, :, :])
                    kb.dma("sp", sinT[:], ropecs[1, :, :])
                wi = [0]

                def project(c0, M, dst_fn):
                    wt = wch[wi[0] % 2]
                    wi[0] += 1
                    kb.dma("pool", wt[:, :, 0:M], cd_w_in[0, :, c0:c0 + M].rearrange("(k p) c -> p k c", p=128))
                    for nt in range(2):
                        pp = pbank[4 + nt]
                        for k in range(8):
                            kb.mm(pp[0:M, :], wt[:, k, 0:M], Hb[:, k, nt * 512:(nt + 1) * 512], start=(k == 0), stop=(k == 7))
                        dst_fn(nt, pp)

                for j in range(3):
                    project(j * 128, 128, lambda nt, pp, j=j: kb.copy("act", pqF[:, j, nt * 512:(nt + 1) * 512], pp[:]))
                for j in range(2):
                    project(384 + j * 128, 128, lambda nt, pp, j=j: kb.copy("act", ckvF[:, j, nt * 512:(nt + 1) * 512], pp[:]))
                project(576, 96, lambda nt, pp: kb.copy("act", kpe96[64:96, nt * 512:(nt + 1) * 512], pp[64:96, :]))

                def rmsn(src, nj, gname, outs):
                    for nt in range(2):
                        sl = slice(nt * 512, (nt + 1) * 512)
                        pst = pbank[4 + nt]
                        for j in range(nj):
                            kb.act(sq[j % 2][:], src[:, j, sl], AF.Square)
                            kb.mm(pst[:], ones[:], sq[j % 2][:], start=(j == 0), stop=(j == nj - 1))
                        kb.act(rq[:, sl], pst[:], AF.Sqrt, bias=EPS, scale=1.0 / (nj * 128))
                    kb.recip(rq[:], rq[:])
                    for j in range(nj):
                        for o in outs:
                            kb.stt(o(j), src[:, j, :], P(gname)[:, j:j + 1], rq[:], ALU.mult, ALU.mult)

                rmsn(pqF, 3, "mqn", [lambda j: qnF[:, j, :]])
                rmsn(ckvF, 2, "mkvn", [lambda j: ckvB[:, j, 0:HALF], lambda j: ckvF[:, j, :]])
                if hf == 0:
                    for j in range(2):
                        kb.dma("sp", ckv_out[j * 128:(j + 1) * 128, :], ckvF[:, j, :])
                    kb.dma("sp", kpe_out[:, :], kpe96[64:96, 0:HALF])
                    kb.copy("dve", kpeB[64:96, 0:HALF], kpe96[64:96, 0:HALF])
                else:
                    for j in range(2):
                        kb.dma("pool", ckvB[:, j, HALF:1280], cckvT[j * 128:(j + 1) * 128, :])
                    kb.dma("sp", kpe96[64:96, HALF:1280], ckpeT[:, :])
                    kb.copy("dve", kpeB[64:96, HALF:1280], kpe96[64:96, HALF:1280])

                if stage == 1:
                    kb.barrier()
                    return

                def rope(src, dstb):
                    for nt in range(2):
                        sl = slice(nt * 512, (nt + 1) * 512)
                        pj = pbank[4 + nt]
                        kb.mm(pj[0:96, :], C("jpad")[0:96, 0:96], src[0:96, sl])
                        kb.tt("dve", t1[64:96, :], src[64:96, sl], cosT[64:96, sl], ALU.mult)
                        kb.tt("dve", t2[64:96, :], pj[64:96, :], sinT[64:96, sl], ALU.mult)
                        kb.tt("dve", dstb[64:96, sl], t1[64:96, :], t2[64:96, :], ALU.add)

                if hf == 1:
                    rope(kpe96, kpeB)
                if stage == 2:
                    kb.barrier()
                    return
                kb.memset("dve", Ve[:].rearrange("p a b -> p (a b)"), 0.0)
                kb.memset("dve", Vo[:].rearrange("p a b -> p (a b)"), 0.0)
                scale = 96.0 ** -0.5
                if hf == 0:
                    qranges = [(si * 256, 256, [2 * si, 2 * si + 1]) for si in range(4)]
                else:
                    qranges = [(nt * 512, 512, list(range(10))) for nt in range(2)]
                pti = [0]
                for c in range(4):
                    for kt in range(NK // 128):
                        pv = pbank[4 + kt % 2]
                        for e in range(2):
                            v0 = (2 * c + e) * 128 + 64
                            for k in range(2):
                                kb.mm(pv[:, e * 64:(e + 1) * 64], ckvB[:, k, kt * 128:(kt + 1) * 128], wukv[:, k, v0:v0 + 64], start=(k == 0), stop=(k == 1))
                        kb.copy("act", Ve[:, kt, 0:64], pv[:, 0:64])
                        kb.copy("dve", Vo[:, kt, 64:128], pv[:, 64:128])
                    if stage == 31:
                        continue
                    for e in range(2):
                        h = 2 * c + e
                        for nt in range(2):
                            sl = slice(nt * 512, (nt + 1) * 512)
                            pqh = pbank[4 + nt]
                            for k in range(3):
                                kb.mm(pqh[0:96, :], wuq[:, k, h * 96:(h + 1) * 96], qnF[:, k, sl], start=(k == 0), stop=(k == 2))
                            kb.copy("act", qTb[e][0:64, sl], pqh[0:64, :])
                            if hf == 1:
                                kb.copy("dve", qTf[64:96, sl], pqh[64:96, :])
                            else:
                                kb.copy("dve", qTb[e][64:96, sl], pqh[64:96, :])
                        if hf == 1:
                            rope(qTf, qTb[e])
                        if stage == 32:
                            continue
                        for k0 in range(0, NK, 512):
                            n = min(512, NK - k0)
                            pk = pbank[4 + (k0 // 512) % 2]
                            for k in range(2):
                                kb.mm(pk[:, 0:n], wukv[:, k, h * 128:(h + 1) * 128], ckvB[:, k, k0:k0 + n], start=(k == 0), stop=(k == 1))
                            kb.copy("act", KTb[e][0:64, k0:k0 + n], pk[0:64, 0:n])
                        kb.copy("dve", KTb[e][64:96, 0:NK], kpeB[64:96, 0:NK])
                    if stage == 3:
                        continue
                    for (q0, nq, kts) in qranges:
                        po, psm = pbank[2], pbank[3]
                        first = True
                        for e in range(2):
                            Vsrc = Ve if e == 0 else Vo
                            osrc = oE if e == 0 else oO
                            for ki, kt in enumerate(kts):
                                ps_ = pbank[pti[0] % 2]
                                pt = PT[pti[0] % 2]
                                pti[0] += 1
                                ksl = slice(kt * 128, (kt + 1) * 128)
                                kb.mm(ps_[:, 0:nq], KTb[e][0:96, ksl], qTb[e][0:96, q0:q0 + nq])
                                kb.act(pt[:, 0:nq], ps_[:, 0:nq], AF.Exp, scale=scale)
                                last = (e == 1 and ki == len(kts) - 1)
                                kb.mm(po[:, 0:nq], Vsrc[:, kt, :], pt[:, 0:nq], start=first, stop=last)
                                kb.mm(psm[:, 0:nq], osrc[:], pt[:, 0:nq], start=first, stop=last)
                                first = False
                        kb.recip(rs_[:, 0:nq], psm[:, 0:nq])
                        kb.tt("dve", mch[:, q0:q0 + nq], po[:, 0:nq], rs_[:, 0:nq], ALU.mult)
                    wo_apply(c, mch)
                kb.barrier()

        def hyena(hf, Hb, wo_apply):
            Ls = 256 if hf == 0 else 1024
            nT = Ls // 128
            with contextlib.ExitStack() as ph:
                f = lambda name, shape, dt=F32: sbx(ph, "y_" + name, shape, dt)
                wch = [f("wch%d" % i, [128, 8, 128], BF16) for i in range(1)]
                vF, x1F, x2F, T1 = f("vF", [128, HALF]), f("x1F", [128, HALF]), f("x2F", [128, HALF]), f("T1", [128, HALF])
                fwd = f("fwd", [128, nT, 2, Ls], BF16)
                invp = [f("invp%d" % i, [128, 2, Ls], BF16) for i in range(2)]
                w3 = f("w3", [128, 2048], BF16)
                featT = x1F[:, 0:Ls]
                h1s, h2s = T1[:, 0:Ls], x2F[:, 0:Ls]
                h2b = f("h2b", [128, Ls], BF16)
                win = f("win", [128, nT, 128])
                fr3, fb3 = f("fr3", [128, 2]), f("fb3", [128, 2])
                hsum, hdif = f("hsum", [128, nT, 128], BF16), f("hdif", [128, nT, 128], BF16)
                Hc, Hs = f("Hc", [128, nT, 128]), f("Hs", [128, nT, 128])
                zT = f("zT", [128, nT, 128], BF16)
                Yc, Ys = f("Yc", [128, nT, 128], BF16), f("Ys", [128, nT, 128], BF16)
                ta, tb, tc_ = f("ta", [128, 512]), f("tb", [128, 512]), f("tc", [128, 512])
                mch = f("mch", [128, HALF], BF16)
                for i in range(2):
                    kb.dma("sp", fwd[:, :, i, :], dftd[Ls][i].rearrange("(st p) f -> p st f", p=128))
                kb.dma("pool", w3[0:64, :], hy_w3[0, :, :])
                kb.dma("sp", featT, featd[Ls][:, :])
                kb.ts("dve", fr3[0:64, :], P("hfreq")[0:64, :], 1.0 / 3, ALU.mult)
                kb.tt("dve", fb3[0:64, 0:1], fr3[0:64, 0:1], P("hb1")[0:64, :], ALU.mult)
                kb.tt("dve", fb3[0:64, 1:2], fr3[0:64, 1:2], P("hb2")[0:64, :], ALU.mult)

                def sin3(dst, pin, li, n):
                    kb.act(ta[0:64, 0:n], pin, AF.Sin, bias=fb3[0:64, li:li + 1], scale=fr3[0:64, li:li + 1])
                    kb.tt("dve", tb[0:64, 0:n], ta[0:64, 0:n], ta[0:64, 0:n], ALU.mult)
                    kb.ts("dve", tb[0:64, 0:n], tb[0:64, 0:n], -4.0, ALU.mult, 3.0, ALU.add)
                    kb.tt("dve", dst, ta[0:64, 0:n], tb[0:64, 0:n], ALU.mult)

                for c0 in range(0, Ls, 512):
                    n = min(512, Ls - c0)
                    p1 = pbank[4]
                    kb.mm(p1[0:64, 0:n], P("hw1")[0:64, :], featT[0:64, c0:c0 + n])
                    sin3(h1s[0:64, c0:c0 + n], p1[0:64, 0:n], 0, n)
                    p2 = pbank[5]
                    kb.mm(p2[0:64, 0:n], P("hw2")[0:64, :], h1s[0:64, c0:c0 + n])
                    sin3(h2s[0:64, c0:c0 + n], p2[0:64, 0:n], 1, n)
                kb.copy("dve", h2b[0:64, :], h2s[0:64, :])
                wi = [0]
                hcv = P("hconv").rearrange("p (t c) -> p t c", t=3)
                ipi = [0]
                for cc in range(4):
                    kb.dma("sp", win[:], wind[Ls][:, cc * 128:(cc + 1) * 128].rearrange("(t p) c -> p t c", p=128))
                    for qi, dst in enumerate([vF, x1F, x2F]):
                        wt = wch[0]
                        wi[0] += 1
                        c0 = 672 + qi * 512 + cc * 128
                        kb.dma("pool", wt[:], cd_w_in[0, :, c0:c0 + 128].rearrange("(k p) c -> p k c", p=128))
                        for nt in range(2):
                            pp = pbank[4 + nt]
                            for k in range(8):
                                kb.mm(pp[:], wt[:, k, :], Hb[:, k, nt * 512:(nt + 1) * 512], start=(k == 0), stop=(k == 7))
                            kb.copy("act", T1[:, nt * 512:(nt + 1) * 512], pp[:])
                        ci = qi * 4 + cc
                        kb.ts("dve", dst[:], T1[:], hcv[:, 1, ci:ci + 1], ALU.mult)
                        for (s0, Lq) in SEQS[hf]:
                            kb.stt(dst[:, s0 + 1:s0 + Lq], T1[:, s0:s0 + Lq - 1], hcv[:, 0, ci:ci + 1], dst[:, s0 + 1:s0 + Lq], ALU.mult, ALU.add)
                            kb.stt(dst[:, s0:s0 + Lq - 1], T1[:, s0 + 1:s0 + Lq], hcv[:, 2, ci:ci + 1], dst[:, s0:s0 + Lq - 1], ALU.mult, ALU.add)
                    z = vF
                    for n in range(2):
                        gate = x1F if n == 0 else x2F
                        bcol = P("hbias").rearrange("p (n c) -> p n c", n=2)[:, n, cc:cc + 1]
                        for dt in range(nT):
                            pt_ = pbank[4 + dt % 2]
                            for di in range(2):
                                w0 = (n * 2 + di) * 512 + cc * 128
                                kb.mm(pt_[:, di * 128:(di + 1) * 128], h2b[0:64, dt * 128:(dt + 1) * 128], w3[0:64, w0:w0 + 128])
                            kb.tt("dve", ta[:, 0:128], pt_[:, 0:128], win[:, dt, :], ALU.mult)
                            kb.tt("dve", tb[:, 0:128], pt_[:, 128:256], win[:, dt, :], ALU.mult)
                            if dt == 0:
                                kb.ts("dve", tb[:, 0:128], tb[:, 0:128], P("nz0"), ALU.mult)
                            kb.tt("dve", hsum[:, dt, :], ta[:, 0:128], tb[:, 0:128], ALU.add)
                            kb.tt("dve", hdif[:, dt, :], ta[:, 0:128], tb[:, 0:128], ALU.subtract)
                        for ft in range(nT):
                            pc_ = pbank[4 + ft % 2]
                            for dt in range(nT):
                                kb.mm(pc_[:, 0:128], fwd[:, dt, 0, ft * 128:(ft + 1) * 128], hsum[:, dt, :], start=(dt == 0), stop=(dt == nT - 1))
                            kb.copy("act", Hc[:, ft, :], pc_[:, 0:128])
                            for dt in range(nT):
                                kb.mm(pc_[:, 128:256], fwd[:, dt, 1, ft * 128:(ft + 1) * 128], hdif[:, dt, :], start=(dt == 0), stop=(dt == nT - 1))
                            kb.copy("act", Hs[:, ft, :], pc_[:, 128:256])
                        for (s0, Lq) in SEQS[hf]:
                            for st in range(nT):
                                ptr = pbank[4 + st % 2]
                                kb.tr(ptr[:, 0:128], z[:, s0 + st * 128:s0 + (st + 1) * 128], ident)
                                kb.copy("act", zT[:, st, :], ptr[:, 0:128])
                            for ft in range(nT):
                                pzc, pzs = pbank[2], pbank[3]
                                for st in range(nT):
                                    kb.mm(pzc[:, 0:128], fwd[:, st, 0, ft * 128:(ft + 1) * 128], zT[:, st, :], start=(st == 0), stop=(st == nT - 1))
                                for st in range(nT):
                                    kb.mm(pzs[:, 0:128], fwd[:, st, 1, ft * 128:(ft + 1) * 128], zT[:, st, :], start=(st == 0), stop=(st == nT - 1))
                                kb.tt("dve", ta[:, 0:128], pzc[:, 0:128], Hc[:, ft, :], ALU.mult)
                                kb.tt("dve", tb[:, 0:128], pzs[:, 0:128], Hs[:, ft, :], ALU.mult)
                                kb.tt("dve", Yc[:, ft, :], ta[:, 0:128], tb[:, 0:128], ALU.subtract)
                                kb.tt("dve", ta[:, 0:128], pzc[:, 0:128], Hs[:, ft, :], ALU.mult)
                                kb.tt("dve", tb[:, 0:128], pzs[:, 0:128], Hc[:, ft, :], ALU.mult)
                                kb.tt("dve", Ys[:, ft, :], ta[:, 0:128], tb[:, 0:128], ALU.add)
                            blocks = [(b0, min(512, Lq - b0)) for b0 in range(0, Lq, 512)]
                            pys = [pbank[0], pbank[1]]
                            for ft in range(nT):
                                ip = invp[ipi[0] % 2]
                                ipi[0] += 1
                                for i in range(2):
                                    kb.dma("sp", ip[:, i, :], dftd[Ls][2 + i, ft * 128:(ft + 1) * 128, :])
                                for bi, (b0, bn) in enumerate(blocks):
                                    kb.mm(pys[bi][:, 0:bn], Yc[:, ft, :], ip[:, 0, b0:b0 + bn], start=(ft == 0), stop=False)
                                    kb.mm(pys[bi][:, 0:bn], Ys[:, ft, :], ip[:, 1, b0:b0 + bn], start=False, stop=(ft == nT - 1))
                            for bi, (b0, bn) in enumerate(blocks):
                                zs = z[:, s0 + b0:s0 + b0 + bn]
                                kb.ts("dve", tc_[:, 0:bn], zs, bcol, ALU.mult)
                                kb.stt(tc_[:, 0:bn], pys[bi][:, 0:bn], 1.0 / Lq, tc_[:, 0:bn], ALU.mult, ALU.add)
                                kb.tt("dve", zs, tc_[:, 0:bn], gate[:, s0 + b0:s0 + b0 + bn], ALU.mult)
                    kb.copy("dve", mch[:], z[:])
                    wo_apply(4 + cc, mch)
                kb.barrier()

        def mixer_phase(L):
            with contextlib.ExitStack() as ph:
                Hb = sbx(ph, "Hbm", [128, 8, HALF], BF16)
                wo = [sbx(ph, "wo%d" % i, [128, D], BF16) for i in range(2)]
                woi = [0]
                for hf in range(2):
                    rms_stats(hf)
                    make_coefs(L, hf, 1, "nmx_%d" % L)
                    make_H(Hb, hf)

                    def wo_apply(j, mc, hf=hf):
                        w = wo[woi[0] % 2]
                        woi[0] += 1
                        kb.dma("pool", w[:], w_out_d[L, j * 128:(j + 1) * 128, :])
                        for m in range(8):
                            for nt in range(2):
                                po = pbank[6 + (m * 2 + nt) % 2]
                                kb.mm(po[:], w[:, m * 128:(m + 1) * 128], mc[:, nt * 512:(nt + 1) * 512])
                                t0 = hf * HALF + nt * 512
                                kb.stt(X[:, m, t0:t0 + 512], po[:], coef[:, 16 + m:17 + m], X[:, m, t0:t0 + 512], ALU.mult, ALU.add)

                    if L == 0:
                        if test in (None, "rwkv", "l0"):
                            rwkv(hf, Hb, wo_apply)
                        if test in (None, "gdn", "l0"):
                            gdn(hf, Hb, wo_apply)
                    else:
                        if test in (None, "mla", "l1"):
                            mla(hf, Hb, wo_apply)
                        if test in (None, "hy", "l1"):
                            hyena(hf, Hb, wo_apply)
                kb.barrier()

        def final_norm_and_store():
            for hf in range(2):
                rms_stats(hf)
                for j in range(8):
                    for nt in range(2):
                        t0 = hf * HALF + nt * 512
                        tf = tmpf[(j * 2 + nt) % 2]
                        kb.stt(tf[:], X[:, j, t0:t0 + 512], P("fnorm")[:, j:j + 1], rstd[:, nt * 512:(nt + 1) * 512], ALU.mult, ALU.mult)
                        kb.dma("sp", yT[j * 128:(j + 1) * 128, t0:t0 + 512], tf[:])

        if test in ("gdn", "rwkv", "l0"):
            mixer_phase(0)
        elif test in ("mla", "hy", "l1"):
            mixer_phase(1)
        else:
            for L in range(2):
                ffn_phase(L, 0)
                mixer_phase(L)
                ffn_phase(L, 1)
        final_norm_and_store()
        kb.barrier()
        print("instructions:", kb.ninst)
    return nc


def prep_inputs(inp):
    inp = {k: np.asarray(v) for k, v in inp.items()}
    maps = []
    offs = None
    cnames, carr = make_consts()
    ropecs = rope_tables()
    hc = {L: hyena_consts(L) for L in (256, 1024)}
    for c in range(NCORE):
        xp = inp["x_prompt"][4 * c:4 * c + 4].reshape(HALF, D)
        xs = inp["x_sample"][c]
        xT = np.ascontiguousarray(np.concatenate([xp, xs], 0).T)
        Pk = pack_params(inp, c)
        prm = Pk.build()
        offs = Pk.off
        m = {"xT": xT, "prm": prm, "cst": carr, "w_mod": inp["w_mod"],
             "ffn1_w_gu": inp["ffn1_w_gu"], "ffn2_w_gu": inp["ffn2_w_gu"],
             "ffn1_w_down": inp["ffn1_w_down"], "ffn2_w_down": inp["ffn2_w_down"],
             "w_out": inp["w_out"], "ab_w_in": inp["ab_w_in"],
             "sgdn_in": np.ascontiguousarray(inp["state_gdn"][c, 0]),
             "srwkv_in": np.ascontiguousarray(inp["state_rwkv"][c, 0].transpose(0, 1, 3, 2)),
             "rwkv_w_up": inp["rwkv_w_up"], "rwkv_a_up": inp["rwkv_a_up"], "rwkv_g_up": inp["rwkv_g_up"],
             "cd_w_in": inp["cd_w_in"], "mla_w_uq": inp["mla_w_uq"], "mla_w_ukv": inp["mla_w_ukv"],
             "cckvT": np.ascontiguousarray(inp["cache_ckv"][c, 0].T), "ckpeT": np.ascontiguousarray(inp["cache_kpe"][c, 0].T),
             "ropecs": ropecs, "hy_w1": inp["hy_w1"], "hy_w2": inp["hy_w2"], "hy_w3": inp["hy_w3"],
             "dft256": hc[256][0], "dft1024": hc[1024][0], "feat256": hc[256][1], "feat1024": hc[1024][1],
             "win256": hc[256][2], "win1024": hc[1024][2]}
        maps.append(m)
    return maps, offs


def kernel(**inputs):
    maps, offs = prep_inputs(inputs)
    NP = maps[0]["prm"].shape[1]
    nc = build_program(offs, NP)
    res = run_bass_kernel_spmd(nc, maps, core_ids=list(range(NCORE)))
    yp = np.zeros((32, 256, D), np.float32)
    ys = np.zeros((8, 1024, D), np.float32)
    srw = np.zeros((32, 1, 2, 8, 64, 64), np.float32)
    sgd = np.zeros((32, 1, 2, 4, 128, 128), np.float32)
    ckv = np.zeros((32, 1, 256, 256), np.float32)
    kpe = np.zeros((32, 1, 256, 32), np.float32)
    for c in range(NCORE):
        r = res.results[c]
        yT = r["yT"]
        yp[4 * c:4 * c + 4] = yT[:, :HALF].T.reshape(4, 256, D)
        ys[c] = yT[:, HALF:].T
        srw[4 * c:4 * c + 4, 0] = np.asarray(r["srwkv_out"]).transpose(0, 1, 2, 4, 3)
        sgd[4 * c:4 * c + 4, 0] = np.asarray(r["sgdn_out"])
        ckv[4 * c:4 * c + 4, 0] = np.asarray(r["ckv_out"]).T.reshape(4, 256, 256)
        kpe[4 * c:4 * c + 4, 0] = np.asarray(r["kpe_out"]).T.reshape(4, 256, 32)
    return yp, ys, srw, sgd, ckv, kpe
```

```python
import contextlib
import numpy as np
import concourse.bass as bass
import concourse.mybir as mybir
from concourse.bass_utils import run_bass_kernel_spmd

F32 = mybir.dt.float32
BF16 = mybir.dt.bfloat16
F32R = mybir.dt.float32r
AF = mybir.ActivationFunctionType
ALU = mybir.AluOpType

NCORE = 8
D = 1024
DFF = 2816
TOK = 2048
HALF = 1024
EPS = 1e-6


class KB:
    def __init__(self, nc, es, n_dma_sems=24):
        self.nc = nc
        self.es = es
        self.eng = {"pe": nc.tensor, "act": nc.scalar, "dve": nc.vector, "pool": nc.gpsimd, "sp": nc.sync}
        self.sem = {k: es.enter_context(nc.semaphore("sem_" + k)) for k in self.eng}
        self.cnt = {k: 0 for k in self.eng}
        self.seen = {k: {} for k in self.eng}
        self.dsem = [es.enter_context(nc.semaphore("dsem%d" % i)) for i in range(n_dma_sems)]
        self.dcnt = [0] * n_dma_sems
        self.drr = {"sp": 0, "pool": 0, "act": 0}
        self.dpool = {"sp": list(range(0, n_dma_sems // 2)), "act": list(range(0, n_dma_sems // 2)),
                      "pool": list(range(n_dma_sems // 2, n_dma_sems))}
        self.recs = {}
        self.semobj = {}
        for k in self.eng:
            self.semobj[k] = self.sem[k]
        for i, s in enumerate(self.dsem):
            self.semobj[("d", i)] = s
        self.ninst = 0
        self.rt = set()

    def R(self, ap):
        if ap is not None and not isinstance(ap, (int, float)) and ap.tensor.name in self.rt:
            return ap.bitcast(F32R)
        return ap

    @staticmethod
    def box(ap):
        t = ap.tensor
        dims = list(ap.ap)
        if type(t).__name__.startswith("DRam"):
            f0 = ap.offset
            f1 = f0 + sum((c - 1) * abs(s) for s, c in dims) + 1
            return (t.name, 0, 1, f0, f1)
        ps, pc = dims[0]
        if ps == 0:
            ps = 1 << 40
        if type(t).__name__.startswith("PSum"):
            return (t.name, 0, 128, 0, 1 << 30)
        p0 = ap.offset // ps
        f0 = ap.offset % ps
        f1 = f0 + sum((c - 1) * abs(s) for s, c in dims[1:]) + 1
        return (t.name, p0, p0 + pc, f0, f1)

    def _deps(self, b, write, deps, eng=None):
        name, p0, p1, f0, f1 = b
        lst = self.recs.get(name)
        if not lst:
            return
        psum = f1 == (1 << 30)
        for r in lst:
            if r[0] < p1 and p0 < r[1] and r[2] < f1 and f0 < r[3]:
                if write or r[6] or (psum and r[7] != eng):
                    k = r[4]
                    if deps.get(k, 0) < r[5]:
                        deps[k] = r[5]

    def _record(self, b, write, semkey, val, eng):
        name, p0, p1, f0, f1 = b
        lst = self.recs.setdefault(name, [])
        if write:
            lst[:] = [r for r in lst if not (p0 <= r[0] and r[1] <= p1 and f0 <= r[2] and r[3] <= f1)]
        else:
            lst[:] = [r for r in lst if not ((not r[6]) and r[7] == eng and r[4] == semkey
                                             and p0 <= r[0] and r[1] <= p1 and f0 <= r[2] and r[3] <= f1)]
        lst.append((p0, p1, f0, f1, semkey, val, write, eng))

    def _wait(self, e, deps):
        seen = self.seen[e]
        for k, v in deps.items():
            if e == "pe" and k == "pe":
                continue
            if seen.get(k, 0) < v:
                self.eng[e].wait_ge(self.semobj[k], v)
                seen[k] = v

    def emit(self, e, fn, outs, ins):
        deps = {}
        ob = [self.box(a) for a in outs]
        ib = [self.box(a) for a in ins if a is not None and not isinstance(a, (int, float))]
        for b in ib:
            self._deps(b, False, deps, e)
        for b in ob:
            self._deps(b, True, deps, e)
        self._wait(e, deps)
        inst = fn()
        self.cnt[e] += 1
        inst.then_inc(self.sem[e], 1)
        v = self.cnt[e]
        for b in ib:
            self._record(b, False, e, v, e)
        for b in ob:
            self._record(b, True, e, v, e)
        self.ninst += 1
        return inst

    def dma(self, q, out, in_):
        deps = {}
        ob = self.box(out)
        ib = self.box(in_)
        self._deps(ib, False, deps)
        self._deps(ob, True, deps)
        pl = self.dpool[q]
        i = pl[self.drr[q] % len(pl)]
        self.drr[q] += 1
        k = ("d", i)
        if self.dcnt[i] > 0:
            deps[k] = max(deps.get(k, 0), self.dcnt[i])
        self._wait(q, deps)
        inst = self.eng[q].dma_start(out=out, in_=in_)
        self.dcnt[i] += 16
        inst.then_inc(self.dsem[i], 16)
        self._record(ib, False, k, self.dcnt[i], "dma")
        self._record(ob, True, k, self.dcnt[i], "dma")
        self.ninst += 1

    def barrier(self, engines=None):
        engines = engines or list(self.eng)
        for e in engines:
            deps = {o: self.cnt[o] for o in self.eng if o != e and self.cnt[o] > 0}
            for i, c in enumerate(self.dcnt):
                if c > 0:
                    deps[("d", i)] = c
            self._wait(e, deps)

    def mm(self, out, lhsT, rhs, start=True, stop=True):
        nc = self.nc
        if lhsT.tensor.name in self.rt and rhs.tensor.name in self.rt:
            lhsT, rhs = lhsT.bitcast(F32R), rhs.bitcast(F32R)
        return self.emit("pe", lambda: nc.tensor.matmul(out, lhsT, rhs, start=start, stop=stop), [out], [lhsT, rhs])

    def tr(self, out, in_, ident):
        nc = self.nc
        return self.emit("pe", lambda: nc.tensor.transpose(out, in_, ident), [out], [in_, ident])

    def act(self, out, in_, func, bias=None, scale=1.0):
        nc = self.nc
        kw = {}
        if bias is not None:
            kw["bias"] = bias
        ins = [in_]
        if bias is not None and not isinstance(bias, (int, float)):
            ins.append(bias)
        if not isinstance(scale, (int, float)):
            ins.append(scale)
        out = self.R(out)
        return self.emit("act", lambda: nc.scalar.activation(out=out, in_=in_, func=func, scale=scale, **kw), [out], ins)

    def tt(self, e, out, in0, in1, op):
        eng = self.eng[e]
        out = self.R(out)
        return self.emit(e, lambda: eng.tensor_tensor(out=out, in0=in0, in1=in1, op=op), [out], [in0, in1])

    def ts(self, e, out, in0, s1, op0, s2=None, op1=None):
        eng = self.eng[e]
        out = self.R(out)
        ins = [in0] + [s for s in (s1, s2) if s is not None and not isinstance(s, (int, float))]
        if op1 is None:
            return self.emit(e, lambda: eng.tensor_scalar(out=out, in0=in0, scalar1=s1, scalar2=None, op0=op0), [out], ins)
        return self.emit(e, lambda: eng.tensor_scalar(out=out, in0=in0, scalar1=s1, scalar2=s2, op0=op0, op1=op1), [out], ins)

    def stt(self, out, in0, scalar, in1, op0, op1):
        nc = self.nc
        out = self.R(out)
        ins = [in0, in1] + ([scalar] if not isinstance(scalar, (int, float)) else [])
        return self.emit("dve", lambda: nc.vector.scalar_tensor_tensor(out=out, in0=in0, scalar=scalar, in1=in1, op0=op0, op1=op1), [out], ins)

    def copy(self, e, out, in_):
        eng = self.eng[e]
        out = self.R(out)
        if e == "act":
            return self.emit(e, lambda: eng.copy(out=out, in_=in_), [out], [in_])
        return self.emit(e, lambda: eng.tensor_copy(out=out, in_=in_), [out], [in_])

    def recip(self, out, in_):
        nc = self.nc
        return self.emit("dve", lambda: nc.vector.reciprocal(out=out, in_=in_), [out], [in_])

    def memset(self, e, out, val):
        eng = self.eng[e]
        if out.tensor.name in self.rt:
            return self.ts("dve", out, self.ones_ap, float(val), ALU.mult)
        return self.emit(e, lambda: eng.memset(out, val), [out], [])


def fm(v, nchunk):
    return np.ascontiguousarray(np.asarray(v, np.float32).reshape(nchunk, 128).T)


def rows(v):
    v = np.asarray(v, np.float32).reshape(1, -1)
    return np.ascontiguousarray(np.repeat(v, 128, axis=0))


class Pack:
    def __init__(self):
        self.cols = []
        self.off = {}
        self.n = 0

    def add(self, name, arr):
        arr = np.asarray(arr, np.float32)
        if arr.shape[0] < 128:
            arr = np.concatenate([arr, np.zeros((128 - arr.shape[0],) + arr.shape[1:], np.float32)], 0)
        arr = arr.reshape(128, -1)
        self.off[name] = (self.n, arr.shape[1])
        self.n += arr.shape[1]
        self.cols.append(arr)

    def build(self):
        return np.ascontiguousarray(np.concatenate(self.cols, 1))


def pack_params(inp, core):
    P = Pack()
    c = inp["c"][core]
    cc = np.stack([fm(inp["c_ctx"], 8), fm(c, 8)], axis=2)
    P.add("cT", cc)
    for i in range(2):
        P.add("bmod%d" % i, fm(inp["b_mod"][i], 72))
        P.add("nf1_%d" % i, fm(inp["norm_ffn1"][i], 8))
        P.add("nmx_%d" % i, fm(inp["norm_mix"][i], 8))
        P.add("nf2_%d" % i, fm(inp["norm_ffn2"][i], 8))
    P.add("fnorm", fm(inp["final_norm"], 8))
    P.add("gconv", np.stack([fm(inp["gdn_conv"][0][t], 12) for t in range(3)], axis=1))
    P.add("gnorm", np.asarray(inp["gdn_norm"][0], np.float32).reshape(128, 1))
    P.add("galog", rows(inp["gdn_a_log"][0]))
    P.add("gdtb", rows(inp["gdn_dt_bias"][0]))
    P.add("rmu", fm(inp["rwkv_mu"][0], 15))
    P.add("rw0", np.stack([fm(inp["rwkv_w0"][0][d], 4) for d in range(2)], axis=1))
    P.add("ra0", np.stack([fm(inp["rwkv_a0"][0][d], 4) for d in range(2)], axis=1))
    P.add("rkk", fm(inp["rwkv_k_k"][0], 4))
    P.add("rka", fm(inp["rwkv_k_a"][0], 4))
    P.add("rrk", fm(inp["rwkv_r_k"][0].reshape(-1), 4))
    P.add("rlnw", fm(inp["rwkv_ln_w"][0], 4))
    P.add("rlnb", fm(inp["rwkv_ln_b"][0], 4))
    P.add("mqn", fm(inp["mla_q_norm"][0], 3))
    P.add("mkvn", fm(inp["mla_kv_norm"][0], 2))
    P.add("hconv", np.stack([fm(inp["hy_conv"][0][t], 12) for t in range(3)], axis=1))
    P.add("hbias", np.stack([fm(inp["hy_bias"][0][n], 4) for n in range(2)], axis=1))
    P.add("hb1", np.asarray(inp["hy_b1"][0], np.float32).reshape(64, 1))
    P.add("hb2", np.asarray(inp["hy_b2"][0], np.float32).reshape(64, 1))
    P.add("hfreq", np.ascontiguousarray(np.asarray(inp["hy_freq"][0], np.float32).T))
    nz0 = np.ones((128, 1), np.float32)
    nz0[0, 0] = 0.0
    P.add("nz0", nz0)
    w1p = np.zeros((128, 64), np.float32)
    w1p[:33] = np.asarray(inp["hy_w1"][0], np.float32)
    P.add("hw1", w1p)
    P.add("hw2", np.asarray(inp["hy_w2"][0], np.float32))
    hm = np.zeros((128, 2), np.float32)
    hm[:64, 0] = 1.0
    hm[64:, 1] = 1.0
    P.add("hmask", hm)
    rm = np.ones((128, 1024), np.float32)
    rm[:, ::128] = 0.0
    P.add("rmask", rm)
    return P


def make_consts():
    p = np.arange(128)[:, None]
    f = np.arange(128)[None, :]
    ident = (p == f).astype(np.float32)
    le = (p <= f).astype(np.float32)
    lt = (p < f).astype(np.float32)
    ge = (p >= f).astype(np.float32)
    gt = (p > f).astype(np.float32)
    BIG = 30000.0
    bo = ((p // 64) == (f // 64)).astype(np.float32)
    jpad = np.zeros((128, 128), np.float32)
    for i in range(16):
        jpad[64 + 2 * i + 1, 64 + 2 * i] = -1.0
        jpad[64 + 2 * i, 64 + 2 * i + 1] = 1.0
    onesE = np.zeros((128, 128), np.float32)
    onesE[:, :64] = 1.0
    onesO = np.zeros((128, 128), np.float32)
    onesO[:, 64:] = 1.0
    b32 = ((p // 32) == (f // 32)).astype(np.float32)
    m1 = (((p // 64) == (f // 64)) & ((p // 32) != (f // 32))).astype(np.float32)
    m2 = ((p // 64) != (f // 64)).astype(np.float32)
    names = ["ident", "le", "lt", "ge", "gt", "pos_gt", "pos_lt", "neg_le", "neg_ge", "bo", "jpad", "onesE", "onesO", "b32", "m1", "m2"]
    arrs = [ident, le, lt, ge, gt, BIG * (1 - gt), BIG * (1 - lt), -BIG * (1 - le), -BIG * (1 - ge), bo, jpad, onesE, onesO, b32, m1, m2]
    return names, np.ascontiguousarray(np.concatenate(arrs, 1).astype(np.float32))


def rope_tables():
    L = 1024
    row = np.repeat(np.arange(L // 64), 64).astype(np.float32)
    col = (np.arange(L) % 64).astype(np.float32)
    n = 8
    inv = (10000.0 ** (-np.arange(n, dtype=np.float32) / n)).astype(np.float32)
    ang = np.concatenate([row[:, None] * inv, col[:, None] * inv], axis=-1)
    cs = np.zeros((2, 128, L), np.float32)
    for r in range(32):
        cs[0, 64 + r] = np.cos(ang[:, r // 2])
        cs[1, 64 + r] = np.sin(ang[:, r // 2])
    return cs


def hyena_consts(L):
    import ml_dtypes
    t = np.arange(L, dtype=np.float64)
    w = 2.0 * np.pi * (t + 0.5) / (2 * L)
    ph = np.outer(t, w)
    Cm = np.cos(ph)
    Sm = np.sin(ph)
    dft = np.stack([Cm, Sm, Cm.T, Sm.T]).astype(np.float32).astype(ml_dtypes.bfloat16)
    t32 = np.arange(L, dtype=np.float32)
    t_norm = t32 / max(L - 1, 1)
    bands = np.linspace(1e-4, 16 - 1, 16, dtype=np.float32)
    ang = (np.float32(2.0 * np.pi / L) * t32[:, None] * bands).astype(np.float32)
    feats = np.concatenate([t_norm[:, None], np.cos(ang), -np.sin(ang)], axis=-1).astype(np.float32)
    min_decay = np.log(1e-2) / 1.5
    max_decay = np.log(1e-2) / 0.3
    deltas = np.abs(np.linspace(min_decay, max_decay, 512, dtype=np.float32))
    window = np.exp(-t_norm[:, None] * deltas).astype(np.float32)
    featsT = np.zeros((128, L), np.float32)
    featsT[:33] = feats.T
    return dft, featsT, window


A_COLS = 1920
SEQS = {0: [(0, 256), (256, 256), (512, 256), (768, 256)], 1: [(0, 1024)]}


def build_program(offs, NP, test=None, stage=99):
    nc = bass.Bass("TRN2", target_bir_lowering=False)
    dr = {}

    def din(name, shape, dt=F32):
        dr[name] = nc.dram_tensor(name, list(shape), dt, kind="ExternalInput").ap()
        return dr[name]

    def dout(name, shape, dt=F32):
        dr[name] = nc.dram_tensor(name, list(shape), dt, kind="ExternalOutput").ap()
        return dr[name]

    cnames, carr = make_consts()
    xT = din("xT", [D, TOK])
    prm_d = din("prm", [128, NP])
    cst_d = din("cst", [128, carr.shape[1]])
    w_mod = din("w_mod", [2, D, 9 * D])
    ffn_gu = [din("ffn1_w_gu", [2, D, 2 * DFF]), din("ffn2_w_gu", [2, D, 2 * DFF])]
    ffn_dn = [din("ffn1_w_down", [2, DFF, D]), din("ffn2_w_down", [2, DFF, D])]
    w_out_d = din("w_out", [2, D, D])
    ab_w_in = din("ab_w_in", [1, D, 3984])
    sgdn_in = din("sgdn_in", [2, 4, 128, 128])
    srwkv_in = din("srwkv_in", [2, 8, 64, 64])
    rwkv_w_up = din("rwkv_w_up", [1, 2, 64, 512])
    rwkv_a_up = din("rwkv_a_up", [1, 2, 64, 512])
    rwkv_g_up = din("rwkv_g_up", [1, 128, 512])
    srwkv_out = dout("srwkv_out", [4, 2, 8, 64, 64])
    cd_w_in = din("cd_w_in", [1, D, 2208])
    mla_w_uq = din("mla_w_uq", [1, 384, 768])
    mla_w_ukv = din("mla_w_ukv", [1, 256, 1024])
    cckvT = din("cckvT", [256, 256])
    ckpeT = din("ckpeT", [32, 256])
    ropecs = din("ropecs", [2, 128, 1024])
    hy_w1 = din("hy_w1", [1, 33, 64])
    hy_w2 = din("hy_w2", [1, 64, 64])
    hy_w3 = din("hy_w3", [1, 64, 2048])
    dftd = {256: din("dft256", [4, 256, 256], BF16), 1024: din("dft1024", [4, 1024, 1024], BF16)}
    featd = {256: din("feat256", [128, 256]), 1024: din("feat1024", [128, 1024])}
    wind = {256: din("win256", [256, 512]), 1024: din("win1024", [1024, 512])}
    ckv_out = dout("ckv_out", [256, 1024])
    kpe_out = dout("kpe_out", [32, 1024])
    yT = dout("yT", [D, TOK])
    sgdn_out = dout("sgdn_out", [4, 2, 4, 128, 128])

    es = contextlib.ExitStack()
    with es:
        kb = KB(nc, es)

        uid = [0]

        def sbx(stack, name, shape, dt=F32, r=False):
            uid[0] += 1
            nm = "%s_%d" % (name, uid[0])
            if r:
                kb.rt.add(nm)
            return stack.enter_context(nc.sbuf_tensor(nm, list(shape), dt))

        def sb(name, shape, dt=F32):
            return sbx(es, name, shape, dt)

        X = sb("X", [128, 8, TOK])
        prm = sb("prm_sb", [128, NP])
        cst = sb("cst_sb", [128, carr.shape[1]])
        ones = sb("ones", [128, 128])
        modT = sb("modT", [128, 2, 2, 72])
        cs = sb("cs", [128, 8, 2], BF16)
        sq = [sb("sq%d" % i, [128, 512]) for i in range(2)]
        rstd = sb("rstd", [128, HALF])
        tmpf = [sb("tmpf%d" % i, [128, 512]) for i in range(2)]
        coef = sb("coef", [128, 64])
        pbank = [es.enter_context(nc.psum_tensor("pb%d" % i, [128, 512], F32)) for i in range(8)]

        def P(name):
            o, n = offs[name]
            return prm[:, o:o + n]

        def C(name):
            i = cnames.index(name)
            return cst[:, i * 128:(i + 1) * 128]

        slot = [0]

        def pq():
            s = slot[0]
            slot[0] = (s + 1) % 16
            return pbank[s % 4][:, (s // 4) * 128:(s // 4 + 1) * 128]

        kb.dma("sp", prm[:], prm_d[:, :])
        kb.dma("sp", cst[:], cst_d[:, :])
        for j in range(8):
            kb.dma("sp", X[:, j, :], xT[j * 128:(j + 1) * 128, :])
        kb.memset("dve", ones[:], 1.0)
        kb.ones_ap = ones[:]
        ident = C("ident")

        with contextlib.ExitStack() as ph:
            wm = [sbx(ph, "wm%d" % i, [128, 8, 1024], BF16) for i in range(2)]
            cT = P("cT")
            kb.act(cs[:].rearrange("p a b -> p (a b)"), cT, AF.Silu)
            for L in range(2):
                pm = pbank[7]
                pmv = pm[:, 0:144].rearrange("p (a b) -> p a b", b=2)
                for n in range(9):
                    wt = wm[n % 2]
                    kb.dma("pool", wt[:], w_mod[L, :, n * 1024:(n + 1) * 1024].rearrange("(k p) c -> p k c", p=128))
                    for j in range(8):
                        for k in range(8):
                            kb.mm(pmv[:, n * 8 + j, :], wt[:, k, j * 128:(j + 1) * 128], cs[:, k, :], start=(k == 0), stop=(k == 7))
                for v in range(2):
                    kb.tt("dve", modT[:, L, v, :], pmv[:, :, v], P("bmod%d" % L), ALU.add)
            kb.barrier()

        def rms_stats(hf):
            for nt in range(2):
                t0 = hf * HALF + nt * 512
                pst = pbank[6]
                for j in range(8):
                    s = sq[j % 2]
                    kb.act(s[:], X[:, j, t0:t0 + 512], AF.Square)
                    kb.mm(pst[:], ones[:], s[:], start=(j == 0), stop=(j == 7))
                kb.act(rstd[:, nt * 512:(nt + 1) * 512], pst[:], AF.Sqrt, bias=EPS, scale=1.0 / D)
            kb.recip(rstd[:], rstd[:])

        def make_coefs(L, hf, sub, gname):
            m = modT[:, L, hf, :]
            kb.stt(coef[:, 0:8], m[:, (3 * sub + 1) * 8:(3 * sub + 2) * 8], 1.0, P(gname), ALU.add, ALU.mult)
            kb.copy("dve", coef[:, 8:16], m[:, (3 * sub) * 8:(3 * sub + 1) * 8])
            kb.ts("dve", coef[:, 16:24], m[:, (3 * sub + 2) * 8:(3 * sub + 3) * 8], 0.5 if sub != 1 else 1.0, ALU.mult)

        def make_H(Hb, hf):
            for j in range(8):
                for nt in range(2):
                    t0 = hf * HALF + nt * 512
                    tf = tmpf[(j * 2 + nt) % 2]
                    kb.stt(tf[:], X[:, j, t0:t0 + 512], coef[:, j:j + 1], rstd[:, nt * 512:(nt + 1) * 512], ALU.mult, ALU.mult)
                    kb.act(Hb[:, j, nt * 512:(nt + 1) * 512], tf[:], AF.Identity, bias=coef[:, 8 + j:9 + j], scale=1.0)

        gu_groups = [(0, 3), (3, 3), (6, 3), (9, 2)]

        def ffn_phase(L, which):
            with contextlib.ExitStack() as ph:
                Hb = sbx(ph, "Hb", [128, 8, HALF], BF16)
                hh = sbx(ph, "hh", [128, 11, HALF], BF16)
                wgu = [sbx(ph, "wgu%d" % i, [128, 8, 2, 384], BF16) for i in range(2)]
                wdn = sbx(ph, "wdn", [128, 11, D], BF16)
                sgt = [sbx(ph, "sgt%d" % i, [128, 512], BF16) for i in range(2)]
                sub = 0 if which == 0 else 2
                wg = ffn_gu[which]
                wd = ffn_dn[which]
                gi = 0
                for hf in range(2):
                    rms_stats(hf)
                    make_coefs(L, hf, sub, ("nf1_%d" if which == 0 else "nf2_%d") % L)
                    make_H(Hb, hf)
                    for fh in range(2):
                        kb.dma("pool", wdn[:], wd[L, fh * 1408:(fh + 1) * 1408, :].rearrange("(j p) c -> p j c", p=128))
                        for (g0, gn) in gu_groups:
                            wt = wgu[gi % 2]
                            gi += 1
                            c0 = fh * 1408 + g0 * 128
                            for gu in range(2):
                                kb.dma("pool", wt[:, :, gu, 0:gn * 128],
                                       wg[L, :, gu * DFF + c0: gu * DFF + c0 + gn * 128].rearrange("(k p) c -> p k c", p=128))
                            for jj in range(gn):
                                for nt in range(2):
                                    pg = pbank[(2 * (jj * 2 + nt)) % 4]
                                    pu = pbank[(2 * (jj * 2 + nt)) % 4 + 1]
                                    for k in range(8):
                                        kb.mm(pg[:], wt[:, k, 0, jj * 128:(jj + 1) * 128], Hb[:, k, nt * 512:(nt + 1) * 512], start=(k == 0), stop=(k == 7))
                                    for k in range(8):
                                        kb.mm(pu[:], wt[:, k, 1, jj * 128:(jj + 1) * 128], Hb[:, k, nt * 512:(nt + 1) * 512], start=(k == 0), stop=(k == 7))
                                    sg = sgt[(jj * 2 + nt) % 2]
                                    kb.act(sg[:], pg[:], AF.Silu)
                                    kb.tt("dve", hh[:, g0 + jj, nt * 512:(nt + 1) * 512], sg[:], pu[:], ALU.mult)
                        for m in range(8):
                            for nt in range(2):
                                po = pbank[4 + (m * 2 + nt) % 2]
                                for j in range(11):
                                    kb.mm(po[:], wdn[:, j, m * 128:(m + 1) * 128], hh[:, j, nt * 512:(nt + 1) * 512], start=(j == 0), stop=(j == 10))
                                t0 = hf * HALF + nt * 512
                                kb.stt(X[:, m, t0:t0 + 512], po[:], coef[:, 16 + m:17 + m], X[:, m, t0:t0 + 512], ALU.mult, ALU.add)
                kb.barrier()

        def interleave(gens):
            gens = list(gens)
            while gens:
                for g in list(gens):
                    try:
                        next(g)
                    except StopIteration:
                        gens.remove(g)

        hslot = [0]

        def pq2():
            k = hslot[0]
            hslot[0] = (k + 1) % 4
            return pbank[4 + k % 2][:, (k // 2) * 256:(k // 2 + 1) * 256]

        def tri_inv_T(ph_tiles, N, NT):
            T = ph_tiles
            Nd, Pa, Pb, Mt, Tun, PXa, PXb = T["Nd"], T["Pa"], T["Pb"], T["Mt"], T["Tun"], T["PXa"], T["PXb"]
            kb.tt("dve", Nd[:], N[:], C("b32"), ALU.mult)
            kb.tt("dve", PXa[:, 0:128], NT[:], C("b32"), ALU.mult)
            kb.copy("dve", PXa[:, 128:256], ident)
            yield
            Pc, Pn = Nd, Pa
            PXc, PXn = PXa, PXb
            for k in range(1, 5):
                px = pq2()
                kb.mm(px, Pc[:], PXc[:])
                pp = pq()
                kb.mm(pp, PXc[:, 0:128], Pc[:])
                if k < 4:
                    kb.copy("act", PXn[:, 0:128], px[:, 0:128])
                kb.tt("dve", PXn[:, 128:256], px[:, 128:256], PXc[:, 128:256], ALU.add)
                kb.copy("act", Pn[:], pp)
                yield
                Pc = Pn
                Pn = Pb if Pn is Pa else Pa
                PXc, PXn = PXn, PXc
            Xa = PXc[:, 128:256]
            pf = pq()
            kb.mm(pf, Pc[:], Xa)
            kb.tt("dve", Xa, pf, Xa, ALU.add)
            yield
            for mname in ("m1", "m2"):
                kb.tt("dve", Nd[:], N[:], C(mname), ALU.mult)
                pm = pq()
                kb.mm(pm, Nd[:], Xa)
                kb.copy("act", Mt[:], pm)
                ptr = pq()
                kb.tr(ptr, Xa, ident)
                kb.copy("act", Tun[:], ptr)
                yield
                pw = pq()
                kb.mm(pw, Tun[:], Mt[:])
                kb.tt("dve", Xa, pw, Xa, ALU.add)
                yield
            T["X"] = Xa

        def inv_tile_set(fr, pfx):
            d = {n: fr(pfx + n, [128, 128]) for n in ["Nd", "Pa", "Pb", "Mt", "Tun"]}
            d["PXa"] = fr(pfx + "PXa", [128, 256])
            d["PXb"] = fr(pfx + "PXb", [128, 256])
            return d

        def gdn(hf, Hb, wo_apply):
            with contextlib.ExitStack() as ph:
                f = lambda name, shape, dt=F32, r=False: sbx(ph, "g_" + name, shape, dt, r)
                fr = lambda name, shape: f(name, shape, F32, True)
                wb = [f("wb%d" % i, [128, 8, 4, 128], BF16) for i in range(1)]
                wab = f("wab", [128, 8, 16], BF16)
                praw = f("praw", [128, HALF])
                cv = f("cv", [128, HALF])
                qF, kF, vF, zF = fr("qF", [128, HALF]), fr("kF", [128, HALF]), f("vF", [128, HALF]), f("zF", [128, HALF])
                oacc = f("oacc", [128, HALF])
                mch = f("mch", [128, HALF], BF16)
                rn = f("rn", [128, HALF])
                abt = f("abt", [128, 8, 16])
                gsb = f("gsb", [128, 2, 8, 4])
                bsb = f("bsb", [128, 2, 8, 4])
                nbsb = f("nbsb", [128, 2, 8, 4])
                Gsb = f("Gsb", [128, 2, 8, 4])
                Gtot = f("Gtot", [128, 2, 8, 4])
                eG = f("eG", [128, 2, 8, 4])
                beG = f("beG", [128, 2, 8, 4])
                eGL = f("eGL", [128, 2, 8, 4])
                eGlast = f("eGlast", [128, 2, 8, 4])
                ea = f("ea", [128, 8])
                tsm = f("tsm", [128, 2, 8, 4])
                kT = f("kT", [128, 8, 128])
                vT = f("vT", [128, 8, 128])
                st_qkT = f("st_qkT", [128, 8, 128], BF16)
                st_u = f("st_u", [128, 8, 128])
                st_nwT = f("st_nwT", [128, 8, 128], BF16)
                st_qin = f("st_qin", [128, 8, 128], BF16)
                st_kout = f("st_kout", [128, 8, 128], BF16)
                S = f("S", [128, 128])
                Sb = f("Sb", [128, 128], BF16)
                NCH = 3
                t_vnew = f("t_vnew", [128, 128], BF16)
                chT = []
                for ci in range(NCH):
                    T = {n: f("c%d_%s" % (ci, n), [128, 128]) for n in ["diag", "d1", "d2", "E1", "E2", "eGr"]}
                    T.update({n: fr("c%d_%s" % (ci, n), [128, 128]) for n in ["N", "NT", "vb", "kbg"]})
                    T["inv"] = inv_tile_set(fr, "c%d_iv" % ci)
                    chT.append(T)
                c_ab = A_COLS + 2048
                kb.dma("pool", wab[:], ab_w_in[0, :, c_ab:c_ab + 16].rearrange("(k p) c -> p k c", p=128))
                pab = pbank[6]
                for tt in range(8):
                    for k in range(8):
                        kb.mm(pab[:, tt * 16:(tt + 1) * 16], Hb[:, k, tt * 128:(tt + 1) * 128], wab[:, k, :], start=(k == 0), stop=(k == 7))
                kb.copy("dve", abt[:].rearrange("p a b -> p (a b)"), pab[:, 0:128])
                abv = abt[:].rearrange("p t (d a h) -> p t d a h", d=2, a=2)
                kb.act(ea[:], P("galog"), AF.Exp)
                for d in range(2):
                    dtb = P("gdtb")[:, d * 4:(d + 1) * 4]
                    for tt in range(8):
                        kb.tt("dve", tsm[:, d, tt, :], abv[:, tt, d, 0, :], dtb, ALU.add)
                        kb.copy("dve", bsb[:, d, tt, :], abv[:, tt, d, 1, :])
                g2 = lambda t: t[:].rearrange("p d t h -> p (d t h)")
                kb.act(g2(tsm), g2(tsm), AF.Exp)
                kb.act(g2(tsm), g2(tsm), AF.Ln, bias=1.0)
                for d in range(2):
                    for tt in range(8):
                        kb.stt(gsb[:, d, tt, :], tsm[:, d, tt, :], -1.0, ea[:, d * 4:(d + 1) * 4], ALU.mult, ALU.mult)
                kb.act(g2(bsb), g2(bsb), AF.Sigmoid)
                kb.ts("dve", g2(nbsb), g2(bsb), -1.0, ALU.mult)
                pG = pbank[6]
                g3 = lambda t, d: t[:, d, :, :].rearrange("p t h -> p (t h)")
                kb.mm(pG[:, 0:32], C("le"), g3(gsb, 0))
                kb.mm(pG[:, 32:64], C("ge"), g3(gsb, 1))
                kb.mm(pG[:, 64:128], ones[:], g2(gsb))
                kb.copy("dve", g2(Gsb), pG[:, 0:64])
                kb.copy("dve", g2(Gtot), pG[:, 64:128])
                kb.act(g2(eG), g2(Gsb), AF.Exp)
                kb.tt("dve", g2(beG), g2(eG), g2(bsb), ALU.mult)
                kb.tt("dve", g2(eGL), g2(Gtot), g2(Gsb), ALU.subtract)
                kb.act(g2(eGL), g2(eGL), AF.Exp)
                kb.act(g2(eGlast), g2(Gtot), AF.Exp)

                for h in range(4):
                    wt = wb[0]
                    for qi in range(4):
                        c0 = A_COLS + qi * 512 + h * 128
                        kb.dma("pool", wt[:, :, qi, :], ab_w_in[0, :, c0:c0 + 128].rearrange("(k p) c -> p k c", p=128))
                    gc = P("gconv").rearrange("p (t c) -> p t c", t=3)
                    for qi, dst in enumerate([qF, kF, vF, zF]):
                        for nt in range(2):
                            pp = pbank[nt]
                            for k in range(8):
                                kb.mm(pp[:], wt[:, k, qi, :], Hb[:, k, nt * 512:(nt + 1) * 512], start=(k == 0), stop=(k == 7))
                            if qi == 3:
                                kb.act(zF[:, nt * 512:(nt + 1) * 512], pp[:], AF.Silu)
                            else:
                                kb.copy("act", praw[:, nt * 512:(nt + 1) * 512], pp[:])
                        if qi == 3:
                            continue
                        cc = qi * 4 + h
                        kb.ts("dve", cv[:], praw[:], gc[:, 1, cc:cc + 1], ALU.mult)
                        for (s0, Ls) in SEQS[hf]:
                            kb.stt(cv[:, s0 + 1:s0 + Ls], praw[:, s0:s0 + Ls - 1], gc[:, 0, cc:cc + 1], cv[:, s0 + 1:s0 + Ls], ALU.mult, ALU.add)
                            kb.stt(cv[:, s0:s0 + Ls - 1], praw[:, s0 + 1:s0 + Ls], gc[:, 2, cc:cc + 1], cv[:, s0:s0 + Ls - 1], ALU.mult, ALU.add)
                        kb.act(dst[:], cv[:], AF.Silu)
                        if qi < 2:
                            for nt in range(2):
                                s = sq[nt]
                                kb.act(s[:], dst[:, nt * 512:(nt + 1) * 512], AF.Square)
                                pst = pbank[2 + nt]
                                kb.mm(pst[:], ones[:], s[:])
                                kb.act(rn[:, nt * 512:(nt + 1) * 512], pst[:], AF.Sqrt, bias=1e-6, scale=1.0)
                            kb.recip(rn[:], rn[:])
                            kb.stt(dst[:], dst[:], (128.0 ** -0.5) if qi == 0 else 1.0, rn[:], ALU.mult, ALU.mult)
                    for tt in range(8):
                        p1 = pq()
                        kb.tr(p1, kF[:, tt * 128:(tt + 1) * 128], ident)
                        kb.copy("act", kT[:, tt, :], p1)
                        p2 = pq()
                        kb.tr(p2, vF[:, tt * 128:(tt + 1) * 128], ident)
                        kb.copy("act", vT[:, tt, :], p2)
                    for d in range(2):
                        posm = C("pos_gt") if d == 0 else C("pos_lt")
                        negm = C("neg_le") if d == 0 else C("neg_ge")
                        def pre_tile(tt, T, d=d, h=h, posm=posm, negm=negm):
                            tsl = slice(tt * 128, (tt + 1) * 128)
                            col = lambda t: t[:, d, tt, h:h + 1]
                            kb.ts("dve", T["diag"][:], ident, col(Gsb), ALU.mult)
                            prb = pq()
                            kb.mm(prb, ones[:], T["diag"][:])
                            yield
                            kb.stt(T["d1"][:], prb, col(Gsb), posm, ALU.subtract, ALU.add)
                            kb.act(T["E1"][:], T["d1"][:], AF.Exp, scale=-1.0)
                            kb.stt(T["d2"][:], prb, col(Gsb), negm, ALU.subtract, ALU.add)
                            kb.act(T["E2"][:], T["d2"][:], AF.Exp)
                            kb.act(T["eGr"][:], prb, AF.Exp)
                            pkk = pq()
                            kb.mm(pkk, kF[:, tsl], kF[:, tsl])
                            yield
                            kb.stt(T["N"][:], pkk, col(nbsb), T["E1"][:], ALU.mult, ALU.mult)
                            pnt = pq()
                            kb.tr(pnt, T["N"][:], ident)
                            kb.copy("act", T["NT"][:], pnt)
                            pqk = pq()
                            kb.mm(pqk, kF[:, tsl], qF[:, tsl])
                            kb.tt("dve", st_qkT[:, tt, :], pqk, T["E2"][:], ALU.mult)
                            kb.ts("dve", T["vb"][:], vT[:, tt, :], col(bsb), ALU.mult)
                            kb.ts("dve", T["kbg"][:], kT[:, tt, :], col(beG), ALU.mult)
                            kb.ts("dve", st_kout[:, tt, :], kT[:, tt, :], col(eGL), ALU.mult)
                            kb.tt("dve", st_qin[:, tt, :], qF[:, tsl], T["eGr"][:], ALU.mult)
                            yield
                            yield from tri_inv_T(T["inv"], T["N"], T["NT"])
                            Xi = T["inv"]["X"]
                            pu_ = pq()
                            kb.mm(pu_, Xi, T["vb"][:])
                            kb.copy("act", st_u[:, tt, :], pu_)
                            pw_ = pq()
                            kb.mm(pw_, T["kbg"][:], Xi)
                            kb.ts("dve", st_nwT[:, tt, :], pw_, -1.0, ALU.mult)
                            yield

                        for t0_ in range(0, 8, NCH):
                            interleave([pre_tile(tt, chT[ci]) for ci, tt in enumerate(range(t0_, min(8, t0_ + NCH)))])
                        for si, (s0, Ls) in enumerate(SEQS[hf]):
                            tiles = list(range(s0 // 128, (s0 + Ls) // 128))
                            if d == 1:
                                tiles = tiles[::-1]
                            if hf == 0:
                                kb.memset("dve", S[:], 0.0)
                            else:
                                kb.dma("sp", S[:], sgdn_in[d, h, :, :])
                            kb.copy("dve", Sb[:], S[:])
                            for tt in tiles:
                                tsl = slice(tt * 128, (tt + 1) * 128)
                                pv = pq()
                                kb.mm(pv, st_nwT[:, tt, :], Sb[:])
                                kb.tt("dve", t_vnew[:], pv, st_u[:, tt, :], ALU.add)
                                po_ = pq()
                                kb.mm(po_, Sb[:], st_qin[:, tt, :], start=True, stop=False)
                                kb.mm(po_, t_vnew[:], st_qkT[:, tt, :], start=False, stop=True)
                                if d == 0:
                                    kb.copy("act", oacc[:, tsl], po_)
                                else:
                                    kb.tt("dve", oacc[:, tsl], po_, oacc[:, tsl], ALU.add)
                                ps_ = pq()
                                kb.mm(ps_, st_kout[:, tt, :], t_vnew[:])
                                kb.stt(S[:], S[:], eGlast[:, d, tt, h:h + 1], ps_, ALU.mult, ALU.add)
                                kb.copy("act", Sb[:], S[:])
                            if hf == 0:
                                kb.dma("sp", sgdn_out[si, d, h, :, :], S[:])
                    for nt in range(2):
                        s = sq[nt]
                        kb.act(s[:], oacc[:, nt * 512:(nt + 1) * 512], AF.Square)
                        pst = pbank[2 + nt]
                        kb.mm(pst[:], ones[:], s[:])
                        kb.act(rn[:, nt * 512:(nt + 1) * 512], pst[:], AF.Sqrt, bias=EPS, scale=1.0 / 128)
                    kb.recip(rn[:], rn[:])
                    kb.stt(oacc[:], oacc[:], P("gnorm"), rn[:], ALU.mult, ALU.mult)
                    kb.tt("dve", mch[:], oacc[:], zF[:], ALU.mult)
                    wo_apply(4 + h, mch)
                kb.barrier()

        def rwkv(hf, Hb, wo_apply):
            with contextlib.ExitStack() as ph:
                f = lambda name, shape, dt=F32, r=False: sbx(ph, "r_" + name, shape, dt, r)
                fr = lambda name, shape: f(name, shape, F32, True)
                wch = [f("wch%d" % i, [128, 8, 128], BF16) for i in range(2)]
                wup, aup, gup = f("wup", [128, 512], BF16), f("aup", [128, 512], BF16), f("gup", [128, 512], BF16)
                twd, tad, sgd = f("twd", [128, HALF], BF16), f("tad", [128, HALF], BF16), f("sgd", [128, HALF], BF16)
                T1 = f("T1", [128, HALF])
                rF, kF0, vF, kkF = f("rF", [128, HALF]), f("kF0", [128, HALF]), f("vF", [128, HALF]), f("kkF", [128, HALF])
                asum, yacc = f("asum", [128, HALF]), f("yacc", [128, HALF])
                lw, G, kd, bF = f("lw", [128, HALF]), f("G", [128, HALF]), f("kd", [128, HALF]), f("bF", [128, HALF])
                mch = f("mch", [128, HALF], BF16)
                omm, hmu, omka = f("omm", [128, 15]), f("hmu", [128, 15]), f("omka", [128, 4])
                Mp = f("Mp", [128, 128])
                Mpb = f("Mpb", [128, 128], BF16)
                pC = f("pC", [128, 1])
                nm = ["eX", "Rt", "Ct", "kinv", "binv", "khat", "bhat", "Z"]
                bset = {"Rt", "Ct", "kinv", "binv", "Z"}
                nm = nm + ["eX2"]
                tlp = [{n: f("t%d_%s" % (par, n), [128, 128], BF16 if n in bset else F32) for n in nm} for par in range(2)]
                pCp = [f("pC%d" % par, [128, 1]) for par in range(2)]
                hdp = [[{n: f("h%d%d_%s" % (par, h, n), [128, 128], BF16) for n in ["X", "BmT", "QKT", "QBT", "vT", "khT", "bhT", "nU"]} for h in range(2)] for par in range(2)]
                hT = []
                for h_ in range(2):
                    T = {n: fr("p%d_%s" % (h_, n), [128, 128]) for n in ["N", "NT"]}
                    T.update({n: f("p%d_%s" % (h_, n), [128, 128], BF16) for n in ["Ctm", "Rtm"]})
                    T["msk"] = [f("p%d_msk%d" % (h_, i), [128, 128]) for i in range(3)]
                    T["inv"] = inv_tile_set(fr, "p%d_iv" % h_)
                    hT.append(T)
                hmask = P("hmask")
                mu = P("rmu")
                kb.ts("dve", omm[:], mu, -1.0, ALU.mult, 1.0, ALU.add)
                kb.ts("dve", hmu[:], mu, 0.5, ALU.mult)
                kb.ts("dve", omka[:], P("rka"), -1.0, ALU.mult, 1.0, ALU.add)
                for d in range(2):
                    kb.dma("pool", wup[d * 64:(d + 1) * 64, :], rwkv_w_up[0, d, :, :])
                    kb.dma("pool", aup[d * 64:(d + 1) * 64, :], rwkv_a_up[0, d, :, :])
                kb.dma("pool", gup[:], rwkv_g_up[0, :, :])
                wi = [0]

                def project_shift(chunk, dst, func=None):
                    wt = wch[wi[0] % 2]
                    wi[0] += 1
                    kb.dma("pool", wt[:], ab_w_in[0, :, chunk * 128:(chunk + 1) * 128].rearrange("(k p) c -> p k c", p=128))
                    for nt in range(2):
                        pp = pbank[nt]
                        for k in range(8):
                            kb.mm(pp[:], wt[:, k, :], Hb[:, k, nt * 512:(nt + 1) * 512], start=(k == 0), stop=(k == 7))
                        kb.copy("act", T1[:, nt * 512:(nt + 1) * 512], pp[:])
                    tgt = dst if func is None else lw
                    kb.ts("dve", tgt[:], T1[:], omm[:, chunk:chunk + 1], ALU.mult)
                    for (s0, Ls) in SEQS[hf]:
                        kb.stt(tgt[:, s0 + 1:s0 + Ls], T1[:, s0:s0 + Ls - 1], hmu[:, chunk:chunk + 1], tgt[:, s0 + 1:s0 + Ls], ALU.mult, ALU.add)
                        kb.stt(tgt[:, s0:s0 + Ls - 1], T1[:, s0 + 1:s0 + Ls], hmu[:, chunk:chunk + 1], tgt[:, s0:s0 + Ls - 1], ALU.mult, ALU.add)
                    if func is not None:
                        kb.act(dst[:], tgt[:], func)

                project_shift(12, twd, AF.Tanh)
                project_shift(13, tad, AF.Identity)
                project_shift(14, sgd, AF.Sigmoid)
                bo = C("bo")
                for c in range(4):
                    project_shift(c, rF)
                    project_shift(4 + c, kF0)
                    project_shift(8 + c, vF)
                    kb.ts("dve", kkF[:], kF0[:], P("rkk")[:, c:c + 1], ALU.mult)
                    for nt in range(2):
                        sl = slice(nt * 512, (nt + 1) * 512)
                        kb.act(sq[nt][:], kkF[:, sl], AF.Square)
                        pst = pbank[2 + nt]
                        kb.mm(pst[:], bo, sq[nt][:])
                        kb.act(T1[:, sl], pst[:], AF.Sqrt, bias=1e-6, scale=1.0)
                    kb.recip(T1[:], T1[:])
                    kb.tt("dve", kkF[:], kkF[:], T1[:], ALU.mult)
                    for d in range(2):
                        m_ts = C("gt") if d == 0 else C("lt")
                        m_st = C("lt") if d == 0 else C("gt")
                        m_in = C("le") if d == 0 else C("ge")
                        rows = slice(d * 64, (d + 1) * 64)
                        for nt in range(2):
                            sl = slice(nt * 512, (nt + 1) * 512)
                            pw = pbank[nt]
                            kb.mm(pw[:], wup[rows, c * 128:(c + 1) * 128], twd[rows, sl])
                            kb.act(lw[:, sl], pw[:], AF.Sigmoid, bias=P("rw0").rearrange("p (d c) -> p d c", d=2)[:, d, c:c + 1])
                            pa = pbank[2 + nt]
                            kb.mm(pa[:], aup[rows, c * 128:(c + 1) * 128], tad[rows, sl])
                            kb.act(T1[:, sl], pa[:], AF.Sigmoid, bias=P("ra0").rearrange("p (d c) -> p d c", d=2)[:, d, c:c + 1])
                        kb.ts("dve", lw[:], lw[:], -0.6065306597126334, ALU.mult)
                        if d == 0:
                            kb.copy("dve", asum[:], T1[:])
                        else:
                            kb.tt("dve", asum[:], asum[:], T1[:], ALU.add)
                        kb.tt("dve", bF[:], kkF[:], T1[:], ALU.mult)
                        kb.ts("dve", T1[:], T1[:], P("rka")[:, c:c + 1], ALU.mult, omka[:, c:c + 1], ALU.add)
                        kb.tt("dve", kd[:], kF0[:], T1[:], ALU.mult)
                        nc_ = nc
                        kb.emit("dve", lambda: nc_.vector.tensor_tensor_scan(out=G[:], data0=P("rmask"), data1=lw[:], initial=0.0, op0=ALU.mult, op1=ALU.add),
                                [G[:]], [P("rmask"), lw[:]])
                        if d == 1:
                            for tt in range(8):
                                tsl = slice(tt * 128, (tt + 1) * 128)
                                kb.stt(T1[:, tsl], G[:, tsl], -1.0, lw[:, tsl], ALU.mult, ALU.add)
                                kb.ts("dve", T1[:, tsl], T1[:, tsl], G[:, tt * 128 + 127:tt * 128 + 128], ALU.add)
                            kb.copy("dve", G[:], T1[:])
                        kb.tt("dve", lw[:], G[:], lw[:], ALU.subtract)
                        for par in range(2):
                            for h in range(2):
                                kb.memset("dve", hdp[par][h]["nU"][:], 0.0)
                        order = []
                        for si, (s0, Ls) in enumerate(SEQS[hf]):
                            tiles = list(range(s0 // 128, (s0 + Ls) // 128))
                            if d == 1:
                                tiles = tiles[::-1]
                            for k_, tt in enumerate(tiles):
                                order.append((si, tt, k_ == 0, k_ == len(tiles) - 1))

                        def prep_gen(idx, d=d, c=c, m_ts=m_ts, m_st=m_st, m_in=m_in):
                            si, tt, first, last = order[idx]
                            tl = tlp[idx % 2]
                            hd = hdp[idx % 2]
                            pCt = pCp[idx % 2]
                            tsl = slice(tt * 128, (tt + 1) * 128)
                            e_end = tt * 128 + (127 if d == 0 else 0)
                            gtot = G[:, e_end:e_end + 1]
                            kb.act(tl["eX"][:], G[:, tsl], AF.Exp)
                            kb.tt("dve", tl["Rt"][:], rF[:, tsl], tl["eX"][:], ALU.mult)
                            kb.act(tl["eX2"][:], lw[:, tsl], AF.Exp)
                            kb.tt("dve", tl["Ct"][:], kkF[:, tsl], tl["eX2"][:], ALU.mult)
                            yield
                            kb.act(tl["eX"][:], G[:, tsl], AF.Exp, scale=-1.0)
                            kb.tt("dve", tl["kinv"][:], kd[:, tsl], tl["eX"][:], ALU.mult)
                            kb.tt("dve", tl["binv"][:], bF[:, tsl], tl["eX"][:], ALU.mult)
                            kb.act(tl["eX2"][:], G[:, tsl], AF.Exp, bias=gtot, scale=-1.0)
                            kb.tt("dve", tl["khat"][:], kd[:, tsl], tl["eX2"][:], ALU.mult)
                            kb.tt("dve", tl["bhat"][:], bF[:, tsl], tl["eX2"][:], ALU.mult)
                            kb.act(pCt[:], gtot, AF.Exp)
                            yield

                            def pre_head(h):
                                H = hd[h]
                                T = hT[h]
                                hm = hmask[:, h:h + 1]
                                kb.ts("dve", T["Ctm"][:], tl["Ct"][:], hm, ALU.mult)
                                kb.ts("dve", T["Rtm"][:], tl["Rt"][:], hm, ALU.mult)
                                p_ = pq()
                                kb.mm(p_, T["Ctm"][:], tl["binv"][:])
                                kb.stt(T["N"][:], p_, -1.0, m_ts, ALU.mult, ALU.mult)
                                p_ = pq()
                                kb.mm(p_, tl["binv"][:], T["Ctm"][:])
                                kb.stt(T["NT"][:], p_, -1.0, m_st, ALU.mult, ALU.mult)
                                yield
                                p_ = pq()
                                kb.mm(p_, tl["kinv"][:], T["Ctm"][:])
                                kb.tt("dve", H["BmT"][:], p_, m_st, ALU.mult)
                                p_ = pq()
                                kb.mm(p_, tl["kinv"][:], T["Rtm"][:])
                                kb.tt("dve", H["QKT"][:], p_, m_in, ALU.mult)
                                p_ = pq()
                                kb.mm(p_, tl["binv"][:], T["Rtm"][:])
                                kb.tt("dve", H["QBT"][:], p_, m_in, ALU.mult)
                                yield
                                for mi, (src, dstn) in enumerate([(vF[:, tsl], "vT"), (tl["khat"][:], "khT"), (tl["bhat"][:], "bhT")]):
                                    kb.ts("dve", T["msk"][mi][:], src, hm, ALU.mult)
                                    p_ = pq()
                                    kb.tr(p_, T["msk"][mi][:], ident)
                                    kb.copy("act", H[dstn][:], p_)
                                yield
                                yield from tri_inv_T(T["inv"], T["N"], T["NT"])
                                kb.copy("act", H["X"][:], T["inv"]["X"])
                                yield

                            gens = [pre_head(0), pre_head(1)]
                            while gens:
                                for g_ in list(gens):
                                    try:
                                        next(g_)
                                    except StopIteration:
                                        gens.remove(g_)
                                yield

                        def seq_gen(idx, d=d, c=c):
                            si, tt, first, last = order[idx]
                            tl = tlp[idx % 2]
                            hd = hdp[idx % 2]
                            pCt = pCp[idx % 2]
                            tsl = slice(tt * 128, (tt + 1) * 128)
                            if first:
                                kb.memset("dve", Mp[:], 0.0)
                                if hf == 1:
                                    for h in range(2):
                                        kb.dma("sp", Mp[h * 64:(h + 1) * 64, h * 64:(h + 1) * 64], srwkv_in[d, 2 * c + h, :, :])
                                kb.copy("dve", Mpb[:], Mp[:])
                            pz = pq()
                            kb.mm(pz, tl["Ct"][:], Mpb[:], start=True, stop=False)
                            kb.mm(pz, hd[0]["BmT"][:], hd[0]["vT"][:], start=False, stop=False)
                            kb.mm(pz, hd[1]["BmT"][:], hd[1]["vT"][:], start=False, stop=True)
                            kb.copy("act", tl["Z"][:], pz)
                            yield
                            for h in range(2):
                                p_ = pq()
                                kb.mm(p_, hd[h]["X"][:], tl["Z"][:])
                                kb.ts("dve", hd[h]["nU"][:, h * 64:(h + 1) * 64], p_[:, h * 64:(h + 1) * 64], -1.0, ALU.mult)
                            yield
                            py = pq()
                            kb.mm(py, Mpb[:], tl["Rt"][:], start=True, stop=False)
                            for h in range(2):
                                kb.mm(py, hd[h]["vT"][:], hd[h]["QKT"][:], start=False, stop=False)
                                kb.mm(py, hd[h]["nU"][:], hd[h]["QBT"][:], start=False, stop=(h == 1))
                            if d == 0:
                                kb.copy("act", yacc[:, tsl], py)
                            else:
                                kb.tt("dve", yacc[:, tsl], py, yacc[:, tsl], ALU.add)
                            pm_ = pq()
                            for h in range(2):
                                kb.mm(pm_, hd[h]["khT"][:], hd[h]["vT"][:], start=(h == 0), stop=False)
                                kb.mm(pm_, hd[h]["bhT"][:], hd[h]["nU"][:], start=False, stop=(h == 1))
                            kb.stt(Mp[:], Mp[:], pCt[:, 0:1], pm_, ALU.mult, ALU.add)
                            kb.copy("act", Mpb[:], Mp[:])
                            if last and hf == 0:
                                for h in range(2):
                                    kb.dma("sp", srwkv_out[si, d, 2 * c + h, :, :], Mp[h * 64:(h + 1) * 64, h * 64:(h + 1) * 64])
                            yield

                        interleave([prep_gen(0)])
                        for idx in range(len(order)):
                            gl = [seq_gen(idx)]
                            if idx + 1 < len(order):
                                gl.append(prep_gen(idx + 1))
                            interleave(gl)
                    for nt in range(2):
                        sl = slice(nt * 512, (nt + 1) * 512)
                        pst = pbank[nt]
                        kb.mm(pst[:], bo, yacc[:, sl])
                        kb.stt(yacc[:, sl], pst[:], -1.0 / 64, yacc[:, sl], ALU.mult, ALU.add)
                        kb.act(sq[nt][:], yacc[:, sl], AF.Square)
                        pv = pbank[2 + nt]
                        kb.mm(pv[:], bo, sq[nt][:])
                        kb.act(T1[:, sl], pv[:], AF.Sqrt, bias=64e-5, scale=1.0 / 64)
                    kb.recip(T1[:], T1[:])
                    kb.stt(yacc[:], yacc[:], P("rlnw")[:, c:c + 1], T1[:], ALU.mult, ALU.mult)
                    kb.ts("dve", yacc[:], yacc[:], P("rlnb")[:, c:c + 1], ALU.add)
                    kb.ts("dve", asum[:], asum[:], 0.5, ALU.mult)
                    kb.ts("dve", asum[:], asum[:], P("rka")[:, c:c + 1], ALU.mult, omka[:, c:c + 1], ALU.add)
                    kb.tt("dve", asum[:], asum[:], kF0[:], ALU.mult)
                    kb.stt(asum[:], asum[:], P("rrk")[:, c:c + 1], rF[:], ALU.mult, ALU.mult)
                    for nt in range(2):
                        sl = slice(nt * 512, (nt + 1) * 512)
                        pb_ = pbank[nt]
                        kb.mm(pb_[:], bo, asum[:, sl])
                        kb.tt("dve", T1[:, sl], pb_[:], vF[:, sl], ALU.mult)
                        kb.tt("dve", yacc[:, sl], yacc[:, sl], T1[:, sl], ALU.add)
                        pg_ = pbank[2 + nt]
                        kb.mm(pg_[:], gup[:, c * 128:(c + 1) * 128], sgd[:, sl])
                        kb.tt("dve", mch[:, sl], pg_[:], yacc[:, sl], ALU.mult)
                    wo_apply(c, mch)
                kb.barrier()

        def mla(hf, Hb, wo_apply):
            NK = 1024 if hf == 0 else 1280
            with contextlib.ExitStack() as ph:
                f = lambda name, shape, dt=F32: sbx(ph, "m_" + name, shape, dt)
                wch = [f("wch%d" % i, [128, 8, 128], BF16) for i in range(2)]
                wuq = f("wuq", [128, 3, 768], BF16)
                wukv = f("wukv", [128, 2, 1024], BF16)
                pqF = f("pqF", [128, 3, HALF])
                qnF = f("qnF", [128, 3, HALF], BF16)
                ckvF = f("ckvF", [128, 2, HALF])
                ckvB = f("ckvB", [128, 2, 1280], BF16)
                kpe96 = f("kpe96", [128, 1280])
                kpeB = f("kpeB", [128, 1280], BF16)
                rq = f("rq", [128, HALF])
                Ve = f("Ve", [128, 10, 128], BF16)
                Vo = f("Vo", [128, 10, 128], BF16)
                qTf = f("qTf", [128, HALF])
                qTb = [f("qTb%d" % e, [128, HALF], BF16) for e in range(2)]
                KTb = [f("KTb%d" % e, [128, 1280], BF16) for e in range(2)]
                PT = [f("PT%d" % i, [128, 512], BF16) for i in range(2)]
                oE = f("oE", [128, 128], BF16)
                oO = f("oO", [128, 128], BF16)
                rs_ = f("rs", [128, 512])
                mch = f("mch", [128, HALF], BF16)
                t1, t2 = f("t1", [128, 512]), f("t2", [128, 512])
                cosT, sinT = f("cosT", [128, HALF]), f("sinT", [128, HALF])
                kb.copy("dve", oE[:], C("onesE"))
                kb.copy("dve", oO[:], C("onesO"))
                kb.memset("dve", kpe96[:], 0.0)
                kb.memset("dve", qTf[:], 0.0)
                kb.dma("pool", wuq[:], mla_w_uq[0].rearrange("(k p) c -> p k c", p=128))
                kb.dma("pool", wukv[:], mla_w_ukv[0].rearrange("(k p) c -> p k c", p=128))
                if hf == 1:
                    kb.dma("sp", cosT[:], ropecs[0, :, :])
                    kb.dma("sp", sinT[:], ropecs[1, :, :])
                wi = [0]

                def project(c0, M, dst_fn):
                    wt = wch[wi[0] % 2]
                    wi[0] += 1
                    kb.dma("pool", wt[:, :, 0:M], cd_w_in[0, :, c0:c0 + M].rearrange("(k p) c -> p k c", p=128))
                    for nt in range(2):
                        pp = pbank[4 + nt]
                        for k in range(8):
                            kb.mm(pp[0:M, :], wt[:, k, 0:M], Hb[:, k, nt * 512:(nt + 1) * 512], start=(k == 0), stop=(k == 7))
                        dst_fn(nt, pp)

                for j in range(3):
                    project(j * 128, 128, lambda nt, pp, j=j: kb.copy("act", pqF[:, j, nt * 512:(nt + 1) * 512], pp[:]))
                for j in range(2):
                    project(384 + j * 128, 128, lambda nt, pp, j=j: kb.copy("act", ckvF[:, j, nt * 512:(nt + 1) * 512], pp[:]))
                project(576, 96, lambda nt, pp: kb.copy("act", kpe96[64:96, nt * 512:(nt + 1) * 512], pp[64:96, :]))

                def rmsn(src, nj, gname, outs):
                    for nt in range(2):
                        sl = slice(nt * 512, (nt + 1) * 512)
                        pst = pbank[4 + nt]
                        for j in range(nj):
                            kb.act(sq[j % 2][:], src[:, j, sl], AF.Square)
                            kb.mm(pst[:], ones[:], sq[j % 2][:], start=(j == 0), stop=(j == nj - 1))
                        kb.act(rq[:, sl], pst[:], AF.Sqrt, bias=EPS, scale=1.0 / (nj * 128))
                    kb.recip(rq[:], rq[:])
                    for j in range(nj):
                        for o in outs:
                            kb.stt(o(j), src[:, j, :], P(gname)[:, j:j + 1], rq[:], ALU.mult, ALU.mult)

                rmsn(pqF, 3, "mqn", [lambda j: qnF[:, j, :]])
                rmsn(ckvF, 2, "mkvn", [lambda j: ckvB[:, j, 0:HALF], lambda j: ckvF[:, j, :]])
                if hf == 0:
                    for j in range(2):
                        kb.dma("sp", ckv_out[j * 128:(j + 1) * 128, :], ckvF[:, j, :])
                    kb.dma("sp", kpe_out[:, :], kpe96[64:96, 0:HALF])
                    kb.copy("dve", kpeB[64:96, 0:HALF], kpe96[64:96, 0:HALF])
                else:
                    for j in range(2):
                        kb.dma("pool", ckvB[:, j, HALF:1280], cckvT[j * 128:(j + 1) * 128, :])
                    kb.dma("sp", kpe96[64:96, HALF:1280], ckpeT[:, :])
                    kb.copy("dve", kpeB[64:96, HALF:1280], kpe96[64:96, HALF:1280])

                if stage == 1:
                    kb.barrier()
                    return

                def rope(src, dstb):
                    for nt in range(2):
                        sl = slice(nt * 512, (nt + 1) * 512)
                        pj = pbank[4 + nt]
                        kb.mm(pj[0:96, :], C("jpad")[0:96, 0:96], src[0:96, sl])
                        kb.tt("dve", t1[64:96, :], src[64:96, sl], cosT[64:96, sl], ALU.mult)
                        kb.tt("dve", t2[64:96, :], pj[64:96, :], sinT[64:96, sl], ALU.mult)
                        kb.tt("dve", dstb[64:96, sl], t1[64:96, :], t2[64:96, :], ALU.add)

                if hf == 1:
                    rope(kpe96, kpeB)
                if stage == 2:
                    kb.barrier()
                    return
                kb.memset("dve", Ve[:].rearrange("p a b -> p (a b)"), 0.0)
                kb.memset("dve", Vo[:].rearrange("p a b -> p (a b)"), 0.0)
                scale = 96.0 ** -0.5
                if hf == 0:
                    qranges = [(si * 256, 256, [2 * si, 2 * si + 1]) for si in range(4)]
                else:
                    qranges = [(nt * 512, 512, list(range(10))) for nt in range(2)]
                pti = [0]
                for c in range(4):
                    for kt in range(NK // 128):
                        pv = pbank[4 + kt % 2]
                        for e in range(2):
                            v0 = (2 * c + e) * 128 + 64
                            for k in range(2):
                                kb.mm(pv[:, e * 64:(e + 1) * 64], ckvB[:, k, kt * 128:(kt + 1) * 128], wukv[:, k, v0:v0 + 64], start=(k == 0), stop=(k == 1))
                        kb.copy("act", Ve[:, kt, 0:64], pv[:, 0:64])
                        kb.copy("dve", Vo[:, kt, 64:128], pv[:, 64:128])
                    if stage == 31:
                        continue
                    for e in range(2):
                        h = 2 * c + e
                        for nt in range(2):
                            sl = slice(nt * 512, (nt + 1) * 512)
                            pqh = pbank[4 + nt]
                            for k in range(3):
                                kb.mm(pqh[0:96, :], wuq[:, k, h * 96:(h + 1) * 96], qnF[:, k, sl], start=(k == 0), stop=(k == 2))
                            kb.copy("act", qTb[e][0:64, sl], pqh[0:64, :])
                            if hf == 1:
                                kb.copy("dve", qTf[64:96, sl], pqh[64:96, :])
                            else:
                                kb.copy("dve", qTb[e][64:96, sl], pqh[64:96, :])
                        if hf == 1:
                            rope(qTf, qTb[e])
                        if stage == 32:
                            continue
                        for k0 in range(0, NK, 512):
                            n = min(512, NK - k0)
                            pk = pbank[4 + (k0 // 512) % 2]
                            for k in range(2):
                                kb.mm(pk[:, 0:n], wukv[:, k, h * 128:(h + 1) * 128], ckvB[:, k, k0:k0 + n], start=(k == 0), stop=(k == 1))
                            kb.copy("act", KTb[e][0:64, k0:k0 + n], pk[0:64, 0:n])
                        kb.copy("dve", KTb[e][64:96, 0:NK], kpeB[64:96, 0:NK])
                    if stage == 3:
                        continue
                    for (q0, nq, kts) in qranges:
                        po, psm = pbank[2], pbank[3]
                        first = True
                        for e in range(2):
                            Vsrc = Ve if e == 0 else Vo
                            osrc = oE if e == 0 else oO
                            for ki, kt in enumerate(kts):
                                ps_ = pbank[pti[0] % 2]
                                pt = PT[pti[0] % 2]
                                pti[0] += 1
                                ksl = slice(kt * 128, (kt + 1) * 128)
                                kb.mm(ps_[:, 0:nq], KTb[e][0:96, ksl], qTb[e][0:96, q0:q0 + nq])
                                kb.act(pt[:, 0:nq], ps_[:, 0:nq], AF.Exp, scale=scale)
                                last = (e == 1 and ki == len(kts) - 1)
                                kb.mm(po[:, 0:nq], Vsrc[:, kt, :], pt[:, 0:nq], start=first, stop=last)
                                kb.mm(psm[:, 0:nq], osrc[:], pt[:, 0:nq], start=first, stop=last)
                                first = False
                        kb.recip(rs_[:, 0:nq], psm[:, 0:nq])
                        kb.tt("dve", mch[:, q0:q0 + nq], po[:, 0:nq], rs_[:, 0:nq], ALU.mult)
                    wo_apply(c, mch)
                kb.barrier()

        def hyena(hf, Hb, wo_apply):
            Ls = 256 if hf == 0 else 1024
            nT = Ls // 128
            with contextlib.ExitStack() as ph:
                f = lambda name, shape, dt=F32: sbx(ph, "y_" + name, shape, dt)
                wch = [f("wch%d" % i, [128, 8, 128], BF16) for i in range(1)]
                vF, x1F, x2F, T1 = f("vF", [128, HALF]), f("x1F", [128, HALF]), f("x2F", [128, HALF]), f("T1", [128, HALF])
                fwd = f("fwd", [128, nT, 2, Ls], BF16)
                invp = [f("invp%d" % i, [128, 2, Ls], BF16) for i in range(2)]
                w3 = f("w3", [128, 2048], BF16)
                featT = x1F[:, 0:Ls]
                h1s, h2s = T1[:, 0:Ls], x2F[:, 0:Ls]
                h2b = f("h2b", [128, Ls], BF16)
                win = f("win", [128, nT, 128])
                fr3, fb3 = f("fr3", [128, 2]), f("fb3", [128, 2])
                hsum, hdif = f("hsum", [128, nT, 128], BF16), f("hdif", [128, nT, 128], BF16)
                Hc, Hs = f("Hc", [128, nT, 128]), f("Hs", [128, nT, 128])
                zT = f("zT", [128, nT, 128], BF16)
                Yc, Ys = f("Yc", [128, nT, 128], BF16), f("Ys", [128, nT, 128], BF16)
                ta, tb, tc_ = f("ta", [128, 512]), f("tb", [128, 512]), f("tc", [128, 512])
                mch = f("mch", [128, HALF], BF16)
                for i in range(2):
                    kb.dma("sp", fwd[:, :, i, :], dftd[Ls][i].rearrange("(st p) f -> p st f", p=128))
                kb.dma("pool", w3[0:64, :], hy_w3[0, :, :])
                kb.dma("sp", featT, featd[Ls][:, :])
                kb.ts("dve", fr3[0:64, :], P("hfreq")[0:64, :], 1.0 / 3, ALU.mult)
                kb.tt("dve", fb3[0:64, 0:1], fr3[0:64, 0:1], P("hb1")[0:64, :], ALU.mult)
                kb.tt("dve", fb3[0:64, 1:2], fr3[0:64, 1:2], P("hb2")[0:64, :], ALU.mult)

                def sin3(dst, pin, li, n):
                    kb.act(ta[0:64, 0:n], pin, AF.Sin, bias=fb3[0:64, li:li + 1], scale=fr3[0:64, li:li + 1])
                    kb.tt("dve", tb[0:64, 0:n], ta[0:64, 0:n], ta[0:64, 0:n], ALU.mult)
                    kb.ts("dve", tb[0:64, 0:n], tb[0:64, 0:n], -4.0, ALU.mult, 3.0, ALU.add)
                    kb.tt("dve", dst, ta[0:64, 0:n], tb[0:64, 0:n], ALU.mult)

                for c0 in range(0, Ls, 512):
                    n = min(512, Ls - c0)
                    p1 = pbank[4]
                    kb.mm(p1[0:64, 0:n], P("hw1")[0:64, :], featT[0:64, c0:c0 + n])
                    sin3(h1s[0:64, c0:c0 + n], p1[0:64, 0:n], 0, n)
                    p2 = pbank[5]
                    kb.mm(p2[0:64, 0:n], P("hw2")[0:64, :], h1s[0:64, c0:c0 + n])
                    sin3(h2s[0:64, c0:c0 + n], p2[0:64, 0:n], 1, n)
                kb.copy("dve", h2b[0:64, :], h2s[0:64, :])
                wi = [0]
                hcv = P("hconv").rearrange("p (t c) -> p t c", t=3)
                ipi = [0]
                for cc in range(4):
                    kb.dma("sp", win[:], wind[Ls][:, cc * 128:(cc + 1) * 128].rearrange("(t p) c -> p t c", p=128))
                    for qi, dst in enumerate([vF, x1F, x2F]):
                        wt = wch[0]
                        wi[0] += 1
                        c0 = 672 + qi * 512 + cc * 128
                        kb.dma("pool", wt[:], cd_w_in[0, :, c0:c0 + 128].rearrange("(k p) c -> p k c", p=128))
                        for nt in range(2):
                            pp = pbank[4 + nt]
                            for k in range(8):
                                kb.mm(pp[:], wt[:, k, :], Hb[:, k, nt * 512:(nt + 1) * 512], start=(k == 0), stop=(k == 7))
                            kb.copy("act", T1[:, nt * 512:(nt + 1) * 512], pp[:])
                        ci = qi * 4 + cc
                        kb.ts("dve", dst[:], T1[:], hcv[:, 1, ci:ci + 1], ALU.mult)
                        for (s0, Lq) in SEQS[hf]:
                            kb.stt(dst[:, s0 + 1:s0 + Lq], T1[:, s0:s0 + Lq - 1], hcv[:, 0, ci:ci + 1], dst[:, s0 + 1:s0 + Lq], ALU.mult, ALU.add)
                            kb.stt(dst[:, s0:s0 + Lq - 1], T1[:, s0 + 1:s0 + Lq], hcv[:, 2, ci:ci + 1], dst[:, s0:s0 + Lq - 1], ALU.mult, ALU.add)
                    z = vF
                    for n in range(2):
                        gate = x1F if n == 0 else x2F
                        bcol = P("hbias").rearrange("p (n c) -> p n c", n=2)[:, n, cc:cc + 1]
                        for dt in range(nT):
                            pt_ = pbank[4 + dt % 2]
                            for di in range(2):
                                w0 = (n * 2 + di) * 512 + cc * 128
                                kb.mm(pt_[:, di * 128:(di + 1) * 128], h2b[0:64, dt * 128:(dt + 1) * 128], w3[0:64, w0:w0 + 128])
                            kb.tt("dve", ta[:, 0:128], pt_[:, 0:128], win[:, dt, :], ALU.mult)
                            kb.tt("dve", tb[:, 0:128], pt_[:, 128:256], win[:, dt, :], ALU.mult)
                            if dt == 0:
                                kb.ts("dve", tb[:, 0:128], tb[:, 0:128], P("nz0"), ALU.mult)
                            kb.tt("dve", hsum[:, dt, :], ta[:, 0:128], tb[:, 0:128], ALU.add)
                            kb.tt("dve", hdif[:, dt, :], ta[:, 0:128], tb[:, 0:128], ALU.subtract)
                        for ft in range(nT):
                            pc_ = pbank[4 + ft % 2]
                            for dt in range(nT):
                                kb.mm(pc_[:, 0:128], fwd[:, dt, 0, ft * 128:(ft + 1) * 128], hsum[:, dt, :], start=(dt == 0), stop=(dt == nT - 1))
                            kb.copy("act", Hc[:, ft, :], pc_[:, 0:128])
                            for dt in range(nT):
                                kb.mm(pc_[:, 128:256], fwd[:, dt, 1, ft * 128:(ft + 1) * 128], hdif[:, dt, :], start=(dt == 0), stop=(dt == nT - 1))
                            kb.copy("act", Hs[:, ft, :], pc_[:, 128:256])
                        for (s0, Lq) in SEQS[hf]:
                            for st in range(nT):
                                ptr = pbank[4 + st % 2]
                                kb.tr(ptr[:, 0:128], z[:, s0 + st * 128:s0 + (st + 1) * 128], ident)
                                kb.copy("act", zT[:, st, :], ptr[:, 0:128])
                            for ft in range(nT):
                                pzc, pzs = pbank[2], pbank[3]
                                for st in range(nT):
                                    kb.mm(pzc[:, 0:128], fwd[:, st, 0, ft * 128:(ft + 1) * 128], zT[:, st, :], start=(st == 0), stop=(st == nT - 1))
                                for st in range(nT):
                                    kb.mm(pzs[:, 0:128], fwd[:, st, 1, ft * 128:(ft + 1) * 128], zT[:, st, :], start=(st == 0), stop=(st == nT - 1))
                                kb.tt("dve", ta[:, 0:128], pzc[:, 0:128], Hc[:, ft, :], ALU.mult)
                                kb.tt("dve", tb[:, 0:128], pzs[:, 0:128], Hs[:, ft, :], ALU.mult)
                                kb.tt("dve", Yc[:, ft, :], ta[:, 0:128], tb[:, 0:128], ALU.subtract)
                                kb.tt("dve", ta[:, 0:128], pzc[:, 0:128], Hs[:, ft, :], ALU.mult)
                                kb.tt("dve", tb[:, 0:128], pzs[:, 0:128], Hc[:, ft, :], ALU.mult)
                                kb.tt("dve", Ys[:, ft, :], ta[:, 0:128], tb[:, 0:128], ALU.add)
                            blocks = [(b0, min(512, Lq - b0)) for b0 in range(0, Lq, 512)]
                            pys = [pbank[0], pbank[1]]
                            for ft in range(nT):
                                ip = invp[ipi[0] % 2]
                                ipi[0] += 1
                                for i in range(2):
                                    kb.dma("sp", ip[:, i, :], dftd[Ls][2 + i, ft * 128:(ft + 1) * 128, :])
                                for bi, (b0, bn) in enumerate(blocks):
                                    kb.mm(pys[bi][:, 0:bn], Yc[:, ft, :], ip[:, 0, b0:b0 + bn], start=(ft == 0), stop=False)
                                    kb.mm(pys[bi][:, 0:bn], Ys[:, ft, :], ip[:, 1, b0:b0 + bn], start=False, stop=(ft == nT - 1))
                            for bi, (b0, bn) in enumerate(blocks):
                                zs = z[:, s0 + b0:s0 + b0 + bn]
                                kb.ts("dve", tc_[:, 0:bn], zs, bcol, ALU.mult)
                                kb.stt(tc_[:, 0:bn], pys[bi][:, 0:bn], 1.0 / Lq, tc_[:, 0:bn], ALU.mult, ALU.add)
                                kb.tt("dve", zs, tc_[:, 0:bn], gate[:, s0 + b0:s0 + b0 + bn], ALU.mult)
                    kb.copy("dve", mch[:], z[:])
                    wo_apply(4 + cc, mch)
                kb.barrier()

        def mixer_phase(L):
            with contextlib.ExitStack() as ph:
                Hb = sbx(ph, "Hbm", [128, 8, HALF], BF16)
                wo = [sbx(ph, "wo%d" % i, [128, D], BF16) for i in range(2)]
                woi = [0]
                for hf in range(2):
                    rms_stats(hf)
                    make_coefs(L, hf, 1, "nmx_%d" % L)
                    make_H(Hb, hf)

                    def wo_apply(j, mc, hf=hf):
                        w = wo[woi[0] % 2]
                        woi[0] += 1
                        kb.dma("pool", w[:], w_out_d[L, j * 128:(j + 1) * 128, :])
                        for m in range(8):
                            for nt in range(2):
                                po = pbank[6 + (m * 2 + nt) % 2]
                                kb.mm(po[:], w[:, m * 128:(m + 1) * 128], mc[:, nt * 512:(nt + 1) * 512])
                                t0 = hf * HALF + nt * 512
                                kb.stt(X[:, m, t0:t0 + 512], po[:], coef[:, 16 + m:17 + m], X[:, m, t0:t0 + 512], ALU.mult, ALU.add)

                    if L == 0:
                        if test in (None, "rwkv", "l0"):
                            rwkv(hf, Hb, wo_apply)
                        if test in (None, "gdn", "l0"):
                            gdn(hf, Hb, wo_apply)
                    else:
                        if test in (None, "mla", "l1"):
                            mla(hf, Hb, wo_apply)
                        if test in (None, "hy", "l1"):
                            hyena(hf, Hb, wo_apply)
                kb.barrier()

        def final_norm_and_store():
            for hf in range(2):
                rms_stats(hf)
                for j in range(8):
                    for nt in range(2):
                        t0 = hf * HALF + nt * 512
                        tf = tmpf[(j * 2 + nt) % 2]
                        kb.stt(tf[:], X[:, j, t0:t0 + 512], P("fnorm")[:, j:j + 1], rstd[:, nt * 512:(nt + 1) * 512], ALU.mult, ALU.mult)
                        kb.dma("sp", yT[j * 128:(j + 1) * 128, t0:t0 + 512], tf[:])

        if test in ("gdn", "rwkv", "l0"):
            mixer_phase(0)
        elif test in ("mla", "hy", "l1"):
            mixer_phase(1)
        else:
            for L in range(2):
                ffn_phase(L, 0)
                mixer_phase(L)
                ffn_phase(L, 1)
        final_norm_and_store()
        kb.barrier()
        print("instructions:", kb.ninst)
    return nc


def prep_inputs(inp):
    inp = {k: np.asarray(v) for k, v in inp.items()}
    maps = []
    offs = None
    cnames, carr = make_consts()
    ropecs = rope_tables()
    hc = {L: hyena_consts(L) for L in (256, 1024)}
    for c in range(NCORE):
        xp = inp["x_prompt"][4 * c:4 * c + 4].reshape(HALF, D)
        xs = inp["x_sample"][c]
        xT = np.ascontiguousarray(np.concatenate([xp, xs], 0).T)
        Pk = pack_params(inp, c)
        prm = Pk.build()
        offs = Pk.off
        m = {"xT": xT, "prm": prm, "cst": carr, "w_mod": inp["w_mod"],
             "ffn1_w_gu": inp["ffn1_w_gu"], "ffn2_w_gu": inp["ffn2_w_gu"],
             "ffn1_w_down": inp["ffn1_w_down"], "ffn2_w_down": inp["ffn2_w_down"],
             "w_out": inp["w_out"], "ab_w_in": inp["ab_w_in"],
             "sgdn_in": np.ascontiguousarray(inp["state_gdn"][c, 0]),
             "srwkv_in": np.ascontiguousarray(inp["state_rwkv"][c, 0].transpose(0, 1, 3, 2)),
             "rwkv_w_up": inp["rwkv_w_up"], "rwkv_a_up": inp["rwkv_a_up"], "rwkv_g_up": inp["rwkv_g_up"],
             "cd_w_in": inp["cd_w_in"], "mla_w_uq": inp["mla_w_uq"], "mla_w_ukv": inp["mla_w_ukv"],
             "cckvT": np.ascontiguousarray(inp["cache_ckv"][c, 0].T), "ckpeT": np.ascontiguousarray(inp["cache_kpe"][c, 0].T),
             "ropecs": ropecs, "hy_w1": inp["hy_w1"], "hy_w2": inp["hy_w2"], "hy_w3": inp["hy_w3"],
             "dft256": hc[256][0], "dft1024": hc[1024][0], "feat256": hc[256][1], "feat1024": hc[1024][1],
             "win256": hc[256][2], "win1024": hc[1024][2]}
        maps.append(m)
    return maps, offs


def kernel(**inputs):
    maps, offs = prep_inputs(inputs)
    NP = maps[0]["prm"].shape[1]
    nc = build_program(offs, NP)
    res = run_bass_kernel_spmd(nc, maps, core_ids=list(range(NCORE)))
    yp = np.zeros((32, 256, D), np.float32)
    ys = np.zeros((8, 1024, D), np.float32)
    srw = np.zeros((32, 1, 2, 8, 64, 64), np.float32)
    sgd = np.zeros((32, 1, 2, 4, 128, 128), np.float32)
    ckv = np.zeros((32, 1, 256, 256), np.float32)
    kpe = np.zeros((32, 1, 256, 32), np.float32)
    for c in range(NCORE):
        r = res.results[c]
        yT = r["yT"]
        yp[4 * c:4 * c + 4] = yT[:, :HALF].T.reshape(4, 256, D)
        ys[c] = yT[:, HALF:].T
        srw[4 * c:4 * c + 4, 0] = np.asarray(r["srwkv_out"]).transpose(0, 1, 2, 4, 3)
        sgd[4 * c:4 * c + 4, 0] = np.asarray(r["sgdn_out"])
        ckv[4 * c:4 * c + 4, 0] = np.asarray(r["ckv_out"]).T.reshape(4, 256, 256)
        kpe[4 * c:4 * c + 4, 0] = np.asarray(r["kpe_out"]).T.reshape(4, 256, 32)
    return yp, ys, srw, sgd, ckv, kpe
```

```python
import contextlib
import numpy as np
import concourse.bass as bass
import concourse.mybir as mybir
from concourse.bass_utils import run_bass_kernel_spmd

F32 = mybir.dt.float32
BF16 = mybir.dt.bfloat16
F32R = mybir.dt.float32r
FP16 = mybir.dt.float16
AF = mybir.ActivationFunctionType
ALU = mybir.AluOpType

NCORE = 8
D = 1024
DFF = 2816
TOK = 2048
HALF = 1024
EPS = 1e-6


class KB:
    def __init__(self, nc, es, n_dma_sems=24):
        self.nc = nc
        self.es = es
        self.eng = {"pe": nc.tensor, "act": nc.scalar, "dve": nc.vector, "pool": nc.gpsimd, "sp": nc.sync}
        self.sem = {k: es.enter_context(nc.semaphore("sem_" + k)) for k in self.eng}
        self.cnt = {k: 0 for k in self.eng}
        self.seen = {k: {} for k in self.eng}
        self.dsem = [es.enter_context(nc.semaphore("dsem%d" % i)) for i in range(n_dma_sems)]
        self.dcnt = [0] * n_dma_sems
        self.drr = {"sp": 0, "pool": 0, "act": 0}
        self.dpool = {"sp": list(range(0, n_dma_sems // 2)), "act": list(range(0, n_dma_sems // 2)),
                      "pool": list(range(n_dma_sems // 2, n_dma_sems))}
        self.recs = {}
        self.semobj = {}
        for k in self.eng:
            self.semobj[k] = self.sem[k]
        for i, s in enumerate(self.dsem):
            self.semobj[("d", i)] = s
        self.ninst = 0
        self.rt = set()

    def R(self, ap):
        if ap is not None and not isinstance(ap, (int, float)) and ap.tensor.name in self.rt:
            return ap.bitcast(F32R)
        return ap

    @staticmethod
    def box(ap):
        t = ap.tensor
        dims = list(ap.ap)
        if type(t).__name__.startswith("DRam"):
            f0 = ap.offset
            f1 = f0 + sum((c - 1) * abs(s) for s, c in dims) + 1
            return (t.name, 0, 1, f0, f1)
        ps, pc = dims[0]
        if ps == 0:
            ps = 1 << 40
        if type(t).__name__.startswith("PSum"):
            return (t.name, 0, 128, 0, 1 << 30)
        p0 = ap.offset // ps
        f0 = ap.offset % ps
        f1 = f0 + sum((c - 1) * abs(s) for s, c in dims[1:]) + 1
        return (t.name, p0, p0 + pc, f0, f1)

    def _deps(self, b, write, deps, eng=None):
        name, p0, p1, f0, f1 = b
        lst = self.recs.get(name)
        if not lst:
            return
        psum = f1 == (1 << 30)
        for r in lst:
            if r[0] < p1 and p0 < r[1] and r[2] < f1 and f0 < r[3]:
                if write or r[6] or (psum and r[7] != eng):
                    k = r[4]
                    if deps.get(k, 0) < r[5]:
                        deps[k] = r[5]

    def _record(self, b, write, semkey, val, eng):
        name, p0, p1, f0, f1 = b
        lst = self.recs.setdefault(name, [])
        if write:
            lst[:] = [r for r in lst if not (p0 <= r[0] and r[1] <= p1 and f0 <= r[2] and r[3] <= f1)]
        else:
            lst[:] = [r for r in lst if not ((not r[6]) and r[7] == eng and r[4] == semkey
                                             and p0 <= r[0] and r[1] <= p1 and f0 <= r[2] and r[3] <= f1)]
        lst.append((p0, p1, f0, f1, semkey, val, write, eng))

    def _wait(self, e, deps):
        seen = self.seen[e]
        for k, v in deps.items():
            if e == "pe" and k == "pe":
                continue
            if seen.get(k, 0) < v:
                self.eng[e].wait_ge(self.semobj[k], v)
                seen[k] = v

    def emit(self, e, fn, outs, ins):
        deps = {}
        ob = [self.box(a) for a in outs]
        ib = [self.box(a) for a in ins if a is not None and not isinstance(a, (int, float))]
        for b in ib:
            self._deps(b, False, deps, e)
        for b in ob:
            self._deps(b, True, deps, e)
        self._wait(e, deps)
        inst = fn()
        self.cnt[e] += 1
        inst.then_inc(self.sem[e], 1)
        v = self.cnt[e]
        for b in ib:
            self._record(b, False, e, v, e)
        for b in ob:
            self._record(b, True, e, v, e)
        self.ninst += 1
        return inst

    def dma(self, q, out, in_):
        deps = {}
        ob = self.box(out)
        ib = self.box(in_)
        self._deps(ib, False, deps)
        self._deps(ob, True, deps)
        pl = self.dpool[q]
        i = pl[self.drr[q] % len(pl)]
        self.drr[q] += 1
        k = ("d", i)
        if self.dcnt[i] > 0:
            deps[k] = max(deps.get(k, 0), self.dcnt[i])
        self._wait(q, deps)
        inst = self.eng[q].dma_start(out=out, in_=in_)
        self.dcnt[i] += 16
        inst.then_inc(self.dsem[i], 16)
        self._record(ib, False, k, self.dcnt[i], "dma")
        self._record(ob, True, k, self.dcnt[i], "dma")
        self.ninst += 1

    def barrier(self, engines=None):
        engines = engines or list(self.eng)
        for e in engines:
            deps = {o: self.cnt[o] for o in self.eng if o != e and self.cnt[o] > 0}
            for i, c in enumerate(self.dcnt):
                if c > 0:
                    deps[("d", i)] = c
            self._wait(e, deps)

    def mm(self, out, lhsT, rhs, start=True, stop=True):
        nc = self.nc
        if lhsT.tensor.name in self.rt and rhs.tensor.name in self.rt:
            lhsT, rhs = lhsT.bitcast(F32R), rhs.bitcast(F32R)
        return self.emit("pe", lambda: nc.tensor.matmul(out, lhsT, rhs, start=start, stop=stop), [out], [lhsT, rhs])

    def tr(self, out, in_, ident):
        nc = self.nc
        return self.emit("pe", lambda: nc.tensor.transpose(out, in_, ident), [out], [in_, ident])

    def act(self, out, in_, func, bias=None, scale=1.0):
        nc = self.nc
        kw = {}
        if bias is not None:
            kw["bias"] = bias
        ins = [in_]
        if bias is not None and not isinstance(bias, (int, float)):
            ins.append(bias)
        if not isinstance(scale, (int, float)):
            ins.append(scale)
        out = self.R(out)
        return self.emit("act", lambda: nc.scalar.activation(out=out, in_=in_, func=func, scale=scale, **kw), [out], ins)

    def tt(self, e, out, in0, in1, op):
        eng = self.eng[e]
        out = self.R(out)
        return self.emit(e, lambda: eng.tensor_tensor(out=out, in0=in0, in1=in1, op=op), [out], [in0, in1])

    def ts(self, e, out, in0, s1, op0, s2=None, op1=None):
        eng = self.eng[e]
        out = self.R(out)
        ins = [in0] + [s for s in (s1, s2) if s is not None and not isinstance(s, (int, float))]
        if op1 is None:
            return self.emit(e, lambda: eng.tensor_scalar(out=out, in0=in0, scalar1=s1, scalar2=None, op0=op0), [out], ins)
        return self.emit(e, lambda: eng.tensor_scalar(out=out, in0=in0, scalar1=s1, scalar2=s2, op0=op0, op1=op1), [out], ins)

    def stt(self, out, in0, scalar, in1, op0, op1):
        nc = self.nc
        out = self.R(out)
        ins = [in0, in1] + ([scalar] if not isinstance(scalar, (int, float)) else [])
        return self.emit("dve", lambda: nc.vector.scalar_tensor_tensor(out=out, in0=in0, scalar=scalar, in1=in1, op0=op0, op1=op1), [out], ins)

    def copy(self, e, out, in_):
        eng = self.eng[e]
        out = self.R(out)
        if e == "act":
            return self.emit(e, lambda: eng.copy(out=out, in_=in_), [out], [in_])
        return self.emit(e, lambda: eng.tensor_copy(out=out, in_=in_), [out], [in_])

    def recip(self, out, in_):
        nc = self.nc
        return self.emit("dve", lambda: nc.vector.reciprocal(out=out, in_=in_), [out], [in_])

    def memset(self, e, out, val):
        eng = self.eng[e]
        if out.tensor.name in self.rt:
            return self.ts("dve", out, self.ones_ap, float(val), ALU.mult)
        return self.emit(e, lambda: eng.memset(out, val), [out], [])


def fm(v, nchunk):
    return np.ascontiguousarray(np.asarray(v, np.float32).reshape(nchunk, 128).T)


def rows(v):
    v = np.asarray(v, np.float32).reshape(1, -1)
    return np.ascontiguousarray(np.repeat(v, 128, axis=0))


class Pack:
    def __init__(self):
        self.cols = []
        self.off = {}
        self.n = 0

    def add(self, name, arr):
        arr = np.asarray(arr, np.float32)
        if arr.shape[0] < 128:
            arr = np.concatenate([arr, np.zeros((128 - arr.shape[0],) + arr.shape[1:], np.float32)], 0)
        arr = arr.reshape(128, -1)
        self.off[name] = (self.n, arr.shape[1])
        self.n += arr.shape[1]
        self.cols.append(arr)

    def build(self):
        return np.ascontiguousarray(np.concatenate(self.cols, 1))


def pack_params(inp, core):
    P = Pack()
    c = inp["c"][core]
    cc = np.stack([fm(inp["c_ctx"], 8), fm(c, 8)], axis=2)
    P.add("cT", cc)
    for i in range(2):
        P.add("bmod%d" % i, fm(inp["b_mod"][i], 72))
        P.add("nf1_%d" % i, fm(inp["norm_ffn1"][i], 8))
        P.add("nmx_%d" % i, fm(inp["norm_mix"][i], 8))
        P.add("nf2_%d" % i, fm(inp["norm_ffn2"][i], 8))
    P.add("fnorm", fm(inp["final_norm"], 8))
    P.add("gconv", np.stack([fm(inp["gdn_conv"][0][t], 12) for t in range(3)], axis=1))
    P.add("gnorm", np.asarray(inp["gdn_norm"][0], np.float32).reshape(128, 1))
    P.add("galog", rows(inp["gdn_a_log"][0]))
    P.add("gdtb", rows(inp["gdn_dt_bias"][0]))
    P.add("rmu", fm(inp["rwkv_mu"][0], 15))
    P.add("rw0", np.stack([fm(inp["rwkv_w0"][0][d], 4) for d in range(2)], axis=1))
    P.add("ra0", np.stack([fm(inp["rwkv_a0"][0][d], 4) for d in range(2)], axis=1))
    P.add("rkk", fm(inp["rwkv_k_k"][0], 4))
    P.add("rka", fm(inp["rwkv_k_a"][0], 4))
    P.add("rrk", fm(inp["rwkv_r_k"][0].reshape(-1), 4))
    P.add("rlnw", fm(inp["rwkv_ln_w"][0], 4))
    P.add("rlnb", fm(inp["rwkv_ln_b"][0], 4))
    P.add("mqn", fm(inp["mla_q_norm"][0], 3))
    P.add("mkvn", fm(inp["mla_kv_norm"][0], 2))
    P.add("hconv", np.stack([fm(inp["hy_conv"][0][t], 12) for t in range(3)], axis=1))
    P.add("hbias", np.stack([fm(inp["hy_bias"][0][n], 4) for n in range(2)], axis=1))
    P.add("hb1", np.asarray(inp["hy_b1"][0], np.float32).reshape(64, 1))
    P.add("hb2", np.asarray(inp["hy_b2"][0], np.float32).reshape(64, 1))
    P.add("hfreq", np.ascontiguousarray(np.asarray(inp["hy_freq"][0], np.float32).T))
    nz0 = np.ones((128, 1), np.float32)
    nz0[0, 0] = 0.0
    P.add("nz0", nz0)
    w1p = np.zeros((128, 64), np.float32)
    w1p[:33] = np.asarray(inp["hy_w1"][0], np.float32)
    P.add("hw1", w1p)
    P.add("hw2", np.asarray(inp["hy_w2"][0], np.float32))
    hm = np.zeros((128, 2), np.float32)
    hm[:64, 0] = 1.0
    hm[64:, 1] = 1.0
    P.add("hmask", hm)
    rm = np.ones((128, 1024), np.float32)
    rm[:, ::128] = 0.0
    P.add("rmask", rm)
    return P


def make_consts():
    p = np.arange(128)[:, None]
    f = np.arange(128)[None, :]
    ident = (p == f).astype(np.float32)
    le = (p <= f).astype(np.float32)
    lt = (p < f).astype(np.float32)
    ge = (p >= f).astype(np.float32)
    gt = (p > f).astype(np.float32)
    BIG = 30000.0
    bo = ((p // 64) == (f // 64)).astype(np.float32)
    jpad = np.zeros((128, 128), np.float32)
    for i in range(16):
        jpad[64 + 2 * i + 1, 64 + 2 * i] = -1.0
        jpad[64 + 2 * i, 64 + 2 * i + 1] = 1.0
    onesE = np.zeros((128, 128), np.float32)
    onesE[:, :64] = 1.0
    onesO = np.zeros((128, 128), np.float32)
    onesO[:, 64:] = 1.0
    b32 = ((p // 32) == (f // 32)).astype(np.float32)
    m1 = (((p // 64) == (f // 64)) & ((p // 32) != (f // 32))).astype(np.float32)
    m2 = ((p // 64) != (f // 64)).astype(np.float32)
    names = ["ident", "le", "lt", "ge", "gt", "pos_gt", "pos_lt", "neg_le", "neg_ge", "bo", "jpad", "onesE", "onesO", "b32", "m1", "m2"]
    arrs = [ident, le, lt, ge, gt, BIG * (1 - gt), BIG * (1 - lt), -BIG * (1 - le), -BIG * (1 - ge), bo, jpad, onesE, onesO, b32, m1, m2]
    return names, np.ascontiguousarray(np.concatenate(arrs, 1).astype(np.float32))


def rope_tables():
    L = 1024
    row = np.repeat(np.arange(L // 64), 64).astype(np.float32)
    col = (np.arange(L) % 64).astype(np.float32)
    n = 8
    inv = (10000.0 ** (-np.arange(n, dtype=np.float32) / n)).astype(np.float32)
    ang = np.concatenate([row[:, None] * inv, col[:, None] * inv], axis=-1)
    cs = np.zeros((2, 128, L), np.float32)
    for r in range(32):
        cs[0, 64 + r] = np.cos(ang[:, r // 2])
        cs[1, 64 + r] = np.sin(ang[:, r // 2])
    return cs


def hyena_consts(L):
    import ml_dtypes
    t = np.arange(L, dtype=np.float64)
    w = 2.0 * np.pi * (t + 0.5) / (2 * L)
    ph = np.outer(t, w)
    Cm = np.cos(ph)
    Sm = np.sin(ph)
    dft = np.stack([Cm, Sm, Cm.T, Sm.T]).astype(np.float32).astype(ml_dtypes.bfloat16)
    t32 = np.arange(L, dtype=np.float32)
    t_norm = t32 / max(L - 1, 1)
    bands = np.linspace(1e-4, 16 - 1, 16, dtype=np.float32)
    ang = (np.float32(2.0 * np.pi / L) * t32[:, None] * bands).astype(np.float32)
    feats = np.concatenate([t_norm[:, None], np.cos(ang), -np.sin(ang)], axis=-1).astype(np.float32)
    min_decay = np.log(1e-2) / 1.5
    max_decay = np.log(1e-2) / 0.3
    deltas = np.abs(np.linspace(min_decay, max_decay, 512, dtype=np.float32))
    window = np.exp(-t_norm[:, None] * deltas).astype(np.float32)
    featsT = np.zeros((128, L), np.float32)
    featsT[:33] = feats.T
    return dft, featsT, window


A_COLS = 1920
SEQS = {0: [(0, 256), (256, 256), (512, 256), (768, 256)], 1: [(0, 1024)]}


def build_program(offs, NP, test=None, stage=99):
    nc = bass.Bass("TRN2", target_bir_lowering=False)
    dr = {}

    def din(name, shape, dt=F32):
        dr[name] = nc.dram_tensor(name, list(shape), dt, kind="ExternalInput").ap()
        return dr[name]

    def dout(name, shape, dt=F32):
        dr[name] = nc.dram_tensor(name, list(shape), dt, kind="ExternalOutput").ap()
        return dr[name]

    cnames, carr = make_consts()
    xT = din("xT", [D, TOK])
    prm_d = din("prm", [128, NP])
    cst_d = din("cst", [128, carr.shape[1]])
    w_mod = din("w_mod", [2, D, 9 * D])
    ffn_gu = [din("ffn1_w_gu", [2, D, 2 * DFF]), din("ffn2_w_gu", [2, D, 2 * DFF])]
    ffn_dn = [din("ffn1_w_down", [2, DFF, D]), din("ffn2_w_down", [2, DFF, D])]
    w_out_d = din("w_out", [2, D, D])
    ab_w_in = din("ab_w_in", [1, D, 3984])
    sgdn_in = din("sgdn_in", [2, 4, 128, 128])
    srwkv_in = din("srwkv_in", [2, 8, 64, 64])
    rwkv_w_up = din("rwkv_w_up", [1, 2, 64, 512])
    rwkv_a_up = din("rwkv_a_up", [1, 2, 64, 512])
    rwkv_g_up = din("rwkv_g_up", [1, 128, 512])
    srwkv_out = dout("srwkv_out", [4, 2, 8, 64, 64])
    cd_w_in = din("cd_w_in", [1, D, 2208])
    mla_w_uq = din("mla_w_uq", [1, 384, 768])
    mla_w_ukv = din("mla_w_ukv", [1, 256, 1024])
    cckvT = din("cckvT", [256, 256])
    ckpeT = din("ckpeT", [32, 256])
    ropecs = din("ropecs", [2, 128, 1024])
    hy_w1 = din("hy_w1", [1, 33, 64])
    hy_w2 = din("hy_w2", [1, 64, 64])
    hy_w3 = din("hy_w3", [1, 64, 2048])
    dftd = {256: din("dft256", [4, 256, 256], BF16), 1024: din("dft1024", [4, 1024, 1024], BF16)}
    featd = {256: din("feat256", [128, 256]), 1024: din("feat1024", [128, 1024])}
    wind = {256: din("win256", [256, 512]), 1024: din("win1024", [1024, 512])}
    ckv_out = dout("ckv_out", [256, 1024])
    kpe_out = dout("kpe_out", [32, 1024])
    yT = dout("yT", [D, TOK])
    sgdn_out = dout("sgdn_out", [4, 2, 4, 128, 128])

    es = contextlib.ExitStack()
    with es:
        kb = KB(nc, es)

        uid = [0]

        def sbx(stack, name, shape, dt=F32, r=False):
            uid[0] += 1
            nm = "%s_%d" % (name, uid[0])
            if r:
                kb.rt.add(nm)
            return stack.enter_context(nc.sbuf_tensor(nm, list(shape), dt))

        def sb(name, shape, dt=F32):
            return sbx(es, name, shape, dt)

        X = sb("X", [128, 8, TOK])
        prm = sb("prm_sb", [128, NP])
        cst = sb("cst_sb", [128, carr.shape[1]])
        ones = sb("ones", [128, 128])
        modT = sb("modT", [128, 2, 2, 72])
        cs = sb("cs", [128, 8, 2], BF16)
        sq = [sb("sq%d" % i, [128, 512]) for i in range(2)]
        rstd = sb("rstd", [128, HALF])
        tmpf = [sb("tmpf%d" % i, [128, 512]) for i in range(2)]
        coef = sb("coef", [128, 64])
        pbank = [es.enter_context(nc.psum_tensor("pb%d" % i, [128, 512], F32)) for i in range(8)]

        def P(name):
            o, n = offs[name]
            return prm[:, o:o + n]

        def C(name):
            i = cnames.index(name)
            return cst[:, i * 128:(i + 1) * 128]

        slot = [0]

        def pq():
            s = slot[0]
            slot[0] = (s + 1) % 16
            return pbank[s % 4][:, (s // 4) * 128:(s // 4 + 1) * 128]

        kb.dma("sp", prm[:], prm_d[:, :])
        kb.dma("sp", cst[:], cst_d[:, :])
        for j in range(8):
            kb.dma("sp", X[:, j, :], xT[j * 128:(j + 1) * 128, :])
        kb.memset("dve", ones[:], 1.0)
        kb.ones_ap = ones[:]
        ident = C("ident")
        ident16 = sb("ident16", [128, 128], FP16)
        kb.copy("dve", ident16[:], ident)

        with contextlib.ExitStack() as ph:
            wm = [sbx(ph, "wm%d" % i, [128, 8, 1024], BF16) for i in range(2)]
            cT = P("cT")
            kb.act(cs[:].rearrange("p a b -> p (a b)"), cT, AF.Silu)
            for L in range(2):
                pm = pbank[7]
                pmv = pm[:, 0:144].rearrange("p (a b) -> p a b", b=2)
                for n in range(9):
                    wt = wm[n % 2]
                    kb.dma("pool", wt[:], w_mod[L, :, n * 1024:(n + 1) * 1024].rearrange("(k p) c -> p k c", p=128))
                    for j in range(8):
                        for k in range(8):
                            kb.mm(pmv[:, n * 8 + j, :], wt[:, k, j * 128:(j + 1) * 128], cs[:, k, :], start=(k == 0), stop=(k == 7))
                for v in range(2):
                    kb.tt("dve", modT[:, L, v, :], pmv[:, :, v], P("bmod%d" % L), ALU.add)
            kb.barrier()

        def rms_stats(hf):
            for nt in range(2):
                t0 = hf * HALF + nt * 512
                pst = pbank[6]
                for j in range(8):
                    s = sq[j % 2]
                    kb.act(s[:], X[:, j, t0:t0 + 512], AF.Square)
                    kb.mm(pst[:], ones[:], s[:], start=(j == 0), stop=(j == 7))
                kb.act(rstd[:, nt * 512:(nt + 1) * 512], pst[:], AF.Sqrt, bias=EPS, scale=1.0 / D)
            kb.recip(rstd[:], rstd[:])

        def make_coefs(L, hf, sub, gname):
            m = modT[:, L, hf, :]
            kb.stt(coef[:, 0:8], m[:, (3 * sub + 1) * 8:(3 * sub + 2) * 8], 1.0, P(gname), ALU.add, ALU.mult)
            kb.copy("dve", coef[:, 8:16], m[:, (3 * sub) * 8:(3 * sub + 1) * 8])
            kb.ts("dve", coef[:, 16:24], m[:, (3 * sub + 2) * 8:(3 * sub + 3) * 8], 0.5 if sub != 1 else 1.0, ALU.mult)

        def make_H(Hb, hf):
            for j in range(8):
                for nt in range(2):
                    t0 = hf * HALF + nt * 512
                    tf = tmpf[(j * 2 + nt) % 2]
                    kb.stt(tf[:], X[:, j, t0:t0 + 512], coef[:, j:j + 1], rstd[:, nt * 512:(nt + 1) * 512], ALU.mult, ALU.mult)
                    kb.act(Hb[:, j, nt * 512:(nt + 1) * 512], tf[:], AF.Identity, bias=coef[:, 8 + j:9 + j], scale=1.0)

        gu_groups = [(0, 3), (3, 3), (6, 3), (9, 2)]

        def ffn_phase(L, which):
            with contextlib.ExitStack() as ph:
                Hb = sbx(ph, "Hb", [128, 8, HALF], BF16)
                hh = sbx(ph, "hh", [128, 11, HALF], BF16)
                wgu = [sbx(ph, "wgu%d" % i, [128, 8, 2, 384], BF16) for i in range(2)]
                wdn = sbx(ph, "wdn", [128, 11, D], BF16)
                sgt = [sbx(ph, "sgt%d" % i, [128, 512], BF16) for i in range(2)]
                sub = 0 if which == 0 else 2
                wg = ffn_gu[which]
                wd = ffn_dn[which]
                gi = 0
                for hf in range(2):
                    rms_stats(hf)
                    make_coefs(L, hf, sub, ("nf1_%d" if which == 0 else "nf2_%d") % L)
                    make_H(Hb, hf)
                    for fh in range(2):
                        kb.dma("pool", wdn[:], wd[L, fh * 1408:(fh + 1) * 1408, :].rearrange("(j p) c -> p j c", p=128))
                        for (g0, gn) in gu_groups:
                            wt = wgu[gi % 2]
                            gi += 1
                            c0 = fh * 1408 + g0 * 128
                            for gu in range(2):
                                kb.dma("pool", wt[:, :, gu, 0:gn * 128],
                                       wg[L, :, gu * DFF + c0: gu * DFF + c0 + gn * 128].rearrange("(k p) c -> p k c", p=128))
                            for jj in range(gn):
                                for nt in range(2):
                                    pg = pbank[(2 * (jj * 2 + nt)) % 4]
                                    pu = pbank[(2 * (jj * 2 + nt)) % 4 + 1]
                                    for k in range(8):
                                        kb.mm(pg[:], wt[:, k, 0, jj * 128:(jj + 1) * 128], Hb[:, k, nt * 512:(nt + 1) * 512], start=(k == 0), stop=(k == 7))
                                    for k in range(8):
                                        kb.mm(pu[:], wt[:, k, 1, jj * 128:(jj + 1) * 128], Hb[:, k, nt * 512:(nt + 1) * 512], start=(k == 0), stop=(k == 7))
                                    sg = sgt[(jj * 2 + nt) % 2]
                                    kb.act(sg[:], pg[:], AF.Silu)
                                    kb.tt("dve", hh[:, g0 + jj, nt * 512:(nt + 1) * 512], sg[:], pu[:], ALU.mult)
                        for m in range(8):
                            for nt in range(2):
                                po = pbank[4 + (m * 2 + nt) % 2]
                                for j in range(11):
                                    kb.mm(po[:], wdn[:, j, m * 128:(m + 1) * 128], hh[:, j, nt * 512:(nt + 1) * 512], start=(j == 0), stop=(j == 10))
                                t0 = hf * HALF + nt * 512
                                kb.stt(X[:, m, t0:t0 + 512], po[:], coef[:, 16 + m:17 + m], X[:, m, t0:t0 + 512], ALU.mult, ALU.add)
                kb.barrier()

        def interleave(gens):
            gens = list(gens)
            while gens:
                for g in list(gens):
                    try:
                        next(g)
                    except StopIteration:
                        gens.remove(g)

        hslot = [0]

        def pq2():
            k = hslot[0]
            hslot[0] = (k + 1) % 4
            return pbank[4 + k % 2][:, (k // 2) * 256:(k // 2 + 1) * 256]

        def tri_inv_T(ph_tiles, N, NT):
            T = ph_tiles
            Nd, Pa, Pb, Mt, Tun, PXa, PXb = T["Nd"], T["Pa"], T["Pb"], T["Mt"], T["Tun"], T["PXa"], T["PXb"]
            kb.tt("dve", Nd[:], N[:], C("b32"), ALU.mult)
            kb.tt("dve", PXa[:, 0:128], NT[:], C("b32"), ALU.mult)
            kb.copy("dve", PXa[:, 128:256], ident)
            yield
            Pc, Pn = Nd, Pa
            PXc, PXn = PXa, PXb
            for k in range(1, 5):
                px = pq2()
                kb.mm(px, Pc[:], PXc[:])
                pp = pq()
                kb.mm(pp, PXc[:, 0:128], Pc[:])
                if k < 4:
                    kb.copy("act", PXn[:, 0:128], px[:, 0:128])
                kb.tt("dve", PXn[:, 128:256], px[:, 128:256], PXc[:, 128:256], ALU.add)
                kb.copy("act", Pn[:], pp)
                yield
                Pc = Pn
                Pn = Pb if Pn is Pa else Pa
                PXc, PXn = PXn, PXc
            Xa = PXc[:, 128:256]
            pf = pq()
            kb.mm(pf, Pc[:], Xa)
            kb.tt("dve", Xa, pf, Xa, ALU.add)
            yield
            for mname in ("m1", "m2"):
                kb.tt("dve", Nd[:], N[:], C(mname), ALU.mult)
                pm = pq()
                kb.mm(pm, Nd[:], Xa)
                kb.copy("act", Mt[:], pm)
                ptr = pq().bitcast(FP16)[:, 0:128]
                kb.tr(ptr, Xa, ident16[:])
                kb.copy("act", Tun[:], ptr)
                yield
                pw = pq()
                kb.mm(pw, Tun[:], Mt[:])
                kb.tt("dve", Xa, pw, Xa, ALU.add)
                yield
            T["X"] = Xa

        def inv_tile_set(fr, pfx):
            d = {n: fr(pfx + n, [128, 128]) for n in ["Nd", "Pa", "Pb", "Mt", "Tun"]}
            d["PXa"] = fr(pfx + "PXa", [128, 256])
            d["PXb"] = fr(pfx + "PXb", [128, 256])
            return d

        def gdn(hf, Hb, wo_apply):
            with contextlib.ExitStack() as ph:
                f = lambda name, shape, dt=F32, r=False: sbx(ph, "g_" + name, shape, dt, r)
                fr = lambda name, shape: f(name, shape, F32, True)
                fh = lambda name, shape: f(name, shape, FP16)
                wb = [f("wb%d" % i, [128, 8, 4, 128], BF16) for i in range(1)]
                wab = f("wab", [128, 8, 16], BF16)
                praw = f("praw", [128, HALF])
                cv = f("cv", [128, HALF])
                qF, kF, vF, zF = fr("qF", [128, HALF]), fr("kF", [128, HALF]), f("vF", [128, HALF]), f("zF", [128, HALF])
                oacc = f("oacc", [128, HALF])
                mch = f("mch", [128, HALF], BF16)
                rn = f("rn", [128, HALF])
                abt = f("abt", [128, 8, 16])
                gsb = f("gsb", [128, 2, 8, 4])
                bsb = f("bsb", [128, 2, 8, 4])
                nbsb = f("nbsb", [128, 2, 8, 4])
                Gsb = f("Gsb", [128, 2, 8, 4])
                Gtot = f("Gtot", [128, 2, 8, 4])
                eG = f("eG", [128, 2, 8, 4])
                beG = f("beG", [128, 2, 8, 4])
                eGL = f("eGL", [128, 2, 8, 4])
                eGlast = f("eGlast", [128, 2, 8, 4])
                ea = f("ea", [128, 8])
                tsm = f("tsm", [128, 2, 8, 4])
                kT = f("kT", [128, 8, 128])
                vT = f("vT", [128, 8, 128])
                st_qkT = f("st_qkT", [128, 8, 128], BF16)
                st_u = f("st_u", [128, 8, 128])
                st_nwT = f("st_nwT", [128, 8, 128], BF16)
                st_qin = f("st_qin", [128, 8, 128], BF16)
                st_kout = f("st_kout", [128, 8, 128], BF16)
                S = f("S", [128, 128])
                Sb = f("Sb", [128, 128], BF16)
                NCH = 3
                t_vnew = f("t_vnew", [128, 128], BF16)
                chT = []
                for ci in range(NCH):
                    T = {n: f("c%d_%s" % (ci, n), [128, 128]) for n in ["diag", "d1", "d2", "E1", "E2", "eGr"]}
                    T.update({n: fh("c%d_%s" % (ci, n), [128, 128]) for n in ["N", "NT", "vb", "kbg"]})
                    T["inv"] = inv_tile_set(fh, "c%d_iv" % ci)
                    chT.append(T)
                c_ab = A_COLS + 2048
                kb.dma("pool", wab[:], ab_w_in[0, :, c_ab:c_ab + 16].rearrange("(k p) c -> p k c", p=128))
                pab = pbank[6]
                for tt in range(8):
                    for k in range(8):
                        kb.mm(pab[:, tt * 16:(tt + 1) * 16], Hb[:, k, tt * 128:(tt + 1) * 128], wab[:, k, :], start=(k == 0), stop=(k == 7))
                kb.copy("dve", abt[:].rearrange("p a b -> p (a b)"), pab[:, 0:128])
                abv = abt[:].rearrange("p t (d a h) -> p t d a h", d=2, a=2)
                kb.act(ea[:], P("galog"), AF.Exp)
                for d in range(2):
                    dtb = P("gdtb")[:, d * 4:(d + 1) * 4]
                    for tt in range(8):
                        kb.tt("dve", tsm[:, d, tt, :], abv[:, tt, d, 0, :], dtb, ALU.add)
                        kb.copy("dve", bsb[:, d, tt, :], abv[:, tt, d, 1, :])
                g2 = lambda t: t[:].rearrange("p d t h -> p (d t h)")
                kb.act(g2(tsm), g2(tsm), AF.Exp)
                kb.act(g2(tsm), g2(tsm), AF.Ln, bias=1.0)
                for d in range(2):
                    for tt in range(8):
                        kb.stt(gsb[:, d, tt, :], tsm[:, d, tt, :], -1.0, ea[:, d * 4:(d + 1) * 4], ALU.mult, ALU.mult)
                kb.act(g2(bsb), g2(bsb), AF.Sigmoid)
                kb.ts("dve", g2(nbsb), g2(bsb), -1.0, ALU.mult)
                pG = pbank[6]
                g3 = lambda t, d: t[:, d, :, :].rearrange("p t h -> p (t h)")
                kb.mm(pG[:, 0:32], C("le"), g3(gsb, 0))
                kb.mm(pG[:, 32:64], C("ge"), g3(gsb, 1))
                kb.mm(pG[:, 64:128], ones[:], g2(gsb))
                kb.copy("dve", g2(Gsb), pG[:, 0:64])
                kb.copy("dve", g2(Gtot), pG[:, 64:128])
                kb.act(g2(eG), g2(Gsb), AF.Exp)
                kb.tt("dve", g2(beG), g2(eG), g2(bsb), ALU.mult)
                kb.tt("dve", g2(eGL), g2(Gtot), g2(Gsb), ALU.subtract)
                kb.act(g2(eGL), g2(eGL), AF.Exp)
                kb.act(g2(eGlast), g2(Gtot), AF.Exp)

                for h in range(4):
                    wt = wb[0]
                    for qi in range(4):
                        c0 = A_COLS + qi * 512 + h * 128
                        kb.dma("pool", wt[:, :, qi, :], ab_w_in[0, :, c0:c0 + 128].rearrange("(k p) c -> p k c", p=128))
                    gc = P("gconv").rearrange("p (t c) -> p t c", t=3)
                    for qi, dst in enumerate([qF, kF, vF, zF]):
                        for nt in range(2):
                            pp = pbank[nt]
                            for k in range(8):
                                kb.mm(pp[:], wt[:, k, qi, :], Hb[:, k, nt * 512:(nt + 1) * 512], start=(k == 0), stop=(k == 7))
                            if qi == 3:
                                kb.act(zF[:, nt * 512:(nt + 1) * 512], pp[:], AF.Silu)
                            else:
                                kb.copy("act", praw[:, nt * 512:(nt + 1) * 512], pp[:])
                        if qi == 3:
                            continue
                        cc = qi * 4 + h
                        kb.ts("dve", cv[:], praw[:], gc[:, 1, cc:cc + 1], ALU.mult)
                        for (s0, Ls) in SEQS[hf]:
                            kb.stt(cv[:, s0 + 1:s0 + Ls], praw[:, s0:s0 + Ls - 1], gc[:, 0, cc:cc + 1], cv[:, s0 + 1:s0 + Ls], ALU.mult, ALU.add)
                            kb.stt(cv[:, s0:s0 + Ls - 1], praw[:, s0 + 1:s0 + Ls], gc[:, 2, cc:cc + 1], cv[:, s0:s0 + Ls - 1], ALU.mult, ALU.add)
                        kb.act(dst[:], cv[:], AF.Silu)
                        if qi < 2:
                            for nt in range(2):
                                s = sq[nt]
                                kb.act(s[:], dst[:, nt * 512:(nt + 1) * 512], AF.Square)
                                pst = pbank[2 + nt]
                                kb.mm(pst[:], ones[:], s[:])
                                kb.act(rn[:, nt * 512:(nt + 1) * 512], pst[:], AF.Sqrt, bias=1e-6, scale=1.0)
                            kb.recip(rn[:], rn[:])
                            kb.stt(dst[:], dst[:], (128.0 ** -0.5) if qi == 0 else 1.0, rn[:], ALU.mult, ALU.mult)
                    for tt in range(8):
                        p1 = pq()
                        kb.tr(p1, kF[:, tt * 128:(tt + 1) * 128], ident)
                        kb.copy("act", kT[:, tt, :], p1)
                        p2 = pq()
                        kb.tr(p2, vF[:, tt * 128:(tt + 1) * 128], ident)
                        kb.copy("act", vT[:, tt, :], p2)
                    for d in range(2):
                        posm = C("pos_gt") if d == 0 else C("pos_lt")
                        negm = C("neg_le") if d == 0 else C("neg_ge")
                        def pre_tile(tt, T, d=d, h=h, posm=posm, negm=negm):
                            tsl = slice(tt * 128, (tt + 1) * 128)
                            col = lambda t: t[:, d, tt, h:h + 1]
                            kb.ts("dve", T["diag"][:], ident, col(Gsb), ALU.mult)
                            prb = pq()
                            kb.mm(prb, ones[:], T["diag"][:])
                            yield
                            kb.stt(T["d1"][:], prb, col(Gsb), posm, ALU.subtract, ALU.add)
                            kb.act(T["E1"][:], T["d1"][:], AF.Exp, scale=-1.0)
                            kb.stt(T["d2"][:], prb, col(Gsb), negm, ALU.subtract, ALU.add)
                            kb.act(T["E2"][:], T["d2"][:], AF.Exp)
                            kb.act(T["eGr"][:], prb, AF.Exp)
                            pkk = pq()
                            kb.mm(pkk, kF[:, tsl], kF[:, tsl])
                            yield
                            kb.stt(T["N"][:], pkk, col(nbsb), T["E1"][:], ALU.mult, ALU.mult)
                            pnt = pq().bitcast(FP16)[:, 0:128]
                            kb.tr(pnt, T["N"][:], ident16[:])
                            kb.copy("act", T["NT"][:], pnt)
                            pqk = pq()
                            kb.mm(pqk, kF[:, tsl], qF[:, tsl])
                            kb.tt("dve", st_qkT[:, tt, :], pqk, T["E2"][:], ALU.mult)
                            kb.ts("dve", T["vb"][:], vT[:, tt, :], col(bsb), ALU.mult)
                            kb.ts("dve", T["kbg"][:], kT[:, tt, :], col(beG), ALU.mult)
                            kb.ts("dve", st_kout[:, tt, :], kT[:, tt, :], col(eGL), ALU.mult)
                            kb.tt("dve", st_qin[:, tt, :], qF[:, tsl], T["eGr"][:], ALU.mult)
                            yield
                            yield from tri_inv_T(T["inv"], T["N"], T["NT"])
                            Xi = T["inv"]["X"]
                            pu_ = pq()
                            kb.mm(pu_, Xi, T["vb"][:])
                            kb.copy("act", st_u[:, tt, :], pu_)
                            pw_ = pq()
                            kb.mm(pw_, T["kbg"][:], Xi)
                            kb.ts("dve", st_nwT[:, tt, :], pw_, -1.0, ALU.mult)
                            yield

                        for t0_ in range(0, 8, NCH):
                            interleave([pre_tile(tt, chT[ci]) for ci, tt in enumerate(range(t0_, min(8, t0_ + NCH)))])
                        for si, (s0, Ls) in enumerate(SEQS[hf]):
                            tiles = list(range(s0 // 128, (s0 + Ls) // 128))
                            if d == 1:
                                tiles = tiles[::-1]
                            if hf == 0:
                                kb.memset("dve", S[:], 0.0)
                            else:
                                kb.dma("sp", S[:], sgdn_in[d, h, :, :])
                            kb.copy("dve", Sb[:], S[:])
                            for tt in tiles:
                                tsl = slice(tt * 128, (tt + 1) * 128)
                                pv = pq()
                                kb.mm(pv, st_nwT[:, tt, :], Sb[:])
                                kb.tt("dve", t_vnew[:], pv, st_u[:, tt, :], ALU.add)
                                po_ = pq()
                                kb.mm(po_, Sb[:], st_qin[:, tt, :], start=True, stop=False)
                                kb.mm(po_, t_vnew[:], st_qkT[:, tt, :], start=False, stop=True)
                                if d == 0:
                                    kb.copy("act", oacc[:, tsl], po_)
                                else:
                                    kb.tt("dve", oacc[:, tsl], po_, oacc[:, tsl], ALU.add)
                                ps_ = pq()
                                kb.mm(ps_, st_kout[:, tt, :], t_vnew[:])
                                kb.stt(S[:], S[:], eGlast[:, d, tt, h:h + 1], ps_, ALU.mult, ALU.add)
                                kb.copy("act", Sb[:], S[:])
                            if hf == 0:
                                kb.dma("sp", sgdn_out[si, d, h, :, :], S[:])
                    for nt in range(2):
                        s = sq[nt]
                        kb.act(s[:], oacc[:, nt * 512:(nt + 1) * 512], AF.Square)
                        pst = pbank[2 + nt]
                        kb.mm(pst[:], ones[:], s[:])
                        kb.act(rn[:, nt * 512:(nt + 1) * 512], pst[:], AF.Sqrt, bias=EPS, scale=1.0 / 128)
                    kb.recip(rn[:], rn[:])
                    kb.stt(oacc[:], oacc[:], P("gnorm"), rn[:], ALU.mult, ALU.mult)
                    kb.tt("dve", mch[:], oacc[:], zF[:], ALU.mult)
                    wo_apply(4 + h, mch)
                kb.barrier()

        def rwkv(hf, Hb, wo_apply):
            with contextlib.ExitStack() as ph:
                f = lambda name, shape, dt=F32, r=False: sbx(ph, "r_" + name, shape, dt, r)
                fr = lambda name, shape: f(name, shape, F32, True)
                fh = lambda name, shape: f(name, shape, FP16)
                wch = [f("wch%d" % i, [128, 8, 128], BF16) for i in range(2)]
                wup, aup, gup = f("wup", [128, 512], BF16), f("aup", [128, 512], BF16), f("gup", [128, 512], BF16)
                twd, tad, sgd = f("twd", [128, HALF], BF16), f("tad", [128, HALF], BF16), f("sgd", [128, HALF], BF16)
                T1 = f("T1", [128, HALF])
                rF, kF0, vF, kkF = f("rF", [128, HALF]), f("kF0", [128, HALF]), f("vF", [128, HALF]), f("kkF", [128, HALF])
                asum, yacc = f("asum", [128, HALF]), f("yacc", [128, HALF])
                lw, G, kd, bF = f("lw", [128, HALF]), f("G", [128, HALF]), f("kd", [128, HALF]), f("bF", [128, HALF])
                mch = f("mch", [128, HALF], BF16)
                omm, hmu, omka = f("omm", [128, 15]), f("hmu", [128, 15]), f("omka", [128, 4])
                Mp = f("Mp", [128, 128])
                Mpb = f("Mpb", [128, 128], BF16)
                pC = f("pC", [128, 1])
                nm = ["eX", "Rt", "Ct", "kinv", "binv", "khat", "bhat", "Z"]
                bset = {"Rt", "Ct", "kinv", "binv", "Z"}
                nm = nm + ["eX2"]
                tlp = [{n: f("t%d_%s" % (par, n), [128, 128], BF16 if n in bset else F32) for n in nm} for par in range(2)]
                pCp = [f("pC%d" % par, [128, 1]) for par in range(2)]
                hdp = [[{n: f("h%d%d_%s" % (par, h, n), [128, 128], BF16) for n in ["X", "BmT", "QKT", "QBT", "vT", "khT", "bhT", "nU"]} for h in range(2)] for par in range(2)]
                hT = []
                for h_ in range(2):
                    T = {n: fh("p%d_%s" % (h_, n), [128, 128]) for n in ["N", "NT"]}
                    T.update({n: f("p%d_%s" % (h_, n), [128, 128], BF16) for n in ["Ctm", "Rtm"]})
                    T["msk"] = [f("p%d_msk%d" % (h_, i), [128, 128]) for i in range(3)]
                    T["inv"] = inv_tile_set(fh, "p%d_iv" % h_)
                    hT.append(T)
                hmask = P("hmask")
                mu = P("rmu")
                kb.ts("dve", omm[:], mu, -1.0, ALU.mult, 1.0, ALU.add)
                kb.ts("dve", hmu[:], mu, 0.5, ALU.mult)
                kb.ts("dve", omka[:], P("rka"), -1.0, ALU.mult, 1.0, ALU.add)
                for d in range(2):
                    kb.dma("pool", wup[d * 64:(d + 1) * 64, :], rwkv_w_up[0, d, :, :])
                    kb.dma("pool", aup[d * 64:(d + 1) * 64, :], rwkv_a_up[0, d, :, :])
                kb.dma("pool", gup[:], rwkv_g_up[0, :, :])
                wi = [0]

                def project_shift(chunk, dst, func=None):
                    wt = wch[wi[0] % 2]
                    wi[0] += 1
                    kb.dma("pool", wt[:], ab_w_in[0, :, chunk * 128:(chunk + 1) * 128].rearrange("(k p) c -> p k c", p=128))
                    for nt in range(2):
                        pp = pbank[nt]
                        for k in range(8):
                            kb.mm(pp[:], wt[:, k, :], Hb[:, k, nt * 512:(nt + 1) * 512], start=(k == 0), stop=(k == 7))
                        kb.copy("act", T1[:, nt * 512:(nt + 1) * 512], pp[:])
                    tgt = dst if func is None else lw
                    kb.ts("dve", tgt[:], T1[:], omm[:, chunk:chunk + 1], ALU.mult)
                    for (s0, Ls) in SEQS[hf]:
                        kb.stt(tgt[:, s0 + 1:s0 + Ls], T1[:, s0:s0 + Ls - 1], hmu[:, chunk:chunk + 1], tgt[:, s0 + 1:s0 + Ls], ALU.mult, ALU.add)
                        kb.stt(tgt[:, s0:s0 + Ls - 1], T1[:, s0 + 1:s0 + Ls], hmu[:, chunk:chunk + 1], tgt[:, s0:s0 + Ls - 1], ALU.mult, ALU.add)
                    if func is not None:
                        kb.act(dst[:], tgt[:], func)

                project_shift(12, twd, AF.Tanh)
                project_shift(13, tad, AF.Identity)
                project_shift(14, sgd, AF.Sigmoid)
                bo = C("bo")
                for c in range(4):
                    project_shift(c, rF)
                    project_shift(4 + c, kF0)
                    project_shift(8 + c, vF)
                    kb.ts("dve", kkF[:], kF0[:], P("rkk")[:, c:c + 1], ALU.mult)
                    for nt in range(2):
                        sl = slice(nt * 512, (nt + 1) * 512)
                        kb.act(sq[nt][:], kkF[:, sl], AF.Square)
                        pst = pbank[2 + nt]
                        kb.mm(pst[:], bo, sq[nt][:])
                        kb.act(T1[:, sl], pst[:], AF.Sqrt, bias=1e-6, scale=1.0)
                    kb.recip(T1[:], T1[:])
                    kb.tt("dve", kkF[:], kkF[:], T1[:], ALU.mult)
                    for d in range(2):
                        m_ts = C("gt") if d == 0 else C("lt")
                        m_st = C("lt") if d == 0 else C("gt")
                        m_in = C("le") if d == 0 else C("ge")
                        rows = slice(d * 64, (d + 1) * 64)
                        for nt in range(2):
                            sl = slice(nt * 512, (nt + 1) * 512)
                            pw = pbank[nt]
                            kb.mm(pw[:], wup[rows, c * 128:(c + 1) * 128], twd[rows, sl])
                            kb.act(lw[:, sl], pw[:], AF.Sigmoid, bias=P("rw0").rearrange("p (d c) -> p d c", d=2)[:, d, c:c + 1])
                            pa = pbank[2 + nt]
                            kb.mm(pa[:], aup[rows, c * 128:(c + 1) * 128], tad[rows, sl])
                            kb.act(T1[:, sl], pa[:], AF.Sigmoid, bias=P("ra0").rearrange("p (d c) -> p d c", d=2)[:, d, c:c + 1])
                        kb.ts("dve", lw[:], lw[:], -0.6065306597126334, ALU.mult)
                        if d == 0:
                            kb.copy("dve", asum[:], T1[:])
                        else:
                            kb.tt("dve", asum[:], asum[:], T1[:], ALU.add)
                        kb.tt("dve", bF[:], kkF[:], T1[:], ALU.mult)
                        kb.ts("dve", T1[:], T1[:], P("rka")[:, c:c + 1], ALU.mult, omka[:, c:c + 1], ALU.add)
                        kb.tt("dve", kd[:], kF0[:], T1[:], ALU.mult)
                        nc_ = nc
                        kb.emit("dve", lambda: nc_.vector.tensor_tensor_scan(out=G[:], data0=P("rmask"), data1=lw[:], initial=0.0, op0=ALU.mult, op1=ALU.add),
                                [G[:]], [P("rmask"), lw[:]])
                        if d == 1:
                            for tt in range(8):
                                tsl = slice(tt * 128, (tt + 1) * 128)
                                kb.stt(T1[:, tsl], G[:, tsl], -1.0, lw[:, tsl], ALU.mult, ALU.add)
                                kb.ts("dve", T1[:, tsl], T1[:, tsl], G[:, tt * 128 + 127:tt * 128 + 128], ALU.add)
                            kb.copy("dve", G[:], T1[:])
                        kb.tt("dve", lw[:], G[:], lw[:], ALU.subtract)
                        for par in range(2):
                            for h in range(2):
                                kb.memset("dve", hdp[par][h]["nU"][:], 0.0)
                        order = []
                        for si, (s0, Ls) in enumerate(SEQS[hf]):
                            tiles = list(range(s0 // 128, (s0 + Ls) // 128))
                            if d == 1:
                                tiles = tiles[::-1]
                            for k_, tt in enumerate(tiles):
                                order.append((si, tt, k_ == 0, k_ == len(tiles) - 1))

                        def prep_gen(idx, d=d, c=c, m_ts=m_ts, m_st=m_st, m_in=m_in):
                            si, tt, first, last = order[idx]
                            tl = tlp[idx % 2]
                            hd = hdp[idx % 2]
                            pCt = pCp[idx % 2]
                            tsl = slice(tt * 128, (tt + 1) * 128)
                            e_end = tt * 128 + (127 if d == 0 else 0)
                            gtot = G[:, e_end:e_end + 1]
                            kb.act(tl["eX"][:], G[:, tsl], AF.Exp)
                            kb.tt("dve", tl["Rt"][:], rF[:, tsl], tl["eX"][:], ALU.mult)
                            kb.act(tl["eX2"][:], lw[:, tsl], AF.Exp)
                            kb.tt("dve", tl["Ct"][:], kkF[:, tsl], tl["eX2"][:], ALU.mult)
                            yield
                            kb.act(tl["eX"][:], G[:, tsl], AF.Exp, scale=-1.0)
                            kb.tt("dve", tl["kinv"][:], kd[:, tsl], tl["eX"][:], ALU.mult)
                            kb.tt("dve", tl["binv"][:], bF[:, tsl], tl["eX"][:], ALU.mult)
                            kb.act(tl["eX2"][:], G[:, tsl], AF.Exp, bias=gtot, scale=-1.0)
                            kb.tt("dve", tl["khat"][:], kd[:, tsl], tl["eX2"][:], ALU.mult)
                            kb.tt("dve", tl["bhat"][:], bF[:, tsl], tl["eX2"][:], ALU.mult)
                            kb.act(pCt[:], gtot, AF.Exp)
                            yield

                            def pre_head(h):
                                H = hd[h]
                                T = hT[h]
                                hm = hmask[:, h:h + 1]
                                kb.ts("dve", T["Ctm"][:], tl["Ct"][:], hm, ALU.mult)
                                kb.ts("dve", T["Rtm"][:], tl["Rt"][:], hm, ALU.mult)
                                p_ = pq()
                                kb.mm(p_, T["Ctm"][:], tl["binv"][:])
                                kb.stt(T["N"][:], p_, -1.0, m_ts, ALU.mult, ALU.mult)
                                p_ = pq()
                                kb.mm(p_, tl["binv"][:], T["Ctm"][:])
                                kb.stt(T["NT"][:], p_, -1.0, m_st, ALU.mult, ALU.mult)
                                yield
                                p_ = pq()
                                kb.mm(p_, tl["kinv"][:], T["Ctm"][:])
                                kb.tt("dve", H["BmT"][:], p_, m_st, ALU.mult)
                                p_ = pq()
                                kb.mm(p_, tl["kinv"][:], T["Rtm"][:])
                                kb.tt("dve", H["QKT"][:], p_, m_in, ALU.mult)
                                p_ = pq()
                                kb.mm(p_, tl["binv"][:], T["Rtm"][:])
                                kb.tt("dve", H["QBT"][:], p_, m_in, ALU.mult)
                                yield
                                for mi, (src, dstn) in enumerate([(vF[:, tsl], "vT"), (tl["khat"][:], "khT"), (tl["bhat"][:], "bhT")]):
                                    kb.ts("dve", T["msk"][mi][:], src, hm, ALU.mult)
                                    p_ = pq()
                                    kb.tr(p_, T["msk"][mi][:], ident)
                                    kb.copy("act", H[dstn][:], p_)
                                yield
                                yield from tri_inv_T(T["inv"], T["N"], T["NT"])
                                kb.copy("act", H["X"][:], T["inv"]["X"])
                                yield

                            gens = [pre_head(0), pre_head(1)]
                            while gens:
                                for g_ in list(gens):
                                    try:
                                        next(g_)
                                    except StopIteration:
                                        gens.remove(g_)
                                yield

                        def seq_gen(idx, d=d, c=c):
                            si, tt, first, last = order[idx]
                            tl = tlp[idx % 2]
                            hd = hdp[idx % 2]
                            pCt = pCp[idx % 2]
                            tsl = slice(tt * 128, (tt + 1) * 128)
                            if first:
                                kb.memset("dve", Mp[:], 0.0)
                                if hf == 1:
                                    for h in range(2):
                                        kb.dma("sp", Mp[h * 64:(h + 1) * 64, h * 64:(h + 1) * 64], srwkv_in[d, 2 * c + h, :, :])
                                kb.copy("dve", Mpb[:], Mp[:])
                            pz = pq()
                            kb.mm(pz, tl["Ct"][:], Mpb[:], start=True, stop=False)
                            kb.mm(pz, hd[0]["BmT"][:], hd[0]["vT"][:], start=False, stop=False)
                            kb.mm(pz, hd[1]["BmT"][:], hd[1]["vT"][:], start=False, stop=True)
                            kb.copy("act", tl["Z"][:], pz)
                            yield
                            for h in range(2):
                                p_ = pq()
                                kb.mm(p_, hd[h]["X"][:], tl["Z"][:])
                                kb.ts("dve", hd[h]["nU"][:, h * 64:(h + 1) * 64], p_[:, h * 64:(h + 1) * 64], -1.0, ALU.mult)
                            yield
                            py = pq()
                            kb.mm(py, Mpb[:], tl["Rt"][:], start=True, stop=False)
                            for h in range(2):
                                kb.mm(py, hd[h]["vT"][:], hd[h]["QKT"][:], start=False, stop=False)
                                kb.mm(py, hd[h]["nU"][:], hd[h]["QBT"][:], start=False, stop=(h == 1))
                            if d == 0:
                                kb.copy("act", yacc[:, tsl], py)
                            else:
                                kb.tt("dve", yacc[:, tsl], py, yacc[:, tsl], ALU.add)
                            pm_ = pq()
                            for h in range(2):
                                kb.mm(pm_, hd[h]["khT"][:], hd[h]["vT"][:], start=(h == 0), stop=False)
                                kb.mm(pm_, hd[h]["bhT"][:], hd[h]["nU"][:], start=False, stop=(h == 1))
                            kb.stt(Mp[:], Mp[:], pCt[:, 0:1], pm_, ALU.mult, ALU.add)
                            kb.copy("act", Mpb[:], Mp[:])
                            if last and hf == 0:
                                for h in range(2):
                                    kb.dma("sp", srwkv_out[si, d, 2 * c + h, :, :], Mp[h * 64:(h + 1) * 64, h * 64:(h + 1) * 64])
                            yield

                        interleave([prep_gen(0)])
                        for idx in range(len(order)):
                            gl = [seq_gen(idx)]
                            if idx + 1 < len(order):
                                gl.append(prep_gen(idx + 1))
                            interleave(gl)
                    for nt in range(2):
                        sl = slice(nt * 512, (nt + 1) * 512)
                        pst = pbank[nt]
                        kb.mm(pst[:], bo, yacc[:, sl])
                        kb.stt(yacc[:, sl], pst[:], -1.0 / 64, yacc[:, sl], ALU.mult, ALU.add)
                        kb.act(sq[nt][:], yacc[:, sl], AF.Square)
                        pv = pbank[2 + nt]
                        kb.mm(pv[:], bo, sq[nt][:])
                        kb.act(T1[:, sl], pv[:], AF.Sqrt, bias=64e-5, scale=1.0 / 64)
                    kb.recip(T1[:], T1[:])
                    kb.stt(yacc[:], yacc[:], P("rlnw")[:, c:c + 1], T1[:], ALU.mult, ALU.mult)
                    kb.ts("dve", yacc[:], yacc[:], P("rlnb")[:, c:c + 1], ALU.add)
                    kb.ts("dve", asum[:], asum[:], 0.5, ALU.mult)
                    kb.ts("dve", asum[:], asum[:], P("rka")[:, c:c + 1], ALU.mult, omka[:, c:c + 1], ALU.add)
                    kb.tt("dve", asum[:], asum[:], kF0[:], ALU.mult)
                    kb.stt(asum[:], asum[:], P("rrk")[:, c:c + 1], rF[:], ALU.mult, ALU.mult)
                    for nt in range(2):
                        sl = slice(nt * 512, (nt + 1) * 512)
                        pb_ = pbank[nt]
                        kb.mm(pb_[:], bo, asum[:, sl])
                        kb.tt("dve", T1[:, sl], pb_[:], vF[:, sl], ALU.mult)
                        kb.tt("dve", yacc[:, sl], yacc[:, sl], T1[:, sl], ALU.add)
                        pg_ = pbank[2 + nt]
                        kb.mm(pg_[:], gup[:, c * 128:(c + 1) * 128], sgd[:, sl])
                        kb.tt("dve", mch[:, sl], pg_[:], yacc[:, sl], ALU.mult)
                    wo_apply(c, mch)
                kb.barrier()

        def mla(hf, Hb, wo_apply):
            NK = 1024 if hf == 0 else 1280
            with contextlib.ExitStack() as ph:
                f = lambda name, shape, dt=F32: sbx(ph, "m_" + name, shape, dt)
                wch = [f("wch%d" % i, [128, 8, 128], BF16) for i in range(2)]
                wuq = f("wuq", [128, 3, 768], BF16)
                wukv = f("wukv", [128, 2, 1024], BF16)
                pqF = f("pqF", [128, 3, HALF])
                qnF = f("qnF", [128, 3, HALF], BF16)
                ckvF = f("ckvF", [128, 2, HALF])
                ckvB = f("ckvB", [128, 2, 1280], BF16)
                kpe96 = f("kpe96", [128, 1280])
                kpeB = f("kpeB", [128, 1280], BF16)
                rq = f("rq", [128, HALF])
                Ve = f("Ve", [128, 10, 128], BF16)
                Vo = f("Vo", [128, 10, 128], BF16)
                qTf = f("qTf", [128, HALF])
                qTb = [f("qTb%d" % e, [128, HALF], BF16) for e in range(2)]
                KTb = [f("KTb%d" % e, [128, 1280], BF16) for e in range(2)]
                PT = [f("PT%d" % i, [128, 512], BF16) for i in range(2)]
                oE = f("oE", [128, 128], BF16)
                oO = f("oO", [128, 128], BF16)
                rs_ = f("rs", [128, 512])
                mch = f("mch", [128, HALF], BF16)
                t1, t2 = f("t1", [128, 512]), f("t2", [128, 512])
                cosT, sinT = f("cosT", [128, HALF]), f("sinT", [128, HALF])
                kb.copy("dve", oE[:], C("onesE"))
                kb.copy("dve", oO[:], C("onesO"))
                kb.memset("dve", kpe96[:], 0.0)
                kb.memset("dve", qTf[:], 0.0)
                kb.dma("pool", wuq[:], mla_w_uq[0].rearrange("(k p) c -> p k c", p=128))
                kb.dma("pool", wukv[:], mla_w_ukv[0].rearrange("(k p) c -> p k c", p=128))
                if hf == 1:
                    kb.dma("sp", cosT[:], ropecs[0, :, :])
                    kb.dma("sp", sinT[:], ropecs[1, :, :])
                wi = [0]

                def project(c0, M, dst_fn):
                    wt = wch[wi[0] % 2]
                    wi[0] += 1
                    kb.dma("pool", wt[:, :, 0:M], cd_w_in[0, :, c0:c0 + M].rearrange("(k p) c -> p k c", p=128))
                    for nt in range(2):
                        pp = pbank[4 + nt]
                        for k in range(8):
                            kb.mm(pp[0:M, :], wt[:, k, 0:M], Hb[:, k, nt * 512:(nt + 1) * 512], start=(k == 0), stop=(k == 7))
                        dst_fn(nt, pp)

                for j in range(3):
                    project(j * 128, 128, lambda nt, pp, j=j: kb.copy("act", pqF[:, j, nt * 512:(nt + 1) * 512], pp[:]))
                for j in range(2):
                    project(384 + j * 128, 128, lambda nt, pp, j=j: kb.copy("act", ckvF[:, j, nt * 512:(nt + 1) * 512], pp[:]))
                project(576, 96, lambda nt, pp: kb.copy("act", kpe96[64:96, nt * 512:(nt + 1) * 512], pp[64:96, :]))

                def rmsn(src, nj, gname, outs):
                    for nt in range(2):
                        sl = slice(nt * 512, (nt + 1) * 512)
                        pst = pbank[4 + nt]
                        for j in range(nj):
                            kb.act(sq[j % 2][:], src[:, j, sl], AF.Square)
                            kb.mm(pst[:], ones[:], sq[j % 2][:], start=(j == 0), stop=(j == nj - 1))
                        kb.act(rq[:, sl], pst[:], AF.Sqrt, bias=EPS, scale=1.0 / (nj * 128))
                    kb.recip(rq[:], rq[:])
                    for j in range(nj):
                        for o in outs:
                            kb.stt(o(j), src[:, j, :], P(gname)[:, j:j + 1], rq[:], ALU.mult, ALU.mult)

                rmsn(pqF, 3, "mqn", [lambda j: qnF[:, j, :]])
                rmsn(ckvF, 2, "mkvn", [lambda j: ckvB[:, j, 0:HALF], lambda j: ckvF[:, j, :]])
                if hf == 0:
                    for j in range(2):
                        kb.dma("sp", ckv_out[j * 128:(j + 1) * 128, :], ckvF[:, j, :])
                    kb.dma("sp", kpe_out[:, :], kpe96[64:96, 0:HALF])
                    kb.copy("dve", kpeB[64:96, 0:HALF], kpe96[64:96, 0:HALF])
                else:
                    for j in range(2):
                        kb.dma("pool", ckvB[:, j, HALF:1280], cckvT[j * 128:(j + 1) * 128, :])
                    kb.dma("sp", kpe96[64:96, HALF:1280], ckpeT[:, :])
                    kb.copy("dve", kpeB[64:96, HALF:1280], kpe96[64:96, HALF:1280])

                if stage == 1:
                    kb.barrier()
                    return

                def rope(src, dstb):
                    for nt in range(2):
                        sl = slice(nt * 512, (nt + 1) * 512)
                        pj = pbank[4 + nt]
                        kb.mm(pj[0:96, :], C("jpad")[0:96, 0:96], src[0:96, sl])
                        kb.tt("dve", t1[64:96, :], src[64:96, sl], cosT[64:96, sl], ALU.mult)
                        kb.tt("dve", t2[64:96, :], pj[64:96, :], sinT[64:96, sl], ALU.mult)
                        kb.tt("dve", dstb[64:96, sl], t1[64:96, :], t2[64:96, :], ALU.add)

                if hf == 1:
                    rope(kpe96, kpeB)
                if stage == 2:
                    kb.barrier()
                    return
                kb.memset("dve", Ve[:].rearrange("p a b -> p (a b)"), 0.0)
                kb.memset("dve", Vo[:].rearrange("p a b -> p (a b)"), 0.0)
                scale = 96.0 ** -0.5
                if hf == 0:
                    qranges = [(si * 256, 256, [2 * si, 2 * si + 1]) for si in range(4)]
                else:
                    qranges = [(nt * 512, 512, list(range(10))) for nt in range(2)]
                pti = [0]
                for c in range(4):
                    for kt in range(NK // 128):
                        pv = pbank[4 + kt % 2]
                        for e in range(2):
                            v0 = (2 * c + e) * 128 + 64
                            for k in range(2):
                                kb.mm(pv[:, e * 64:(e + 1) * 64], ckvB[:, k, kt * 128:(kt + 1) * 128], wukv[:, k, v0:v0 + 64], start=(k == 0), stop=(k == 1))
                        kb.copy("act", Ve[:, kt, 0:64], pv[:, 0:64])
                        kb.copy("dve", Vo[:, kt, 64:128], pv[:, 64:128])
                    if stage == 31:
                        continue
                    for e in range(2):
                        h = 2 * c + e
                        for nt in range(2):
                            sl = slice(nt * 512, (nt + 1) * 512)
                            pqh = pbank[4 + nt]
                            for k in range(3):
                                kb.mm(pqh[0:96, :], wuq[:, k, h * 96:(h + 1) * 96], qnF[:, k, sl], start=(k == 0), stop=(k == 2))
                            kb.copy("act", qTb[e][0:64, sl], pqh[0:64, :])
                            if hf == 1:
                                kb.copy("dve", qTf[64:96, sl], pqh[64:96, :])
                            else:
                                kb.copy("dve", qTb[e][64:96, sl], pqh[64:96, :])
                        if hf == 1:
                            rope(qTf, qTb[e])
                        if stage == 32:
                            continue
                        for k0 in range(0, NK, 512):
                            n = min(512, NK - k0)
                            pk = pbank[4 + (k0 // 512) % 2]
                            for k in range(2):
                                kb.mm(pk[:, 0:n], wukv[:, k, h * 128:(h + 1) * 128], ckvB[:, k, k0:k0 + n], start=(k == 0), stop=(k == 1))
                            kb.copy("act", KTb[e][0:64, k0:k0 + n], pk[0:64, 0:n])
                        kb.copy("dve", KTb[e][64:96, 0:NK], kpeB[64:96, 0:NK])
                    if stage == 3:
                        continue
                    for (q0, nq, kts) in qranges:
                        po, psm = pbank[2], pbank[3]
                        first = True
                        for e in range(2):
                            Vsrc = Ve if e == 0 else Vo
                            osrc = oE if e == 0 else oO
                            for ki, kt in enumerate(kts):
                                ps_ = pbank[pti[0] % 2]
                                pt = PT[pti[0] % 2]
                                pti[0] += 1
                                ksl = slice(kt * 128, (kt + 1) * 128)
                                kb.mm(ps_[:, 0:nq], KTb[e][0:96, ksl], qTb[e][0:96, q0:q0 + nq])
                                kb.act(pt[:, 0:nq], ps_[:, 0:nq], AF.Exp, scale=scale)
                                last = (e == 1 and ki == len(kts) - 1)
                                kb.mm(po[:, 0:nq], Vsrc[:, kt, :], pt[:, 0:nq], start=first, stop=last)
                                kb.mm(psm[:, 0:nq], osrc[:], pt[:, 0:nq], start=first, stop=last)
                                first = False
                        kb.recip(rs_[:, 0:nq], psm[:, 0:nq])
                        kb.tt("dve", mch[:, q0:q0 + nq], po[:, 0:nq], rs_[:, 0:nq], ALU.mult)
                    wo_apply(c, mch)
                kb.barrier()

        def hyena(hf, Hb, wo_apply):
            Ls = 256 if hf == 0 else 1024
            nT = Ls // 128
            with contextlib.ExitStack() as ph:
                f = lambda name, shape, dt=F32: sbx(ph, "y_" + name, shape, dt)
                wch = [f("wch%d" % i, [128, 8, 128], BF16) for i in range(1)]
                vF, x1F, x2F, T1 = f("vF", [128, HALF]), f("x1F", [128, HALF]), f("x2F", [128, HALF]), f("T1", [128, HALF])
                fwd = f("fwd", [128, nT, 2, Ls], BF16)
                invp = [f("invp%d" % i, [128, 2, Ls], BF16) for i in range(2)]
                w3 = f("w3", [128, 2048], BF16)
                featT = x1F[:, 0:Ls]
                h1s, h2s = T1[:, 0:Ls], x2F[:, 0:Ls]
                h2b = f("h2b", [128, Ls], BF16)
                win = f("win", [128, nT, 128])
                fr3, fb3 = f("fr3", [128, 2]), f("fb3", [128, 2])
                hsum, hdif = f("hsum", [128, nT, 128], BF16), f("hdif", [128, nT, 128], BF16)
                Hc, Hs = f("Hc", [128, nT, 128]), f("Hs", [128, nT, 128])
                zT = f("zT", [128, nT, 128], BF16)
                Yc, Ys = f("Yc", [128, nT, 128], BF16), f("Ys", [128, nT, 128], BF16)
                ta, tb, tc_ = f("ta", [128, 512]), f("tb", [128, 512]), f("tc", [128, 512])
                mch = f("mch", [128, HALF], BF16)
                for i in range(2):
                    kb.dma("sp", fwd[:, :, i, :], dftd[Ls][i].rearrange("(st p) f -> p st f", p=128))
                kb.dma("pool", w3[0:64, :], hy_w3[0, :, :])
                kb.dma("sp", featT, featd[Ls][:, :])
                kb.ts("dve", fr3[0:64, :], P("hfreq")[0:64, :], 1.0 / 3, ALU.mult)
                kb.tt("dve", fb3[0:64, 0:1], fr3[0:64, 0:1], P("hb1")[0:64, :], ALU.mult)
                kb.tt("dve", fb3[0:64, 1:2], fr3[0:64, 1:2], P("hb2")[0:64, :], ALU.mult)

                def sin3(dst, pin, li, n):
                    kb.act(ta[0:64, 0:n], pin, AF.Sin, bias=fb3[0:64, li:li + 1], scale=fr3[0:64, li:li + 1])
                    kb.tt("dve", tb[0:64, 0:n], ta[0:64, 0:n], ta[0:64, 0:n], ALU.mult)
                    kb.ts("dve", tb[0:64, 0:n], tb[0:64, 0:n], -4.0, ALU.mult, 3.0, ALU.add)
                    kb.tt("dve", dst, ta[0:64, 0:n], tb[0:64, 0:n], ALU.mult)

                for c0 in range(0, Ls, 512):
                    n = min(512, Ls - c0)
                    p1 = pbank[4]
                    kb.mm(p1[0:64, 0:n], P("hw1")[0:64, :], featT[0:64, c0:c0 + n])
                    sin3(h1s[0:64, c0:c0 + n], p1[0:64, 0:n], 0, n)
                    p2 = pbank[5]
                    kb.mm(p2[0:64, 0:n], P("hw2")[0:64, :], h1s[0:64, c0:c0 + n])
                    sin3(h2s[0:64, c0:c0 + n], p2[0:64, 0:n], 1, n)
                kb.copy("dve", h2b[0:64, :], h2s[0:64, :])
                wi = [0]
                hcv = P("hconv").rearrange("p (t c) -> p t c", t=3)
                ipi = [0]
                for cc in range(4):
                    kb.dma("sp", win[:], wind[Ls][:, cc * 128:(cc + 1) * 128].rearrange("(t p) c -> p t c", p=128))
                    for qi, dst in enumerate([vF, x1F, x2F]):
                        wt = wch[0]
                        wi[0] += 1
                        c0 = 672 + qi * 512 + cc * 128
                        kb.dma("pool", wt[:], cd_w_in[0, :, c0:c0 + 128].rearrange("(k p) c -> p k c", p=128))
                        for nt in range(2):
                            pp = pbank[4 + nt]
                            for k in range(8):
                                kb.mm(pp[:], wt[:, k, :], Hb[:, k, nt * 512:(nt + 1) * 512], start=(k == 0), stop=(k == 7))
                            kb.copy("act", T1[:, nt * 512:(nt + 1) * 512], pp[:])
                        ci = qi * 4 + cc
                        kb.ts("dve", dst[:], T1[:], hcv[:, 1, ci:ci + 1], ALU.mult)
                        for (s0, Lq) in SEQS[hf]:
                            kb.stt(dst[:, s0 + 1:s0 + Lq], T1[:, s0:s0 + Lq - 1], hcv[:, 0, ci:ci + 1], dst[:, s0 + 1:s0 + Lq], ALU.mult, ALU.add)
                            kb.stt(dst[:, s0:s0 + Lq - 1], T1[:, s0 + 1:s0 + Lq], hcv[:, 2, ci:ci + 1], dst[:, s0:s0 + Lq - 1], ALU.mult, ALU.add)
                    z = vF
                    for n in range(2):
                        gate = x1F if n == 0 else x2F
                        bcol = P("hbias").rearrange("p (n c) -> p n c", n=2)[:, n, cc:cc + 1]
                        for dt in range(nT):
                            pt_ = pbank[4 + dt % 2]
                            for di in range(2):
                                w0 = (n * 2 + di) * 512 + cc * 128
                                kb.mm(pt_[:, di * 128:(di + 1) * 128], h2b[0:64, dt * 128:(dt + 1) * 128], w3[0:64, w0:w0 + 128])
                            kb.tt("dve", ta[:, 0:128], pt_[:, 0:128], win[:, dt, :], ALU.mult)
                            kb.tt("dve", tb[:, 0:128], pt_[:, 128:256], win[:, dt, :], ALU.mult)
                            if dt == 0:
                                kb.ts("dve", tb[:, 0:128], tb[:, 0:128], P("nz0"), ALU.mult)
                            kb.tt("dve", hsum[:, dt, :], ta[:, 0:128], tb[:, 0:128], ALU.add)
                            kb.tt("dve", hdif[:, dt, :], ta[:, 0:128], tb[:, 0:128], ALU.subtract)
                        for ft in range(nT):
                            pc_ = pbank[4 + ft % 2]
                            for dt in range(nT):
                                kb.mm(pc_[:, 0:128], fwd[:, dt, 0, ft * 128:(ft + 1) * 128], hsum[:, dt, :], start=(dt == 0), stop=(dt == nT - 1))
                            kb.copy("act", Hc[:, ft, :], pc_[:, 0:128])
                            for dt in range(nT):
                                kb.mm(pc_[:, 128:256], fwd[:, dt, 1, ft * 128:(ft + 1) * 128], hdif[:, dt, :], start=(dt == 0), stop=(dt == nT - 1))
                            kb.copy("act", Hs[:, ft, :], pc_[:, 128:256])
                        for (s0, Lq) in SEQS[hf]:
                            for st in range(nT):
                                ptr = pbank[4 + st % 2]
                                kb.tr(ptr[:, 0:128], z[:, s0 + st * 128:s0 + (st + 1) * 128], ident)
                                kb.copy("act", zT[:, st, :], ptr[:, 0:128])
                            for ft in range(nT):
                                pzc, pzs = pbank[2], pbank[3]
                                for st in range(nT):
                                    kb.mm(pzc[:, 0:128], fwd[:, st, 0, ft * 128:(ft + 1) * 128], zT[:, st, :], start=(st == 0), stop=(st == nT - 1))
                                for st in range(nT):
                                    kb.mm(pzs[:, 0:128], fwd[:, st, 1, ft * 128:(ft + 1) * 128], zT[:, st, :], start=(st == 0), stop=(st == nT - 1))
                                kb.tt("dve", ta[:, 0:128], pzc[:, 0:128], Hc[:, ft, :], ALU.mult)
                                kb.tt("dve", tb[:, 0:128], pzs[:, 0:128], Hs[:, ft, :], ALU.mult)
                                kb.tt("dve", Yc[:, ft, :], ta[:, 0:128], tb[:, 0:128], ALU.subtract)
                                kb.tt("dve", ta[:, 0:128], pzc[:, 0:128], Hs[:, ft, :], ALU.mult)
                                kb.tt("dve", tb[:, 0:128], pzs[:, 0:128], Hc[:, ft, :], ALU.mult)
                                kb.tt("dve", Ys[:, ft, :], ta[:, 0:128], tb[:, 0:128], ALU.add)
                            blocks = [(b0, min(512, Lq - b0)) for b0 in range(0, Lq, 512)]
                            pys = [pbank[0], pbank[1]]
                            for ft in range(nT):
                                ip = invp[ipi[0] % 2]
                                ipi[0] += 1
                                for i in range(2):
                                    kb.dma("sp", ip[:, i, :], dftd[Ls][2 + i, ft * 128:(ft + 1) * 128, :])
                                for bi, (b0, bn) in enumerate(blocks):
                                    kb.mm(pys[bi][:, 0:bn], Yc[:, ft, :], ip[:, 0, b0:b0 + bn], start=(ft == 0), stop=False)
                                    kb.mm(pys[bi][:, 0:bn], Ys[:, ft, :], ip[:, 1, b0:b0 + bn], start=False, stop=(ft == nT - 1))
                            for bi, (b0, bn) in enumerate(blocks):
                                zs = z[:, s0 + b0:s0 + b0 + bn]
                                kb.ts("dve", tc_[:, 0:bn], zs, bcol, ALU.mult)
                                kb.stt(tc_[:, 0:bn], pys[bi][:, 0:bn], 1.0 / Lq, tc_[:, 0:bn], ALU.mult, ALU.add)
                                kb.tt("dve", zs, tc_[:, 0:bn], gate[:, s0 + b0:s0 + b0 + bn], ALU.mult)
                    kb.copy("dve", mch[:], z[:])
                    wo_apply(4 + cc, mch)
                kb.barrier()

        def mixer_phase(L):
            with contextlib.ExitStack() as ph:
                Hb = sbx(ph, "Hbm", [128, 8, HALF], BF16)
                wo = [sbx(ph, "wo%d" % i, [128, D], BF16) for i in range(2)]
                woi = [0]
                for hf in range(2):
                    rms_stats(hf)
                    make_coefs(L, hf, 1, "nmx_%d" % L)
                    make_H(Hb, hf)

                    def wo_apply(j, mc, hf=hf):
                        w = wo[woi[0] % 2]
                        woi[0] += 1
                        kb.dma("pool", w[:], w_out_d[L, j * 128:(j + 1) * 128, :])
                        for m in range(8):
                            for nt in range(2):
                                po = pbank[6 + (m * 2 + nt) % 2]
                                kb.mm(po[:], w[:, m * 128:(m + 1) * 128], mc[:, nt * 512:(nt + 1) * 512])
                                t0 = hf * HALF + nt * 512
                                kb.stt(X[:, m, t0:t0 + 512], po[:], coef[:, 16 + m:17 + m], X[:, m, t0:t0 + 512], ALU.mult, ALU.add)

                    if L == 0:
                        if test in (None, "rwkv", "l0"):
                            rwkv(hf, Hb, wo_apply)
                        if test in (None, "gdn", "l0"):
                            gdn(hf, Hb, wo_apply)
                    else:
                        if test in (None, "mla", "l1"):
                            mla(hf, Hb, wo_apply)
                        if test in (None, "hy", "l1"):
                            hyena(hf, Hb, wo_apply)
                kb.barrier()

        def final_norm_and_store():
            for hf in range(2):
                rms_stats(hf)
                for j in range(8):
                    for nt in range(2):
                        t0 = hf * HALF + nt * 512
                        tf = tmpf[(j * 2 + nt) % 2]
                        kb.stt(tf[:], X[:, j, t0:t0 + 512], P("fnorm")[:, j:j + 1], rstd[:, nt * 512:(nt + 1) * 512], ALU.mult, ALU.mult)
                        kb.dma("sp", yT[j * 128:(j + 1) * 128, t0:t0 + 512], tf[:])

        if test in ("gdn", "rwkv", "l0"):
            mixer_phase(0)
        elif test in ("mla", "hy", "l1"):
            mixer_phase(1)
        else:
            for L in range(2):
                ffn_phase(L, 0)
                mixer_phase(L)
                ffn_phase(L, 1)
        final_norm_and_store()
        kb.barrier()
        print("instructions:", kb.ninst)
    return nc


def prep_inputs(inp):
    inp = {k: np.asarray(v) for k, v in inp.items()}
    maps = []
    offs = None
    cnames, carr = make_consts()
    ropecs = rope_tables()
    hc = {L: hyena_consts(L) for L in (256, 1024)}
    for c in range(NCORE):
        xp = inp["x_prompt"][4 * c:4 * c + 4].reshape(HALF, D)
        xs = inp["x_sample"][c]
        xT = np.ascontiguousarray(np.concatenate([xp, xs], 0).T)
        Pk = pack_params(inp, c)
        prm = Pk.build()
        offs = Pk.off
        m = {"xT": xT, "prm": prm, "cst": carr, "w_mod": inp["w_mod"],
             "ffn1_w_gu": inp["ffn1_w_gu"], "ffn2_w_gu": inp["ffn2_w_gu"],
             "ffn1_w_down": inp["ffn1_w_down"], "ffn2_w_down": inp["ffn2_w_down"],
             "w_out": inp["w_out"], "ab_w_in": inp["ab_w_in"],
             "sgdn_in": np.ascontiguousarray(inp["state_gdn"][c, 0]),
             "srwkv_in": np.ascontiguousarray(inp["state_rwkv"][c, 0].transpose(0, 1, 3, 2)),
             "rwkv_w_up": inp["rwkv_w_up"], "rwkv_a_up": inp["rwkv_a_up"], "rwkv_g_up": inp["rwkv_g_up"],
             "cd_w_in": inp["cd_w_in"], "mla_w_uq": inp["mla_w_uq"], "mla_w_ukv": inp["mla_w_ukv"],
             "cckvT": np.ascontiguousarray(inp["cache_ckv"][c, 0].T), "ckpeT": np.ascontiguousarray(inp["cache_kpe"][c, 0].T),
             "ropecs": ropecs, "hy_w1": inp["hy_w1"], "hy_w2": inp["hy_w2"], "hy_w3": inp["hy_w3"],
             "dft256": hc[256][0], "dft1024": hc[1024][0], "feat256": hc[256][1], "feat1024": hc[1024][1],
             "win256": hc[256][2], "win1024": hc[1024][2]}
        maps.append(m)
    return maps, offs


def kernel(**inputs):
    maps, offs = prep_inputs(inputs)
    NP = maps[0]["prm"].shape[1]
    nc = build_program(offs, NP)
    res = run_bass_kernel_spmd(nc, maps, core_ids=list(range(NCORE)))
    yp = np.zeros((32, 256, D), np.float32)
    ys = np.zeros((8, 1024, D), np.float32)
    srw = np.zeros((32, 1, 2, 8, 64, 64), np.float32)
    sgd = np.zeros((32, 1, 2, 4, 128, 128), np.float32)
    ckv = np.zeros((32, 1, 256, 256), np.float32)
    kpe = np.zeros((32, 1, 256, 32), np.float32)
    for c in range(NCORE):
        r = res.results[c]
        yT = r["yT"]
        yp[4 * c:4 * c + 4] = yT[:, :HALF].T.reshape(4, 256, D)
        ys[c] = yT[:, HALF:].T
        srw[4 * c:4 * c + 4, 0] = np.asarray(r["srwkv_out"]).transpose(0, 1, 2, 4, 3)
        sgd[4 * c:4 * c + 4, 0] = np.asarray(r["sgdn_out"])
        ckv[4 * c:4 * c + 4, 0] = np.asarray(r["ckv_out"]).T.reshape(4, 256, 256)
        kpe[4 * c:4 * c + 4, 0] = np.asarray(r["kpe_out"]).T.reshape(4, 256, 32)
    return yp, ys, srw, sgd, ckv, kpe
```

```python
import contextlib
import numpy as np
import concourse.bass as bass
import concourse.mybir as mybir
from concourse.bass_utils import run_bass_kernel_spmd

F32 = mybir.dt.float32
BF16 = mybir.dt.bfloat16
F32R = mybir.dt.float32r
FP16 = mybir.dt.float16
AF = mybir.ActivationFunctionType
ALU = mybir.AluOpType

NCORE = 8
D = 1024
DFF = 2816
TOK = 2048
HALF = 1024
EPS = 1e-6


class KB:
    def __init__(self, nc, es, n_dma_sems=24):
        self.nc = nc
        self.es = es
        self.eng = {"pe": nc.tensor, "act": nc.scalar, "dve": nc.vector, "pool": nc.gpsimd, "sp": nc.sync}
        self.sem = {k: es.enter_context(nc.semaphore("sem_" + k)) for k in self.eng}
        self.cnt = {k: 0 for k in self.eng}
        self.seen = {k: {} for k in self.eng}
        self.dsem = [es.enter_context(nc.semaphore("dsem%d" % i)) for i in range(n_dma_sems)]
        self.dcnt = [0] * n_dma_sems
        self.drr = {"sp": 0, "pool": 0, "act": 0}
        self.dpool = {"sp": list(range(0, n_dma_sems // 2)), "act": list(range(0, n_dma_sems // 2)),
                      "pool": list(range(n_dma_sems // 2, n_dma_sems))}
        self.recs = {}
        self.semobj = {}
        for k in self.eng:
            self.semobj[k] = self.sem[k]
        for i, s in enumerate(self.dsem):
            self.semobj[("d", i)] = s
        self.ninst = 0
        self.rt = set()

    def R(self, ap):
        if ap is not None and not isinstance(ap, (int, float)) and ap.tensor.name in self.rt:
            return ap.bitcast(F32R)
        return ap

    @staticmethod
    def box(ap):
        t = ap.tensor
        dims = list(ap.ap)
        if type(t).__name__.startswith("DRam"):
            f0 = ap.offset
            f1 = f0 + sum((c - 1) * abs(s) for s, c in dims) + 1
            return (t.name, 0, 1, f0, f1)
        ps, pc = dims[0]
        if ps == 0:
            ps = 1 << 40
        if type(t).__name__.startswith("PSum"):
            return (t.name, 0, 128, 0, 1 << 30)
        p0 = ap.offset // ps
        f0 = ap.offset % ps
        f1 = f0 + sum((c - 1) * abs(s) for s, c in dims[1:]) + 1
        return (t.name, p0, p0 + pc, f0, f1)

    def _deps(self, b, write, deps, eng=None):
        name, p0, p1, f0, f1 = b
        lst = self.recs.get(name)
        if not lst:
            return
        psum = f1 == (1 << 30)
        for r in lst:
            if r[0] < p1 and p0 < r[1] and r[2] < f1 and f0 < r[3]:
                if write or r[6] or (psum and r[7] != eng):
                    k = r[4]
                    if deps.get(k, 0) < r[5]:
                        deps[k] = r[5]

    def _record(self, b, write, semkey, val, eng):
        name, p0, p1, f0, f1 = b
        lst = self.recs.setdefault(name, [])
        if write:
            lst[:] = [r for r in lst if not (p0 <= r[0] and r[1] <= p1 and f0 <= r[2] and r[3] <= f1)]
        else:
            lst[:] = [r for r in lst if not ((not r[6]) and r[7] == eng and r[4] == semkey
                                             and p0 <= r[0] and r[1] <= p1 and f0 <= r[2] and r[3] <= f1)]
        lst.append((p0, p1, f0, f1, semkey, val, write, eng))

    def _wait(self, e, deps):
        seen = self.seen[e]
        for k, v in deps.items():
            if e == "pe" and k == "pe":
                continue
            if seen.get(k, 0) < v:
                self.eng[e].wait_ge(self.semobj[k], v)
                seen[k] = v

    def emit(self, e, fn, outs, ins):
        deps = {}
        ob = [self.box(a) for a in outs]
        ib = [self.box(a) for a in ins if a is not None and not isinstance(a, (int, float))]
        for b in ib:
            self._deps(b, False, deps, e)
        for b in ob:
            self._deps(b, True, deps, e)
        self._wait(e, deps)
        inst = fn()
        self.cnt[e] += 1
        inst.then_inc(self.sem[e], 1)
        v = self.cnt[e]
        for b in ib:
            self._record(b, False, e, v, e)
        for b in ob:
            self._record(b, True, e, v, e)
        self.ninst += 1
        return inst

    def dma(self, q, out, in_):
        deps = {}
        ob = self.box(out)
        ib = self.box(in_)
        self._deps(ib, False, deps)
        self._deps(ob, True, deps)
        pl = self.dpool[q]
        i = pl[self.drr[q] % len(pl)]
        self.drr[q] += 1
        k = ("d", i)
        if self.dcnt[i] > 0:
            deps[k] = max(deps.get(k, 0), self.dcnt[i])
        self._wait(q, deps)
        inst = self.eng[q].dma_start(out=out, in_=in_)
        self.dcnt[i] += 16
        inst.then_inc(self.dsem[i], 16)
        self._record(ib, False, k, self.dcnt[i], "dma")
        self._record(ob, True, k, self.dcnt[i], "dma")
        self.ninst += 1

    def barrier(self, engines=None):
        engines = engines or list(self.eng)
        for e in engines:
            deps = {o: self.cnt[o] for o in self.eng if o != e and self.cnt[o] > 0}
            for i, c in enumerate(self.dcnt):
                if c > 0:
                    deps[("d", i)] = c
            self._wait(e, deps)

    def mm(self, out, lhsT, rhs, start=True, stop=True):
        nc = self.nc
        if lhsT.tensor.name in self.rt and rhs.tensor.name in self.rt:
            lhsT, rhs = lhsT.bitcast(F32R), rhs.bitcast(F32R)
        return self.emit("pe", lambda: nc.tensor.matmul(out, lhsT, rhs, start=start, stop=stop), [out], [lhsT, rhs])

    def tr(self, out, in_, ident):
        nc = self.nc
        return self.emit("pe", lambda: nc.tensor.transpose(out, in_, ident), [out], [in_, ident])

    def act(self, out, in_, func, bias=None, scale=1.0):
        nc = self.nc
        kw = {}
        if bias is not None:
            kw["bias"] = bias
        ins = [in_]
        if bias is not None and not isinstance(bias, (int, float)):
            ins.append(bias)
        if not isinstance(scale, (int, float)):
            ins.append(scale)
        out = self.R(out)
        return self.emit("act", lambda: nc.scalar.activation(out=out, in_=in_, func=func, scale=scale, **kw), [out], ins)

    def tt(self, e, out, in0, in1, op):
        eng = self.eng[e]
        out = self.R(out)
        return self.emit(e, lambda: eng.tensor_tensor(out=out, in0=in0, in1=in1, op=op), [out], [in0, in1])

    def ts(self, e, out, in0, s1, op0, s2=None, op1=None):
        eng = self.eng[e]
        out = self.R(out)
        ins = [in0] + [s for s in (s1, s2) if s is not None and not isinstance(s, (int, float))]
        if op1 is None:
            return self.emit(e, lambda: eng.tensor_scalar(out=out, in0=in0, scalar1=s1, scalar2=None, op0=op0), [out], ins)
        return self.emit(e, lambda: eng.tensor_scalar(out=out, in0=in0, scalar1=s1, scalar2=s2, op0=op0, op1=op1), [out], ins)

    def stt(self, out, in0, scalar, in1, op0, op1):
        nc = self.nc
        out = self.R(out)
        ins = [in0, in1] + ([scalar] if not isinstance(scalar, (int, float)) else [])
        return self.emit("dve", lambda: nc.vector.scalar_tensor_tensor(out=out, in0=in0, scalar=scalar, in1=in1, op0=op0, op1=op1), [out], ins)

    def copy(self, e, out, in_):
        eng = self.eng[e]
        out = self.R(out)
        if e == "act":
            return self.emit(e, lambda: eng.copy(out=out, in_=in_), [out], [in_])
        return self.emit(e, lambda: eng.tensor_copy(out=out, in_=in_), [out], [in_])

    def recip(self, out, in_):
        nc = self.nc
        return self.emit("dve", lambda: nc.vector.reciprocal(out=out, in_=in_), [out], [in_])

    def memset(self, e, out, val):
        eng = self.eng[e]
        if out.tensor.name in self.rt:
            return self.ts("dve", out, self.ones_ap, float(val), ALU.mult)
        return self.emit(e, lambda: eng.memset(out, val), [out], [])


def fm(v, nchunk):
    return np.ascontiguousarray(np.asarray(v, np.float32).reshape(nchunk, 128).T)


def rows(v):
    v = np.asarray(v, np.float32).reshape(1, -1)
    return np.ascontiguousarray(np.repeat(v, 128, axis=0))


class Pack:
    def __init__(self):
        self.cols = []
        self.off = {}
        self.n = 0

    def add(self, name, arr):
        arr = np.asarray(arr, np.float32)
        if arr.shape[0] < 128:
            arr = np.concatenate([arr, np.zeros((128 - arr.shape[0],) + arr.shape[1:], np.float32)], 0)
        arr = arr.reshape(128, -1)
        self.off[name] = (self.n, arr.shape[1])
        self.n += arr.shape[1]
        self.cols.append(arr)

    def build(self):
        return np.ascontiguousarray(np.concatenate(self.cols, 1))


def pack_params(inp, core):
    P = Pack()
    c = inp["c"][core]
    cc = np.stack([fm(inp["c_ctx"], 8), fm(c, 8)], axis=2)
    P.add("cT", cc)
    for i in range(2):
        P.add("bmod%d" % i, fm(inp["b_mod"][i], 72))
        P.add("nf1_%d" % i, fm(inp["norm_ffn1"][i], 8))
        P.add("nmx_%d" % i, fm(inp["norm_mix"][i], 8))
        P.add("nf2_%d" % i, fm(inp["norm_ffn2"][i], 8))
    P.add("fnorm", fm(inp["final_norm"], 8))
    P.add("gconv", np.stack([fm(inp["gdn_conv"][0][t], 12) for t in range(3)], axis=1))
    P.add("gnorm", np.asarray(inp["gdn_norm"][0], np.float32).reshape(128, 1))
    P.add("galog", rows(inp["gdn_a_log"][0]))
    P.add("gdtb", rows(inp["gdn_dt_bias"][0]))
    P.add("rmu", fm(inp["rwkv_mu"][0], 15))
    P.add("rw0", np.stack([fm(inp["rwkv_w0"][0][d], 4) for d in range(2)], axis=1))
    P.add("ra0", np.stack([fm(inp["rwkv_a0"][0][d], 4) for d in range(2)], axis=1))
    P.add("rkk", fm(inp["rwkv_k_k"][0], 4))
    P.add("rka", fm(inp["rwkv_k_a"][0], 4))
    P.add("rrk", fm(inp["rwkv_r_k"][0].reshape(-1), 4))
    P.add("rlnw", fm(inp["rwkv_ln_w"][0], 4))
    P.add("rlnb", fm(inp["rwkv_ln_b"][0], 4))
    P.add("mqn", fm(inp["mla_q_norm"][0], 3))
    P.add("mkvn", fm(inp["mla_kv_norm"][0], 2))
    P.add("hconv", np.stack([fm(inp["hy_conv"][0][t], 12) for t in range(3)], axis=1))
    P.add("hbias", np.stack([fm(inp["hy_bias"][0][n], 4) for n in range(2)], axis=1))
    P.add("hb1", np.asarray(inp["hy_b1"][0], np.float32).reshape(64, 1))
    P.add("hb2", np.asarray(inp["hy_b2"][0], np.float32).reshape(64, 1))
    P.add("hfreq", np.ascontiguousarray(np.asarray(inp["hy_freq"][0], np.float32).T))
    nz0 = np.ones((128, 1), np.float32)
    nz0[0, 0] = 0.0
    P.add("nz0", nz0)
    w1p = np.zeros((128, 64), np.float32)
    w1p[:33] = np.asarray(inp["hy_w1"][0], np.float32)
    P.add("hw1", w1p)
    P.add("hw2", np.asarray(inp["hy_w2"][0], np.float32))
    hm = np.zeros((128, 2), np.float32)
    hm[:64, 0] = 1.0
    hm[64:, 1] = 1.0
    P.add("hmask", hm)
    rm = np.ones((128, 1024), np.float32)
    rm[:, ::128] = 0.0
    P.add("rmask", rm)
    return P


def make_consts():
    p = np.arange(128)[:, None]
    f = np.arange(128)[None, :]
    ident = (p == f).astype(np.float32)
    le = (p <= f).astype(np.float32)
    lt = (p < f).astype(np.float32)
    ge = (p >= f).astype(np.float32)
    gt = (p > f).astype(np.float32)
    BIG = 30000.0
    bo = ((p // 64) == (f // 64)).astype(np.float32)
    jpad = np.zeros((128, 128), np.float32)
    for i in range(16):
        jpad[64 + 2 * i + 1, 64 + 2 * i] = -1.0
        jpad[64 + 2 * i, 64 + 2 * i + 1] = 1.0
    onesE = np.zeros((128, 128), np.float32)
    onesE[:, :64] = 1.0
    onesO = np.zeros((128, 128), np.float32)
    onesO[:, 64:] = 1.0
    b32 = ((p // 32) == (f // 32)).astype(np.float32)
    m1 = (((p // 64) == (f // 64)) & ((p // 32) != (f // 32))).astype(np.float32)
    m2 = ((p // 64) != (f // 64)).astype(np.float32)
    names = ["ident", "le", "lt", "ge", "gt", "pos_gt", "pos_lt", "neg_le", "neg_ge", "bo", "jpad", "onesE", "onesO", "b32", "m1", "m2"]
    arrs = [ident, le, lt, ge, gt, BIG * (1 - gt), BIG * (1 - lt), -BIG * (1 - le), -BIG * (1 - ge), bo, jpad, onesE, onesO, b32, m1, m2]
    return names, np.ascontiguousarray(np.concatenate(arrs, 1).astype(np.float32))


def rope_tables():
    L = 1024
    row = np.repeat(np.arange(L // 64), 64).astype(np.float32)
    col = (np.arange(L) % 64).astype(np.float32)
    n = 8
    inv = (10000.0 ** (-np.arange(n, dtype=np.float32) / n)).astype(np.float32)
    ang = np.concatenate([row[:, None] * inv, col[:, None] * inv], axis=-1)
    cs = np.zeros((2, 128, L), np.float32)
    for r in range(32):
        cs[0, 64 + r] = np.cos(ang[:, r // 2])
        cs[1, 64 + r] = np.sin(ang[:, r // 2])
    return cs


def hyena_consts(L):
    import ml_dtypes
    t = np.arange(L, dtype=np.float64)
    w = 2.0 * np.pi * (t + 0.5) / (2 * L)
    ph = np.outer(t, w)
    Cm = np.cos(ph)
    Sm = np.sin(ph)
    dft = np.stack([Cm, Sm, Cm.T, Sm.T]).astype(np.float32).astype(ml_dtypes.bfloat16)
    t32 = np.arange(L, dtype=np.float32)
    t_norm = t32 / max(L - 1, 1)
    bands = np.linspace(1e-4, 16 - 1, 16, dtype=np.float32)
    ang = (np.float32(2.0 * np.pi / L) * t32[:, None] * bands).astype(np.float32)
    feats = np.concatenate([t_norm[:, None], np.cos(ang), -np.sin(ang)], axis=-1).astype(np.float32)
    min_decay = np.log(1e-2) / 1.5
    max_decay = np.log(1e-2) / 0.3
    deltas = np.abs(np.linspace(min_decay, max_decay, 512, dtype=np.float32))
    window = np.exp(-t_norm[:, None] * deltas).astype(np.float32)
    featsT = np.zeros((128, L), np.float32)
    featsT[:33] = feats.T
    return dft, featsT, window


A_COLS = 1920
SEQS = {0: [(0, 256), (256, 256), (512, 256), (768, 256)], 1: [(0, 1024)]}


def build_program(offs, NP, test=None, stage=99):
    nc = bass.Bass("TRN2", target_bir_lowering=False)
    dr = {}

    def din(name, shape, dt=F32):
        dr[name] = nc.dram_tensor(name, list(shape), dt, kind="ExternalInput").ap()
        return dr[name]

    def dout(name, shape, dt=F32):
        dr[name] = nc.dram_tensor(name, list(shape), dt, kind="ExternalOutput").ap()
        return dr[name]

    cnames, carr = make_consts()
    xT = din("xT", [D, TOK])
    prm_d = din("prm", [128, NP])
    cst_d = din("cst", [128, carr.shape[1]])
    w_mod = din("w_mod", [2, D, 9 * D])
    ffn_gu = [din("ffn1_w_gu", [2, D, 2 * DFF]), din("ffn2_w_gu", [2, D, 2 * DFF])]
    ffn_dn = [din("ffn1_w_down", [2, DFF, D]), din("ffn2_w_down", [2, DFF, D])]
    w_out_d = din("w_out", [2, D, D])
    ab_w_in = din("ab_w_in", [1, D, 3984])
    sgdn_in = din("sgdn_in", [2, 4, 128, 128])
    srwkv_in = din("srwkv_in", [2, 8, 64, 64])
    rwkv_w_up = din("rwkv_w_up", [1, 2, 64, 512])
    rwkv_a_up = din("rwkv_a_up", [1, 2, 64, 512])
    rwkv_g_up = din("rwkv_g_up", [1, 128, 512])
    srwkv_out = dout("srwkv_out", [4, 2, 8, 64, 64])
    cd_w_in = din("cd_w_in", [1, D, 2208])
    mla_w_uq = din("mla_w_uq", [1, 384, 768])
    mla_w_ukv = din("mla_w_ukv", [1, 256, 1024])
    cckvT = din("cckvT", [256, 256])
    ckpeT = din("ckpeT", [32, 256])
    ropecs = din("ropecs", [2, 128, 1024])
    hy_w1 = din("hy_w1", [1, 33, 64])
    hy_w2 = din("hy_w2", [1, 64, 64])
    hy_w3 = din("hy_w3", [1, 64, 2048])
    dftd = {256: din("dft256", [4, 256, 256], BF16), 1024: din("dft1024", [4, 1024, 1024], BF16)}
    featd = {256: din("feat256", [128, 256]), 1024: din("feat1024", [128, 1024])}
    wind = {256: din("win256", [256, 512]), 1024: din("win1024", [1024, 512])}
    ckv_out = dout("ckv_out", [256, 1024])
    kpe_out = dout("kpe_out", [32, 1024])
    yT = dout("yT", [D, TOK])
    sgdn_out = dout("sgdn_out", [4, 2, 4, 128, 128])

    es = contextlib.ExitStack()
    with es:
        kb = KB(nc, es)

        uid = [0]

        def sbx(stack, name, shape, dt=F32, r=False):
            uid[0] += 1
            nm = "%s_%d" % (name, uid[0])
            if r:
                kb.rt.add(nm)
            return stack.enter_context(nc.sbuf_tensor(nm, list(shape), dt))

        def sb(name, shape, dt=F32):
            return sbx(es, name, shape, dt)

        X = sb("X", [128, 8, TOK])
        prm = sb("prm_sb", [128, NP])
        cst = sb("cst_sb", [128, carr.shape[1]])
        ones = sb("ones", [128, 128])
        modT = sb("modT", [128, 2, 2, 72])
        cs = sb("cs", [128, 8, 2], BF16)
        sq = [sb("sq%d" % i, [128, 512]) for i in range(2)]
        rstd = sb("rstd", [128, HALF])
        tmpf = [sb("tmpf%d" % i, [128, 512]) for i in range(2)]
        coef = sb("coef", [128, 64])
        pbank = [es.enter_context(nc.psum_tensor("pb%d" % i, [128, 512], F32)) for i in range(8)]

        def P(name):
            o, n = offs[name]
            return prm[:, o:o + n]

        def C(name):
            i = cnames.index(name)
            return cst[:, i * 128:(i + 1) * 128]

        slot = [0]

        def pq():
            s = slot[0]
            slot[0] = (s + 1) % 16
            return pbank[s % 4][:, (s // 4) * 128:(s // 4 + 1) * 128]

        kb.dma("sp", prm[:], prm_d[:, :])
        kb.dma("sp", cst[:], cst_d[:, :])
        for j in range(8):
            kb.dma("sp", X[:, j, :], xT[j * 128:(j + 1) * 128, :])
        kb.memset("dve", ones[:], 1.0)
        kb.ones_ap = ones[:]
        ident = C("ident")
        ident16 = sb("ident16", [128, 128], FP16)
        kb.copy("dve", ident16[:], ident)

        with contextlib.ExitStack() as ph:
            wm = [sbx(ph, "wm%d" % i, [128, 8, 1024], BF16) for i in range(2)]
            cT = P("cT")
            kb.act(cs[:].rearrange("p a b -> p (a b)"), cT, AF.Silu)
            for L in range(2):
                pm = pbank[7]
                pmv = pm[:, 0:144].rearrange("p (a b) -> p a b", b=2)
                for n in range(9):
                    wt = wm[n % 2]
                    kb.dma("pool", wt[:], w_mod[L, :, n * 1024:(n + 1) * 1024].rearrange("(k p) c -> p k c", p=128))
                    for j in range(8):
                        for k in range(8):
                            kb.mm(pmv[:, n * 8 + j, :], wt[:, k, j * 128:(j + 1) * 128], cs[:, k, :], start=(k == 0), stop=(k == 7))
                for v in range(2):
                    kb.tt("dve", modT[:, L, v, :], pmv[:, :, v], P("bmod%d" % L), ALU.add)
            kb.barrier()

        def rms_stats(hf):
            for nt in range(2):
                t0 = hf * HALF + nt * 512
                pst = pbank[6]
                for j in range(8):
                    s = sq[j % 2]
                    kb.act(s[:], X[:, j, t0:t0 + 512], AF.Square)
                    kb.mm(pst[:], ones[:], s[:], start=(j == 0), stop=(j == 7))
                kb.act(rstd[:, nt * 512:(nt + 1) * 512], pst[:], AF.Sqrt, bias=EPS, scale=1.0 / D)
            kb.recip(rstd[:], rstd[:])

        def make_coefs(L, hf, sub, gname):
            m = modT[:, L, hf, :]
            kb.stt(coef[:, 0:8], m[:, (3 * sub + 1) * 8:(3 * sub + 2) * 8], 1.0, P(gname), ALU.add, ALU.mult)
            kb.copy("dve", coef[:, 8:16], m[:, (3 * sub) * 8:(3 * sub + 1) * 8])
            kb.ts("dve", coef[:, 16:24], m[:, (3 * sub + 2) * 8:(3 * sub + 3) * 8], 0.5 if sub != 1 else 1.0, ALU.mult)

        def make_H(Hb, hf):
            for j in range(8):
                for nt in range(2):
                    t0 = hf * HALF + nt * 512
                    tf = tmpf[(j * 2 + nt) % 2]
                    kb.stt(tf[:], X[:, j, t0:t0 + 512], coef[:, j:j + 1], rstd[:, nt * 512:(nt + 1) * 512], ALU.mult, ALU.mult)
                    kb.act(Hb[:, j, nt * 512:(nt + 1) * 512], tf[:], AF.Identity, bias=coef[:, 8 + j:9 + j], scale=1.0)

        gu_groups = [(0, 3), (3, 3), (6, 3), (9, 2)]

        def ffn_phase(L, which):
            with contextlib.ExitStack() as ph:
                Hb = sbx(ph, "Hb", [128, 8, HALF], BF16)
                hh = sbx(ph, "hh", [128, 11, HALF], BF16)
                wgu = [sbx(ph, "wgu%d" % i, [128, 8, 2, 384], BF16) for i in range(2)]
                wdns = [sbx(ph, "wdn%d" % i, [128, 11, D], BF16) for i in range(2)]
                wdi = 0
                sgt = [sbx(ph, "sgt%d" % i, [128, 512], BF16) for i in range(2)]
                sub = 0 if which == 0 else 2
                wg = ffn_gu[which]
                wd = ffn_dn[which]
                gi = 0
                for hf in range(2):
                    rms_stats(hf)
                    make_coefs(L, hf, sub, ("nf1_%d" if which == 0 else "nf2_%d") % L)
                    make_H(Hb, hf)
                    for fh in range(2):
                        wdn = wdns[wdi % 2]
                        wdi += 1
                        kb.dma("pool", wdn[:], wd[L, fh * 1408:(fh + 1) * 1408, :].rearrange("(j p) c -> p j c", p=128))
                        for (g0, gn) in gu_groups:
                            wt = wgu[gi % 2]
                            gi += 1
                            c0 = fh * 1408 + g0 * 128
                            for gu in range(2):
                                kb.dma("pool", wt[:, :, gu, 0:gn * 128],
                                       wg[L, :, gu * DFF + c0: gu * DFF + c0 + gn * 128].rearrange("(k p) c -> p k c", p=128))
                            for jj in range(gn):
                                for nt in range(2):
                                    pg = pbank[(2 * (jj * 2 + nt)) % 4]
                                    pu = pbank[(2 * (jj * 2 + nt)) % 4 + 1]
                                    for k in range(8):
                                        kb.mm(pg[:], wt[:, k, 0, jj * 128:(jj + 1) * 128], Hb[:, k, nt * 512:(nt + 1) * 512], start=(k == 0), stop=(k == 7))
                                    for k in range(8):
                                        kb.mm(pu[:], wt[:, k, 1, jj * 128:(jj + 1) * 128], Hb[:, k, nt * 512:(nt + 1) * 512], start=(k == 0), stop=(k == 7))
                                    sg = sgt[(jj * 2 + nt) % 2]
                                    kb.act(sg[:], pg[:], AF.Silu)
                                    kb.tt("dve", hh[:, g0 + jj, nt * 512:(nt + 1) * 512], sg[:], pu[:], ALU.mult)
                        for m in range(8):
                            for nt in range(2):
                                po = pbank[4 + (m * 2 + nt) % 2]
                                for j in range(11):
                                    kb.mm(po[:], wdn[:, j, m * 128:(m + 1) * 128], hh[:, j, nt * 512:(nt + 1) * 512], start=(j == 0), stop=(j == 10))
                                t0 = hf * HALF + nt * 512
                                kb.stt(X[:, m, t0:t0 + 512], po[:], coef[:, 16 + m:17 + m], X[:, m, t0:t0 + 512], ALU.mult, ALU.add)
                kb.barrier()

        def interleave(gens):
            gens = list(gens)
            while gens:
                for g in list(gens):
                    try:
                        next(g)
                    except StopIteration:
                        gens.remove(g)

        hslot = [0]

        def pq2():
            k = hslot[0]
            hslot[0] = (k + 1) % 4
            return pbank[4 + k % 2][:, (k // 2) * 256:(k // 2 + 1) * 256]

        def tri_inv_T(ph_tiles, N, NT):
            T = ph_tiles
            Nd, Pa, Pb, Mt, Tun, PXa, PXb = T["Nd"], T["Pa"], T["Pb"], T["Mt"], T["Tun"], T["PXa"], T["PXb"]
            kb.tt("dve", Nd[:], N[:], C("b32"), ALU.mult)
            kb.tt("dve", PXa[:, 0:128], NT[:], C("b32"), ALU.mult)
            kb.copy("dve", PXa[:, 128:256], ident)
            yield
            Pc, Pn = Nd, Pa
            PXc, PXn = PXa, PXb
            for k in range(1, 5):
                px = pq2()
                kb.mm(px, Pc[:], PXc[:])
                pp = pq()
                kb.mm(pp, PXc[:, 0:128], Pc[:])
                if k < 4:
                    kb.copy("act", PXn[:, 0:128], px[:, 0:128])
                kb.tt("dve", PXn[:, 128:256], px[:, 128:256], PXc[:, 128:256], ALU.add)
                kb.copy("act", Pn[:], pp)
                yield
                Pc = Pn
                Pn = Pb if Pn is Pa else Pa
                PXc, PXn = PXn, PXc
            Xa = PXc[:, 128:256]
            pf = pq()
            kb.mm(pf, Pc[:], Xa)
            kb.tt("dve", Xa, pf, Xa, ALU.add)
            yield
            for mname in ("m1", "m2"):
                kb.tt("dve", Nd[:], N[:], C(mname), ALU.mult)
                pm = pq()
                kb.mm(pm, Nd[:], Xa)
                kb.copy("act", Mt[:], pm)
                ptr = pq().bitcast(FP16)[:, 0:128]
                kb.tr(ptr, Xa, ident16[:])
                kb.copy("act", Tun[:], ptr)
                yield
                pw = pq()
                kb.mm(pw, Tun[:], Mt[:])
                kb.tt("dve", Xa, pw, Xa, ALU.add)
                yield
            T["X"] = Xa

        def inv_tile_set(fr, pfx):
            d = {n: fr(pfx + n, [128, 128]) for n in ["Nd", "Pa", "Pb", "Mt", "Tun"]}
            d["PXa"] = fr(pfx + "PXa", [128, 256])
            d["PXb"] = fr(pfx + "PXb", [128, 256])
            return d

        def gdn(hf, Hb, wo_apply):
            with contextlib.ExitStack() as ph:
                f = lambda name, shape, dt=F32, r=False: sbx(ph, "g_" + name, shape, dt, r)
                fr = lambda name, shape: f(name, shape, F32, True)
                fh = lambda name, shape: f(name, shape, FP16)
                wb = [f("wb%d" % i, [128, 8, 4, 128], BF16) for i in range(1)]
                wab = f("wab", [128, 8, 16], BF16)
                praw = f("praw", [128, HALF])
                cv = f("cv", [128, HALF])
                qF, kF, vF, zF = fr("qF", [128, HALF]), fr("kF", [128, HALF]), f("vF", [128, HALF]), f("zF", [128, HALF])
                oacc = f("oacc", [128, HALF])
                mch = f("mch", [128, HALF], BF16)
                rn = f("rn", [128, HALF])
                abt = f("abt", [128, 8, 16])
                gsb = f("gsb", [128, 2, 8, 4])
                bsb = f("bsb", [128, 2, 8, 4])
                nbsb = f("nbsb", [128, 2, 8, 4])
                Gsb = f("Gsb", [128, 2, 8, 4])
                Gtot = f("Gtot", [128, 2, 8, 4])
                eG = f("eG", [128, 2, 8, 4])
                beG = f("beG", [128, 2, 8, 4])
                eGL = f("eGL", [128, 2, 8, 4])
                eGlast = f("eGlast", [128, 2, 8, 4])
                ea = f("ea", [128, 8])
                tsm = f("tsm", [128, 2, 8, 4])
                kT = f("kT", [128, 8, 128])
                vT = f("vT", [128, 8, 128])
                st_qkT = f("st_qkT", [128, 8, 128], BF16)
                st_u = f("st_u", [128, 8, 128])
                st_nwT = f("st_nwT", [128, 8, 128], BF16)
                st_qin = f("st_qin", [128, 8, 128], BF16)
                st_kout = f("st_kout", [128, 8, 128], BF16)
                S = f("S", [128, 128])
                Sb = f("Sb", [128, 128], BF16)
                NCH = 3
                t_vnew = f("t_vnew", [128, 128], BF16)
                chT = []
                for ci in range(NCH):
                    T = {n: f("c%d_%s" % (ci, n), [128, 128]) for n in ["diag", "d1", "d2", "E1", "E2", "eGr"]}
                    T.update({n: fh("c%d_%s" % (ci, n), [128, 128]) for n in ["N", "NT", "vb", "kbg"]})
                    T["inv"] = inv_tile_set(fh, "c%d_iv" % ci)
                    chT.append(T)
                c_ab = A_COLS + 2048
                kb.dma("pool", wab[:], ab_w_in[0, :, c_ab:c_ab + 16].rearrange("(k p) c -> p k c", p=128))
                pab = pbank[6]
                for tt in range(8):
                    for k in range(8):
                        kb.mm(pab[:, tt * 16:(tt + 1) * 16], Hb[:, k, tt * 128:(tt + 1) * 128], wab[:, k, :], start=(k == 0), stop=(k == 7))
                kb.copy("dve", abt[:].rearrange("p a b -> p (a b)"), pab[:, 0:128])
                abv = abt[:].rearrange("p t (d a h) -> p t d a h", d=2, a=2)
                kb.act(ea[:], P("galog"), AF.Exp)
                for d in range(2):
                    dtb = P("gdtb")[:, d * 4:(d + 1) * 4]
                    for tt in range(8):
                        kb.tt("dve", tsm[:, d, tt, :], abv[:, tt, d, 0, :], dtb, ALU.add)
                        kb.copy("dve", bsb[:, d, tt, :], abv[:, tt, d, 1, :])
                g2 = lambda t: t[:].rearrange("p d t h -> p (d t h)")
                kb.act(g2(tsm), g2(tsm), AF.Exp)
                kb.act(g2(tsm), g2(tsm), AF.Ln, bias=1.0)
                for d in range(2):
                    for tt in range(8):
                        kb.stt(gsb[:, d, tt, :], tsm[:, d, tt, :], -1.0, ea[:, d * 4:(d + 1) * 4], ALU.mult, ALU.mult)
                kb.act(g2(bsb), g2(bsb), AF.Sigmoid)
                kb.ts("dve", g2(nbsb), g2(bsb), -1.0, ALU.mult)
                pG = pbank[6]
                g3 = lambda t, d: t[:, d, :, :].rearrange("p t h -> p (t h)")
                kb.mm(pG[:, 0:32], C("le"), g3(gsb, 0))
                kb.mm(pG[:, 32:64], C("ge"), g3(gsb, 1))
                kb.mm(pG[:, 64:128], ones[:], g2(gsb))
                kb.copy("dve", g2(Gsb), pG[:, 0:64])
                kb.copy("dve", g2(Gtot), pG[:, 64:128])
                kb.act(g2(eG), g2(Gsb), AF.Exp)
                kb.tt("dve", g2(beG), g2(eG), g2(bsb), ALU.mult)
                kb.tt("dve", g2(eGL), g2(Gtot), g2(Gsb), ALU.subtract)
                kb.act(g2(eGL), g2(eGL), AF.Exp)
                kb.act(g2(eGlast), g2(Gtot), AF.Exp)

                for h in range(4):
                    wt = wb[0]
                    for qi in range(4):
                        c0 = A_COLS + qi * 512 + h * 128
                        kb.dma("pool", wt[:, :, qi, :], ab_w_in[0, :, c0:c0 + 128].rearrange("(k p) c -> p k c", p=128))
                    gc = P("gconv").rearrange("p (t c) -> p t c", t=3)
                    for qi, dst in enumerate([qF, kF, vF, zF]):
                        for nt in range(2):
                            pp = pbank[nt]
                            for k in range(8):
                                kb.mm(pp[:], wt[:, k, qi, :], Hb[:, k, nt * 512:(nt + 1) * 512], start=(k == 0), stop=(k == 7))
                            if qi == 3:
                                kb.act(zF[:, nt * 512:(nt + 1) * 512], pp[:], AF.Silu)
                            else:
                                kb.copy("act", praw[:, nt * 512:(nt + 1) * 512], pp[:])
                        if qi == 3:
                            continue
                        cc = qi * 4 + h
                        kb.ts("dve", cv[:], praw[:], gc[:, 1, cc:cc + 1], ALU.mult)
                        for (s0, Ls) in SEQS[hf]:
                            kb.stt(cv[:, s0 + 1:s0 + Ls], praw[:, s0:s0 + Ls - 1], gc[:, 0, cc:cc + 1], cv[:, s0 + 1:s0 + Ls], ALU.mult, ALU.add)
                            kb.stt(cv[:, s0:s0 + Ls - 1], praw[:, s0 + 1:s0 + Ls], gc[:, 2, cc:cc + 1], cv[:, s0:s0 + Ls - 1], ALU.mult, ALU.add)
                        kb.act(dst[:], cv[:], AF.Silu)
                        if qi < 2:
                            for nt in range(2):
                                s = sq[nt]
                                kb.act(s[:], dst[:, nt * 512:(nt + 1) * 512], AF.Square)
                                pst = pbank[2 + nt]
                                kb.mm(pst[:], ones[:], s[:])
                                kb.act(rn[:, nt * 512:(nt + 1) * 512], pst[:], AF.Sqrt, bias=1e-6, scale=1.0)
                            kb.recip(rn[:], rn[:])
                            kb.stt(dst[:], dst[:], (128.0 ** -0.5) if qi == 0 else 1.0, rn[:], ALU.mult, ALU.mult)
                    for tt in range(8):
                        p1 = pq()
                        kb.tr(p1, kF[:, tt * 128:(tt + 1) * 128], ident)
                        kb.copy("act", kT[:, tt, :], p1)
                        p2 = pq()
                        kb.tr(p2, vF[:, tt * 128:(tt + 1) * 128], ident)
                        kb.copy("act", vT[:, tt, :], p2)
                    for d in range(2):
                        posm = C("pos_gt") if d == 0 else C("pos_lt")
                        negm = C("neg_le") if d == 0 else C("neg_ge")
                        def pre_tile(tt, T, d=d, h=h, posm=posm, negm=negm):
                            tsl = slice(tt * 128, (tt + 1) * 128)
                            col = lambda t: t[:, d, tt, h:h + 1]
                            kb.ts("dve", T["diag"][:], ident, col(Gsb), ALU.mult)
                            prb = pq()
                            kb.mm(prb, ones[:], T["diag"][:])
                            yield
                            kb.stt(T["d1"][:], prb, col(Gsb), posm, ALU.subtract, ALU.add)
                            kb.act(T["E1"][:], T["d1"][:], AF.Exp, scale=-1.0)
                            kb.stt(T["d2"][:], prb, col(Gsb), negm, ALU.subtract, ALU.add)
                            kb.act(T["E2"][:], T["d2"][:], AF.Exp)
                            kb.act(T["eGr"][:], prb, AF.Exp)
                            pkk = pq()
                            kb.mm(pkk, kF[:, tsl], kF[:, tsl])
                            yield
                            kb.stt(T["N"][:], pkk, col(nbsb), T["E1"][:], ALU.mult, ALU.mult)
                            pnt = pq().bitcast(FP16)[:, 0:128]
                            kb.tr(pnt, T["N"][:], ident16[:])
                            kb.copy("act", T["NT"][:], pnt)
                            pqk = pq()
                            kb.mm(pqk, kF[:, tsl], qF[:, tsl])
                            kb.tt("dve", st_qkT[:, tt, :], pqk, T["E2"][:], ALU.mult)
                            kb.ts("pool", T["vb"][:], vT[:, tt, :], col(bsb), ALU.mult, 0.0, ALU.add)
                            kb.ts("pool", T["kbg"][:], kT[:, tt, :], col(beG), ALU.mult, 0.0, ALU.add)
                            kb.ts("pool", st_kout[:, tt, :], kT[:, tt, :], col(eGL), ALU.mult, 0.0, ALU.add)
                            kb.tt("pool", st_qin[:, tt, :], qF[:, tsl], T["eGr"][:], ALU.mult)
                            yield
                            yield from tri_inv_T(T["inv"], T["N"], T["NT"])
                            Xi = T["inv"]["X"]
                            pu_ = pq()
                            kb.mm(pu_, Xi, T["vb"][:])
                            kb.copy("act", st_u[:, tt, :], pu_)
                            pw_ = pq()
                            kb.mm(pw_, T["kbg"][:], Xi)
                            kb.ts("dve", st_nwT[:, tt, :], pw_, -1.0, ALU.mult)
                            yield

                        for t0_ in range(0, 8, NCH):
                            interleave([pre_tile(tt, chT[ci]) for ci, tt in enumerate(range(t0_, min(8, t0_ + NCH)))])
                        for si, (s0, Ls) in enumerate(SEQS[hf]):
                            tiles = list(range(s0 // 128, (s0 + Ls) // 128))
                            if d == 1:
                                tiles = tiles[::-1]
                            if hf == 0:
                                kb.memset("dve", S[:], 0.0)
                            else:
                                kb.dma("sp", S[:], sgdn_in[d, h, :, :])
                            kb.copy("dve", Sb[:], S[:])
                            for tt in tiles:
                                tsl = slice(tt * 128, (tt + 1) * 128)
                                pv = pq()
                                kb.mm(pv, st_nwT[:, tt, :], Sb[:])
                                kb.tt("dve", t_vnew[:], pv, st_u[:, tt, :], ALU.add)
                                po_ = pq()
                                kb.mm(po_, Sb[:], st_qin[:, tt, :], start=True, stop=False)
                                kb.mm(po_, t_vnew[:], st_qkT[:, tt, :], start=False, stop=True)
                                if d == 0:
                                    kb.copy("act", oacc[:, tsl], po_)
                                else:
                                    kb.tt("dve", oacc[:, tsl], po_, oacc[:, tsl], ALU.add)
                                ps_ = pq()
                                kb.mm(ps_, st_kout[:, tt, :], t_vnew[:])
                                kb.stt(S[:], S[:], eGlast[:, d, tt, h:h + 1], ps_, ALU.mult, ALU.add)
                                kb.copy("act", Sb[:], S[:])
                            if hf == 0:
                                kb.dma("sp", sgdn_out[si, d, h, :, :], S[:])
                    for nt in range(2):
                        s = sq[nt]
                        kb.act(s[:], oacc[:, nt * 512:(nt + 1) * 512], AF.Square)
                        pst = pbank[2 + nt]
                        kb.mm(pst[:], ones[:], s[:])
                        kb.act(rn[:, nt * 512:(nt + 1) * 512], pst[:], AF.Sqrt, bias=EPS, scale=1.0 / 128)
                    kb.recip(rn[:], rn[:])
                    kb.stt(oacc[:], oacc[:], P("gnorm"), rn[:], ALU.mult, ALU.mult)
                    kb.tt("dve", mch[:], oacc[:], zF[:], ALU.mult)
                    wo_apply(4 + h, mch)
                kb.barrier()

        def rwkv(hf, Hb, wo_apply):
            with contextlib.ExitStack() as ph:
                f = lambda name, shape, dt=F32, r=False: sbx(ph, "r_" + name, shape, dt, r)
                fr = lambda name, shape: f(name, shape, F32, True)
                fh = lambda name, shape: f(name, shape, FP16)
                wch = [f("wch%d" % i, [128, 8, 128], BF16) for i in range(2)]
                wup, aup, gup = f("wup", [128, 512], BF16), f("aup", [128, 512], BF16), f("gup", [128, 512], BF16)
                twd, tad, sgd = f("twd", [128, HALF], BF16), f("tad", [128, HALF], BF16), f("sgd", [128, HALF], BF16)
                T1 = f("T1", [128, HALF])
                rF, kF0, vF, kkF = f("rF", [128, HALF]), f("kF0", [128, HALF]), f("vF", [128, HALF]), f("kkF", [128, HALF])
                asum, yacc = f("asum", [128, HALF]), f("yacc", [128, HALF])
                lw, G, kd, bF = f("lw", [128, HALF]), f("G", [128, HALF]), f("kd", [128, HALF]), f("bF", [128, HALF])
                mch = f("mch", [128, HALF], BF16)
                omm, hmu, omka = f("omm", [128, 15]), f("hmu", [128, 15]), f("omka", [128, 4])
                Mp = f("Mp", [128, 128])
                Mpb = f("Mpb", [128, 128], BF16)
                pC = f("pC", [128, 1])
                nm = ["eX", "Rt", "Ct", "kinv", "binv", "khat", "bhat", "Z"]
                bset = {"Rt", "Ct", "kinv", "binv", "Z"}
                nm = nm + ["eX2"]
                tlp = [{n: f("t%d_%s" % (par, n), [128, 128], BF16 if n in bset else F32) for n in nm} for par in range(2)]
                pCp = [f("pC%d" % par, [128, 1]) for par in range(2)]
                hdp = [[{n: f("h%d%d_%s" % (par, h, n), [128, 128], BF16) for n in ["X", "BmT", "QKT", "QBT", "vT", "khT", "bhT", "nU"]} for h in range(2)] for par in range(2)]
                hT = []
                for h_ in range(2):
                    T = {n: fh("p%d_%s" % (h_, n), [128, 128]) for n in ["N", "NT"]}
                    T.update({n: f("p%d_%s" % (h_, n), [128, 128], BF16) for n in ["Ctm", "Rtm"]})
                    T["msk"] = [f("p%d_msk%d" % (h_, i), [128, 128]) for i in range(3)]
                    T["inv"] = inv_tile_set(fh, "p%d_iv" % h_)
                    hT.append(T)
                hmask = P("hmask")
                mu = P("rmu")
                kb.ts("dve", omm[:], mu, -1.0, ALU.mult, 1.0, ALU.add)
                kb.ts("dve", hmu[:], mu, 0.5, ALU.mult)
                kb.ts("dve", omka[:], P("rka"), -1.0, ALU.mult, 1.0, ALU.add)
                for d in range(2):
                    kb.dma("pool", wup[d * 64:(d + 1) * 64, :], rwkv_w_up[0, d, :, :])
                    kb.dma("pool", aup[d * 64:(d + 1) * 64, :], rwkv_a_up[0, d, :, :])
                kb.dma("pool", gup[:], rwkv_g_up[0, :, :])
                wi = [0]

                def project_shift(chunk, dst, func=None):
                    wt = wch[wi[0] % 2]
                    wi[0] += 1
                    kb.dma("pool", wt[:], ab_w_in[0, :, chunk * 128:(chunk + 1) * 128].rearrange("(k p) c -> p k c", p=128))
                    for nt in range(2):
                        pp = pbank[nt]
                        for k in range(8):
                            kb.mm(pp[:], wt[:, k, :], Hb[:, k, nt * 512:(nt + 1) * 512], start=(k == 0), stop=(k == 7))
                        kb.copy("act", T1[:, nt * 512:(nt + 1) * 512], pp[:])
                    tgt = dst if func is None else lw
                    kb.ts("dve", tgt[:], T1[:], omm[:, chunk:chunk + 1], ALU.mult)
                    for (s0, Ls) in SEQS[hf]:
                        kb.stt(tgt[:, s0 + 1:s0 + Ls], T1[:, s0:s0 + Ls - 1], hmu[:, chunk:chunk + 1], tgt[:, s0 + 1:s0 + Ls], ALU.mult, ALU.add)
                        kb.stt(tgt[:, s0:s0 + Ls - 1], T1[:, s0 + 1:s0 + Ls], hmu[:, chunk:chunk + 1], tgt[:, s0:s0 + Ls - 1], ALU.mult, ALU.add)
                    if func is not None:
                        kb.act(dst[:], tgt[:], func)

                project_shift(12, twd, AF.Tanh)
                project_shift(13, tad, AF.Identity)
                project_shift(14, sgd, AF.Sigmoid)
                bo = C("bo")
                for c in range(4):
                    project_shift(c, rF)
                    project_shift(4 + c, kF0)
                    project_shift(8 + c, vF)
                    kb.ts("dve", kkF[:], kF0[:], P("rkk")[:, c:c + 1], ALU.mult)
                    for nt in range(2):
                        sl = slice(nt * 512, (nt + 1) * 512)
                        kb.act(sq[nt][:], kkF[:, sl], AF.Square)
                        pst = pbank[2 + nt]
                        kb.mm(pst[:], bo, sq[nt][:])
                        kb.act(T1[:, sl], pst[:], AF.Sqrt, bias=1e-6, scale=1.0)
                    kb.recip(T1[:], T1[:])
                    kb.tt("dve", kkF[:], kkF[:], T1[:], ALU.mult)
                    for d in range(2):
                        m_ts = C("gt") if d == 0 else C("lt")
                        m_st = C("lt") if d == 0 else C("gt")
                        m_in = C("le") if d == 0 else C("ge")
                        rows = slice(d * 64, (d + 1) * 64)
                        for nt in range(2):
                            sl = slice(nt * 512, (nt + 1) * 512)
                            pw = pbank[nt]
                            kb.mm(pw[:], wup[rows, c * 128:(c + 1) * 128], twd[rows, sl])
                            kb.act(lw[:, sl], pw[:], AF.Sigmoid, bias=P("rw0").rearrange("p (d c) -> p d c", d=2)[:, d, c:c + 1])
                            pa = pbank[2 + nt]
                            kb.mm(pa[:], aup[rows, c * 128:(c + 1) * 128], tad[rows, sl])
                            kb.act(T1[:, sl], pa[:], AF.Sigmoid, bias=P("ra0").rearrange("p (d c) -> p d c", d=2)[:, d, c:c + 1])
                        kb.ts("dve", lw[:], lw[:], -0.6065306597126334, ALU.mult)
                        if d == 0:
                            kb.copy("dve", asum[:], T1[:])
                        else:
                            kb.tt("dve", asum[:], asum[:], T1[:], ALU.add)
                        kb.tt("dve", bF[:], kkF[:], T1[:], ALU.mult)
                        kb.ts("dve", T1[:], T1[:], P("rka")[:, c:c + 1], ALU.mult, omka[:, c:c + 1], ALU.add)
                        kb.tt("dve", kd[:], kF0[:], T1[:], ALU.mult)
                        nc_ = nc
                        kb.emit("dve", lambda: nc_.vector.tensor_tensor_scan(out=G[:], data0=P("rmask"), data1=lw[:], initial=0.0, op0=ALU.mult, op1=ALU.add),
                                [G[:]], [P("rmask"), lw[:]])
                        if d == 1:
                            for tt in range(8):
                                tsl = slice(tt * 128, (tt + 1) * 128)
                                kb.stt(T1[:, tsl], G[:, tsl], -1.0, lw[:, tsl], ALU.mult, ALU.add)
                                kb.ts("dve", T1[:, tsl], T1[:, tsl], G[:, tt * 128 + 127:tt * 128 + 128], ALU.add)
                            kb.copy("dve", G[:], T1[:])
                        kb.tt("dve", lw[:], G[:], lw[:], ALU.subtract)
                        for par in range(2):
                            for h in range(2):
                                kb.memset("dve", hdp[par][h]["nU"][:], 0.0)
                        order = []
                        for si, (s0, Ls) in enumerate(SEQS[hf]):
                            tiles = list(range(s0 // 128, (s0 + Ls) // 128))
                            if d == 1:
                                tiles = tiles[::-1]
                            for k_, tt in enumerate(tiles):
                                order.append((si, tt, k_ == 0, k_ == len(tiles) - 1))

                        def prep_gen(idx, d=d, c=c, m_ts=m_ts, m_st=m_st, m_in=m_in):
                            si, tt, first, last = order[idx]
                            tl = tlp[idx % 2]
                            hd = hdp[idx % 2]
                            pCt = pCp[idx % 2]
                            tsl = slice(tt * 128, (tt + 1) * 128)
                            e_end = tt * 128 + (127 if d == 0 else 0)
                            gtot = G[:, e_end:e_end + 1]
                            kb.act(tl["eX"][:], G[:, tsl], AF.Exp)
                            kb.tt("dve", tl["Rt"][:], rF[:, tsl], tl["eX"][:], ALU.mult)
                            kb.act(tl["eX2"][:], lw[:, tsl], AF.Exp)
                            kb.tt("dve", tl["Ct"][:], kkF[:, tsl], tl["eX2"][:], ALU.mult)
                            yield
                            kb.act(tl["eX"][:], G[:, tsl], AF.Exp, scale=-1.0)
                            kb.tt("dve", tl["kinv"][:], kd[:, tsl], tl["eX"][:], ALU.mult)
                            kb.tt("dve", tl["binv"][:], bF[:, tsl], tl["eX"][:], ALU.mult)
                            kb.act(tl["eX2"][:], G[:, tsl], AF.Exp, bias=gtot, scale=-1.0)
                            kb.tt("pool", tl["khat"][:], kd[:, tsl], tl["eX2"][:], ALU.mult)
                            kb.tt("pool", tl["bhat"][:], bF[:, tsl], tl["eX2"][:], ALU.mult)
                            kb.act(pCt[:], gtot, AF.Exp)
                            yield

                            def pre_head(h):
                                H = hd[h]
                                T = hT[h]
                                hm = hmask[:, h:h + 1]
                                kb.ts("dve", T["Ctm"][:], tl["Ct"][:], hm, ALU.mult)
                                kb.ts("dve", T["Rtm"][:], tl["Rt"][:], hm, ALU.mult)
                                p_ = pq()
                                kb.mm(p_, T["Ctm"][:], tl["binv"][:])
                                kb.stt(T["N"][:], p_, -1.0, m_ts, ALU.mult, ALU.mult)
                                p_ = pq()
                                kb.mm(p_, tl["binv"][:], T["Ctm"][:])
                                kb.stt(T["NT"][:], p_, -1.0, m_st, ALU.mult, ALU.mult)
                                yield
                                p_ = pq()
                                kb.mm(p_, tl["kinv"][:], T["Ctm"][:])
                                kb.tt("dve", H["BmT"][:], p_, m_st, ALU.mult)
                                p_ = pq()
                                kb.mm(p_, tl["kinv"][:], T["Rtm"][:])
                                kb.tt("dve", H["QKT"][:], p_, m_in, ALU.mult)
                                p_ = pq()
                                kb.mm(p_, tl["binv"][:], T["Rtm"][:])
                                kb.tt("dve", H["QBT"][:], p_, m_in, ALU.mult)
                                yield
                                for mi, (src, dstn) in enumerate([(vF[:, tsl], "vT"), (tl["khat"][:], "khT"), (tl["bhat"][:], "bhT")]):
                                    kb.ts("pool", T["msk"][mi][:], src, hm, ALU.mult, 0.0, ALU.add)
                                    p_ = pq()
                                    kb.tr(p_, T["msk"][mi][:], ident)
                                    kb.copy("act", H[dstn][:], p_)
                                yield
                                yield from tri_inv_T(T["inv"], T["N"], T["NT"])
                                kb.copy("act", H["X"][:], T["inv"]["X"])
                                yield

                            gens = [pre_head(0), pre_head(1)]
                            while gens:
                                for g_ in list(gens):
                                    try:
                                        next(g_)
                                    except StopIteration:
                                        gens.remove(g_)
                                yield

                        def seq_gen(idx, d=d, c=c):
                            si, tt, first, last = order[idx]
                            tl = tlp[idx % 2]
                            hd = hdp[idx % 2]
                            pCt = pCp[idx % 2]
                            tsl = slice(tt * 128, (tt + 1) * 128)
                            if first:
                                kb.memset("dve", Mp[:], 0.0)
                                if hf == 1:
                                    for h in range(2):
                                        kb.dma("sp", Mp[h * 64:(h + 1) * 64, h * 64:(h + 1) * 64], srwkv_in[d, 2 * c + h, :, :])
                                kb.copy("dve", Mpb[:], Mp[:])
                            pz = pq()
                            kb.mm(pz, tl["Ct"][:], Mpb[:], start=True, stop=False)
                            kb.mm(pz, hd[0]["BmT"][:], hd[0]["vT"][:], start=False, stop=False)
                            kb.mm(pz, hd[1]["BmT"][:], hd[1]["vT"][:], start=False, stop=True)
                            kb.copy("act", tl["Z"][:], pz)
                            yield
                            for h in range(2):
                                p_ = pq()
                                kb.mm(p_, hd[h]["X"][:], tl["Z"][:])
                                kb.ts("dve", hd[h]["nU"][:, h * 64:(h + 1) * 64], p_[:, h * 64:(h + 1) * 64], -1.0, ALU.mult)
                            yield
                            py = pq()
                            kb.mm(py, Mpb[:], tl["Rt"][:], start=True, stop=False)
                            for h in range(2):
                                kb.mm(py, hd[h]["vT"][:], hd[h]["QKT"][:], start=False, stop=False)
                                kb.mm(py, hd[h]["nU"][:], hd[h]["QBT"][:], start=False, stop=(h == 1))
                            if d == 0:
                                kb.copy("act", yacc[:, tsl], py)
                            else:
                                kb.tt("dve", yacc[:, tsl], py, yacc[:, tsl], ALU.add)
                            pm_ = pq()
                            for h in range(2):
                                kb.mm(pm_, hd[h]["khT"][:], hd[h]["vT"][:], start=(h == 0), stop=False)
                                kb.mm(pm_, hd[h]["bhT"][:], hd[h]["nU"][:], start=False, stop=(h == 1))
                            kb.stt(Mp[:], Mp[:], pCt[:, 0:1], pm_, ALU.mult, ALU.add)
                            kb.copy("act", Mpb[:], Mp[:])
                            if last and hf == 0:
                                for h in range(2):
                                    kb.dma("sp", srwkv_out[si, d, 2 * c + h, :, :], Mp[h * 64:(h + 1) * 64, h * 64:(h + 1) * 64])
                            yield

                        interleave([prep_gen(0)])
                        for idx in range(len(order)):
                            gl = [seq_gen(idx)]
                            if idx + 1 < len(order):
                                gl.append(prep_gen(idx + 1))
                            interleave(gl)
                    for nt in range(2):
                        sl = slice(nt * 512, (nt + 1) * 512)
                        pst = pbank[nt]
                        kb.mm(pst[:], bo, yacc[:, sl])
                        kb.stt(yacc[:, sl], pst[:], -1.0 / 64, yacc[:, sl], ALU.mult, ALU.add)
                        kb.act(sq[nt][:], yacc[:, sl], AF.Square)
                        pv = pbank[2 + nt]
                        kb.mm(pv[:], bo, sq[nt][:])
                        kb.act(T1[:, sl], pv[:], AF.Sqrt, bias=64e-5, scale=1.0 / 64)
                    kb.recip(T1[:], T1[:])
                    kb.stt(yacc[:], yacc[:], P("rlnw")[:, c:c + 1], T1[:], ALU.mult, ALU.mult)
                    kb.ts("dve", yacc[:], yacc[:], P("rlnb")[:, c:c + 1], ALU.add)
                    kb.ts("dve", asum[:], asum[:], 0.5, ALU.mult)
                    kb.ts("dve", asum[:], asum[:], P("rka")[:, c:c + 1], ALU.mult, omka[:, c:c + 1], ALU.add)
                    kb.tt("dve", asum[:], asum[:], kF0[:], ALU.mult)
                    kb.stt(asum[:], asum[:], P("rrk")[:, c:c + 1], rF[:], ALU.mult, ALU.mult)
                    for nt in range(2):
                        sl = slice(nt * 512, (nt + 1) * 512)
                        pb_ = pbank[nt]
                        kb.mm(pb_[:], bo, asum[:, sl])
                        kb.tt("dve", T1[:, sl], pb_[:], vF[:, sl], ALU.mult)
                        kb.tt("dve", yacc[:, sl], yacc[:, sl], T1[:, sl], ALU.add)
                        pg_ = pbank[2 + nt]
                        kb.mm(pg_[:], gup[:, c * 128:(c + 1) * 128], sgd[:, sl])
                        kb.tt("dve", mch[:, sl], pg_[:], yacc[:, sl], ALU.mult)
                    wo_apply(c, mch)
                kb.barrier()

        def mla(hf, Hb, wo_apply):
            NK = 1024 if hf == 0 else 1280
            with contextlib.ExitStack() as ph:
                f = lambda name, shape, dt=F32: sbx(ph, "m_" + name, shape, dt)
                wch = [f("wch%d" % i, [128, 8, 128], BF16) for i in range(2)]
                wuq = f("wuq", [128, 3, 768], BF16)
                wukv = f("wukv", [128, 2, 1024], BF16)
                pqF = f("pqF", [128, 3, HALF])
                qnF = f("qnF", [128, 3, HALF], BF16)
                ckvF = f("ckvF", [128, 2, HALF])
                ckvB = f("ckvB", [128, 2, 1280], BF16)
                kpe96 = f("kpe96", [128, 1280])
                kpeB = f("kpeB", [128, 1280], BF16)
                rq = f("rq", [128, HALF])
                Ve = f("Ve", [128, 10, 128], BF16)
                Vo = f("Vo", [128, 10, 128], BF16)
                qTf = f("qTf", [128, HALF])
                qTb = [f("qTb%d" % e, [128, HALF], BF16) for e in range(2)]
                KTb = [f("KTb%d" % e, [128, 1280], BF16) for e in range(2)]
                PT = [f("PT%d" % i, [128, 512], BF16) for i in range(2)]
                oE = f("oE", [128, 128], BF16)
                oO = f("oO", [128, 128], BF16)
                rs_ = f("rs", [128, 512])
                mch = f("mch", [128, HALF], BF16)
                t1, t2 = f("t1", [128, 512]), f("t2", [128, 512])
                cosT, sinT = f("cosT", [128, HALF]), f("sinT", [128, HALF])
                kb.copy("dve", oE[:], C("onesE"))
                kb.copy("dve", oO[:], C("onesO"))
                kb.memset("dve", kpe96[:], 0.0)
                kb.memset("dve", qTf[:], 0.0)
                kb.dma("pool", wuq[:], mla_w_uq[0].rearrange("(k p) c -> p k c", p=128))
                kb.dma("pool", wukv[:], mla_w_ukv[0].rearrange("(k p) c -> p k c", p=128))
                if hf == 1:
                    kb.dma("sp", cosT[:], ropecs[0, :, :])
                    kb.dma("sp", sinT[:], ropecs[1, :, :])
                wi = [0]

                def project(c0, M, dst_fn):
                    wt = wch[wi[0] % 2]
                    wi[0] += 1
                    kb.dma("pool", wt[:, :, 0:M], cd_w_in[0, :, c0:c0 + M].rearrange("(k p) c -> p k c", p=128))
                    for nt in range(2):
                        pp = pbank[4 + nt]
                        for k in range(8):
                            kb.mm(pp[0:M, :], wt[:, k, 0:M], Hb[:, k, nt * 512:(nt + 1) * 512], start=(k == 0), stop=(k == 7))
                        dst_fn(nt, pp)

                for j in range(3):
                    project(j * 128, 128, lambda nt, pp, j=j: kb.copy("act", pqF[:, j, nt * 512:(nt + 1) * 512], pp[:]))
                for j in range(2):
                    project(384 + j * 128, 128, lambda nt, pp, j=j: kb.copy("act", ckvF[:, j, nt * 512:(nt + 1) * 512], pp[:]))
                project(576, 96, lambda nt, pp: kb.copy("act", kpe96[64:96, nt * 512:(nt + 1) * 512], pp[64:96, :]))

                def rmsn(src, nj, gname, outs):
                    for nt in range(2):
                        sl = slice(nt * 512, (nt + 1) * 512)
                        pst = pbank[4 + nt]
                        for j in range(nj):
                            kb.act(sq[j % 2][:], src[:, j, sl], AF.Square)
                            kb.mm(pst[:], ones[:], sq[j % 2][:], start=(j == 0), stop=(j == nj - 1))
                        kb.act(rq[:, sl], pst[:], AF.Sqrt, bias=EPS, scale=1.0 / (nj * 128))
                    kb.recip(rq[:], rq[:])
                    for j in range(nj):
                        for o in outs:
                            kb.stt(o(j), src[:, j, :], P(gname)[:, j:j + 1], rq[:], ALU.mult, ALU.mult)

                rmsn(pqF, 3, "mqn", [lambda j: qnF[:, j, :]])
                rmsn(ckvF, 2, "mkvn", [lambda j: ckvB[:, j, 0:HALF], lambda j: ckvF[:, j, :]])
                if hf == 0:
                    for j in range(2):
                        kb.dma("sp", ckv_out[j * 128:(j + 1) * 128, :], ckvF[:, j, :])
                    kb.dma("sp", kpe_out[:, :], kpe96[64:96, 0:HALF])
                    kb.copy("dve", kpeB[64:96, 0:HALF], kpe96[64:96, 0:HALF])
                else:
                    for j in range(2):
                        kb.dma("pool", ckvB[:, j, HALF:1280], cckvT[j * 128:(j + 1) * 128, :])
                    kb.dma("sp", kpe96[64:96, HALF:1280], ckpeT[:, :])
                    kb.copy("dve", kpeB[64:96, HALF:1280], kpe96[64:96, HALF:1280])

                if stage == 1:
                    kb.barrier()
                    return

                def rope(src, dstb):
                    for nt in range(2):
                        sl = slice(nt * 512, (nt + 1) * 512)
                        pj = pbank[4 + nt]
                        kb.mm(pj[0:96, :], C("jpad")[0:96, 0:96], src[0:96, sl])
                        kb.tt("dve", t1[64:96, :], src[64:96, sl], cosT[64:96, sl], ALU.mult)
                        kb.tt("dve", t2[64:96, :], pj[64:96, :], sinT[64:96, sl], ALU.mult)
                        kb.tt("dve", dstb[64:96, sl], t1[64:96, :], t2[64:96, :], ALU.add)

                if hf == 1:
                    rope(kpe96, kpeB)
                if stage == 2:
                    kb.barrier()
                    return
                kb.memset("dve", Ve[:].rearrange("p a b -> p (a b)"), 0.0)
                kb.memset("dve", Vo[:].rearrange("p a b -> p (a b)"), 0.0)
                scale = 96.0 ** -0.5
                if hf == 0:
                    qranges = [(si * 256, 256, [2 * si, 2 * si + 1]) for si in range(4)]
                else:
                    qranges = [(nt * 512, 512, list(range(10))) for nt in range(2)]
                pti = [0]
                for c in range(4):
                    for kt in range(NK // 128):
                        pv = pbank[4 + kt % 2]
                        for e in range(2):
                            v0 = (2 * c + e) * 128 + 64
                            for k in range(2):
                                kb.mm(pv[:, e * 64:(e + 1) * 64], ckvB[:, k, kt * 128:(kt + 1) * 128], wukv[:, k, v0:v0 + 64], start=(k == 0), stop=(k == 1))
                        kb.copy("act", Ve[:, kt, 0:64], pv[:, 0:64])
                        kb.copy("dve", Vo[:, kt, 64:128], pv[:, 64:128])
                    if stage == 31:
                        continue
                    for e in range(2):
                        h = 2 * c + e
                        for nt in range(2):
                            sl = slice(nt * 512, (nt + 1) * 512)
                            pqh = pbank[4 + nt]
                            for k in range(3):
                                kb.mm(pqh[0:96, :], wuq[:, k, h * 96:(h + 1) * 96], qnF[:, k, sl], start=(k == 0), stop=(k == 2))
                            kb.copy("act", qTb[e][0:64, sl], pqh[0:64, :])
                            if hf == 1:
                                kb.copy("dve", qTf[64:96, sl], pqh[64:96, :])
                            else:
                                kb.copy("dve", qTb[e][64:96, sl], pqh[64:96, :])
                        if hf == 1:
                            rope(qTf, qTb[e])
                        if stage == 32:
                            continue
                        for k0 in range(0, NK, 512):
                            n = min(512, NK - k0)
                            pk = pbank[4 + (k0 // 512) % 2]
                            for k in range(2):
                                kb.mm(pk[:, 0:n], wukv[:, k, h * 128:(h + 1) * 128], ckvB[:, k, k0:k0 + n], start=(k == 0), stop=(k == 1))
                            kb.copy("act", KTb[e][0:64, k0:k0 + n], pk[0:64, 0:n])
                        kb.copy("dve", KTb[e][64:96, 0:NK], kpeB[64:96, 0:NK])
                    if stage == 3:
                        continue
                    for (q0, nq, kts) in qranges:
                        po, psm = pbank[2], pbank[3]
                        first = True
                        for e in range(2):
                            Vsrc = Ve if e == 0 else Vo
                            osrc = oE if e == 0 else oO
                            for ki, kt in enumerate(kts):
                                ps_ = pbank[pti[0] % 2]
                                pt = PT[pti[0] % 2]
                                pti[0] += 1
                                ksl = slice(kt * 128, (kt + 1) * 128)
                                kb.mm(ps_[:, 0:nq], KTb[e][0:96, ksl], qTb[e][0:96, q0:q0 + nq])
                                kb.act(pt[:, 0:nq], ps_[:, 0:nq], AF.Exp, scale=scale)
                                last = (e == 1 and ki == len(kts) - 1)
                                kb.mm(po[:, 0:nq], Vsrc[:, kt, :], pt[:, 0:nq], start=first, stop=last)
                                kb.mm(psm[:, 0:nq], osrc[:], pt[:, 0:nq], start=first, stop=last)
                                first = False
                        kb.recip(rs_[:, 0:nq], psm[:, 0:nq])
                        kb.tt("dve", mch[:, q0:q0 + nq], po[:, 0:nq], rs_[:, 0:nq], ALU.mult)
                    wo_apply(c, mch)
                kb.barrier()

        def hyena(hf, Hb, wo_apply):
            Ls = 256 if hf == 0 else 1024
            nT = Ls // 128
            with contextlib.ExitStack() as ph:
                f = lambda name, shape, dt=F32: sbx(ph, "y_" + name, shape, dt)
                wch = [f("wch%d" % i, [128, 8, 128], BF16) for i in range(1)]
                vF, x1F, x2F, T1 = f("vF", [128, HALF]), f("x1F", [128, HALF]), f("x2F", [128, HALF]), f("T1", [128, HALF])
                fwd = f("fwd", [128, nT, 2, Ls], BF16)
                invp = [f("invp%d" % i, [128, 2, Ls], BF16) for i in range(2)]
                w3 = f("w3", [128, 2048], BF16)
                featT = x1F[:, 0:Ls]
                h1s, h2s = T1[:, 0:Ls], x2F[:, 0:Ls]
                h2b = f("h2b", [128, Ls], BF16)
                win = f("win", [128, nT, 128])
                fr3, fb3 = f("fr3", [128, 2]), f("fb3", [128, 2])
                hsum, hdif = f("hsum", [128, nT, 128], BF16), f("hdif", [128, nT, 128], BF16)
                Hc, Hs = f("Hc", [128, nT, 128]), f("Hs", [128, nT, 128])
                zT = f("zT", [128, nT, 128], BF16)
                Yc, Ys = f("Yc", [128, nT, 128], BF16), f("Ys", [128, nT, 128], BF16)
                ta, tb, tc_ = f("ta", [128, 512]), f("tb", [128, 512]), f("tc", [128, 512])
                mch = f("mch", [128, HALF], BF16)
                for i in range(2):
                    kb.dma("sp", fwd[:, :, i, :], dftd[Ls][i].rearrange("(st p) f -> p st f", p=128))
                kb.dma("pool", w3[0:64, :], hy_w3[0, :, :])
                kb.dma("sp", featT, featd[Ls][:, :])
                kb.ts("dve", fr3[0:64, :], P("hfreq")[0:64, :], 1.0 / 3, ALU.mult)
                kb.tt("dve", fb3[0:64, 0:1], fr3[0:64, 0:1], P("hb1")[0:64, :], ALU.mult)
                kb.tt("dve", fb3[0:64, 1:2], fr3[0:64, 1:2], P("hb2")[0:64, :], ALU.mult)

                def sin3(dst, pin, li, n):
                    kb.act(ta[0:64, 0:n], pin, AF.Sin, bias=fb3[0:64, li:li + 1], scale=fr3[0:64, li:li + 1])
                    kb.tt("dve", tb[0:64, 0:n], ta[0:64, 0:n], ta[0:64, 0:n], ALU.mult)
                    kb.ts("dve", tb[0:64, 0:n], tb[0:64, 0:n], -4.0, ALU.mult, 3.0, ALU.add)
                    kb.tt("dve", dst, ta[0:64, 0:n], tb[0:64, 0:n], ALU.mult)

                for c0 in range(0, Ls, 512):
                    n = min(512, Ls - c0)
                    p1 = pbank[4]
                    kb.mm(p1[0:64, 0:n], P("hw1")[0:64, :], featT[0:64, c0:c0 + n])
                    sin3(h1s[0:64, c0:c0 + n], p1[0:64, 0:n], 0, n)
                    p2 = pbank[5]
                    kb.mm(p2[0:64, 0:n], P("hw2")[0:64, :], h1s[0:64, c0:c0 + n])
                    sin3(h2s[0:64, c0:c0 + n], p2[0:64, 0:n], 1, n)
                kb.copy("dve", h2b[0:64, :], h2s[0:64, :])
                wi = [0]
                hcv = P("hconv").rearrange("p (t c) -> p t c", t=3)
                ipi = [0]
                for cc in range(4):
                    kb.dma("sp", win[:], wind[Ls][:, cc * 128:(cc + 1) * 128].rearrange("(t p) c -> p t c", p=128))
                    for qi, dst in enumerate([vF, x1F, x2F]):
                        wt = wch[0]
                        wi[0] += 1
                        c0 = 672 + qi * 512 + cc * 128
                        kb.dma("pool", wt[:], cd_w_in[0, :, c0:c0 + 128].rearrange("(k p) c -> p k c", p=128))
                        for nt in range(2):
                            pp = pbank[4 + nt]
                            for k in range(8):
                                kb.mm(pp[:], wt[:, k, :], Hb[:, k, nt * 512:(nt + 1) * 512], start=(k == 0), stop=(k == 7))
                            kb.copy("act", T1[:, nt * 512:(nt + 1) * 512], pp[:])
                        ci = qi * 4 + cc
                        kb.ts("dve", dst[:], T1[:], hcv[:, 1, ci:ci + 1], ALU.mult)
                        for (s0, Lq) in SEQS[hf]:
                            kb.stt(dst[:, s0 + 1:s0 + Lq], T1[:, s0:s0 + Lq - 1], hcv[:, 0, ci:ci + 1], dst[:, s0 + 1:s0 + Lq], ALU.mult, ALU.add)
                            kb.stt(dst[:, s0:s0 + Lq - 1], T1[:, s0 + 1:s0 + Lq], hcv[:, 2, ci:ci + 1], dst[:, s0:s0 + Lq - 1], ALU.mult, ALU.add)
                    z = vF
                    for n in range(2):
                        gate = x1F if n == 0 else x2F
                        bcol = P("hbias").rearrange("p (n c) -> p n c", n=2)[:, n, cc:cc + 1]
                        for dt in range(nT):
                            pt_ = pbank[4 + dt % 2]
                            for di in range(2):
                                w0 = (n * 2 + di) * 512 + cc * 128
                                kb.mm(pt_[:, di * 128:(di + 1) * 128], h2b[0:64, dt * 128:(dt + 1) * 128], w3[0:64, w0:w0 + 128])
                            kb.tt("dve", ta[:, 0:128], pt_[:, 0:128], win[:, dt, :], ALU.mult)
                            kb.tt("dve", tb[:, 0:128], pt_[:, 128:256], win[:, dt, :], ALU.mult)
                            if dt == 0:
                                kb.ts("dve", tb[:, 0:128], tb[:, 0:128], P("nz0"), ALU.mult)
                            kb.tt("dve", hsum[:, dt, :], ta[:, 0:128], tb[:, 0:128], ALU.add)
                            kb.tt("dve", hdif[:, dt, :], ta[:, 0:128], tb[:, 0:128], ALU.subtract)
                        for ft in range(nT):
                            pc_ = pbank[4 + ft % 2]
                            for dt in range(nT):
                                kb.mm(pc_[:, 0:128], fwd[:, dt, 0, ft * 128:(ft + 1) * 128], hsum[:, dt, :], start=(dt == 0), stop=(dt == nT - 1))
                            kb.copy("act", Hc[:, ft, :], pc_[:, 0:128])
                            for dt in range(nT):
                                kb.mm(pc_[:, 128:256], fwd[:, dt, 1, ft * 128:(ft + 1) * 128], hdif[:, dt, :], start=(dt == 0), stop=(dt == nT - 1))
                            kb.copy("act", Hs[:, ft, :], pc_[:, 128:256])
                        for (s0, Lq) in SEQS[hf]:
                            for st in range(nT):
                                ptr = pbank[4 + st % 2]
                                kb.tr(ptr[:, 0:128], z[:, s0 + st * 128:s0 + (st + 1) * 128], ident)
                                kb.copy("act", zT[:, st, :], ptr[:, 0:128])
                            for ft in range(nT):
                                pzc, pzs = pbank[2], pbank[3]
                                for st in range(nT):
                                    kb.mm(pzc[:, 0:128], fwd[:, st, 0, ft * 128:(ft + 1) * 128], zT[:, st, :], start=(st == 0), stop=(st == nT - 1))
                                for st in range(nT):
                                    kb.mm(pzs[:, 0:128], fwd[:, st, 1, ft * 128:(ft + 1) * 128], zT[:, st, :], start=(st == 0), stop=(st == nT - 1))
                                kb.tt("dve", ta[:, 0:128], pzc[:, 0:128], Hc[:, ft, :], ALU.mult)
                                kb.tt("dve", tb[:, 0:128], pzs[:, 0:128], Hs[:, ft, :], ALU.mult)
                                kb.tt("dve", Yc[:, ft, :], ta[:, 0:128], tb[:, 0:128], ALU.subtract)
                                kb.tt("dve", ta[:, 0:128], pzc[:, 0:128], Hs[:, ft, :], ALU.mult)
                                kb.tt("dve", tb[:, 0:128], pzs[:, 0:128], Hc[:, ft, :], ALU.mult)
                                kb.tt("dve", Ys[:, ft, :], ta[:, 0:128], tb[:, 0:128], ALU.add)
                            blocks = [(b0, min(512, Lq - b0)) for b0 in range(0, Lq, 512)]
                            pys = [pbank[0], pbank[1]]
                            for ft in range(nT):
                                ip = invp[ipi[0] % 2]
                                ipi[0] += 1
                                for i in range(2):
                                    kb.dma("sp", ip[:, i, :], dftd[Ls][2 + i, ft * 128:(ft + 1) * 128, :])
                                for bi, (b0, bn) in enumerate(blocks):
                                    kb.mm(pys[bi][:, 0:bn], Yc[:, ft, :], ip[:, 0, b0:b0 + bn], start=(ft == 0), stop=False)
                                    kb.mm(pys[bi][:, 0:bn], Ys[:, ft, :], ip[:, 1, b0:b0 + bn], start=False, stop=(ft == nT - 1))
                            for bi, (b0, bn) in enumerate(blocks):
                                zs = z[:, s0 + b0:s0 + b0 + bn]
                                kb.ts("dve", tc_[:, 0:bn], zs, bcol, ALU.mult)
                                kb.stt(tc_[:, 0:bn], pys[bi][:, 0:bn], 1.0 / Lq, tc_[:, 0:bn], ALU.mult, ALU.add)
                                kb.tt("dve", zs, tc_[:, 0:bn], gate[:, s0 + b0:s0 + b0 + bn], ALU.mult)
                    kb.copy("dve", mch[:], z[:])
                    wo_apply(4 + cc, mch)
                kb.barrier()

        def mixer_phase(L):
            with contextlib.ExitStack() as ph:
                Hb = sbx(ph, "Hbm", [128, 8, HALF], BF16)
                wo = [sbx(ph, "wo%d" % i, [128, D], BF16) for i in range(2)]
                woi = [0]
                for hf in range(2):
                    rms_stats(hf)
                    make_coefs(L, hf, 1, "nmx_%d" % L)
                    make_H(Hb, hf)

                    def wo_apply(j, mc, hf=hf):
                        w = wo[woi[0] % 2]
                        woi[0] += 1
                        kb.dma("pool", w[:], w_out_d[L, j * 128:(j + 1) * 128, :])
                        for m in range(8):
                            for nt in range(2):
                                po = pbank[6 + (m * 2 + nt) % 2]
                                kb.mm(po[:], w[:, m * 128:(m + 1) * 128], mc[:, nt * 512:(nt + 1) * 512])
                                t0 = hf * HALF + nt * 512
                                kb.stt(X[:, m, t0:t0 + 512], po[:], coef[:, 16 + m:17 + m], X[:, m, t0:t0 + 512], ALU.mult, ALU.add)

                    if L == 0:
                        if test in (None, "rwkv", "l0"):
                            rwkv(hf, Hb, wo_apply)
                        if test in (None, "gdn", "l0"):
                            gdn(hf, Hb, wo_apply)
                    else:
                        if test in (None, "mla", "l1"):
                            mla(hf, Hb, wo_apply)
                        if test in (None, "hy", "l1"):
                            hyena(hf, Hb, wo_apply)
                kb.barrier()

        def final_norm_and_store():
            for hf in range(2):
                rms_stats(hf)
                for j in range(8):
                    for nt in range(2):
                        t0 = hf * HALF + nt * 512
                        tf = tmpf[(j * 2 + nt) % 2]
                        kb.stt(tf[:], X[:, j, t0:t0 + 512], P("fnorm")[:, j:j + 1], rstd[:, nt * 512:(nt + 1) * 512], ALU.mult, ALU.mult)
                        kb.dma("sp", yT[j * 128:(j + 1) * 128, t0:t0 + 512], tf[:])

        if test in ("gdn", "rwkv", "l0"):
            mixer_phase(0)
        elif test in ("mla", "hy", "l1"):
            mixer_phase(1)
        else:
            for L in range(2):
                ffn_phase(L, 0)
                mixer_phase(L)
                ffn_phase(L, 1)
        final_norm_and_store()
        kb.barrier()
        print("instructions:", kb.ninst)
    return nc


def prep_inputs(inp):
    inp = {k: np.asarray(v) for k, v in inp.items()}
    maps = []
    offs = None
    cnames, carr = make_consts()
    ropecs = rope_tables()
    hc = {L: hyena_consts(L) for L in (256, 1024)}
    for c in range(NCORE):
        xp = inp["x_prompt"][4 * c:4 * c + 4].reshape(HALF, D)
        xs = inp["x_sample"][c]
        xT = np.ascontiguousarray(np.concatenate([xp, xs], 0).T)
        Pk = pack_params(inp, c)
        prm = Pk.build()
        offs = Pk.off
        m = {"xT": xT, "prm": prm, "cst": carr, "w_mod": inp["w_mod"],
             "ffn1_w_gu": inp["ffn1_w_gu"], "ffn2_w_gu": inp["ffn2_w_gu"],
             "ffn1_w_down": inp["ffn1_w_down"], "ffn2_w_down": inp["ffn2_w_down"],
             "w_out": inp["w_out"], "ab_w_in": inp["ab_w_in"],
             "sgdn_in": np.ascontiguousarray(inp["state_gdn"][c, 0]),
             "srwkv_in": np.ascontiguousarray(inp["state_rwkv"][c, 0].transpose(0, 1, 3, 2)),
             "rwkv_w_up": inp["rwkv_w_up"], "rwkv_a_up": inp["rwkv_a_up"], "rwkv_g_up": inp["rwkv_g_up"],
             "cd_w_in": inp["cd_w_in"], "mla_w_uq": inp["mla_w_uq"], "mla_w_ukv": inp["mla_w_ukv"],
             "cckvT": np.ascontiguousarray(inp["cache_ckv"][c, 0].T), "ckpeT": np.ascontiguousarray(inp["cache_kpe"][c, 0].T),
             "ropecs": ropecs, "hy_w1": inp["hy_w1"], "hy_w2": inp["hy_w2"], "hy_w3": inp["hy_w3"],
             "dft256": hc[256][0], "dft1024": hc[1024][0], "feat256": hc[256][1], "feat1024": hc[1024][1],
             "win256": hc[256][2], "win1024": hc[1024][2]}
        maps.append(m)
    return maps, offs


def kernel(**inputs):
    maps, offs = prep_inputs(inputs)
    NP = maps[0]["prm"].shape[1]
    nc = build_program(offs, NP)
    res = run_bass_kernel_spmd(nc, maps, core_ids=list(range(NCORE)))
    yp = np.zeros((32, 256, D), np.float32)
    ys = np.zeros((8, 1024, D), np.float32)
    srw = np.zeros((32, 1, 2, 8, 64, 64), np.float32)
    sgd = np.zeros((32, 1, 2, 4, 128, 128), np.float32)
    ckv = np.zeros((32, 1, 256, 256), np.float32)
    kpe = np.zeros((32, 1, 256, 32), np.float32)
    for c in range(NCORE):
        r = res.results[c]
        yT = r["yT"]
        yp[4 * c:4 * c + 4] = yT[:, :HALF].T.reshape(4, 256, D)
        ys[c] = yT[:, HALF:].T
        srw[4 * c:4 * c + 4, 0] = np.asarray(r["srwkv_out"]).transpose(0, 1, 2, 4, 3)
        sgd[4 * c:4 * c + 4, 0] = np.asarray(r["sgdn_out"])
        ckv[4 * c:4 * c + 4, 0] = np.asarray(r["ckv_out"]).T.reshape(4, 256, 256)
        kpe[4 * c:4 * c + 4, 0] = np.asarray(r["kpe_out"]).T.reshape(4, 256, 32)
    return yp, ys, srw, sgd, ckv, kpe
```

```python
import contextlib
import numpy as np
import concourse.bass as bass
import concourse.mybir as mybir
from concourse.bass_utils import run_bass_kernel_spmd

F32 = mybir.dt.float32
BF16 = mybir.dt.bfloat16
F32R = mybir.dt.float32r
FP16 = mybir.dt.float16
AF = mybir.ActivationFunctionType
ALU = mybir.AluOpType

NCORE = 8
D = 1024
DFF = 2816
TOK = 2048
HALF = 1024
EPS = 1e-6


class KB:
    def __init__(self, nc, es, n_dma_sems=24):
        self.nc = nc
        self.es = es
        self.eng = {"pe": nc.tensor, "act": nc.scalar, "dve": nc.vector, "pool": nc.gpsimd, "sp": nc.sync}
        self.sem = {k: es.enter_context(nc.semaphore("sem_" + k)) for k in self.eng}
        self.cnt = {k: 0 for k in self.eng}
        self.seen = {k: {} for k in self.eng}
        self.dsem = [es.enter_context(nc.semaphore("dsem%d" % i)) for i in range(n_dma_sems)]
        self.dcnt = [0] * n_dma_sems
        self.drr = {"sp": 0, "pool": 0, "act": 0}
        self.dpool = {"sp": list(range(0, n_dma_sems // 2)), "act": list(range(0, n_dma_sems // 2)),
                      "pool": list(range(n_dma_sems // 2, n_dma_sems))}
        self.recs = {}
        self.semobj = {}
        for k in self.eng:
            self.semobj[k] = self.sem[k]
        for i, s in enumerate(self.dsem):
            self.semobj[("d", i)] = s
        self.ninst = 0
        self.rt = set()

    def R(self, ap):
        if ap is not None and not isinstance(ap, (int, float)) and ap.tensor.name in self.rt:
            return ap.bitcast(F32R)
        return ap

    @staticmethod
    def box(ap):
        t = ap.tensor
        dims = list(ap.ap)
        if type(t).__name__.startswith("DRam"):
            f0 = ap.offset
            f1 = f0 + sum((c - 1) * abs(s) for s, c in dims) + 1
            return (t.name, 0, 1, f0, f1)
        ps, pc = dims[0]
        if ps == 0:
            ps = 1 << 40
        if type(t).__name__.startswith("PSum"):
            return (t.name, 0, 128, 0, 1 << 30)
        p0 = ap.offset // ps
        f0 = ap.offset % ps
        f1 = f0 + sum((c - 1) * abs(s) for s, c in dims[1:]) + 1
        return (t.name, p0, p0 + pc, f0, f1)

    def _deps(self, b, write, deps, eng=None):
        name, p0, p1, f0, f1 = b
        lst = self.recs.get(name)
        if not lst:
            return
        psum = f1 == (1 << 30)
        for r in lst:
            if r[0] < p1 and p0 < r[1] and r[2] < f1 and f0 < r[3]:
                if write or r[6] or (psum and r[7] != eng):
                    k = r[4]
                    if deps.get(k, 0) < r[5]:
                        deps[k] = r[5]

    def _record(self, b, write, semkey, val, eng):
        name, p0, p1, f0, f1 = b
        lst = self.recs.setdefault(name, [])
        if write:
            lst[:] = [r for r in lst if not (p0 <= r[0] and r[1] <= p1 and f0 <= r[2] and r[3] <= f1)]
        else:
            lst[:] = [r for r in lst if not ((not r[6]) and r[7] == eng and r[4] == semkey
                                             and p0 <= r[0] and r[1] <= p1 and f0 <= r[2] and r[3] <= f1)]
        lst.append((p0, p1, f0, f1, semkey, val, write, eng))

    def _wait(self, e, deps):
        seen = self.seen[e]
        for k, v in deps.items():
            if e == "pe" and k == "pe":
                continue
            if seen.get(k, 0) < v:
                self.eng[e].wait_ge(self.semobj[k], v)
                seen[k] = v

    def emit(self, e, fn, outs, ins):
        deps = {}
        ob = [self.box(a) for a in outs]
        ib = [self.box(a) for a in ins if a is not None and not isinstance(a, (int, float))]
        for b in ib:
            self._deps(b, False, deps, e)
        for b in ob:
            self._deps(b, True, deps, e)
        self._wait(e, deps)
        inst = fn()
        self.cnt[e] += 1
        inst.then_inc(self.sem[e], 1)
        v = self.cnt[e]
        for b in ib:
            self._record(b, False, e, v, e)
        for b in ob:
            self._record(b, True, e, v, e)
        self.ninst += 1
        return inst

    def dma(self, q, out, in_):
        deps = {}
        ob = self.box(out)
        ib = self.box(in_)
        self._deps(ib, False, deps)
        self._deps(ob, True, deps)
        pl = self.dpool[q]
        i = pl[self.drr[q] % len(pl)]
        self.drr[q] += 1
        k = ("d", i)
        if self.dcnt[i] > 0:
            deps[k] = max(deps.get(k, 0), self.dcnt[i])
        self._wait(q, deps)
        inst = self.eng[q].dma_start(out=out, in_=in_)
        self.dcnt[i] += 16
        inst.then_inc(self.dsem[i], 16)
        self._record(ib, False, k, self.dcnt[i], "dma")
        self._record(ob, True, k, self.dcnt[i], "dma")
        self.ninst += 1

    def barrier(self, engines=None):
        engines = engines or list(self.eng)
        for e in engines:
            deps = {o: self.cnt[o] for o in self.eng if o != e and self.cnt[o] > 0}
            for i, c in enumerate(self.dcnt):
                if c > 0:
                    deps[("d", i)] = c
            self._wait(e, deps)

    def mm(self, out, lhsT, rhs, start=True, stop=True):
        nc = self.nc
        if lhsT.tensor.name in self.rt and rhs.tensor.name in self.rt:
            lhsT, rhs = lhsT.bitcast(F32R), rhs.bitcast(F32R)
        return self.emit("pe", lambda: nc.tensor.matmul(out, lhsT, rhs, start=start, stop=stop), [out], [lhsT, rhs])

    def tr(self, out, in_, ident):
        nc = self.nc
        return self.emit("pe", lambda: nc.tensor.transpose(out, in_, ident), [out], [in_, ident])

    def act(self, out, in_, func, bias=None, scale=1.0):
        nc = self.nc
        kw = {}
        if bias is not None:
            kw["bias"] = bias
        ins = [in_]
        if bias is not None and not isinstance(bias, (int, float)):
            ins.append(bias)
        if not isinstance(scale, (int, float)):
            ins.append(scale)
        out = self.R(out)
        return self.emit("act", lambda: nc.scalar.activation(out=out, in_=in_, func=func, scale=scale, **kw), [out], ins)

    def tt(self, e, out, in0, in1, op):
        eng = self.eng[e]
        out = self.R(out)
        return self.emit(e, lambda: eng.tensor_tensor(out=out, in0=in0, in1=in1, op=op), [out], [in0, in1])

    def ts(self, e, out, in0, s1, op0, s2=None, op1=None):
        eng = self.eng[e]
        out = self.R(out)
        ins = [in0] + [s for s in (s1, s2) if s is not None and not isinstance(s, (int, float))]
        if op1 is None:
            return self.emit(e, lambda: eng.tensor_scalar(out=out, in0=in0, scalar1=s1, scalar2=None, op0=op0), [out], ins)
        return self.emit(e, lambda: eng.tensor_scalar(out=out, in0=in0, scalar1=s1, scalar2=s2, op0=op0, op1=op1), [out], ins)

    def stt(self, out, in0, scalar, in1, op0, op1):
        nc = self.nc
        out = self.R(out)
        ins = [in0, in1] + ([scalar] if not isinstance(scalar, (int, float)) else [])
        return self.emit("dve", lambda: nc.vector.scalar_tensor_tensor(out=out, in0=in0, scalar=scalar, in1=in1, op0=op0, op1=op1), [out], ins)

    def copy(self, e, out, in_):
        eng = self.eng[e]
        out = self.R(out)
        if e == "act":
            return self.emit(e, lambda: eng.copy(out=out, in_=in_), [out], [in_])
        return self.emit(e, lambda: eng.tensor_copy(out=out, in_=in_), [out], [in_])

    def recip(self, out, in_):
        nc = self.nc
        return self.emit("dve", lambda: nc.vector.reciprocal(out=out, in_=in_), [out], [in_])

    def memset(self, e, out, val):
        eng = self.eng[e]
        if out.tensor.name in self.rt:
            return self.ts("dve", out, self.ones_ap, float(val), ALU.mult)
        return self.emit(e, lambda: eng.memset(out, val), [out], [])


def fm(v, nchunk):
    return np.ascontiguousarray(np.asarray(v, np.float32).reshape(nchunk, 128).T)


def rows(v):
    v = np.asarray(v, np.float32).reshape(1, -1)
    return np.ascontiguousarray(np.repeat(v, 128, axis=0))


class Pack:
    def __init__(self):
        self.cols = []
        self.off = {}
        self.n = 0

    def add(self, name, arr):
        arr = np.asarray(arr, np.float32)
        if arr.shape[0] < 128:
            arr = np.concatenate([arr, np.zeros((128 - arr.shape[0],) + arr.shape[1:], np.float32)], 0)
        arr = arr.reshape(128, -1)
        self.off[name] = (self.n, arr.shape[1])
        self.n += arr.shape[1]
        self.cols.append(arr)

    def build(self):
        return np.ascontiguousarray(np.concatenate(self.cols, 1))


def pack_params(inp, core):
    P = Pack()
    c = inp["c"][core]
    cc = np.stack([fm(inp["c_ctx"], 8), fm(c, 8)], axis=2)
    P.add("cT", cc)
    for i in range(2):
        P.add("bmod%d" % i, fm(inp["b_mod"][i], 72))
        P.add("nf1_%d" % i, fm(inp["norm_ffn1"][i], 8))
        P.add("nmx_%d" % i, fm(inp["norm_mix"][i], 8))
        P.add("nf2_%d" % i, fm(inp["norm_ffn2"][i], 8))
    P.add("fnorm", fm(inp["final_norm"], 8))
    P.add("gconv", np.stack([fm(inp["gdn_conv"][0][t], 12) for t in range(3)], axis=1))
    P.add("gnorm", np.asarray(inp["gdn_norm"][0], np.float32).reshape(128, 1))
    P.add("galog", rows(inp["gdn_a_log"][0]))
    P.add("gdtb", rows(inp["gdn_dt_bias"][0]))
    P.add("rmu", fm(inp["rwkv_mu"][0], 15))
    P.add("rw0", np.stack([fm(inp["rwkv_w0"][0][d], 4) for d in range(2)], axis=1))
    P.add("ra0", np.stack([fm(inp["rwkv_a0"][0][d], 4) for d in range(2)], axis=1))
    P.add("rkk", fm(inp["rwkv_k_k"][0], 4))
    P.add("rka", fm(inp["rwkv_k_a"][0], 4))
    P.add("rrk", fm(inp["rwkv_r_k"][0].reshape(-1), 4))
    P.add("rlnw", fm(inp["rwkv_ln_w"][0], 4))
    P.add("rlnb", fm(inp["rwkv_ln_b"][0], 4))
    P.add("mqn", fm(inp["mla_q_norm"][0], 3))
    P.add("mkvn", fm(inp["mla_kv_norm"][0], 2))
    P.add("hconv", np.stack([fm(inp["hy_conv"][0][t], 12) for t in range(3)], axis=1))
    P.add("hbias", np.stack([fm(inp["hy_bias"][0][n], 4) for n in range(2)], axis=1))
    P.add("hb1", np.asarray(inp["hy_b1"][0], np.float32).reshape(64, 1))
    P.add("hb2", np.asarray(inp["hy_b2"][0], np.float32).reshape(64, 1))
    P.add("hfreq", np.ascontiguousarray(np.asarray(inp["hy_freq"][0], np.float32).T))
    nz0 = np.ones((128, 1), np.float32)
    nz0[0, 0] = 0.0
    P.add("nz0", nz0)
    w1p = np.zeros((128, 64), np.float32)
    w1p[:33] = np.asarray(inp["hy_w1"][0], np.float32)
    P.add("hw1", w1p)
    P.add("hw2", np.asarray(inp["hy_w2"][0], np.float32))
    hm = np.zeros((128, 2), np.float32)
    hm[:64, 0] = 1.0
    hm[64:, 1] = 1.0
    P.add("hmask", hm)
    rm = np.ones((128, 1024), np.float32)
    rm[:, ::128] = 0.0
    P.add("rmask", rm)
    return P


def make_consts():
    p = np.arange(128)[:, None]
    f = np.arange(128)[None, :]
    ident = (p == f).astype(np.float32)
    le = (p <= f).astype(np.float32)
    lt = (p < f).astype(np.float32)
    ge = (p >= f).astype(np.float32)
    gt = (p > f).astype(np.float32)
    BIG = 30000.0
    bo = ((p // 64) == (f // 64)).astype(np.float32)
    jpad = np.zeros((128, 128), np.float32)
    for i in range(16):
        jpad[64 + 2 * i + 1, 64 + 2 * i] = -1.0
        jpad[64 + 2 * i, 64 + 2 * i + 1] = 1.0
    onesE = np.zeros((128, 128), np.float32)
    onesE[:, :64] = 1.0
    onesO = np.zeros((128, 128), np.float32)
    onesO[:, 64:] = 1.0
    b32 = ((p // 32) == (f // 32)).astype(np.float32)
    m1 = (((p // 64) == (f // 64)) & ((p // 32) != (f // 32))).astype(np.float32)
    m2 = ((p // 64) != (f // 64)).astype(np.float32)
    names = ["ident", "le", "lt", "ge", "gt", "pos_gt", "pos_lt", "neg_le", "neg_ge", "bo", "jpad", "onesE", "onesO", "b32", "m1", "m2"]
    arrs = [ident, le, lt, ge, gt, BIG * (1 - gt), BIG * (1 - lt), -BIG * (1 - le), -BIG * (1 - ge), bo, jpad, onesE, onesO, b32, m1, m2]
    return names, np.ascontiguousarray(np.concatenate(arrs, 1).astype(np.float32))


def rope_tables():
    L = 1024
    row = np.repeat(np.arange(L // 64), 64).astype(np.float32)
    col = (np.arange(L) % 64).astype(np.float32)
    n = 8
    inv = (10000.0 ** (-np.arange(n, dtype=np.float32) / n)).astype(np.float32)
    ang = np.concatenate([row[:, None] * inv, col[:, None] * inv], axis=-1)
    cs = np.zeros((2, 128, L), np.float32)
    for r in range(32):
        cs[0, 64 + r] = np.cos(ang[:, r // 2])
        cs[1, 64 + r] = np.sin(ang[:, r // 2])
    return cs


def hyena_consts(L):
    import ml_dtypes
    t = np.arange(L, dtype=np.float64)
    w = 2.0 * np.pi * (t + 0.5) / (2 * L)
    ph = np.outer(t, w)
    Cm = np.cos(ph)
    Sm = np.sin(ph)
    dft = np.stack([Cm, Sm, Cm.T, Sm.T]).astype(np.float32).astype(ml_dtypes.bfloat16)
    t32 = np.arange(L, dtype=np.float32)
    t_norm = t32 / max(L - 1, 1)
    bands = np.linspace(1e-4, 16 - 1, 16, dtype=np.float32)
    ang = (np.float32(2.0 * np.pi / L) * t32[:, None] * bands).astype(np.float32)
    feats = np.concatenate([t_norm[:, None], np.cos(ang), -np.sin(ang)], axis=-1).astype(np.float32)
    min_decay = np.log(1e-2) / 1.5
    max_decay = np.log(1e-2) / 0.3
    deltas = np.abs(np.linspace(min_decay, max_decay, 512, dtype=np.float32))
    window = np.exp(-t_norm[:, None] * deltas).astype(np.float32)
    featsT = np.zeros((128, L), np.float32)
    featsT[:33] = feats.T
    return dft, featsT, window


A_COLS = 1920
SEQS = {0: [(0, 256), (256, 256), (512, 256), (768, 256)], 1: [(0, 1024)]}


def build_program(offs, NP, test=None, stage=99):
    nc = bass.Bass("TRN2", target_bir_lowering=False)
    dr = {}

    def din(name, shape, dt=F32):
        dr[name] = nc.dram_tensor(name, list(shape), dt, kind="ExternalInput").ap()
        return dr[name]

    def dout(name, shape, dt=F32):
        dr[name] = nc.dram_tensor(name, list(shape), dt, kind="ExternalOutput").ap()
        return dr[name]

    cnames, carr = make_consts()
    xT = din("xT", [D, TOK])
    prm_d = din("prm", [128, NP])
    cst_d = din("cst", [128, carr.shape[1]])
    w_mod = din("w_mod", [2, D, 9 * D])
    ffn_gu = [din("ffn1_w_gu", [2, D, 2 * DFF]), din("ffn2_w_gu", [2, D, 2 * DFF])]
    ffn_dn = [din("ffn1_w_down", [2, DFF, D]), din("ffn2_w_down", [2, DFF, D])]
    w_out_d = din("w_out", [2, D, D])
    ab_w_in = din("ab_w_in", [1, D, 3984])
    sgdn_in = din("sgdn_in", [2, 4, 128, 128])
    srwkv_in = din("srwkv_in", [2, 8, 64, 64])
    rwkv_w_up = din("rwkv_w_up", [1, 2, 64, 512])
    rwkv_a_up = din("rwkv_a_up", [1, 2, 64, 512])
    rwkv_g_up = din("rwkv_g_up", [1, 128, 512])
    srwkv_out = dout("srwkv_out", [4, 2, 8, 64, 64])
    cd_w_in = din("cd_w_in", [1, D, 2208])
    mla_w_uq = din("mla_w_uq", [1, 384, 768])
    mla_w_ukv = din("mla_w_ukv", [1, 256, 1024])
    cckvT = din("cckvT", [256, 256])
    ckpeT = din("ckpeT", [32, 256])
    ropecs = din("ropecs", [2, 128, 1024])
    hy_w1 = din("hy_w1", [1, 33, 64])
    hy_w2 = din("hy_w2", [1, 64, 64])
    hy_w3 = din("hy_w3", [1, 64, 2048])
    dftd = {256: din("dft256", [4, 256, 256], BF16), 1024: din("dft1024", [4, 1024, 1024], BF16)}
    featd = {256: din("feat256", [128, 256]), 1024: din("feat1024", [128, 1024])}
    wind = {256: din("win256", [256, 512]), 1024: din("win1024", [1024, 512])}
    ckv_out = dout("ckv_out", [256, 1024])
    kpe_out = dout("kpe_out", [32, 1024])
    yT = dout("yT", [D, TOK])
    sgdn_out = dout("sgdn_out", [4, 2, 4, 128, 128])

    es = contextlib.ExitStack()
    with es:
        kb = KB(nc, es)

        uid = [0]

        def sbx(stack, name, shape, dt=F32, r=False):
            uid[0] += 1
            nm = "%s_%d" % (name, uid[0])
            if r:
                kb.rt.add(nm)
            return stack.enter_context(nc.sbuf_tensor(nm, list(shape), dt))

        def sb(name, shape, dt=F32):
            return sbx(es, name, shape, dt)

        X = sb("X", [128, 8, TOK])
        prm = sb("prm_sb", [128, NP])
        cst = sb("cst_sb", [128, carr.shape[1]])
        ones = sb("ones", [128, 128])
        modT = sb("modT", [128, 2, 2, 72])
        cs = sb("cs", [128, 8, 2], BF16)
        sq = [sb("sq%d" % i, [128, 512]) for i in range(2)]
        rstd = sb("rstd", [128, HALF])
        tmpf = [sb("tmpf%d" % i, [128, 512]) for i in range(2)]
        coef = sb("coef", [128, 64])
        pbank = [es.enter_context(nc.psum_tensor("pb%d" % i, [128, 512], F32)) for i in range(8)]

        def P(name):
            o, n = offs[name]
            return prm[:, o:o + n]

        def C(name):
            i = cnames.index(name)
            return cst[:, i * 128:(i + 1) * 128]

        slot = [0]

        def pq():
            s = slot[0]
            slot[0] = (s + 1) % 16
            return pbank[s % 4][:, (s // 4) * 128:(s // 4 + 1) * 128]

        kb.dma("sp", prm[:], prm_d[:, :])
        kb.dma("sp", cst[:], cst_d[:, :])
        for j in range(8):
            kb.dma("sp", X[:, j, :], xT[j * 128:(j + 1) * 128, :])
        kb.memset("dve", ones[:], 1.0)
        kb.ones_ap = ones[:]
        ident = C("ident")
        ident16 = sb("ident16", [128, 128], FP16)
        kb.copy("dve", ident16[:], ident)

        with contextlib.ExitStack() as ph:
            wm = [sbx(ph, "wm%d" % i, [128, 8, 1024], BF16) for i in range(2)]
            cT = P("cT")
            kb.act(cs[:].rearrange("p a b -> p (a b)"), cT, AF.Silu)
            for L in range(2):
                pm = pbank[7]
                pmv = pm[:, 0:144].rearrange("p (a b) -> p a b", b=2)
                for n in range(9):
                    wt = wm[n % 2]
                    kb.dma("pool", wt[:], w_mod[L, :, n * 1024:(n + 1) * 1024].rearrange("(k p) c -> p k c", p=128))
                    for j in range(8):
                        for k in range(8):
                            kb.mm(pmv[:, n * 8 + j, :], wt[:, k, j * 128:(j + 1) * 128], cs[:, k, :], start=(k == 0), stop=(k == 7))
                for v in range(2):
                    kb.tt("dve", modT[:, L, v, :], pmv[:, :, v], P("bmod%d" % L), ALU.add)
            kb.barrier()

        def rms_stats(hf):
            for nt in range(2):
                t0 = hf * HALF + nt * 512
                pst = pbank[6]
                for j in range(8):
                    s = sq[j % 2]
                    kb.act(s[:], X[:, j, t0:t0 + 512], AF.Square)
                    kb.mm(pst[:], ones[:], s[:], start=(j == 0), stop=(j == 7))
                kb.act(rstd[:, nt * 512:(nt + 1) * 512], pst[:], AF.Sqrt, bias=EPS, scale=1.0 / D)
            kb.recip(rstd[:], rstd[:])

        def make_coefs(L, hf, sub, gname):
            m = modT[:, L, hf, :]
            kb.stt(coef[:, 0:8], m[:, (3 * sub + 1) * 8:(3 * sub + 2) * 8], 1.0, P(gname), ALU.add, ALU.mult)
            kb.copy("dve", coef[:, 8:16], m[:, (3 * sub) * 8:(3 * sub + 1) * 8])
            kb.ts("dve", coef[:, 16:24], m[:, (3 * sub + 2) * 8:(3 * sub + 3) * 8], 0.5 if sub != 1 else 1.0, ALU.mult)

        def make_H(Hb, hf):
            for j in range(8):
                for nt in range(2):
                    t0 = hf * HALF + nt * 512
                    tf = tmpf[(j * 2 + nt) % 2]
                    kb.stt(tf[:], X[:, j, t0:t0 + 512], coef[:, j:j + 1], rstd[:, nt * 512:(nt + 1) * 512], ALU.mult, ALU.mult)
                    kb.act(Hb[:, j, nt * 512:(nt + 1) * 512], tf[:], AF.Identity, bias=coef[:, 8 + j:9 + j], scale=1.0)

        gu_groups = [(0, 3), (3, 3), (6, 3), (9, 2)]

        def ffn_phase(L, which):
            with contextlib.ExitStack() as ph:
                Hb = sbx(ph, "Hb", [128, 8, HALF], BF16)
                hh = sbx(ph, "hh", [128, 11, HALF], BF16)
                wgu = [sbx(ph, "wgu%d" % i, [128, 8, 2, 384], BF16) for i in range(2)]
                wdns = [sbx(ph, "wdn%d" % i, [128, 11, D], BF16) for i in range(2)]
                wdi = 0
                sgt = [sbx(ph, "sgt%d" % i, [128, 512], BF16) for i in range(2)]
                sub = 0 if which == 0 else 2
                wg = ffn_gu[which]
                wd = ffn_dn[which]
                gi = 0
                for hf in range(2):
                    rms_stats(hf)
                    make_coefs(L, hf, sub, ("nf1_%d" if which == 0 else "nf2_%d") % L)
                    make_H(Hb, hf)
                    for fh in range(2):
                        wdn = wdns[wdi % 2]
                        wdi += 1
                        kb.dma("pool", wdn[:], wd[L, fh * 1408:(fh + 1) * 1408, :].rearrange("(j p) c -> p j c", p=128))
                        for (g0, gn) in gu_groups:
                            wt = wgu[gi % 2]
                            gi += 1
                            c0 = fh * 1408 + g0 * 128
                            for gu in range(2):
                                kb.dma("pool", wt[:, :, gu, 0:gn * 128],
                                       wg[L, :, gu * DFF + c0: gu * DFF + c0 + gn * 128].rearrange("(k p) c -> p k c", p=128))
                            for jj in range(gn):
                                for nt in range(2):
                                    pg = pbank[(2 * (jj * 2 + nt)) % 4]
                                    pu = pbank[(2 * (jj * 2 + nt)) % 4 + 1]
                                    for k in range(8):
                                        kb.mm(pg[:], wt[:, k, 0, jj * 128:(jj + 1) * 128], Hb[:, k, nt * 512:(nt + 1) * 512], start=(k == 0), stop=(k == 7))
                                    for k in range(8):
                                        kb.mm(pu[:], wt[:, k, 1, jj * 128:(jj + 1) * 128], Hb[:, k, nt * 512:(nt + 1) * 512], start=(k == 0), stop=(k == 7))
                                    sg = sgt[(jj * 2 + nt) % 2]
                                    kb.act(sg[:], pg[:], AF.Silu)
                                    kb.tt("dve", hh[:, g0 + jj, nt * 512:(nt + 1) * 512], sg[:], pu[:], ALU.mult)
                        for m in range(8):
                            for nt in range(2):
                                po = pbank[4 + (m * 2 + nt) % 2]
                                for j in range(11):
                                    kb.mm(po[:], wdn[:, j, m * 128:(m + 1) * 128], hh[:, j, nt * 512:(nt + 1) * 512], start=(j == 0), stop=(j == 10))
                                t0 = hf * HALF + nt * 512
                                kb.stt(X[:, m, t0:t0 + 512], po[:], coef[:, 16 + m:17 + m], X[:, m, t0:t0 + 512], ALU.mult, ALU.add)
                kb.barrier()

        def interleave(gens):
            gens = list(gens)
            while gens:
                for g in list(gens):
                    try:
                        next(g)
                    except StopIteration:
                        gens.remove(g)

        hslot = [0]

        def pq2():
            k = hslot[0]
            hslot[0] = (k + 1) % 4
            return pbank[4 + k % 2][:, (k // 2) * 256:(k // 2 + 1) * 256]

        def tri_inv_T(ph_tiles, N, NT):
            T = ph_tiles
            Nd, Pa, Pb, Mt, Tun, PXa, PXb = T["Nd"], T["Pa"], T["Pb"], T["Mt"], T["Tun"], T["PXa"], T["PXb"]
            kb.tt("dve", Nd[:], N[:], C("b32"), ALU.mult)
            kb.tt("dve", PXa[:, 0:128], NT[:], C("b32"), ALU.mult)
            kb.copy("dve", PXa[:, 128:256], ident)
            kb.tt("dve", T["Nl1"][:], N[:], C("m1"), ALU.mult)
            kb.tt("dve", T["Nl2"][:], N[:], C("m2"), ALU.mult)
            yield
            Pc, Pn = Nd, Pa
            PXc, PXn = PXa, PXb
            for k in range(1, 5):
                px = pq2()
                kb.mm(px, Pc[:], PXc[:])
                pp = pq()
                kb.mm(pp, PXc[:, 0:128], Pc[:])
                if k < 4:
                    kb.copy("act", PXn[:, 0:128], px[:, 0:128])
                kb.tt("dve", PXn[:, 128:256], px[:, 128:256], PXc[:, 128:256], ALU.add)
                kb.copy("act", Pn[:], pp)
                yield
                Pc = Pn
                Pn = Pb if Pn is Pa else Pa
                PXc, PXn = PXn, PXc
            Xa = PXc[:, 128:256]
            pf = pq()
            kb.mm(pf, Pc[:], Xa)
            kb.tt("dve", Xa, pf, Xa, ALU.add)
            yield
            for mname in ("Nl1", "Nl2"):
                pm = pq()
                kb.mm(pm, T[mname][:], Xa)
                kb.copy("act", Mt[:], pm)
                ptr = pq().bitcast(FP16)[:, 0:128]
                kb.tr(ptr, Xa, ident16[:])
                kb.copy("act", Tun[:], ptr)
                yield
                pw = pq()
                kb.mm(pw, Tun[:], Mt[:])
                kb.tt("dve", Xa, pw, Xa, ALU.add)
                yield
            T["X"] = Xa

        def inv_tile_set(fr, pfx):
            d = {n: fr(pfx + n, [128, 128]) for n in ["Nd", "Pa", "Pb", "Mt", "Tun", "Nl1", "Nl2"]}
            d["PXa"] = fr(pfx + "PXa", [128, 256])
            d["PXb"] = fr(pfx + "PXb", [128, 256])
            return d

        def gdn(hf, Hb, wo_apply):
            with contextlib.ExitStack() as ph:
                f = lambda name, shape, dt=F32, r=False: sbx(ph, "g_" + name, shape, dt, r)
                fr = lambda name, shape: f(name, shape, F32, True)
                fh = lambda name, shape: f(name, shape, FP16)
                wb = [f("wb%d" % i, [128, 8, 4, 128], BF16) for i in range(1)]
                wab = f("wab", [128, 8, 16], BF16)
                praw = f("praw", [128, HALF])
                cv = f("cv", [128, HALF])
                qF, kF, vF, zF = fr("qF", [128, HALF]), fr("kF", [128, HALF]), f("vF", [128, HALF]), f("zF", [128, HALF])
                oacc = f("oacc", [128, HALF])
                mch = f("mch", [128, HALF], BF16)
                rn = f("rn", [128, HALF])
                abt = f("abt", [128, 8, 16])
                gsb = f("gsb", [128, 2, 8, 4])
                bsb = f("bsb", [128, 2, 8, 4])
                nbsb = f("nbsb", [128, 2, 8, 4])
                Gsb = f("Gsb", [128, 2, 8, 4])
                Gtot = f("Gtot", [128, 2, 8, 4])
                eG = f("eG", [128, 2, 8, 4])
                beG = f("beG", [128, 2, 8, 4])
                eGL = f("eGL", [128, 2, 8, 4])
                eGlast = f("eGlast", [128, 2, 8, 4])
                ea = f("ea", [128, 8])
                tsm = f("tsm", [128, 2, 8, 4])
                kT = f("kT", [128, 8, 128])
                vT = f("vT", [128, 8, 128])
                st_qkT = f("st_qkT", [128, 8, 128], BF16)
                st_u = f("st_u", [128, 8, 128])
                st_nwT = f("st_nwT", [128, 8, 128], BF16)
                st_qin = f("st_qin", [128, 8, 128], BF16)
                st_kout = f("st_kout", [128, 8, 128], BF16)
                S = f("S", [128, 128])
                Sb = f("Sb", [128, 128], BF16)
                NCH = 3
                t_vnew = f("t_vnew", [128, 128], BF16)
                chT = []
                for ci in range(NCH):
                    T = {n: f("c%d_%s" % (ci, n), [128, 128]) for n in ["diag", "d1", "d2", "E1", "E2", "eGr"]}
                    T.update({n: fh("c%d_%s" % (ci, n), [128, 128]) for n in ["N", "NT", "vb", "kbg"]})
                    T["inv"] = inv_tile_set(fh, "c%d_iv" % ci)
                    chT.append(T)
                c_ab = A_COLS + 2048
                kb.dma("pool", wab[:], ab_w_in[0, :, c_ab:c_ab + 16].rearrange("(k p) c -> p k c", p=128))
                pab = pbank[6]
                for tt in range(8):
                    for k in range(8):
                        kb.mm(pab[:, tt * 16:(tt + 1) * 16], Hb[:, k, tt * 128:(tt + 1) * 128], wab[:, k, :], start=(k == 0), stop=(k == 7))
                kb.copy("dve", abt[:].rearrange("p a b -> p (a b)"), pab[:, 0:128])
                abv = abt[:].rearrange("p t (d a h) -> p t d a h", d=2, a=2)
                kb.act(ea[:], P("galog"), AF.Exp)
                for d in range(2):
                    dtb = P("gdtb")[:, d * 4:(d + 1) * 4]
                    for tt in range(8):
                        kb.tt("dve", tsm[:, d, tt, :], abv[:, tt, d, 0, :], dtb, ALU.add)
                        kb.copy("dve", bsb[:, d, tt, :], abv[:, tt, d, 1, :])
                g2 = lambda t: t[:].rearrange("p d t h -> p (d t h)")
                kb.act(g2(tsm), g2(tsm), AF.Exp)
                kb.act(g2(tsm), g2(tsm), AF.Ln, bias=1.0)
                for d in range(2):
                    for tt in range(8):
                        kb.stt(gsb[:, d, tt, :], tsm[:, d, tt, :], -1.0, ea[:, d * 4:(d + 1) * 4], ALU.mult, ALU.mult)
                kb.act(g2(bsb), g2(bsb), AF.Sigmoid)
                kb.ts("dve", g2(nbsb), g2(bsb), -1.0, ALU.mult)
                pG = pbank[6]
                g3 = lambda t, d: t[:, d, :, :].rearrange("p t h -> p (t h)")
                kb.mm(pG[:, 0:32], C("le"), g3(gsb, 0))
                kb.mm(pG[:, 32:64], C("ge"), g3(gsb, 1))
                kb.mm(pG[:, 64:128], ones[:], g2(gsb))
                kb.copy("dve", g2(Gsb), pG[:, 0:64])
                kb.copy("dve", g2(Gtot), pG[:, 64:128])
                kb.act(g2(eG), g2(Gsb), AF.Exp)
                kb.tt("dve", g2(beG), g2(eG), g2(bsb), ALU.mult)
                kb.tt("dve", g2(eGL), g2(Gtot), g2(Gsb), ALU.subtract)
                kb.act(g2(eGL), g2(eGL), AF.Exp)
                kb.act(g2(eGlast), g2(Gtot), AF.Exp)

                for h in range(4):
                    wt = wb[0]
                    for qi in range(4):
                        c0 = A_COLS + qi * 512 + h * 128
                        kb.dma("pool", wt[:, :, qi, :], ab_w_in[0, :, c0:c0 + 128].rearrange("(k p) c -> p k c", p=128))
                    gc = P("gconv").rearrange("p (t c) -> p t c", t=3)
                    for qi, dst in enumerate([qF, kF, vF, zF]):
                        for nt in range(2):
                            pp = pbank[nt]
                            for k in range(8):
                                kb.mm(pp[:], wt[:, k, qi, :], Hb[:, k, nt * 512:(nt + 1) * 512], start=(k == 0), stop=(k == 7))
                            if qi == 3:
                                kb.act(zF[:, nt * 512:(nt + 1) * 512], pp[:], AF.Silu)
                            else:
                                kb.copy("act", praw[:, nt * 512:(nt + 1) * 512], pp[:])
                        if qi == 3:
                            continue
                        cc = qi * 4 + h
                        kb.ts("dve", cv[:], praw[:], gc[:, 1, cc:cc + 1], ALU.mult)
                        for (s0, Ls) in SEQS[hf]:
                            kb.stt(cv[:, s0 + 1:s0 + Ls], praw[:, s0:s0 + Ls - 1], gc[:, 0, cc:cc + 1], cv[:, s0 + 1:s0 + Ls], ALU.mult, ALU.add)
                            kb.stt(cv[:, s0:s0 + Ls - 1], praw[:, s0 + 1:s0 + Ls], gc[:, 2, cc:cc + 1], cv[:, s0:s0 + Ls - 1], ALU.mult, ALU.add)
                        kb.act(dst[:], cv[:], AF.Silu)
                        if qi < 2:
                            for nt in range(2):
                                s = sq[nt]
                                kb.act(s[:], dst[:, nt * 512:(nt + 1) * 512], AF.Square)
                                pst = pbank[2 + nt]
                                kb.mm(pst[:], ones[:], s[:])
                                kb.act(rn[:, nt * 512:(nt + 1) * 512], pst[:], AF.Sqrt, bias=1e-6, scale=1.0)
                            kb.recip(rn[:], rn[:])
                            kb.stt(dst[:], dst[:], (128.0 ** -0.5) if qi == 0 else 1.0, rn[:], ALU.mult, ALU.mult)
                    for tt in range(8):
                        p1 = pq()
                        kb.tr(p1, kF[:, tt * 128:(tt + 1) * 128], ident)
                        kb.copy("act", kT[:, tt, :], p1)
                        p2 = pq()
                        kb.tr(p2, vF[:, tt * 128:(tt + 1) * 128], ident)
                        kb.copy("act", vT[:, tt, :], p2)
                    for d in range(2):
                        posm = C("pos_gt") if d == 0 else C("pos_lt")
                        negm = C("neg_le") if d == 0 else C("neg_ge")
                        def pre_tile(tt, T, d=d, h=h, posm=posm, negm=negm):
                            tsl = slice(tt * 128, (tt + 1) * 128)
                            col = lambda t: t[:, d, tt, h:h + 1]
                            kb.ts("dve", T["diag"][:], ident, col(Gsb), ALU.mult)
                            prb = pq()
                            kb.mm(prb, ones[:], T["diag"][:])
                            yield
                            kb.stt(T["d1"][:], prb, col(Gsb), posm, ALU.subtract, ALU.add)
                            kb.act(T["E1"][:], T["d1"][:], AF.Exp, scale=-1.0)
                            kb.stt(T["d2"][:], prb, col(Gsb), negm, ALU.subtract, ALU.add)
                            kb.act(T["E2"][:], T["d2"][:], AF.Exp)
                            kb.act(T["eGr"][:], prb, AF.Exp)
                            pkk = pq()
                            kb.mm(pkk, kF[:, tsl], kF[:, tsl])
                            yield
                            kb.stt(T["N"][:], pkk, col(nbsb), T["E1"][:], ALU.mult, ALU.mult)
                            pnt = pq().bitcast(FP16)[:, 0:128]
                            kb.tr(pnt, T["N"][:], ident16[:])
                            kb.copy("act", T["NT"][:], pnt)
                            pqk = pq()
                            kb.mm(pqk, kF[:, tsl], qF[:, tsl])
                            kb.tt("dve", st_qkT[:, tt, :], pqk, T["E2"][:], ALU.mult)
                            kb.ts("pool", T["vb"][:], vT[:, tt, :], col(bsb), ALU.mult, 0.0, ALU.add)
                            kb.ts("pool", T["kbg"][:], kT[:, tt, :], col(beG), ALU.mult, 0.0, ALU.add)
                            kb.ts("pool", st_kout[:, tt, :], kT[:, tt, :], col(eGL), ALU.mult, 0.0, ALU.add)
                            kb.tt("pool", st_qin[:, tt, :], qF[:, tsl], T["eGr"][:], ALU.mult)
                            yield
                            yield from tri_inv_T(T["inv"], T["N"], T["NT"])
                            Xi = T["inv"]["X"]
                            pu_ = pq()
                            kb.mm(pu_, Xi, T["vb"][:])
                            kb.copy("act", st_u[:, tt, :], pu_)
                            pw_ = pq()
                            kb.mm(pw_, T["kbg"][:], Xi)
                            kb.act(st_nwT[:, tt, :], pw_, AF.Identity, scale=-1.0)
                            yield

                        for t0_ in range(0, 8, NCH):
                            interleave([pre_tile(tt, chT[ci]) for ci, tt in enumerate(range(t0_, min(8, t0_ + NCH)))])
                        for si, (s0, Ls) in enumerate(SEQS[hf]):
                            tiles = list(range(s0 // 128, (s0 + Ls) // 128))
                            if d == 1:
                                tiles = tiles[::-1]
                            if hf == 0:
                                kb.memset("dve", S[:], 0.0)
                            else:
                                kb.dma("sp", S[:], sgdn_in[d, h, :, :])
                            kb.copy("dve", Sb[:], S[:])
                            for tt in tiles:
                                tsl = slice(tt * 128, (tt + 1) * 128)
                                pv = pq()
                                kb.mm(pv, st_nwT[:, tt, :], Sb[:])
                                kb.tt("dve", t_vnew[:], pv, st_u[:, tt, :], ALU.add)
                                po_ = pq()
                                kb.mm(po_, Sb[:], st_qin[:, tt, :], start=True, stop=False)
                                kb.mm(po_, t_vnew[:], st_qkT[:, tt, :], start=False, stop=True)
                                if d == 0:
                                    kb.copy("act", oacc[:, tsl], po_)
                                else:
                                    kb.tt("dve", oacc[:, tsl], po_, oacc[:, tsl], ALU.add)
                                ps_ = pq()
                                kb.mm(ps_, st_kout[:, tt, :], t_vnew[:])
                                kb.stt(S[:], S[:], eGlast[:, d, tt, h:h + 1], ps_, ALU.mult, ALU.add)
                                kb.copy("act", Sb[:], S[:])
                            if hf == 0:
                                kb.dma("sp", sgdn_out[si, d, h, :, :], S[:])
                    for nt in range(2):
                        s = sq[nt]
                        kb.act(s[:], oacc[:, nt * 512:(nt + 1) * 512], AF.Square)
                        pst = pbank[2 + nt]
                        kb.mm(pst[:], ones[:], s[:])
                        kb.act(rn[:, nt * 512:(nt + 1) * 512], pst[:], AF.Sqrt, bias=EPS, scale=1.0 / 128)
                    kb.recip(rn[:], rn[:])
                    kb.stt(oacc[:], oacc[:], P("gnorm"), rn[:], ALU.mult, ALU.mult)
                    kb.tt("dve", mch[:], oacc[:], zF[:], ALU.mult)
                    wo_apply(4 + h, mch)
                kb.barrier()

        def rwkv(hf, Hb, wo_apply):
            with contextlib.ExitStack() as ph:
                f = lambda name, shape, dt=F32, r=False: sbx(ph, "r_" + name, shape, dt, r)
                fr = lambda name, shape: f(name, shape, F32, True)
                fh = lambda name, shape: f(name, shape, FP16)
                wch = [f("wch%d" % i, [128, 8, 128], BF16) for i in range(2)]
                wup, aup, gup = f("wup", [128, 512], BF16), f("aup", [128, 512], BF16), f("gup", [128, 512], BF16)
                twd, tad, sgd = f("twd", [128, HALF], BF16), f("tad", [128, HALF], BF16), f("sgd", [128, HALF], BF16)
                T1 = f("T1", [128, HALF])
                rF, kF0, vF, kkF = f("rF", [128, HALF]), f("kF0", [128, HALF]), f("vF", [128, HALF]), f("kkF", [128, HALF])
                asum, yacc = f("asum", [128, HALF]), f("yacc", [128, HALF])
                lw, G, kd, bF = f("lw", [128, HALF]), f("G", [128, HALF]), f("kd", [128, HALF]), f("bF", [128, HALF])
                mch = f("mch", [128, HALF], BF16)
                omm, hmu, omka = f("omm", [128, 15]), f("hmu", [128, 15]), f("omka", [128, 4])
                Mp = f("Mp", [128, 128])
                Mpb = f("Mpb", [128, 128], BF16)
                pC = f("pC", [128, 1])
                nm = ["eX", "Rt", "Ct", "kinv", "binv", "khat", "bhat", "Z"]
                bset = {"Rt", "Ct", "kinv", "binv", "Z"}
                nm = nm + ["eX2"]
                tlp = [{n: f("t%d_%s" % (par, n), [128, 128], BF16 if n in bset else F32) for n in nm} for par in range(2)]
                pCp = [f("pC%d" % par, [128, 1]) for par in range(2)]
                hdp = [[{n: f("h%d%d_%s" % (par, h, n), [128, 128], BF16) for n in ["X", "BmT", "QKT", "QBT", "vT", "khT", "bhT", "nU"]} for h in range(2)] for par in range(2)]
                hT = []
                for h_ in range(2):
                    T = {n: fh("p%d_%s" % (h_, n), [128, 128]) for n in ["N", "NT"]}
                    T.update({n: f("p%d_%s" % (h_, n), [128, 128], BF16) for n in ["Ctm", "Rtm"]})
                    T["msk"] = [f("p%d_msk%d" % (h_, i), [128, 128]) for i in range(3)]
                    T["inv"] = inv_tile_set(fh, "p%d_iv" % h_)
                    hT.append(T)
                hmask = P("hmask")
                mu = P("rmu")
                kb.ts("dve", omm[:], mu, -1.0, ALU.mult, 1.0, ALU.add)
                kb.ts("dve", hmu[:], mu, 0.5, ALU.mult)
                kb.ts("dve", omka[:], P("rka"), -1.0, ALU.mult, 1.0, ALU.add)
                for d in range(2):
                    kb.dma("pool", wup[d * 64:(d + 1) * 64, :], rwkv_w_up[0, d, :, :])
                    kb.dma("pool", aup[d * 64:(d + 1) * 64, :], rwkv_a_up[0, d, :, :])
                kb.dma("pool", gup[:], rwkv_g_up[0, :, :])
                wi = [0]

                def project_shift(chunk, dst, func=None):
                    wt = wch[wi[0] % 2]
                    wi[0] += 1
                    kb.dma("pool", wt[:], ab_w_in[0, :, chunk * 128:(chunk + 1) * 128].rearrange("(k p) c -> p k c", p=128))
                    for nt in range(2):
                        pp = pbank[nt]
                        for k in range(8):
                            kb.mm(pp[:], wt[:, k, :], Hb[:, k, nt * 512:(nt + 1) * 512], start=(k == 0), stop=(k == 7))
                        kb.copy("act", T1[:, nt * 512:(nt + 1) * 512], pp[:])
                    tgt = dst if func is None else lw
                    kb.ts("dve", tgt[:], T1[:], omm[:, chunk:chunk + 1], ALU.mult)
                    for (s0, Ls) in SEQS[hf]:
                        kb.stt(tgt[:, s0 + 1:s0 + Ls], T1[:, s0:s0 + Ls - 1], hmu[:, chunk:chunk + 1], tgt[:, s0 + 1:s0 + Ls], ALU.mult, ALU.add)
                        kb.stt(tgt[:, s0:s0 + Ls - 1], T1[:, s0 + 1:s0 + Ls], hmu[:, chunk:chunk + 1], tgt[:, s0:s0 + Ls - 1], ALU.mult, ALU.add)
                    if func is not None:
                        kb.act(dst[:], tgt[:], func)

                project_shift(12, twd, AF.Tanh)
                project_shift(13, tad, AF.Identity)
                project_shift(14, sgd, AF.Sigmoid)
                bo = C("bo")
                for c in range(4):
                    project_shift(c, rF)
                    project_shift(4 + c, kF0)
                    project_shift(8 + c, vF)
                    kb.ts("dve", kkF[:], kF0[:], P("rkk")[:, c:c + 1], ALU.mult)
                    for nt in range(2):
                        sl = slice(nt * 512, (nt + 1) * 512)
                        kb.act(sq[nt][:], kkF[:, sl], AF.Square)
                        pst = pbank[2 + nt]
                        kb.mm(pst[:], bo, sq[nt][:])
                        kb.act(T1[:, sl], pst[:], AF.Sqrt, bias=1e-6, scale=1.0)
                    kb.recip(T1[:], T1[:])
                    kb.tt("dve", kkF[:], kkF[:], T1[:], ALU.mult)
                    for d in range(2):
                        m_ts = C("gt") if d == 0 else C("lt")
                        m_st = C("lt") if d == 0 else C("gt")
                        m_in = C("le") if d == 0 else C("ge")
                        rows = slice(d * 64, (d + 1) * 64)
                        for nt in range(2):
                            sl = slice(nt * 512, (nt + 1) * 512)
                            pw = pbank[nt]
                            kb.mm(pw[:], wup[rows, c * 128:(c + 1) * 128], twd[rows, sl])
                            kb.act(lw[:, sl], pw[:], AF.Sigmoid, bias=P("rw0").rearrange("p (d c) -> p d c", d=2)[:, d, c:c + 1])
                            pa = pbank[2 + nt]
                            kb.mm(pa[:], aup[rows, c * 128:(c + 1) * 128], tad[rows, sl])
                            kb.act(T1[:, sl], pa[:], AF.Sigmoid, bias=P("ra0").rearrange("p (d c) -> p d c", d=2)[:, d, c:c + 1])
                        kb.ts("dve", lw[:], lw[:], -0.6065306597126334, ALU.mult)
                        if d == 0:
                            kb.copy("dve", asum[:], T1[:])
                        else:
                            kb.tt("dve", asum[:], asum[:], T1[:], ALU.add)
                        kb.tt("dve", bF[:], kkF[:], T1[:], ALU.mult)
                        kb.ts("dve", T1[:], T1[:], P("rka")[:, c:c + 1], ALU.mult, omka[:, c:c + 1], ALU.add)
                        kb.tt("dve", kd[:], kF0[:], T1[:], ALU.mult)
                        nc_ = nc
                        kb.emit("dve", lambda: nc_.vector.tensor_tensor_scan(out=G[:], data0=P("rmask"), data1=lw[:], initial=0.0, op0=ALU.mult, op1=ALU.add),
                                [G[:]], [P("rmask"), lw[:]])
                        if d == 1:
                            for tt in range(8):
                                tsl = slice(tt * 128, (tt + 1) * 128)
                                kb.stt(T1[:, tsl], G[:, tsl], -1.0, lw[:, tsl], ALU.mult, ALU.add)
                                kb.ts("dve", T1[:, tsl], T1[:, tsl], G[:, tt * 128 + 127:tt * 128 + 128], ALU.add)
                            kb.copy("dve", G[:], T1[:])
                        kb.tt("dve", lw[:], G[:], lw[:], ALU.subtract)
                        for par in range(2):
                            for h in range(2):
                                kb.memset("dve", hdp[par][h]["nU"][:], 0.0)
                        order = []
                        for si, (s0, Ls) in enumerate(SEQS[hf]):
                            tiles = list(range(s0 // 128, (s0 + Ls) // 128))
                            if d == 1:
                                tiles = tiles[::-1]
                            for k_, tt in enumerate(tiles):
                                order.append((si, tt, k_ == 0, k_ == len(tiles) - 1))

                        def prep_gen(idx, d=d, c=c, m_ts=m_ts, m_st=m_st, m_in=m_in):
                            si, tt, first, last = order[idx]
                            tl = tlp[idx % 2]
                            hd = hdp[idx % 2]
                            pCt = pCp[idx % 2]
                            tsl = slice(tt * 128, (tt + 1) * 128)
                            e_end = tt * 128 + (127 if d == 0 else 0)
                            gtot = G[:, e_end:e_end + 1]
                            kb.act(tl["eX"][:], G[:, tsl], AF.Exp)
                            kb.tt("pool", tl["Rt"][:], rF[:, tsl], tl["eX"][:], ALU.mult)
                            kb.act(tl["eX2"][:], lw[:, tsl], AF.Exp)
                            kb.tt("dve", tl["Ct"][:], kkF[:, tsl], tl["eX2"][:], ALU.mult)
                            yield
                            kb.act(tl["eX"][:], G[:, tsl], AF.Exp, scale=-1.0)
                            kb.tt("dve", tl["kinv"][:], kd[:, tsl], tl["eX"][:], ALU.mult)
                            kb.tt("dve", tl["binv"][:], bF[:, tsl], tl["eX"][:], ALU.mult)
                            kb.act(tl["eX2"][:], G[:, tsl], AF.Exp, bias=gtot, scale=-1.0)
                            kb.tt("pool", tl["khat"][:], kd[:, tsl], tl["eX2"][:], ALU.mult)
                            kb.tt("pool", tl["bhat"][:], bF[:, tsl], tl["eX2"][:], ALU.mult)
                            kb.act(pCt[:], gtot, AF.Exp)
                            yield

                            def pre_head(h):
                                H = hd[h]
                                T = hT[h]
                                hm = hmask[:, h:h + 1]
                                kb.ts("dve", T["Ctm"][:], tl["Ct"][:], hm, ALU.mult)
                                kb.ts("pool", T["Rtm"][:], tl["Rt"][:], hm, ALU.mult, 0.0, ALU.add)
                                p_ = pq()
                                kb.mm(p_, T["Ctm"][:], tl["binv"][:])
                                kb.stt(T["N"][:], p_, -1.0, m_ts, ALU.mult, ALU.mult)
                                p_ = pq()
                                kb.mm(p_, tl["binv"][:], T["Ctm"][:])
                                kb.stt(T["NT"][:], p_, -1.0, m_st, ALU.mult, ALU.mult)
                                yield
                                p_ = pq()
                                kb.mm(p_, tl["kinv"][:], T["Ctm"][:])
                                kb.tt("dve", H["BmT"][:], p_, m_st, ALU.mult)
                                p_ = pq()
                                kb.mm(p_, tl["kinv"][:], T["Rtm"][:])
                                kb.tt("dve", H["QKT"][:], p_, m_in, ALU.mult)
                                p_ = pq()
                                kb.mm(p_, tl["binv"][:], T["Rtm"][:])
                                kb.tt("dve", H["QBT"][:], p_, m_in, ALU.mult)
                                yield
                                for mi, (src, dstn) in enumerate([(vF[:, tsl], "vT"), (tl["khat"][:], "khT"), (tl["bhat"][:], "bhT")]):
                                    kb.ts("pool", T["msk"][mi][:], src, hm, ALU.mult, 0.0, ALU.add)
                                    p_ = pq()
                                    kb.tr(p_, T["msk"][mi][:], ident)
                                    kb.copy("act", H[dstn][:], p_)
                                yield
                                yield from tri_inv_T(T["inv"], T["N"], T["NT"])
                                kb.copy("act", H["X"][:], T["inv"]["X"])
                                yield

                            gens = [pre_head(0), pre_head(1)]
                            while gens:
                                for g_ in list(gens):
                                    try:
                                        next(g_)
                                    except StopIteration:
                                        gens.remove(g_)
                                yield

                        def seq_gen(idx, d=d, c=c):
                            si, tt, first, last = order[idx]
                            tl = tlp[idx % 2]
                            hd = hdp[idx % 2]
                            pCt = pCp[idx % 2]
                            tsl = slice(tt * 128, (tt + 1) * 128)
                            if first:
                                kb.memset("dve", Mp[:], 0.0)
                                if hf == 1:
                                    for h in range(2):
                                        kb.dma("sp", Mp[h * 64:(h + 1) * 64, h * 64:(h + 1) * 64], srwkv_in[d, 2 * c + h, :, :])
                                kb.copy("dve", Mpb[:], Mp[:])
                            pz = pq()
                            kb.mm(pz, tl["Ct"][:], Mpb[:], start=True, stop=False)
                            kb.mm(pz, hd[0]["BmT"][:], hd[0]["vT"][:], start=False, stop=False)
                            kb.mm(pz, hd[1]["BmT"][:], hd[1]["vT"][:], start=False, stop=True)
                            kb.copy("act", tl["Z"][:], pz)
                            yield
                            for h in range(2):
                                p_ = pq()
                                kb.mm(p_, hd[h]["X"][:], tl["Z"][:])
                                kb.act(hd[h]["nU"][:, h * 64:(h + 1) * 64], p_[:, h * 64:(h + 1) * 64], AF.Identity, scale=-1.0)
                            yield
                            py = pq()
                            kb.mm(py, Mpb[:], tl["Rt"][:], start=True, stop=False)
                            for h in range(2):
                                kb.mm(py, hd[h]["vT"][:], hd[h]["QKT"][:], start=False, stop=False)
                                kb.mm(py, hd[h]["nU"][:], hd[h]["QBT"][:], start=False, stop=(h == 1))
                            if d == 0:
                                kb.copy("act", yacc[:, tsl], py)
                            else:
                                kb.tt("dve", yacc[:, tsl], py, yacc[:, tsl], ALU.add)
                            pm_ = pq()
                            for h in range(2):
                                kb.mm(pm_, hd[h]["khT"][:], hd[h]["vT"][:], start=(h == 0), stop=False)
                                kb.mm(pm_, hd[h]["bhT"][:], hd[h]["nU"][:], start=False, stop=(h == 1))
                            kb.stt(Mp[:], Mp[:], pCt[:, 0:1], pm_, ALU.mult, ALU.add)
                            kb.copy("act", Mpb[:], Mp[:])
                            if last and hf == 0:
                                for h in range(2):
                                    kb.dma("sp", srwkv_out[si, d, 2 * c + h, :, :], Mp[h * 64:(h + 1) * 64, h * 64:(h + 1) * 64])
                            yield

                        interleave([prep_gen(0)])
                        for idx in range(len(order)):
                            gl = [seq_gen(idx)]
                            if idx + 1 < len(order):
                                gl.append(prep_gen(idx + 1))
                            interleave(gl)
                    for nt in range(2):
                        sl = slice(nt * 512, (nt + 1) * 512)
                        pst = pbank[nt]
                        kb.mm(pst[:], bo, yacc[:, sl])
                        kb.stt(yacc[:, sl], pst[:], -1.0 / 64, yacc[:, sl], ALU.mult, ALU.add)
                        kb.act(sq[nt][:], yacc[:, sl], AF.Square)
                        pv = pbank[2 + nt]
                        kb.mm(pv[:], bo, sq[nt][:])
                        kb.act(T1[:, sl], pv[:], AF.Sqrt, bias=64e-5, scale=1.0 / 64)
                    kb.recip(T1[:], T1[:])
                    kb.stt(yacc[:], yacc[:], P("rlnw")[:, c:c + 1], T1[:], ALU.mult, ALU.mult)
                    kb.ts("dve", yacc[:], yacc[:], P("rlnb")[:, c:c + 1], ALU.add)
                    kb.ts("dve", asum[:], asum[:], 0.5, ALU.mult)
                    kb.ts("dve", asum[:], asum[:], P("rka")[:, c:c + 1], ALU.mult, omka[:, c:c + 1], ALU.add)
                    kb.tt("dve", asum[:], asum[:], kF0[:], ALU.mult)
                    kb.stt(asum[:], asum[:], P("rrk")[:, c:c + 1], rF[:], ALU.mult, ALU.mult)
                    for nt in range(2):
                        sl = slice(nt * 512, (nt + 1) * 512)
                        pb_ = pbank[nt]
                        kb.mm(pb_[:], bo, asum[:, sl])
                        kb.tt("dve", T1[:, sl], pb_[:], vF[:, sl], ALU.mult)
                        kb.tt("dve", yacc[:, sl], yacc[:, sl], T1[:, sl], ALU.add)
                        pg_ = pbank[2 + nt]
                        kb.mm(pg_[:], gup[:, c * 128:(c + 1) * 128], sgd[:, sl])
                        kb.tt("dve", mch[:, sl], pg_[:], yacc[:, sl], ALU.mult)
                    wo_apply(c, mch)
                kb.barrier()

        def mla(hf, Hb, wo_apply):
            NK = 1024 if hf == 0 else 1280
            with contextlib.ExitStack() as ph:
                f = lambda name, shape, dt=F32: sbx(ph, "m_" + name, shape, dt)
                wch = [f("wch%d" % i, [128, 8, 128], BF16) for i in range(2)]
                wuq = f("wuq", [128, 3, 768], BF16)
                wukv = f("wukv", [128, 2, 1024], BF16)
                pqF = f("pqF", [128, 3, HALF])
                qnF = f("qnF", [128, 3, HALF], BF16)
                ckvF = f("ckvF", [128, 2, HALF])
                ckvB = f("ckvB", [128, 2, 1280], BF16)
                kpe96 = f("kpe96", [128, 1280])
                kpeB = f("kpeB", [128, 1280], BF16)
                rq = f("rq", [128, HALF])
                Ve = f("Ve", [128, 10, 128], BF16)
                Vo = f("Vo", [128, 10, 128], BF16)
                qTf = f("qTf", [128, HALF])
                qTb = [f("qTb%d" % e, [128, HALF], BF16) for e in range(2)]
                KTb = [f("KTb%d" % e, [128, 1280], BF16) for e in range(2)]
                PT = [f("PT%d" % i, [128, 512], BF16) for i in range(2)]
                oE = f("oE", [128, 128], BF16)
                oO = f("oO", [128, 128], BF16)
                rs_ = f("rs", [128, 512])
                mch = f("mch", [128, HALF], BF16)
                t1, t2 = f("t1", [128, 512]), f("t2", [128, 512])
                cosT, sinT = f("cosT", [128, HALF]), f("sinT", [128, HALF])
                kb.copy("dve", oE[:], C("onesE"))
                kb.copy("dve", oO[:], C("onesO"))
                kb.memset("dve", kpe96[:], 0.0)
                kb.memset("dve", qTf[:], 0.0)
                kb.dma("pool", wuq[:], mla_w_uq[0].rearrange("(k p) c -> p k c", p=128))
                kb.dma("pool", wukv[:], mla_w_ukv[0].rearrange("(k p) c -> p k c", p=128))
                if hf == 1:
                    kb.dma("sp", cosT[:], ropecs[0, :, :])
                    kb.dma("sp", sinT[:], ropecs[1, :, :])
                wi = [0]

                def project(c0, M, dst_fn):
                    wt = wch[wi[0] % 2]
                    wi[0] += 1
                    kb.dma("pool", wt[:, :, 0:M], cd_w_in[0, :, c0:c0 + M].rearrange("(k p) c -> p k c", p=128))
                    for nt in range(2):
                        pp = pbank[4 + nt]
                        for k in range(8):
                            kb.mm(pp[0:M, :], wt[:, k, 0:M], Hb[:, k, nt * 512:(nt + 1) * 512], start=(k == 0), stop=(k == 7))
                        dst_fn(nt, pp)

                for j in range(3):
                    project(j * 128, 128, lambda nt, pp, j=j: kb.copy("act", pqF[:, j, nt * 512:(nt + 1) * 512], pp[:]))
                for j in range(2):
                    project(384 + j * 128, 128, lambda nt, pp, j=j: kb.copy("act", ckvF[:, j, nt * 512:(nt + 1) * 512], pp[:]))
                project(576, 96, lambda nt, pp: kb.copy("act", kpe96[64:96, nt * 512:(nt + 1) * 512], pp[64:96, :]))

                def rmsn(src, nj, gname, outs):
                    for nt in range(2):
                        sl = slice(nt * 512, (nt + 1) * 512)
                        pst = pbank[4 + nt]
                        for j in range(nj):
                            kb.act(sq[j % 2][:], src[:, j, sl], AF.Square)
                            kb.mm(pst[:], ones[:], sq[j % 2][:], start=(j == 0), stop=(j == nj - 1))
                        kb.act(rq[:, sl], pst[:], AF.Sqrt, bias=EPS, scale=1.0 / (nj * 128))
                    kb.recip(rq[:], rq[:])
                    for j in range(nj):
                        for o in outs:
                            kb.stt(o(j), src[:, j, :], P(gname)[:, j:j + 1], rq[:], ALU.mult, ALU.mult)

                rmsn(pqF, 3, "mqn", [lambda j: qnF[:, j, :]])
                rmsn(ckvF, 2, "mkvn", [lambda j: ckvB[:, j, 0:HALF], lambda j: ckvF[:, j, :]])
                if hf == 0:
                    for j in range(2):
                        kb.dma("sp", ckv_out[j * 128:(j + 1) * 128, :], ckvF[:, j, :])
                    kb.dma("sp", kpe_out[:, :], kpe96[64:96, 0:HALF])
                    kb.copy("dve", kpeB[64:96, 0:HALF], kpe96[64:96, 0:HALF])
                else:
                    for j in range(2):
                        kb.dma("pool", ckvB[:, j, HALF:1280], cckvT[j * 128:(j + 1) * 128, :])
                    kb.dma("sp", kpe96[64:96, HALF:1280], ckpeT[:, :])
                    kb.copy("dve", kpeB[64:96, HALF:1280], kpe96[64:96, HALF:1280])

                if stage == 1:
                    kb.barrier()
                    return

                def rope(src, dstb):
                    for nt in range(2):
                        sl = slice(nt * 512, (nt + 1) * 512)
                        pj = pbank[4 + nt]
                        kb.mm(pj[0:96, :], C("jpad")[0:96, 0:96], src[0:96, sl])
                        kb.tt("dve", t1[64:96, :], src[64:96, sl], cosT[64:96, sl], ALU.mult)
                        kb.tt("dve", t2[64:96, :], pj[64:96, :], sinT[64:96, sl], ALU.mult)
                        kb.tt("dve", dstb[64:96, sl], t1[64:96, :], t2[64:96, :], ALU.add)

                if hf == 1:
                    rope(kpe96, kpeB)
                if stage == 2:
                    kb.barrier()
                    return
                kb.memset("dve", Ve[:].rearrange("p a b -> p (a b)"), 0.0)
                kb.memset("dve", Vo[:].rearrange("p a b -> p (a b)"), 0.0)
                scale = 96.0 ** -0.5
                if hf == 0:
                    qranges = [(si * 256, 256, [2 * si, 2 * si + 1]) for si in range(4)]
                else:
                    qranges = [(nt * 512, 512, list(range(10))) for nt in range(2)]
                pti = [0]
                for c in range(4):
                    for kt in range(NK // 128):
                        pv = pbank[4 + kt % 2]
                        for e in range(2):
                            v0 = (2 * c + e) * 128 + 64
                            for k in range(2):
                                kb.mm(pv[:, e * 64:(e + 1) * 64], ckvB[:, k, kt * 128:(kt + 1) * 128], wukv[:, k, v0:v0 + 64], start=(k == 0), stop=(k == 1))
                        kb.copy("act", Ve[:, kt, 0:64], pv[:, 0:64])
                        kb.copy("dve", Vo[:, kt, 64:128], pv[:, 64:128])
                    if stage == 31:
                        continue
                    for e in range(2):
                        h = 2 * c + e
                        for nt in range(2):
                            sl = slice(nt * 512, (nt + 1) * 512)
                            pqh = pbank[4 + nt]
                            for k in range(3):
                                kb.mm(pqh[0:96, :], wuq[:, k, h * 96:(h + 1) * 96], qnF[:, k, sl], start=(k == 0), stop=(k == 2))
                            kb.copy("act", qTb[e][0:64, sl], pqh[0:64, :])
                            if hf == 1:
                                kb.copy("dve", qTf[64:96, sl], pqh[64:96, :])
                            else:
                                kb.copy("dve", qTb[e][64:96, sl], pqh[64:96, :])
                        if hf == 1:
                            rope(qTf, qTb[e])
                        if stage == 32:
                            continue
                        for k0 in range(0, NK, 512):
                            n = min(512, NK - k0)
                            pk = pbank[4 + (k0 // 512) % 2]
                            for k in range(2):
                                kb.mm(pk[:, 0:n], wukv[:, k, h * 128:(h + 1) * 128], ckvB[:, k, k0:k0 + n], start=(k == 0), stop=(k == 1))
                            kb.copy("act", KTb[e][0:64, k0:k0 + n], pk[0:64, 0:n])
                        kb.copy("dve", KTb[e][64:96, 0:NK], kpeB[64:96, 0:NK])
                    if stage == 3:
                        continue
                    for (q0, nq, kts) in qranges:
                        po, psm = pbank[2], pbank[3]
                        first = True
                        for e in range(2):
                            Vsrc = Ve if e == 0 else Vo
                            osrc = oE if e == 0 else oO
                            for ki, kt in enumerate(kts):
                                ps_ = pbank[pti[0] % 2]
                                pt = PT[pti[0] % 2]
                                pti[0] += 1
                                ksl = slice(kt * 128, (kt + 1) * 128)
                                kb.mm(ps_[:, 0:nq], KTb[e][0:96, ksl], qTb[e][0:96, q0:q0 + nq])
                                kb.act(pt[:, 0:nq], ps_[:, 0:nq], AF.Exp, scale=scale)
                                last = (e == 1 and ki == len(kts) - 1)
                                kb.mm(po[:, 0:nq], Vsrc[:, kt, :], pt[:, 0:nq], start=first, stop=last)
                                kb.mm(psm[:, 0:nq], osrc[:], pt[:, 0:nq], start=first, stop=last)
                                first = False
                        kb.recip(rs_[:, 0:nq], psm[:, 0:nq])
                        kb.tt("dve", mch[:, q0:q0 + nq], po[:, 0:nq], rs_[:, 0:nq], ALU.mult)
                    wo_apply(c, mch)
                kb.barrier()

        def hyena(hf, Hb, wo_apply):
            Ls = 256 if hf == 0 else 1024
            nT = Ls // 128
            with contextlib.ExitStack() as ph:
                f = lambda name, shape, dt=F32: sbx(ph, "y_" + name, shape, dt)
                wch = [f("wch%d" % i, [128, 8, 128], BF16) for i in range(1)]
                vF, x1F, x2F, T1 = f("vF", [128, HALF]), f("x1F", [128, HALF]), f("x2F", [128, HALF]), f("T1", [128, HALF])
                fwd = f("fwd", [128, nT, 2, Ls], BF16)
                invp = [f("invp%d" % i, [128, 2, Ls], BF16) for i in range(2)]
                w3 = f("w3", [128, 2048], BF16)
                featT = x1F[:, 0:Ls]
                h1s, h2s = T1[:, 0:Ls], x2F[:, 0:Ls]
                h2b = f("h2b", [128, Ls], BF16)
                win = f("win", [128, nT, 128])
                fr3, fb3 = f("fr3", [128, 2]), f("fb3", [128, 2])
                hsum, hdif = f("hsum", [128, nT, 128], BF16), f("hdif", [128, nT, 128], BF16)
                Hc, Hs = f("Hc", [128, nT, 128]), f("Hs", [128, nT, 128])
                zT = f("zT", [128, nT, 128], BF16)
                Yc, Ys = f("Yc", [128, nT, 128], BF16), f("Ys", [128, nT, 128], BF16)
                ta, tb, tc_ = f("ta", [128, 512]), f("tb", [128, 512]), f("tc", [128, 512])
                mch = f("mch", [128, HALF], BF16)
                for i in range(2):
                    kb.dma("sp", fwd[:, :, i, :], dftd[Ls][i].rearrange("(st p) f -> p st f", p=128))
                kb.dma("pool", w3[0:64, :], hy_w3[0, :, :])
                kb.dma("sp", featT, featd[Ls][:, :])
                kb.ts("dve", fr3[0:64, :], P("hfreq")[0:64, :], 1.0 / 3, ALU.mult)
                kb.tt("dve", fb3[0:64, 0:1], fr3[0:64, 0:1], P("hb1")[0:64, :], ALU.mult)
                kb.tt("dve", fb3[0:64, 1:2], fr3[0:64, 1:2], P("hb2")[0:64, :], ALU.mult)

                def sin3(dst, pin, li, n):
                    kb.act(ta[0:64, 0:n], pin, AF.Sin, bias=fb3[0:64, li:li + 1], scale=fr3[0:64, li:li + 1])
                    kb.tt("dve", tb[0:64, 0:n], ta[0:64, 0:n], ta[0:64, 0:n], ALU.mult)
                    kb.ts("dve", tb[0:64, 0:n], tb[0:64, 0:n], -4.0, ALU.mult, 3.0, ALU.add)
                    kb.tt("dve", dst, ta[0:64, 0:n], tb[0:64, 0:n], ALU.mult)

                for c0 in range(0, Ls, 512):
                    n = min(512, Ls - c0)
                    p1 = pbank[4]
                    kb.mm(p1[0:64, 0:n], P("hw1")[0:64, :], featT[0:64, c0:c0 + n])
                    sin3(h1s[0:64, c0:c0 + n], p1[0:64, 0:n], 0, n)
                    p2 = pbank[5]
                    kb.mm(p2[0:64, 0:n], P("hw2")[0:64, :], h1s[0:64, c0:c0 + n])
                    sin3(h2s[0:64, c0:c0 + n], p2[0:64, 0:n], 1, n)
                kb.copy("dve", h2b[0:64, :], h2s[0:64, :])
                wi = [0]
                hcv = P("hconv").rearrange("p (t c) -> p t c", t=3)
                ipi = [0]
                for cc in range(4):
                    kb.dma("sp", win[:], wind[Ls][:, cc * 128:(cc + 1) * 128].rearrange("(t p) c -> p t c", p=128))
                    for qi, dst in enumerate([vF, x1F, x2F]):
                        wt = wch[0]
                        wi[0] += 1
                        c0 = 672 + qi * 512 + cc * 128
                        kb.dma("pool", wt[:], cd_w_in[0, :, c0:c0 + 128].rearrange("(k p) c -> p k c", p=128))
                        for nt in range(2):
                            pp = pbank[4 + nt]
                            for k in range(8):
                                kb.mm(pp[:], wt[:, k, :], Hb[:, k, nt * 512:(nt + 1) * 512], start=(k == 0), stop=(k == 7))
                            kb.copy("act", T1[:, nt * 512:(nt + 1) * 512], pp[:])
                        ci = qi * 4 + cc
                        kb.ts("dve", dst[:], T1[:], hcv[:, 1, ci:ci + 1], ALU.mult)
                        for (s0, Lq) in SEQS[hf]:
                            kb.stt(dst[:, s0 + 1:s0 + Lq], T1[:, s0:s0 + Lq - 1], hcv[:, 0, ci:ci + 1], dst[:, s0 + 1:s0 + Lq], ALU.mult, ALU.add)
                            kb.stt(dst[:, s0:s0 + Lq - 1], T1[:, s0 + 1:s0 + Lq], hcv[:, 2, ci:ci + 1], dst[:, s0:s0 + Lq - 1], ALU.mult, ALU.add)
                    z = vF
                    for n in range(2):
                        gate = x1F if n == 0 else x2F
                        bcol = P("hbias").rearrange("p (n c) -> p n c", n=2)[:, n, cc:cc + 1]
                        for dt in range(nT):
                            pt_ = pbank[4 + dt % 2]
                            for di in range(2):
                                w0 = (n * 2 + di) * 512 + cc * 128
                                kb.mm(pt_[:, di * 128:(di + 1) * 128], h2b[0:64, dt * 128:(dt + 1) * 128], w3[0:64, w0:w0 + 128])
                            kb.tt("dve", ta[:, 0:128], pt_[:, 0:128], win[:, dt, :], ALU.mult)
                            kb.tt("dve", tb[:, 0:128], pt_[:, 128:256], win[:, dt, :], ALU.mult)
                            if dt == 0:
                                kb.ts("dve", tb[:, 0:128], tb[:, 0:128], P("nz0"), ALU.mult)
                            kb.tt("dve", hsum[:, dt, :], ta[:, 0:128], tb[:, 0:128], ALU.add)
                            kb.tt("dve", hdif[:, dt, :], ta[:, 0:128], tb[:, 0:128], ALU.subtract)
                        for ft in range(nT):
                            pc_ = pbank[4 + ft % 2]
                            for dt in range(nT):
                                kb.mm(pc_[:, 0:128], fwd[:, dt, 0, ft * 128:(ft + 1) * 128], hsum[:, dt, :], start=(dt == 0), stop=(dt == nT - 1))
                            kb.copy("act", Hc[:, ft, :], pc_[:, 0:128])
                            for dt in range(nT):
                                kb.mm(pc_[:, 128:256], fwd[:, dt, 1, ft * 128:(ft + 1) * 128], hdif[:, dt, :], start=(dt == 0), stop=(dt == nT - 1))
                            kb.copy("act", Hs[:, ft, :], pc_[:, 128:256])
                        for (s0, Lq) in SEQS[hf]:
                            for st in range(nT):
                                ptr = pbank[4 + st % 2]
                                kb.tr(ptr[:, 0:128], z[:, s0 + st * 128:s0 + (st + 1) * 128], ident)
                                kb.copy("act", zT[:, st, :], ptr[:, 0:128])
                            for ft in range(nT):
                                pzc, pzs = pbank[2], pbank[3]
                                for st in range(nT):
                                    kb.mm(pzc[:, 0:128], fwd[:, st, 0, ft * 128:(ft + 1) * 128], zT[:, st, :], start=(st == 0), stop=(st == nT - 1))
                                for st in range(nT):
                                    kb.mm(pzs[:, 0:128], fwd[:, st, 1, ft * 128:(ft + 1) * 128], zT[:, st, :], start=(st == 0), stop=(st == nT - 1))
                                kb.tt("dve", ta[:, 0:128], pzc[:, 0:128], Hc[:, ft, :], ALU.mult)
                                kb.tt("dve", tb[:, 0:128], pzs[:, 0:128], Hs[:, ft, :], ALU.mult)
                                kb.tt("dve", Yc[:, ft, :], ta[:, 0:128], tb[:, 0:128], ALU.subtract)
                                kb.tt("dve", ta[:, 0:128], pzc[:, 0:128], Hs[:, ft, :], ALU.mult)
                                kb.tt("dve", tb[:, 0:128], pzs[:, 0:128], Hc[:, ft, :], ALU.mult)
                                kb.tt("dve", Ys[:, ft, :], ta[:, 0:128], tb[:, 0:128], ALU.add)
                            blocks = [(b0, min(512, Lq - b0)) for b0 in range(0, Lq, 512)]
                            pys = [pbank[0], pbank[1]]
                            for ft in range(nT):
                                ip = invp[ipi[0] % 2]
                                ipi[0] += 1
                                for i in range(2):
                                    kb.dma("sp", ip[:, i, :], dftd[Ls][2 + i, ft * 128:(ft + 1) * 128, :])
                                for bi, (b0, bn) in enumerate(blocks):
                                    kb.mm(pys[bi][:, 0:bn], Yc[:, ft, :], ip[:, 0, b0:b0 + bn], start=(ft == 0), stop=False)
                                    kb.mm(pys[bi][:, 0:bn], Ys[:, ft, :], ip[:, 1, b0:b0 + bn], start=False, stop=(ft == nT - 1))
                            for bi, (b0, bn) in enumerate(blocks):
                                zs = z[:, s0 + b0:s0 + b0 + bn]
                                kb.ts("dve", tc_[:, 0:bn], zs, bcol, ALU.mult)
                                kb.stt(tc_[:, 0:bn], pys[bi][:, 0:bn], 1.0 / Lq, tc_[:, 0:bn], ALU.mult, ALU.add)
                                kb.tt("dve", zs, tc_[:, 0:bn], gate[:, s0 + b0:s0 + b0 + bn], ALU.mult)
                    kb.copy("dve", mch[:], z[:])
                    wo_apply(4 + cc, mch)
                kb.barrier()

        def mixer_phase(L):
            with contextlib.ExitStack() as ph:
                Hb = sbx(ph, "Hbm", [128, 8, HALF], BF16)
                wo = [sbx(ph, "wo%d" % i, [128, D], BF16) for i in range(2)]
                woi = [0]
                for hf in range(2):
                    rms_stats(hf)
                    make_coefs(L, hf, 1, "nmx_%d" % L)
                    make_H(Hb, hf)

                    def wo_apply(j, mc, hf=hf):
                        w = wo[woi[0] % 2]
                        woi[0] += 1
                        kb.dma("pool", w[:], w_out_d[L, j * 128:(j + 1) * 128, :])
                        for m in range(8):
                            for nt in range(2):
                                po = pbank[6 + (m * 2 + nt) % 2]
                                kb.mm(po[:], w[:, m * 128:(m + 1) * 128], mc[:, nt * 512:(nt + 1) * 512])
                                t0 = hf * HALF + nt * 512
                                kb.stt(X[:, m, t0:t0 + 512], po[:], coef[:, 16 + m:17 + m], X[:, m, t0:t0 + 512], ALU.mult, ALU.add)

                    if L == 0:
                        if test in (None, "rwkv", "l0"):
                            rwkv(hf, Hb, wo_apply)
                        if test in (None, "gdn", "l0"):
                            gdn(hf, Hb, wo_apply)
                    else:
                        if test in (None, "mla", "l1"):
                            mla(hf, Hb, wo_apply)
                        if test in (None, "hy", "l1"):
                            hyena(hf, Hb, wo_apply)
                kb.barrier()

        def final_norm_and_store():
            for hf in range(2):
                rms_stats(hf)
                for j in range(8):
                    for nt in range(2):
                        t0 = hf * HALF + nt * 512
                        tf = tmpf[(j * 2 + nt) % 2]
                        kb.stt(tf[:], X[:, j, t0:t0 + 512], P("fnorm")[:, j:j + 1], rstd[:, nt * 512:(nt + 1) * 512], ALU.mult, ALU.mult)
                        kb.dma("sp", yT[j * 128:(j + 1) * 128, t0:t0 + 512], tf[:])

        if test in ("gdn", "rwkv", "l0"):
            mixer_phase(0)
        elif test in ("mla", "hy", "l1"):
            mixer_phase(1)
        else:
            for L in range(2):
                ffn_phase(L, 0)
                mixer_phase(L)
                ffn_phase(L, 1)
        final_norm_and_store()
        kb.barrier()
        print("instructions:", kb.ninst)
    return nc


def prep_inputs(inp):
    inp = {k: np.asarray(v) for k, v in inp.items()}
    maps = []
    offs = None
    cnames, carr = make_consts()
    ropecs = rope_tables()
    hc = {L: hyena_consts(L) for L in (256, 1024)}
    for c in range(NCORE):
        xp = inp["x_prompt"][4 * c:4 * c + 4].reshape(HALF, D)
        xs = inp["x_sample"][c]
        xT = np.ascontiguousarray(np.concatenate([xp, xs], 0).T)
        Pk = pack_params(inp, c)
        prm = Pk.build()
        offs = Pk.off
        m = {"xT": xT, "prm": prm, "cst": carr, "w_mod": inp["w_mod"],
             "ffn1_w_gu": inp["ffn1_w_gu"], "ffn2_w_gu": inp["ffn2_w_gu"],
             "ffn1_w_down": inp["ffn1_w_down"], "ffn2_w_down": inp["ffn2_w_down"],
             "w_out": inp["w_out"], "ab_w_in": inp["ab_w_in"],
             "sgdn_in": np.ascontiguousarray(inp["state_gdn"][c, 0]),
             "srwkv_in": np.ascontiguousarray(inp["state_rwkv"][c, 0].transpose(0, 1, 3, 2)),
             "rwkv_w_up": inp["rwkv_w_up"], "rwkv_a_up": inp["rwkv_a_up"], "rwkv_g_up": inp["rwkv_g_up"],
             "cd_w_in": inp["cd_w_in"], "mla_w_uq": inp["mla_w_uq"], "mla_w_ukv": inp["mla_w_ukv"],
             "cckvT": np.ascontiguousarray(inp["cache_ckv"][c, 0].T), "ckpeT": np.ascontiguousarray(inp["cache_kpe"][c, 0].T),
             "ropecs": ropecs, "hy_w1": inp["hy_w1"], "hy_w2": inp["hy_w2"], "hy_w3": inp["hy_w3"],
             "dft256": hc[256][0], "dft1024": hc[1024][0], "feat256": hc[256][1], "feat1024": hc[1024][1],
             "win256": hc[256][2], "win1024": hc[1024][2]}
        maps.append(m)
    return maps, offs


def kernel(**inputs):
    maps, offs = prep_inputs(inputs)
    NP = maps[0]["prm"].shape[1]
    nc = build_program(offs, NP)
    res = run_bass_kernel_spmd(nc, maps, core_ids=list(range(NCORE)))
    yp = np.zeros((32, 256, D), np.float32)
    ys = np.zeros((8, 1024, D), np.float32)
    srw = np.zeros((32, 1, 2, 8, 64, 64), np.float32)
    sgd = np.zeros((32, 1, 2, 4, 128, 128), np.float32)
    ckv = np.zeros((32, 1, 256, 256), np.float32)
    kpe = np.zeros((32, 1, 256, 32), np.float32)
    for c in range(NCORE):
        r = res.results[c]
        yT = r["yT"]
        yp[4 * c:4 * c + 4] = yT[:, :HALF].T.reshape(4, 256, D)
        ys[c] = yT[:, HALF:].T
        srw[4 * c:4 * c + 4, 0] = np.asarray(r["srwkv_out"]).transpose(0, 1, 2, 4, 3)
        sgd[4 * c:4 * c + 4, 0] = np.asarray(r["sgdn_out"])
        ckv[4 * c:4 * c + 4, 0] = np.asarray(r["ckv_out"]).T.reshape(4, 256, 256)
        kpe[4 * c:4 * c + 4, 0] = np.asarray(r["kpe_out"]).T.reshape(4, 256, 32)
    return yp, ys, srw, sgd, ckv, kpe
```

```python
import contextlib
import numpy as np
import concourse.bass as bass
import concourse.mybir as mybir
from concourse.bass_utils import run_bass_kernel_spmd

F32 = mybir.dt.float32
BF16 = mybir.dt.bfloat16
F32R = mybir.dt.float32r
FP16 = mybir.dt.float16
AF = mybir.ActivationFunctionType
ALU = mybir.AluOpType

NCORE = 8
D = 1024
DFF = 2816
TOK = 2048
HALF = 1024
EPS = 1e-6


class KB:
    def __init__(self, nc, es, n_dma_sems=24):
        self.nc = nc
        self.es = es
        self.eng = {"pe": nc.tensor, "act": nc.scalar, "dve": nc.vector, "pool": nc.gpsimd, "sp": nc.sync}
        self.sem = {k: es.enter_context(nc.semaphore("sem_" + k)) for k in self.eng}
        self.cnt = {k: 0 for k in self.eng}
        self.seen = {k: {} for k in self.eng}
        self.dsem = [es.enter_context(nc.semaphore("dsem%d" % i)) for i in range(n_dma_sems)]
        self.dcnt = [0] * n_dma_sems
        self.drr = {"sp": 0, "pool": 0, "act": 0}
        self.dpool = {"sp": list(range(0, n_dma_sems // 2)), "act": list(range(0, n_dma_sems // 2)),
                      "pool": list(range(n_dma_sems // 2, n_dma_sems))}
        self.recs = {}
        self.semobj = {}
        for k in self.eng:
            self.semobj[k] = self.sem[k]
        for i, s in enumerate(self.dsem):
            self.semobj[("d", i)] = s
        self.ninst = 0
        self.rt = set()

    def R(self, ap):
        if ap is not None and not isinstance(ap, (int, float)) and ap.tensor.name in self.rt:
            return ap.bitcast(F32R)
        return ap

    @staticmethod
    def box(ap):
        t = ap.tensor
        dims = list(ap.ap)
        if type(t).__name__.startswith("DRam"):
            f0 = ap.offset
            f1 = f0 + sum((c - 1) * abs(s) for s, c in dims) + 1
            return (t.name, 0, 1, f0, f1)
        ps, pc = dims[0]
        if ps == 0:
            ps = 1 << 40
        if type(t).__name__.startswith("PSum"):
            return (t.name, 0, 128, 0, 1 << 30)
        p0 = ap.offset // ps
        f0 = ap.offset % ps
        f1 = f0 + sum((c - 1) * abs(s) for s, c in dims[1:]) + 1
        return (t.name, p0, p0 + pc, f0, f1)

    def _deps(self, b, write, deps, eng=None):
        name, p0, p1, f0, f1 = b
        lst = self.recs.get(name)
        if not lst:
            return
        psum = f1 == (1 << 30)
        for r in lst:
            if r[0] < p1 and p0 < r[1] and r[2] < f1 and f0 < r[3]:
                if write or r[6] or (psum and r[7] != eng):
                    k = r[4]
                    if deps.get(k, 0) < r[5]:
                        deps[k] = r[5]

    def _record(self, b, write, semkey, val, eng):
        name, p0, p1, f0, f1 = b
        lst = self.recs.setdefault(name, [])
        if write:
            lst[:] = [r for r in lst if not (p0 <= r[0] and r[1] <= p1 and f0 <= r[2] and r[3] <= f1)]
        else:
            lst[:] = [r for r in lst if not ((not r[6]) and r[7] == eng and r[4] == semkey
                                             and p0 <= r[0] and r[1] <= p1 and f0 <= r[2] and r[3] <= f1)]
        lst.append((p0, p1, f0, f1, semkey, val, write, eng))

    def _wait(self, e, deps):
        seen = self.seen[e]
        for k, v in deps.items():
            if e == "pe" and k == "pe":
                continue
            if seen.get(k, 0) < v:
                self.eng[e].wait_ge(self.semobj[k], v)
                seen[k] = v

    def emit(self, e, fn, outs, ins):
        deps = {}
        ob = [self.box(a) for a in outs]
        ib = [self.box(a) for a in ins if a is not None and not isinstance(a, (int, float))]
        for b in ib:
            self._deps(b, False, deps, e)
        for b in ob:
            self._deps(b, True, deps, e)
        self._wait(e, deps)
        inst = fn()
        self.cnt[e] += 1
        inst.then_inc(self.sem[e], 1)
        v = self.cnt[e]
        for b in ib:
            self._record(b, False, e, v, e)
        for b in ob:
            self._record(b, True, e, v, e)
        self.ninst += 1
        return inst

    def dma(self, q, out, in_):
        deps = {}
        ob = self.box(out)
        ib = self.box(in_)
        self._deps(ib, False, deps)
        self._deps(ob, True, deps)
        pl = self.dpool[q]
        i = pl[self.drr[q] % len(pl)]
        self.drr[q] += 1
        k = ("d", i)
        if self.dcnt[i] > 0:
            deps[k] = max(deps.get(k, 0), self.dcnt[i])
        self._wait(q, deps)
        inst = self.eng[q].dma_start(out=out, in_=in_)
        self.dcnt[i] += 16
        inst.then_inc(self.dsem[i], 16)
        self._record(ib, False, k, self.dcnt[i], "dma")
        self._record(ob, True, k, self.dcnt[i], "dma")
        self.ninst += 1

    def barrier(self, engines=None):
        engines = engines or list(self.eng)
        for e in engines:
            deps = {o: self.cnt[o] for o in self.eng if o != e and self.cnt[o] > 0}
            for i, c in enumerate(self.dcnt):
                if c > 0:
                    deps[("d", i)] = c
            self._wait(e, deps)

    def mm(self, out, lhsT, rhs, start=True, stop=True):
        nc = self.nc
        if lhsT.tensor.name in self.rt and rhs.tensor.name in self.rt:
            lhsT, rhs = lhsT.bitcast(F32R), rhs.bitcast(F32R)
        return self.emit("pe", lambda: nc.tensor.matmul(out, lhsT, rhs, start=start, stop=stop), [out], [lhsT, rhs])

    def tr(self, out, in_, ident):
        nc = self.nc
        return self.emit("pe", lambda: nc.tensor.transpose(out, in_, ident), [out], [in_, ident])

    def act(self, out, in_, func, bias=None, scale=1.0):
        nc = self.nc
        kw = {}
        if bias is not None:
            kw["bias"] = bias
        ins = [in_]
        if bias is not None and not isinstance(bias, (int, float)):
            ins.append(bias)
        if not isinstance(scale, (int, float)):
            ins.append(scale)
        out = self.R(out)
        return self.emit("act", lambda: nc.scalar.activation(out=out, in_=in_, func=func, scale=scale, **kw), [out], ins)

    def tt(self, e, out, in0, in1, op):
        eng = self.eng[e]
        out = self.R(out)
        return self.emit(e, lambda: eng.tensor_tensor(out=out, in0=in0, in1=in1, op=op), [out], [in0, in1])

    def ts(self, e, out, in0, s1, op0, s2=None, op1=None):
        eng = self.eng[e]
        out = self.R(out)
        ins = [in0] + [s for s in (s1, s2) if s is not None and not isinstance(s, (int, float))]
        if op1 is None:
            return self.emit(e, lambda: eng.tensor_scalar(out=out, in0=in0, scalar1=s1, scalar2=None, op0=op0), [out], ins)
        return self.emit(e, lambda: eng.tensor_scalar(out=out, in0=in0, scalar1=s1, scalar2=s2, op0=op0, op1=op1), [out], ins)

    def stt(self, out, in0, scalar, in1, op0, op1):
        nc = self.nc
        out = self.R(out)
        ins = [in0, in1] + ([scalar] if not isinstance(scalar, (int, float)) else [])
        return self.emit("dve", lambda: nc.vector.scalar_tensor_tensor(out=out, in0=in0, scalar=scalar, in1=in1, op0=op0, op1=op1), [out], ins)

    def copy(self, e, out, in_):
        eng = self.eng[e]
        out = self.R(out)
        if e == "act":
            return self.emit(e, lambda: eng.copy(out=out, in_=in_), [out], [in_])
        return self.emit(e, lambda: eng.tensor_copy(out=out, in_=in_), [out], [in_])

    def recip(self, out, in_):
        nc = self.nc
        return self.emit("dve", lambda: nc.vector.reciprocal(out=out, in_=in_), [out], [in_])

    def memset(self, e, out, val):
        eng = self.eng[e]
        if out.tensor.name in self.rt:
            return self.ts("dve", out, self.ones_ap, float(val), ALU.mult)
        return self.emit(e, lambda: eng.memset(out, val), [out], [])


def fm(v, nchunk):
    return np.ascontiguousarray(np.asarray(v, np.float32).reshape(nchunk, 128).T)


def rows(v):
    v = np.asarray(v, np.float32).reshape(1, -1)
    return np.ascontiguousarray(np.repeat(v, 128, axis=0))


class Pack:
    def __init__(self):
        self.cols = []
        self.off = {}
        self.n = 0

    def add(self, name, arr):
        arr = np.asarray(arr, np.float32)
        if arr.shape[0] < 128:
            arr = np.concatenate([arr, np.zeros((128 - arr.shape[0],) + arr.shape[1:], np.float32)], 0)
        arr = arr.reshape(128, -1)
        self.off[name] = (self.n, arr.shape[1])
        self.n += arr.shape[1]
        self.cols.append(arr)

    def build(self):
        return np.ascontiguousarray(np.concatenate(self.cols, 1))


def pack_params(inp, core):
    P = Pack()
    c = inp["c"][core]
    cc = np.stack([fm(inp["c_ctx"], 8), fm(c, 8)], axis=2)
    P.add("cT", cc)
    for i in range(2):
        P.add("bmod%d" % i, fm(inp["b_mod"][i], 72))
        P.add("nf1_%d" % i, fm(inp["norm_ffn1"][i], 8))
        P.add("nmx_%d" % i, fm(inp["norm_mix"][i], 8))
        P.add("nf2_%d" % i, fm(inp["norm_ffn2"][i], 8))
    P.add("fnorm", fm(inp["final_norm"], 8))
    P.add("gconv", np.stack([fm(inp["gdn_conv"][0][t], 12) for t in range(3)], axis=1))
    P.add("gnorm", np.asarray(inp["gdn_norm"][0], np.float32).reshape(128, 1))
    P.add("galog", rows(inp["gdn_a_log"][0]))
    P.add("gdtb", rows(inp["gdn_dt_bias"][0]))
    P.add("rmu", fm(inp["rwkv_mu"][0], 15))
    P.add("rw0", np.stack([fm(inp["rwkv_w0"][0][d], 4) for d in range(2)], axis=1))
    P.add("ra0", np.stack([fm(inp["rwkv_a0"][0][d], 4) for d in range(2)], axis=1))
    P.add("rkk", fm(inp["rwkv_k_k"][0], 4))
    P.add("rka", fm(inp["rwkv_k_a"][0], 4))
    P.add("rrk", fm(inp["rwkv_r_k"][0].reshape(-1), 4))
    P.add("rlnw", fm(inp["rwkv_ln_w"][0], 4))
    P.add("rlnb", fm(inp["rwkv_ln_b"][0], 4))
    P.add("mqn", fm(inp["mla_q_norm"][0], 3))
    P.add("mkvn", fm(inp["mla_kv_norm"][0], 2))
    P.add("hconv", np.stack([fm(inp["hy_conv"][0][t], 12) for t in range(3)], axis=1))
    P.add("hbias", np.stack([fm(inp["hy_bias"][0][n], 4) for n in range(2)], axis=1))
    P.add("hb1", np.asarray(inp["hy_b1"][0], np.float32).reshape(64, 1))
    P.add("hb2", np.asarray(inp["hy_b2"][0], np.float32).reshape(64, 1))
    P.add("hfreq", np.ascontiguousarray(np.asarray(inp["hy_freq"][0], np.float32).T))
    nz0 = np.ones((128, 1), np.float32)
    nz0[0, 0] = 0.0
    P.add("nz0", nz0)
    w1p = np.zeros((128, 64), np.float32)
    w1p[:33] = np.asarray(inp["hy_w1"][0], np.float32)
    P.add("hw1", w1p)
    P.add("hw2", np.asarray(inp["hy_w2"][0], np.float32))
    hm = np.zeros((128, 2), np.float32)
    hm[:64, 0] = 1.0
    hm[64:, 1] = 1.0
    P.add("hmask", hm)
    rm = np.ones((128, 1024), np.float32)
    rm[:, ::128] = 0.0
    P.add("rmask", rm)
    return P


def make_consts():
    p = np.arange(128)[:, None]
    f = np.arange(128)[None, :]
    ident = (p == f).astype(np.float32)
    le = (p <= f).astype(np.float32)
    lt = (p < f).astype(np.float32)
    ge = (p >= f).astype(np.float32)
    gt = (p > f).astype(np.float32)
    BIG = 30000.0
    bo = ((p // 64) == (f // 64)).astype(np.float32)
    jpad = np.zeros((128, 128), np.float32)
    for i in range(16):
        jpad[64 + 2 * i + 1, 64 + 2 * i] = -1.0
        jpad[64 + 2 * i, 64 + 2 * i + 1] = 1.0
    onesE = np.zeros((128, 128), np.float32)
    onesE[:, :64] = 1.0
    onesO = np.zeros((128, 128), np.float32)
    onesO[:, 64:] = 1.0
    b32 = ((p // 32) == (f // 32)).astype(np.float32)
    m1 = (((p // 64) == (f // 64)) & ((p // 32) != (f // 32))).astype(np.float32)
    m2 = ((p // 64) != (f // 64)).astype(np.float32)
    names = ["ident", "le", "lt", "ge", "gt", "pos_gt", "pos_lt", "neg_le", "neg_ge", "bo", "jpad", "onesE", "onesO", "b32", "m1", "m2"]
    arrs = [ident, le, lt, ge, gt, BIG * (1 - gt), BIG * (1 - lt), -BIG * (1 - le), -BIG * (1 - ge), bo, jpad, onesE, onesO, b32, m1, m2]
    return names, np.ascontiguousarray(np.concatenate(arrs, 1).astype(np.float32))


def rope_tables():
    L = 1024
    row = np.repeat(np.arange(L // 64), 64).astype(np.float32)
    col = (np.arange(L) % 64).astype(np.float32)
    n = 8
    inv = (10000.0 ** (-np.arange(n, dtype=np.float32) / n)).astype(np.float32)
    ang = np.concatenate([row[:, None] * inv, col[:, None] * inv], axis=-1)
    cs = np.zeros((2, 128, L), np.float32)
    for r in range(32):
        cs[0, 64 + r] = np.cos(ang[:, r // 2])
        cs[1, 64 + r] = np.sin(ang[:, r // 2])
    return cs


def hyena_consts(L):
    import ml_dtypes
    t = np.arange(L, dtype=np.float64)
    w = 2.0 * np.pi * (t + 0.5) / (2 * L)
    ph = np.outer(t, w)
    Cm = np.cos(ph)
    Sm = np.sin(ph)
    dft = np.stack([Cm, Sm, Cm.T, Sm.T]).astype(np.float32).astype(ml_dtypes.bfloat16)
    t32 = np.arange(L, dtype=np.float32)
    t_norm = t32 / max(L - 1, 1)
    bands = np.linspace(1e-4, 16 - 1, 16, dtype=np.float32)
    ang = (np.float32(2.0 * np.pi / L) * t32[:, None] * bands).astype(np.float32)
    feats = np.concatenate([t_norm[:, None], np.cos(ang), -np.sin(ang)], axis=-1).astype(np.float32)
    min_decay = np.log(1e-2) / 1.5
    max_decay = np.log(1e-2) / 0.3
    deltas = np.abs(np.linspace(min_decay, max_decay, 512, dtype=np.float32))
    window = np.exp(-t_norm[:, None] * deltas).astype(np.float32)
    featsT = np.zeros((128, L), np.float32)
    featsT[:33] = feats.T
    return dft, featsT, window


A_COLS = 1920
SEQS = {0: [(0, 256), (256, 256), (512, 256), (768, 256)], 1: [(0, 1024)]}


def build_program(offs, NP, test=None, stage=99):
    nc = bass.Bass("TRN2", target_bir_lowering=False)
    dr = {}

    def din(name, shape, dt=F32):
        dr[name] = nc.dram_tensor(name, list(shape), dt, kind="ExternalInput").ap()
        return dr[name]

    def dout(name, shape, dt=F32):
        dr[name] = nc.dram_tensor(name, list(shape), dt, kind="ExternalOutput").ap()
        return dr[name]

    cnames, carr = make_consts()
    xT = din("xT", [D, TOK])
    prm_d = din("prm", [128, NP])
    cst_d = din("cst", [128, carr.shape[1]])
    w_mod = din("w_mod", [2, D, 9 * D])
    ffn_gu = [din("ffn1_w_gu", [2, D, 2 * DFF]), din("ffn2_w_gu", [2, D, 2 * DFF])]
    ffn_dn = [din("ffn1_w_down", [2, DFF, D]), din("ffn2_w_down", [2, DFF, D])]
    w_out_d = din("w_out", [2, D, D])
    ab_w_in = din("ab_w_in", [1, D, 3984])
    sgdn_in = din("sgdn_in", [2, 4, 128, 128])
    srwkv_in = din("srwkv_in", [2, 8, 64, 64])
    rwkv_w_up = din("rwkv_w_up", [1, 2, 64, 512])
    rwkv_a_up = din("rwkv_a_up", [1, 2, 64, 512])
    rwkv_g_up = din("rwkv_g_up", [1, 128, 512])
    srwkv_out = dout("srwkv_out", [4, 2, 8, 64, 64])
    cd_w_in = din("cd_w_in", [1, D, 2208])
    mla_w_uq = din("mla_w_uq", [1, 384, 768])
    mla_w_ukv = din("mla_w_ukv", [1, 256, 1024])
    cckvT = din("cckvT", [256, 256])
    ckpeT = din("ckpeT", [32, 256])
    ropecs = din("ropecs", [2, 128, 1024])
    hy_w1 = din("hy_w1", [1, 33, 64])
    hy_w2 = din("hy_w2", [1, 64, 64])
    hy_w3 = din("hy_w3", [1, 64, 2048])
    dftd = {256: din("dft256", [4, 256, 256], BF16), 1024: din("dft1024", [4, 1024, 1024], BF16)}
    featd = {256: din("feat256", [128, 256]), 1024: din("feat1024", [128, 1024])}
    wind = {256: din("win256", [256, 512]), 1024: din("win1024", [1024, 512])}
    ckv_out = dout("ckv_out", [256, 1024])
    kpe_out = dout("kpe_out", [32, 1024])
    yT = dout("yT", [D, TOK])
    sgdn_out = dout("sgdn_out", [4, 2, 4, 128, 128])

    es = contextlib.ExitStack()
    with es:
        kb = KB(nc, es)

        uid = [0]

        def sbx(stack, name, shape, dt=F32, r=False):
            uid[0] += 1
            nm = "%s_%d" % (name, uid[0])
            if r:
                kb.rt.add(nm)
            return stack.enter_context(nc.sbuf_tensor(nm, list(shape), dt))

        def sb(name, shape, dt=F32):
            return sbx(es, name, shape, dt)

        X = sb("X", [128, 8, TOK])
        prm = sb("prm_sb", [128, NP])
        cst = sb("cst_sb", [128, carr.shape[1]])
        ones = sb("ones", [128, 128])
        modT = sb("modT", [128, 2, 2, 72])
        cs = sb("cs", [128, 8, 2], BF16)
        sq = [sb("sq%d" % i, [128, 512]) for i in range(2)]
        rstd = sb("rstd", [128, HALF])
        tmpf = [sb("tmpf%d" % i, [128, 512]) for i in range(2)]
        coef = sb("coef", [128, 64])
        pbank = [es.enter_context(nc.psum_tensor("pb%d" % i, [128, 512], F32)) for i in range(8)]

        def P(name):
            o, n = offs[name]
            return prm[:, o:o + n]

        def C(name):
            i = cnames.index(name)
            return cst[:, i * 128:(i + 1) * 128]

        slot = [0]

        def pq():
            s = slot[0]
            slot[0] = (s + 1) % 16
            return pbank[s % 4][:, (s // 4) * 128:(s // 4 + 1) * 128]

        kb.dma("sp", prm[:], prm_d[:, :])
        kb.dma("sp", cst[:], cst_d[:, :])
        for j in range(8):
            kb.dma("sp", X[:, j, :], xT[j * 128:(j + 1) * 128, :])
        kb.memset("dve", ones[:], 1.0)
        kb.ones_ap = ones[:]
        ident = C("ident")
        ident16 = sb("ident16", [128, 128], FP16)
        kb.copy("dve", ident16[:], ident)

        with contextlib.ExitStack() as ph:
            wm = [sbx(ph, "wm%d" % i, [128, 8, 1024], BF16) for i in range(2)]
            cT = P("cT")
            kb.act(cs[:].rearrange("p a b -> p (a b)"), cT, AF.Silu)
            for L in range(2):
                pm = pbank[7]
                pmv = pm[:, 0:144].rearrange("p (a b) -> p a b", b=2)
                for n in range(9):
                    wt = wm[n % 2]
                    kb.dma("pool", wt[:], w_mod[L, :, n * 1024:(n + 1) * 1024].rearrange("(k p) c -> p k c", p=128))
                    for j in range(8):
                        for k in range(8):
                            kb.mm(pmv[:, n * 8 + j, :], wt[:, k, j * 128:(j + 1) * 128], cs[:, k, :], start=(k == 0), stop=(k == 7))
                for v in range(2):
                    kb.tt("dve", modT[:, L, v, :], pmv[:, :, v], P("bmod%d" % L), ALU.add)
            kb.barrier()

        def rms_stats(hf):
            for nt in range(2):
                t0 = hf * HALF + nt * 512
                pst = pbank[6]
                for j in range(8):
                    s = sq[j % 2]
                    kb.act(s[:], X[:, j, t0:t0 + 512], AF.Square)
                    kb.mm(pst[:], ones[:], s[:], start=(j == 0), stop=(j == 7))
                kb.act(rstd[:, nt * 512:(nt + 1) * 512], pst[:], AF.Sqrt, bias=EPS, scale=1.0 / D)
            kb.recip(rstd[:], rstd[:])

        def make_coefs(L, hf, sub, gname):
            m = modT[:, L, hf, :]
            kb.stt(coef[:, 0:8], m[:, (3 * sub + 1) * 8:(3 * sub + 2) * 8], 1.0, P(gname), ALU.add, ALU.mult)
            kb.copy("dve", coef[:, 8:16], m[:, (3 * sub) * 8:(3 * sub + 1) * 8])
            kb.ts("dve", coef[:, 16:24], m[:, (3 * sub + 2) * 8:(3 * sub + 3) * 8], 0.5 if sub != 1 else 1.0, ALU.mult)

        def make_H(Hb, hf):
            for j in range(8):
                for nt in range(2):
                    t0 = hf * HALF + nt * 512
                    tf = tmpf[(j * 2 + nt) % 2]
                    kb.stt(tf[:], X[:, j, t0:t0 + 512], coef[:, j:j + 1], rstd[:, nt * 512:(nt + 1) * 512], ALU.mult, ALU.mult)
                    kb.act(Hb[:, j, nt * 512:(nt + 1) * 512], tf[:], AF.Identity, bias=coef[:, 8 + j:9 + j], scale=1.0)

        gu_groups = [(0, 3), (3, 3), (6, 3), (9, 2)]

        def ffn_phase(L, which):
            with contextlib.ExitStack() as ph:
                Hb = sbx(ph, "Hb", [128, 8, HALF], BF16)
                hh = sbx(ph, "hh", [128, 11, HALF], BF16)
                wgu = [sbx(ph, "wgu%d" % i, [128, 8, 2, 384], BF16) for i in range(2)]
                wdns = [sbx(ph, "wdn%d" % i, [128, 11, D], BF16) for i in range(2)]
                wdi = 0
                sgt = [sbx(ph, "sgt%d" % i, [128, 512], BF16) for i in range(2)]
                sub = 0 if which == 0 else 2
                wg = ffn_gu[which]
                wd = ffn_dn[which]
                gi = 0
                for hf in range(2):
                    rms_stats(hf)
                    make_coefs(L, hf, sub, ("nf1_%d" if which == 0 else "nf2_%d") % L)
                    make_H(Hb, hf)
                    for fh in range(2):
                        wdn = wdns[wdi % 2]
                        wdi += 1
                        kb.dma("pool", wdn[:], wd[L, fh * 1408:(fh + 1) * 1408, :].rearrange("(j p) c -> p j c", p=128))
                        for (g0, gn) in gu_groups:
                            wt = wgu[gi % 2]
                            gi += 1
                            c0 = fh * 1408 + g0 * 128
                            for gu in range(2):
                                kb.dma("pool", wt[:, :, gu, 0:gn * 128],
                                       wg[L, :, gu * DFF + c0: gu * DFF + c0 + gn * 128].rearrange("(k p) c -> p k c", p=128))
                            for jj in range(gn):
                                for nt in range(2):
                                    pg = pbank[(2 * (jj * 2 + nt)) % 4]
                                    pu = pbank[(2 * (jj * 2 + nt)) % 4 + 1]
                                    for k in range(8):
                                        kb.mm(pg[:], wt[:, k, 0, jj * 128:(jj + 1) * 128], Hb[:, k, nt * 512:(nt + 1) * 512], start=(k == 0), stop=(k == 7))
                                    for k in range(8):
                                        kb.mm(pu[:], wt[:, k, 1, jj * 128:(jj + 1) * 128], Hb[:, k, nt * 512:(nt + 1) * 512], start=(k == 0), stop=(k == 7))
                                    sg = sgt[(jj * 2 + nt) % 2]
                                    kb.act(sg[:], pg[:], AF.Silu)
                                    kb.tt("dve", hh[:, g0 + jj, nt * 512:(nt + 1) * 512], sg[:], pu[:], ALU.mult)
                        for m in range(8):
                            for nt in range(2):
                                po = pbank[4 + (m * 2 + nt) % 2]
                                for j in range(11):
                                    kb.mm(po[:], wdn[:, j, m * 128:(m + 1) * 128], hh[:, j, nt * 512:(nt + 1) * 512], start=(j == 0), stop=(j == 10))
                                t0 = hf * HALF + nt * 512
                                kb.stt(X[:, m, t0:t0 + 512], po[:], coef[:, 16 + m:17 + m], X[:, m, t0:t0 + 512], ALU.mult, ALU.add)
                kb.barrier()

        def interleave(gens):
            gens = list(gens)
            while gens:
                for g in list(gens):
                    try:
                        next(g)
                    except StopIteration:
                        gens.remove(g)

        hslot = [0]

        def pq2():
            k = hslot[0]
            hslot[0] = (k + 1) % 4
            return pbank[4 + k % 2][:, (k // 2) * 256:(k // 2 + 1) * 256]

        def tri_inv_T(ph_tiles, N, NT):
            T = ph_tiles
            Nd, Pa, Pb, Mt, Tun, PXa, PXb = T["Nd"], T["Pa"], T["Pb"], T["Mt"], T["Tun"], T["PXa"], T["PXb"]
            kb.tt("dve", Nd[:], N[:], C("b32"), ALU.mult)
            kb.tt("dve", PXa[:, 0:128], NT[:], C("b32"), ALU.mult)
            kb.copy("dve", PXa[:, 128:256], ident)
            kb.tt("dve", T["Nl1"][:], N[:], C("m1"), ALU.mult)
            kb.tt("dve", T["Nl2"][:], N[:], C("m2"), ALU.mult)
            yield
            Pc, Pn = Nd, Pa
            PXc, PXn = PXa, PXb
            for k in range(1, 5):
                px = pq2()
                kb.mm(px, Pc[:], PXc[:])
                pp = pq()
                kb.mm(pp, PXc[:, 0:128], Pc[:])
                if k < 4:
                    kb.copy("act", PXn[:, 0:128], px[:, 0:128])
                kb.tt("dve", PXn[:, 128:256], px[:, 128:256], PXc[:, 128:256], ALU.add)
                kb.copy("act", Pn[:], pp)
                yield
                Pc = Pn
                Pn = Pb if Pn is Pa else Pa
                PXc, PXn = PXn, PXc
            Xa = PXc[:, 128:256]
            pf = pq()
            kb.mm(pf, Pc[:], Xa)
            kb.tt("dve", Xa, pf, Xa, ALU.add)
            yield
            for mname in ("Nl1", "Nl2"):
                pm = pq()
                kb.mm(pm, T[mname][:], Xa)
                kb.copy("act", Mt[:], pm)
                ptr = pq().bitcast(FP16)[:, 0:128]
                kb.tr(ptr, Xa, ident16[:])
                kb.copy("act", Tun[:], ptr)
                yield
                pw = pq()
                kb.mm(pw, Tun[:], Mt[:])
                kb.tt("dve", Xa, pw, Xa, ALU.add)
                yield
            T["X"] = Xa

        def inv_tile_set(fr, pfx):
            d = {n: fr(pfx + n, [128, 128]) for n in ["Nd", "Pa", "Pb", "Mt", "Tun", "Nl1", "Nl2"]}
            d["PXa"] = fr(pfx + "PXa", [128, 256])
            d["PXb"] = fr(pfx + "PXb", [128, 256])
            return d

        def gdn(hf, Hb, wo_apply):
            with contextlib.ExitStack() as ph:
                f = lambda name, shape, dt=F32, r=False: sbx(ph, "g_" + name, shape, dt, r)
                fr = lambda name, shape: f(name, shape, F32, True)
                fh = lambda name, shape: f(name, shape, FP16)
                wb = [f("wb%d" % i, [128, 8, 4, 128], BF16) for i in range(1)]
                wab = f("wab", [128, 8, 16], BF16)
                praw = f("praw", [128, HALF])
                cv = f("cv", [128, HALF])
                qF, kF, vF, zF = fr("qF", [128, HALF]), fr("kF", [128, HALF]), f("vF", [128, HALF]), f("zF", [128, HALF])
                oacc = f("oacc", [128, HALF])
                mch = f("mch", [128, HALF], BF16)
                rn = f("rn", [128, HALF])
                abt = f("abt", [128, 8, 16])
                gsb = f("gsb", [128, 2, 8, 4])
                bsb = f("bsb", [128, 2, 8, 4])
                nbsb = f("nbsb", [128, 2, 8, 4])
                Gsb = f("Gsb", [128, 2, 8, 4])
                Gtot = f("Gtot", [128, 2, 8, 4])
                eG = f("eG", [128, 2, 8, 4])
                beG = f("beG", [128, 2, 8, 4])
                eGL = f("eGL", [128, 2, 8, 4])
                eGlast = f("eGlast", [128, 2, 8, 4])
                ea = f("ea", [128, 8])
                tsm = f("tsm", [128, 2, 8, 4])
                kT = f("kT", [128, 8, 128])
                vT = f("vT", [128, 8, 128])
                st_qkT = f("st_qkT", [128, 8, 128], BF16)
                st_u = f("st_u", [128, 8, 128])
                st_nwT = f("st_nwT", [128, 8, 128], BF16)
                st_qin = f("st_qin", [128, 8, 128], BF16)
                st_kout = f("st_kout", [128, 8, 128], BF16)
                S = f("S", [128, 128])
                Sb = f("Sb", [128, 128], BF16)
                NCH = 4
                t_vnew = f("t_vnew", [128, 128], BF16)
                chT = []
                for ci in range(NCH):
                    T = {n: f("c%d_%s" % (ci, n), [128, 128]) for n in ["diag", "d1", "d2", "E1", "E2", "eGr"]}
                    T.update({n: fh("c%d_%s" % (ci, n), [128, 128]) for n in ["N", "NT", "vb", "kbg"]})
                    T["inv"] = inv_tile_set(fh, "c%d_iv" % ci)
                    chT.append(T)
                c_ab = A_COLS + 2048
                kb.dma("pool", wab[:], ab_w_in[0, :, c_ab:c_ab + 16].rearrange("(k p) c -> p k c", p=128))
                pab = pbank[6]
                for tt in range(8):
                    for k in range(8):
                        kb.mm(pab[:, tt * 16:(tt + 1) * 16], Hb[:, k, tt * 128:(tt + 1) * 128], wab[:, k, :], start=(k == 0), stop=(k == 7))
                kb.copy("dve", abt[:].rearrange("p a b -> p (a b)"), pab[:, 0:128])
                abv = abt[:].rearrange("p t (d a h) -> p t d a h", d=2, a=2)
                kb.act(ea[:], P("galog"), AF.Exp)
                for d in range(2):
                    dtb = P("gdtb")[:, d * 4:(d + 1) * 4]
                    for tt in range(8):
                        kb.tt("dve", tsm[:, d, tt, :], abv[:, tt, d, 0, :], dtb, ALU.add)
                        kb.copy("dve", bsb[:, d, tt, :], abv[:, tt, d, 1, :])
                g2 = lambda t: t[:].rearrange("p d t h -> p (d t h)")
                kb.act(g2(tsm), g2(tsm), AF.Exp)
                kb.act(g2(tsm), g2(tsm), AF.Ln, bias=1.0)
                for d in range(2):
                    for tt in range(8):
                        kb.stt(gsb[:, d, tt, :], tsm[:, d, tt, :], -1.0, ea[:, d * 4:(d + 1) * 4], ALU.mult, ALU.mult)
                kb.act(g2(bsb), g2(bsb), AF.Sigmoid)
                kb.ts("dve", g2(nbsb), g2(bsb), -1.0, ALU.mult)
                pG = pbank[6]
                g3 = lambda t, d: t[:, d, :, :].rearrange("p t h -> p (t h)")
                kb.mm(pG[:, 0:32], C("le"), g3(gsb, 0))
                kb.mm(pG[:, 32:64], C("ge"), g3(gsb, 1))
                kb.mm(pG[:, 64:128], ones[:], g2(gsb))
                kb.copy("dve", g2(Gsb), pG[:, 0:64])
                kb.copy("dve", g2(Gtot), pG[:, 64:128])
                kb.act(g2(eG), g2(Gsb), AF.Exp)
                kb.tt("dve", g2(beG), g2(eG), g2(bsb), ALU.mult)
                kb.tt("dve", g2(eGL), g2(Gtot), g2(Gsb), ALU.subtract)
                kb.act(g2(eGL), g2(eGL), AF.Exp)
                kb.act(g2(eGlast), g2(Gtot), AF.Exp)

                for h in range(4):
                    wt = wb[0]
                    for qi in range(4):
                        c0 = A_COLS + qi * 512 + h * 128
                        kb.dma("pool", wt[:, :, qi, :], ab_w_in[0, :, c0:c0 + 128].rearrange("(k p) c -> p k c", p=128))
                    gc = P("gconv").rearrange("p (t c) -> p t c", t=3)
                    for qi, dst in enumerate([qF, kF, vF, zF]):
                        for nt in range(2):
                            pp = pbank[nt]
                            for k in range(8):
                                kb.mm(pp[:], wt[:, k, qi, :], Hb[:, k, nt * 512:(nt + 1) * 512], start=(k == 0), stop=(k == 7))
                            if qi == 3:
                                kb.act(zF[:, nt * 512:(nt + 1) * 512], pp[:], AF.Silu)
                            else:
                                kb.copy("act", praw[:, nt * 512:(nt + 1) * 512], pp[:])
                        if qi == 3:
                            continue
                        cc = qi * 4 + h
                        kb.ts("dve", cv[:], praw[:], gc[:, 1, cc:cc + 1], ALU.mult)
                        for (s0, Ls) in SEQS[hf]:
                            kb.stt(cv[:, s0 + 1:s0 + Ls], praw[:, s0:s0 + Ls - 1], gc[:, 0, cc:cc + 1], cv[:, s0 + 1:s0 + Ls], ALU.mult, ALU.add)
                            kb.stt(cv[:, s0:s0 + Ls - 1], praw[:, s0 + 1:s0 + Ls], gc[:, 2, cc:cc + 1], cv[:, s0:s0 + Ls - 1], ALU.mult, ALU.add)
                        kb.act(dst[:], cv[:], AF.Silu)
                        if qi < 2:
                            for nt in range(2):
                                s = sq[nt]
                                kb.act(s[:], dst[:, nt * 512:(nt + 1) * 512], AF.Square)
                                pst = pbank[2 + nt]
                                kb.mm(pst[:], ones[:], s[:])
                                kb.act(rn[:, nt * 512:(nt + 1) * 512], pst[:], AF.Sqrt, bias=1e-6, scale=1.0)
                            kb.recip(rn[:], rn[:])
                            kb.stt(dst[:], dst[:], (128.0 ** -0.5) if qi == 0 else 1.0, rn[:], ALU.mult, ALU.mult)
                    for tt in range(8):
                        p1 = pq()
                        kb.tr(p1, kF[:, tt * 128:(tt + 1) * 128], ident)
                        kb.copy("act", kT[:, tt, :], p1)
                        p2 = pq()
                        kb.tr(p2, vF[:, tt * 128:(tt + 1) * 128], ident)
                        kb.copy("act", vT[:, tt, :], p2)
                    for d in range(2):
                        posm = C("pos_gt") if d == 0 else C("pos_lt")
                        negm = C("neg_le") if d == 0 else C("neg_ge")
                        def pre_tile(tt, T, d=d, h=h, posm=posm, negm=negm):
                            tsl = slice(tt * 128, (tt + 1) * 128)
                            col = lambda t: t[:, d, tt, h:h + 1]
                            kb.ts("dve", T["diag"][:], ident, col(Gsb), ALU.mult)
                            prb = pq()
                            kb.mm(prb, ones[:], T["diag"][:])
                            yield
                            kb.stt(T["d1"][:], prb, col(Gsb), posm, ALU.subtract, ALU.add)
                            kb.act(T["E1"][:], T["d1"][:], AF.Exp, scale=-1.0)
                            kb.stt(T["d2"][:], prb, col(Gsb), negm, ALU.subtract, ALU.add)
                            kb.act(T["E2"][:], T["d2"][:], AF.Exp)
                            kb.act(T["eGr"][:], prb, AF.Exp)
                            pkk = pq()
                            kb.mm(pkk, kF[:, tsl], kF[:, tsl])
                            yield
                            kb.stt(T["N"][:], pkk, col(nbsb), T["E1"][:], ALU.mult, ALU.mult)
                            pnt = pq().bitcast(FP16)[:, 0:128]
                            kb.tr(pnt, T["N"][:], ident16[:])
                            kb.copy("act", T["NT"][:], pnt)
                            pqk = pq()
                            kb.mm(pqk, kF[:, tsl], qF[:, tsl])
                            kb.tt("dve", st_qkT[:, tt, :], pqk, T["E2"][:], ALU.mult)
                            kb.ts("pool", T["vb"][:], vT[:, tt, :], col(bsb), ALU.mult, 0.0, ALU.add)
                            kb.ts("pool", T["kbg"][:], kT[:, tt, :], col(beG), ALU.mult, 0.0, ALU.add)
                            kb.ts("pool", st_kout[:, tt, :], kT[:, tt, :], col(eGL), ALU.mult, 0.0, ALU.add)
                            kb.tt("pool", st_qin[:, tt, :], qF[:, tsl], T["eGr"][:], ALU.mult)
                            yield
                            yield from tri_inv_T(T["inv"], T["N"], T["NT"])
                            Xi = T["inv"]["X"]
                            pu_ = pq()
                            kb.mm(pu_, Xi, T["vb"][:])
                            kb.copy("act", st_u[:, tt, :], pu_)
                            pw_ = pq()
                            kb.mm(pw_, T["kbg"][:], Xi)
                            kb.act(st_nwT[:, tt, :], pw_, AF.Identity, scale=-1.0)
                            yield

                        for t0_ in range(0, 8, NCH):
                            interleave([pre_tile(tt, chT[ci]) for ci, tt in enumerate(range(t0_, min(8, t0_ + NCH)))])
                        for si, (s0, Ls) in enumerate(SEQS[hf]):
                            tiles = list(range(s0 // 128, (s0 + Ls) // 128))
                            if d == 1:
                                tiles = tiles[::-1]
                            if hf == 0:
                                kb.memset("dve", S[:], 0.0)
                            else:
                                kb.dma("sp", S[:], sgdn_in[d, h, :, :])
                            kb.copy("dve", Sb[:], S[:])
                            for tt in tiles:
                                tsl = slice(tt * 128, (tt + 1) * 128)
                                pv = pq()
                                kb.mm(pv, st_nwT[:, tt, :], Sb[:])
                                kb.tt("dve", t_vnew[:], pv, st_u[:, tt, :], ALU.add)
                                po_ = pq()
                                kb.mm(po_, Sb[:], st_qin[:, tt, :], start=True, stop=False)
                                kb.mm(po_, t_vnew[:], st_qkT[:, tt, :], start=False, stop=True)
                                if d == 0:
                                    kb.copy("act", oacc[:, tsl], po_)
                                else:
                                    kb.tt("dve", oacc[:, tsl], po_, oacc[:, tsl], ALU.add)
                                ps_ = pq()
                                kb.mm(ps_, st_kout[:, tt, :], t_vnew[:])
                                kb.stt(S[:], S[:], eGlast[:, d, tt, h:h + 1], ps_, ALU.mult, ALU.add)
                                kb.copy("act", Sb[:], S[:])
                            if hf == 0:
                                kb.dma("sp", sgdn_out[si, d, h, :, :], S[:])
                    for nt in range(2):
                        s = sq[nt]
                        kb.act(s[:], oacc[:, nt * 512:(nt + 1) * 512], AF.Square)
                        pst = pbank[2 + nt]
                        kb.mm(pst[:], ones[:], s[:])
                        kb.act(rn[:, nt * 512:(nt + 1) * 512], pst[:], AF.Sqrt, bias=EPS, scale=1.0 / 128)
                    kb.recip(rn[:], rn[:])
                    kb.stt(oacc[:], oacc[:], P("gnorm"), rn[:], ALU.mult, ALU.mult)
                    kb.tt("dve", mch[:], oacc[:], zF[:], ALU.mult)
                    wo_apply(4 + h, mch)
                kb.barrier()

        def rwkv(hf, Hb, wo_apply):
            with contextlib.ExitStack() as ph:
                f = lambda name, shape, dt=F32, r=False: sbx(ph, "r_" + name, shape, dt, r)
                fr = lambda name, shape: f(name, shape, F32, True)
                fh = lambda name, shape: f(name, shape, FP16)
                wch = [f("wch%d" % i, [128, 8, 128], BF16) for i in range(2)]
                wup, aup, gup = f("wup", [128, 512], BF16), f("aup", [128, 512], BF16), f("gup", [128, 512], BF16)
                twd, tad, sgd = f("twd", [128, HALF], BF16), f("tad", [128, HALF], BF16), f("sgd", [128, HALF], BF16)
                T1 = f("T1", [128, HALF])
                rF, kF0, vF, kkF = f("rF", [128, HALF]), f("kF0", [128, HALF]), f("vF", [128, HALF]), f("kkF", [128, HALF])
                asum, yacc = f("asum", [128, HALF]), f("yacc", [128, HALF])
                lw, G, kd, bF = f("lw", [128, HALF]), f("G", [128, HALF]), f("kd", [128, HALF]), f("bF", [128, HALF])
                mch = f("mch", [128, HALF], BF16)
                omm, hmu, omka = f("omm", [128, 15]), f("hmu", [128, 15]), f("omka", [128, 4])
                Mp = f("Mp", [128, 128])
                Mpb = f("Mpb", [128, 128], BF16)
                pC = f("pC", [128, 1])
                nm = ["eX", "Rt", "Ct", "kinv", "binv", "khat", "bhat", "Z"]
                bset = {"Rt", "Ct", "kinv", "binv", "Z"}
                nm = nm + ["eX2"]
                tlp = [{n: f("t%d_%s" % (par, n), [128, 128], BF16 if n in bset else F32) for n in nm} for par in range(2)]
                pCp = [f("pC%d" % par, [128, 1]) for par in range(2)]
                hdp = [[{n: f("h%d%d_%s" % (par, h, n), [128, 128], BF16) for n in ["X", "BmT", "QKT", "QBT", "vT", "khT", "bhT", "nU"]} for h in range(2)] for par in range(2)]
                hT = []
                for h_ in range(2):
                    T = {n: fh("p%d_%s" % (h_, n), [128, 128]) for n in ["N", "NT"]}
                    T.update({n: f("p%d_%s" % (h_, n), [128, 128], BF16) for n in ["Ctm", "Rtm"]})
                    T["msk"] = [f("p%d_msk%d" % (h_, i), [128, 128]) for i in range(3)]
                    T["inv"] = inv_tile_set(fh, "p%d_iv" % h_)
                    hT.append(T)
                hmask = P("hmask")
                mu = P("rmu")
                kb.ts("dve", omm[:], mu, -1.0, ALU.mult, 1.0, ALU.add)
                kb.ts("dve", hmu[:], mu, 0.5, ALU.mult)
                kb.ts("dve", omka[:], P("rka"), -1.0, ALU.mult, 1.0, ALU.add)
                for d in range(2):
                    kb.dma("pool", wup[d * 64:(d + 1) * 64, :], rwkv_w_up[0, d, :, :])
                    kb.dma("pool", aup[d * 64:(d + 1) * 64, :], rwkv_a_up[0, d, :, :])
                kb.dma("pool", gup[:], rwkv_g_up[0, :, :])
                wi = [0]

                def project_shift(chunk, dst, func=None):
                    wt = wch[wi[0] % 2]
                    wi[0] += 1
                    kb.dma("pool", wt[:], ab_w_in[0, :, chunk * 128:(chunk + 1) * 128].rearrange("(k p) c -> p k c", p=128))
                    for nt in range(2):
                        pp = pbank[nt]
                        for k in range(8):
                            kb.mm(pp[:], wt[:, k, :], Hb[:, k, nt * 512:(nt + 1) * 512], start=(k == 0), stop=(k == 7))
                        kb.copy("act", T1[:, nt * 512:(nt + 1) * 512], pp[:])
                    tgt = dst if func is None else lw
                    kb.ts("dve", tgt[:], T1[:], omm[:, chunk:chunk + 1], ALU.mult)
                    for (s0, Ls) in SEQS[hf]:
                        kb.stt(tgt[:, s0 + 1:s0 + Ls], T1[:, s0:s0 + Ls - 1], hmu[:, chunk:chunk + 1], tgt[:, s0 + 1:s0 + Ls], ALU.mult, ALU.add)
                        kb.stt(tgt[:, s0:s0 + Ls - 1], T1[:, s0 + 1:s0 + Ls], hmu[:, chunk:chunk + 1], tgt[:, s0:s0 + Ls - 1], ALU.mult, ALU.add)
                    if func is not None:
                        kb.act(dst[:], tgt[:], func)

                project_shift(12, twd, AF.Tanh)
                project_shift(13, tad, AF.Identity)
                project_shift(14, sgd, AF.Sigmoid)
                bo = C("bo")
                for c in range(4):
                    project_shift(c, rF)
                    project_shift(4 + c, kF0)
                    project_shift(8 + c, vF)
                    kb.ts("dve", kkF[:], kF0[:], P("rkk")[:, c:c + 1], ALU.mult)
                    for nt in range(2):
                        sl = slice(nt * 512, (nt + 1) * 512)
                        kb.act(sq[nt][:], kkF[:, sl], AF.Square)
                        pst = pbank[2 + nt]
                        kb.mm(pst[:], bo, sq[nt][:])
                        kb.act(T1[:, sl], pst[:], AF.Sqrt, bias=1e-6, scale=1.0)
                    kb.recip(T1[:], T1[:])
                    kb.tt("dve", kkF[:], kkF[:], T1[:], ALU.mult)
                    for d in range(2):
                        m_ts = C("gt") if d == 0 else C("lt")
                        m_st = C("lt") if d == 0 else C("gt")
                        m_in = C("le") if d == 0 else C("ge")
                        rows = slice(d * 64, (d + 1) * 64)
                        for nt in range(2):
                            sl = slice(nt * 512, (nt + 1) * 512)
                            pw = pbank[nt]
                            kb.mm(pw[:], wup[rows, c * 128:(c + 1) * 128], twd[rows, sl])
                            kb.act(lw[:, sl], pw[:], AF.Sigmoid, bias=P("rw0").rearrange("p (d c) -> p d c", d=2)[:, d, c:c + 1])
                            pa = pbank[2 + nt]
                            kb.mm(pa[:], aup[rows, c * 128:(c + 1) * 128], tad[rows, sl])
                            kb.act(T1[:, sl], pa[:], AF.Sigmoid, bias=P("ra0").rearrange("p (d c) -> p d c", d=2)[:, d, c:c + 1])
                        kb.ts("dve", lw[:], lw[:], -0.6065306597126334, ALU.mult)
                        if d == 0:
                            kb.copy("dve", asum[:], T1[:])
                        else:
                            kb.tt("dve", asum[:], asum[:], T1[:], ALU.add)
                        kb.tt("dve", bF[:], kkF[:], T1[:], ALU.mult)
                        kb.ts("dve", T1[:], T1[:], P("rka")[:, c:c + 1], ALU.mult, omka[:, c:c + 1], ALU.add)
                        kb.tt("dve", kd[:], kF0[:], T1[:], ALU.mult)
                        nc_ = nc
                        kb.emit("dve", lambda: nc_.vector.tensor_tensor_scan(out=G[:], data0=P("rmask"), data1=lw[:], initial=0.0, op0=ALU.mult, op1=ALU.add),
                                [G[:]], [P("rmask"), lw[:]])
                        if d == 1:
                            for tt in range(8):
                                tsl = slice(tt * 128, (tt + 1) * 128)
                                kb.stt(T1[:, tsl], G[:, tsl], -1.0, lw[:, tsl], ALU.mult, ALU.add)
                                kb.ts("dve", T1[:, tsl], T1[:, tsl], G[:, tt * 128 + 127:tt * 128 + 128], ALU.add)
                            kb.copy("dve", G[:], T1[:])
                        kb.tt("dve", lw[:], G[:], lw[:], ALU.subtract)
                        for par in range(2):
                            for h in range(2):
                                kb.memset("dve", hdp[par][h]["nU"][:], 0.0)
                        order = []
                        for si, (s0, Ls) in enumerate(SEQS[hf]):
                            tiles = list(range(s0 // 128, (s0 + Ls) // 128))
                            if d == 1:
                                tiles = tiles[::-1]
                            for k_, tt in enumerate(tiles):
                                order.append((si, tt, k_ == 0, k_ == len(tiles) - 1))

                        def prep_gen(idx, d=d, c=c, m_ts=m_ts, m_st=m_st, m_in=m_in):
                            si, tt, first, last = order[idx]
                            tl = tlp[idx % 2]
                            hd = hdp[idx % 2]
                            pCt = pCp[idx % 2]
                            tsl = slice(tt * 128, (tt + 1) * 128)
                            e_end = tt * 128 + (127 if d == 0 else 0)
                            gtot = G[:, e_end:e_end + 1]
                            kb.act(tl["eX"][:], G[:, tsl], AF.Exp)
                            kb.tt("pool", tl["Rt"][:], rF[:, tsl], tl["eX"][:], ALU.mult)
                            kb.act(tl["eX2"][:], lw[:, tsl], AF.Exp)
                            kb.tt("dve", tl["Ct"][:], kkF[:, tsl], tl["eX2"][:], ALU.mult)
                            yield
                            kb.act(tl["eX"][:], G[:, tsl], AF.Exp, scale=-1.0)
                            kb.tt("dve", tl["kinv"][:], kd[:, tsl], tl["eX"][:], ALU.mult)
                            kb.tt("dve", tl["binv"][:], bF[:, tsl], tl["eX"][:], ALU.mult)
                            kb.act(tl["eX2"][:], G[:, tsl], AF.Exp, bias=gtot, scale=-1.0)
                            kb.tt("pool", tl["khat"][:], kd[:, tsl], tl["eX2"][:], ALU.mult)
                            kb.tt("pool", tl["bhat"][:], bF[:, tsl], tl["eX2"][:], ALU.mult)
                            kb.act(pCt[:], gtot, AF.Exp)
                            yield

                            def pre_head(h):
                                H = hd[h]
                                T = hT[h]
                                hm = hmask[:, h:h + 1]
                                kb.ts("dve", T["Ctm"][:], tl["Ct"][:], hm, ALU.mult)
                                kb.ts("pool", T["Rtm"][:], tl["Rt"][:], hm, ALU.mult, 0.0, ALU.add)
                                p_ = pq()
                                kb.mm(p_, T["Ctm"][:], tl["binv"][:])
                                kb.stt(T["N"][:], p_, -1.0, m_ts, ALU.mult, ALU.mult)
                                p_ = pq()
                                kb.mm(p_, tl["binv"][:], T["Ctm"][:])
                                kb.stt(T["NT"][:], p_, -1.0, m_st, ALU.mult, ALU.mult)
                                yield
                                p_ = pq()
                                kb.mm(p_, tl["kinv"][:], T["Ctm"][:])
                                kb.tt("dve", H["BmT"][:], p_, m_st, ALU.mult)
                                p_ = pq()
                                kb.mm(p_, tl["kinv"][:], T["Rtm"][:])
                                kb.tt("dve", H["QKT"][:], p_, m_in, ALU.mult)
                                p_ = pq()
                                kb.mm(p_, tl["binv"][:], T["Rtm"][:])
                                kb.tt("dve", H["QBT"][:], p_, m_in, ALU.mult)
                                yield
                                for mi, (src, dstn) in enumerate([(vF[:, tsl], "vT"), (tl["khat"][:], "khT"), (tl["bhat"][:], "bhT")]):
                                    kb.ts("pool", T["msk"][mi][:], src, hm, ALU.mult, 0.0, ALU.add)
                                    p_ = pq()
                                    kb.tr(p_, T["msk"][mi][:], ident)
                                    kb.copy("act", H[dstn][:], p_)
                                yield
                                yield from tri_inv_T(T["inv"], T["N"], T["NT"])
                                kb.copy("act", H["X"][:], T["inv"]["X"])
                                yield

                            gens = [pre_head(0), pre_head(1)]
                            while gens:
                                for g_ in list(gens):
                                    try:
                                        next(g_)
                                    except StopIteration:
                                        gens.remove(g_)
                                yield

                        def seq_gen(idx, d=d, c=c):
                            si, tt, first, last = order[idx]
                            tl = tlp[idx % 2]
                            hd = hdp[idx % 2]
                            pCt = pCp[idx % 2]
                            tsl = slice(tt * 128, (tt + 1) * 128)
                            if first:
                                kb.memset("dve", Mp[:], 0.0)
                                if hf == 1:
                                    for h in range(2):
                                        kb.dma("sp", Mp[h * 64:(h + 1) * 64, h * 64:(h + 1) * 64], srwkv_in[d, 2 * c + h, :, :])
                                kb.copy("dve", Mpb[:], Mp[:])
                            pz = pq()
                            kb.mm(pz, tl["Ct"][:], Mpb[:], start=True, stop=False)
                            kb.mm(pz, hd[0]["BmT"][:], hd[0]["vT"][:], start=False, stop=False)
                            kb.mm(pz, hd[1]["BmT"][:], hd[1]["vT"][:], start=False, stop=True)
                            kb.copy("act", tl["Z"][:], pz)
                            yield
                            for h in range(2):
                                p_ = pq()
                                kb.mm(p_, hd[h]["X"][:], tl["Z"][:])
                                kb.act(hd[h]["nU"][:, h * 64:(h + 1) * 64], p_[:, h * 64:(h + 1) * 64], AF.Identity, scale=-1.0)
                            yield
                            py = pq()
                            kb.mm(py, Mpb[:], tl["Rt"][:], start=True, stop=False)
                            for h in range(2):
                                kb.mm(py, hd[h]["vT"][:], hd[h]["QKT"][:], start=False, stop=False)
                                kb.mm(py, hd[h]["nU"][:], hd[h]["QBT"][:], start=False, stop=(h == 1))
                            if d == 0:
                                kb.copy("act", yacc[:, tsl], py)
                            else:
                                kb.tt("dve", yacc[:, tsl], py, yacc[:, tsl], ALU.add)
                            pm_ = pq()
                            for h in range(2):
                                kb.mm(pm_, hd[h]["khT"][:], hd[h]["vT"][:], start=(h == 0), stop=False)
                                kb.mm(pm_, hd[h]["bhT"][:], hd[h]["nU"][:], start=False, stop=(h == 1))
                            kb.stt(Mp[:], Mp[:], pCt[:, 0:1], pm_, ALU.mult, ALU.add)
                            kb.copy("act", Mpb[:], Mp[:])
                            if last and hf == 0:
                                for h in range(2):
                                    kb.dma("sp", srwkv_out[si, d, 2 * c + h, :, :], Mp[h * 64:(h + 1) * 64, h * 64:(h + 1) * 64])
                            yield

                        interleave([prep_gen(0)])
                        for idx in range(len(order)):
                            gl = [seq_gen(idx)]
                            if idx + 1 < len(order):
                                gl.append(prep_gen(idx + 1))
                            interleave(gl)
                    for nt in range(2):
                        sl = slice(nt * 512, (nt + 1) * 512)
                        pst = pbank[nt]
                        kb.mm(pst[:], bo, yacc[:, sl])
                        kb.stt(yacc[:, sl], pst[:], -1.0 / 64, yacc[:, sl], ALU.mult, ALU.add)
                        kb.act(sq[nt][:], yacc[:, sl], AF.Square)
                        pv = pbank[2 + nt]
                        kb.mm(pv[:], bo, sq[nt][:])
                        kb.act(T1[:, sl], pv[:], AF.Sqrt, bias=64e-5, scale=1.0 / 64)
                    kb.recip(T1[:], T1[:])
                    kb.stt(yacc[:], yacc[:], P("rlnw")[:, c:c + 1], T1[:], ALU.mult, ALU.mult)
                    kb.ts("dve", yacc[:], yacc[:], P("rlnb")[:, c:c + 1], ALU.add)
                    kb.ts("dve", asum[:], asum[:], 0.5, ALU.mult)
                    kb.ts("dve", asum[:], asum[:], P("rka")[:, c:c + 1], ALU.mult, omka[:, c:c + 1], ALU.add)
                    kb.tt("dve", asum[:], asum[:], kF0[:], ALU.mult)
                    kb.stt(asum[:], asum[:], P("rrk")[:, c:c + 1], rF[:], ALU.mult, ALU.mult)
                    for nt in range(2):
                        sl = slice(nt * 512, (nt + 1) * 512)
                        pb_ = pbank[nt]
                        kb.mm(pb_[:], bo, asum[:, sl])
                        kb.tt("dve", T1[:, sl], pb_[:], vF[:, sl], ALU.mult)
                        kb.tt("dve", yacc[:, sl], yacc[:, sl], T1[:, sl], ALU.add)
                        pg_ = pbank[2 + nt]
                        kb.mm(pg_[:], gup[:, c * 128:(c + 1) * 128], sgd[:, sl])
                        kb.tt("dve", mch[:, sl], pg_[:], yacc[:, sl], ALU.mult)
                    wo_apply(c, mch)
                kb.barrier()

        def mla(hf, Hb, wo_apply):
            NK = 1024 if hf == 0 else 1280
            with contextlib.ExitStack() as ph:
                f = lambda name, shape, dt=F32: sbx(ph, "m_" + name, shape, dt)
                wch = [f("wch%d" % i, [128, 8, 128], BF16) for i in range(2)]
                wuq = f("wuq", [128, 3, 768], BF16)
                wukv = f("wukv", [128, 2, 1024], BF16)
                pqF = f("pqF", [128, 3, HALF])
                qnF = f("qnF", [128, 3, HALF], BF16)
                ckvF = f("ckvF", [128, 2, HALF])
                ckvB = f("ckvB", [128, 2, 1280], BF16)
                kpe96 = f("kpe96", [128, 1280])
                kpeB = f("kpeB", [128, 1280], BF16)
                rq = f("rq", [128, HALF])
                Ve = f("Ve", [128, 10, 128], BF16)
                Vo = f("Vo", [128, 10, 128], BF16)
                qTf = f("qTf", [128, HALF])
                qTb = [f("qTb%d" % e, [128, HALF], BF16) for e in range(2)]
                KTb = [f("KTb%d" % e, [128, 1280], BF16) for e in range(2)]
                PT = [f("PT%d" % i, [128, 512], BF16) for i in range(2)]
                oE = f("oE", [128, 128], BF16)
                oO = f("oO", [128, 128], BF16)
                rs_ = f("rs", [128, 512])
                mch = f("mch", [128, HALF], BF16)
                t1, t2 = f("t1", [128, 512]), f("t2", [128, 512])
                cosT, sinT = f("cosT", [128, HALF]), f("sinT", [128, HALF])
                kb.copy("dve", oE[:], C("onesE"))
                kb.copy("dve", oO[:], C("onesO"))
                kb.memset("dve", kpe96[:], 0.0)
                kb.memset("dve", qTf[:], 0.0)
                kb.dma("pool", wuq[:], mla_w_uq[0].rearrange("(k p) c -> p k c", p=128))
                kb.dma("pool", wukv[:], mla_w_ukv[0].rearrange("(k p) c -> p k c", p=128))
                if hf == 1:
                    kb.dma("sp", cosT[:], ropecs[0, :, :])
                    kb.dma("sp", sinT[:], ropecs[1, :, :])
                wi = [0]

                def project(c0, M, dst_fn):
                    wt = wch[wi[0] % 2]
                    wi[0] += 1
                    kb.dma("pool", wt[:, :, 0:M], cd_w_in[0, :, c0:c0 + M].rearrange("(k p) c -> p k c", p=128))
                    for nt in range(2):
                        pp = pbank[4 + nt]
                        for k in range(8):
                            kb.mm(pp[0:M, :], wt[:, k, 0:M], Hb[:, k, nt * 512:(nt + 1) * 512], start=(k == 0), stop=(k == 7))
                        dst_fn(nt, pp)

                for j in range(3):
                    project(j * 128, 128, lambda nt, pp, j=j: kb.copy("act", pqF[:, j, nt * 512:(nt + 1) * 512], pp[:]))
                for j in range(2):
                    project(384 + j * 128, 128, lambda nt, pp, j=j: kb.copy("act", ckvF[:, j, nt * 512:(nt + 1) * 512], pp[:]))
                project(576, 96, lambda nt, pp: kb.copy("act", kpe96[64:96, nt * 512:(nt + 1) * 512], pp[64:96, :]))

                def rmsn(src, nj, gname, outs):
                    for nt in range(2):
                        sl = slice(nt * 512, (nt + 1) * 512)
                        pst = pbank[4 + nt]
                        for j in range(nj):
                            kb.act(sq[j % 2][:], src[:, j, sl], AF.Square)
                            kb.mm(pst[:], ones[:], sq[j % 2][:], start=(j == 0), stop=(j == nj - 1))
                        kb.act(rq[:, sl], pst[:], AF.Sqrt, bias=EPS, scale=1.0 / (nj * 128))
                    kb.recip(rq[:], rq[:])
                    for j in range(nj):
                        for o in outs:
                            kb.stt(o(j), src[:, j, :], P(gname)[:, j:j + 1], rq[:], ALU.mult, ALU.mult)

                rmsn(pqF, 3, "mqn", [lambda j: qnF[:, j, :]])
                rmsn(ckvF, 2, "mkvn", [lambda j: ckvB[:, j, 0:HALF], lambda j: ckvF[:, j, :]])
                if hf == 0:
                    for j in range(2):
                        kb.dma("sp", ckv_out[j * 128:(j + 1) * 128, :], ckvF[:, j, :])
                    kb.dma("sp", kpe_out[:, :], kpe96[64:96, 0:HALF])
                    kb.copy("dve", kpeB[64:96, 0:HALF], kpe96[64:96, 0:HALF])
                else:
                    for j in range(2):
                        kb.dma("pool", ckvB[:, j, HALF:1280], cckvT[j * 128:(j + 1) * 128, :])
                    kb.dma("sp", kpe96[64:96, HALF:1280], ckpeT[:, :])
                    kb.copy("dve", kpeB[64:96, HALF:1280], kpe96[64:96, HALF:1280])

                if stage == 1:
                    kb.barrier()
                    return

                def rope(src, dstb):
                    for nt in range(2):
                        sl = slice(nt * 512, (nt + 1) * 512)
                        pj = pbank[4 + nt]
                        kb.mm(pj[0:96, :], C("jpad")[0:96, 0:96], src[0:96, sl])
                        kb.tt("dve", t1[64:96, :], src[64:96, sl], cosT[64:96, sl], ALU.mult)
                        kb.tt("dve", t2[64:96, :], pj[64:96, :], sinT[64:96, sl], ALU.mult)
                        kb.tt("dve", dstb[64:96, sl], t1[64:96, :], t2[64:96, :], ALU.add)

                if hf == 1:
                    rope(kpe96, kpeB)
                if stage == 2:
                    kb.barrier()
                    return
                kb.memset("dve", Ve[:].rearrange("p a b -> p (a b)"), 0.0)
                kb.memset("dve", Vo[:].rearrange("p a b -> p (a b)"), 0.0)
                scale = 96.0 ** -0.5
                if hf == 0:
                    qranges = [(si * 256, 256, [2 * si, 2 * si + 1]) for si in range(4)]
                else:
                    qranges = [(nt * 512, 512, list(range(10))) for nt in range(2)]
                pti = [0]
                for c in range(4):
                    for kt in range(NK // 128):
                        pv = pbank[4 + kt % 2]
                        for e in range(2):
                            v0 = (2 * c + e) * 128 + 64
                            for k in range(2):
                                kb.mm(pv[:, e * 64:(e + 1) * 64], ckvB[:, k, kt * 128:(kt + 1) * 128], wukv[:, k, v0:v0 + 64], start=(k == 0), stop=(k == 1))
                        kb.copy("act", Ve[:, kt, 0:64], pv[:, 0:64])
                        kb.copy("dve", Vo[:, kt, 64:128], pv[:, 64:128])
                    if stage == 31:
                        continue
                    for e in range(2):
                        h = 2 * c + e
                        for nt in range(2):
                            sl = slice(nt * 512, (nt + 1) * 512)
                            pqh = pbank[4 + nt]
                            for k in range(3):
                                kb.mm(pqh[0:96, :], wuq[:, k, h * 96:(h + 1) * 96], qnF[:, k, sl], start=(k == 0), stop=(k == 2))
                            kb.copy("act", qTb[e][0:64, sl], pqh[0:64, :])
                            if hf == 1:
                                kb.copy("dve", qTf[64:96, sl], pqh[64:96, :])
                            else:
                                kb.copy("dve", qTb[e][64:96, sl], pqh[64:96, :])
                        if hf == 1:
                            rope(qTf, qTb[e])
                        if stage == 32:
                            continue
                        for k0 in range(0, NK, 512):
                            n = min(512, NK - k0)
                            pk = pbank[4 + (k0 // 512) % 2]
                            for k in range(2):
                                kb.mm(pk[:, 0:n], wukv[:, k, h * 128:(h + 1) * 128], ckvB[:, k, k0:k0 + n], start=(k == 0), stop=(k == 1))
                            kb.copy("act", KTb[e][0:64, k0:k0 + n], pk[0:64, 0:n])
                        kb.copy("dve", KTb[e][64:96, 0:NK], kpeB[64:96, 0:NK])
                    if stage == 3:
                        continue
                    for (q0, nq, kts) in qranges:
                        po, psm = pbank[2], pbank[3]
                        first = True
                        for e in range(2):
                            Vsrc = Ve if e == 0 else Vo
                            osrc = oE if e == 0 else oO
                            for ki, kt in enumerate(kts):
                                ps_ = pbank[pti[0] % 2]
                                pt = PT[pti[0] % 2]
                                pti[0] += 1
                                ksl = slice(kt * 128, (kt + 1) * 128)
                                kb.mm(ps_[:, 0:nq], KTb[e][0:96, ksl], qTb[e][0:96, q0:q0 + nq])
                                kb.act(pt[:, 0:nq], ps_[:, 0:nq], AF.Exp, scale=scale)
                                last = (e == 1 and ki == len(kts) - 1)
                                kb.mm(po[:, 0:nq], Vsrc[:, kt, :], pt[:, 0:nq], start=first, stop=last)
                                kb.mm(psm[:, 0:nq], osrc[:], pt[:, 0:nq], start=first, stop=last)
                                first = False
                        kb.recip(rs_[:, 0:nq], psm[:, 0:nq])
                        kb.tt("dve", mch[:, q0:q0 + nq], po[:, 0:nq], rs_[:, 0:nq], ALU.mult)
                    wo_apply(c, mch)
                kb.barrier()

        def hyena(hf, Hb, wo_apply):
            Ls = 256 if hf == 0 else 1024
            nT = Ls // 128
            with contextlib.ExitStack() as ph:
                f = lambda name, shape, dt=F32: sbx(ph, "y_" + name, shape, dt)
                wch = [f("wch%d" % i, [128, 8, 128], BF16) for i in range(1)]
                vF, x1F, x2F, T1 = f("vF", [128, HALF]), f("x1F", [128, HALF]), f("x2F", [128, HALF]), f("T1", [128, HALF])
                fwd = f("fwd", [128, nT, 2, Ls], BF16)
                invp = [f("invp%d" % i, [128, 2, Ls], BF16) for i in range(2)]
                w3 = f("w3", [128, 2048], BF16)
                featT = x1F[:, 0:Ls]
                h1s, h2s = T1[:, 0:Ls], x2F[:, 0:Ls]
                h2b = f("h2b", [128, Ls], BF16)
                win = f("win", [128, nT, 128])
                fr3, fb3 = f("fr3", [128, 2]), f("fb3", [128, 2])
                hsum, hdif = f("hsum", [128, nT, 128], BF16), f("hdif", [128, nT, 128], BF16)
                Hc, Hs = f("Hc", [128, nT, 128]), f("Hs", [128, nT, 128])
                zT = f("zT", [128, nT, 128], BF16)
                Yc, Ys = f("Yc", [128, nT, 128], BF16), f("Ys", [128, nT, 128], BF16)
                ta, tb, tc_ = f("ta", [128, 512]), f("tb", [128, 512]), f("tc", [128, 512])
                mch = f("mch", [128, HALF], BF16)
                for i in range(2):
                    kb.dma("sp", fwd[:, :, i, :], dftd[Ls][i].rearrange("(st p) f -> p st f", p=128))
                kb.dma("pool", w3[0:64, :], hy_w3[0, :, :])
                kb.dma("sp", featT, featd[Ls][:, :])
                kb.ts("dve", fr3[0:64, :], P("hfreq")[0:64, :], 1.0 / 3, ALU.mult)
                kb.tt("dve", fb3[0:64, 0:1], fr3[0:64, 0:1], P("hb1")[0:64, :], ALU.mult)
                kb.tt("dve", fb3[0:64, 1:2], fr3[0:64, 1:2], P("hb2")[0:64, :], ALU.mult)

                def sin3(dst, pin, li, n):
                    kb.act(ta[0:64, 0:n], pin, AF.Sin, bias=fb3[0:64, li:li + 1], scale=fr3[0:64, li:li + 1])
                    kb.tt("dve", tb[0:64, 0:n], ta[0:64, 0:n], ta[0:64, 0:n], ALU.mult)
                    kb.ts("dve", tb[0:64, 0:n], tb[0:64, 0:n], -4.0, ALU.mult, 3.0, ALU.add)
                    kb.tt("dve", dst, ta[0:64, 0:n], tb[0:64, 0:n], ALU.mult)

                for c0 in range(0, Ls, 512):
                    n = min(512, Ls - c0)
                    p1 = pbank[4]
                    kb.mm(p1[0:64, 0:n], P("hw1")[0:64, :], featT[0:64, c0:c0 + n])
                    sin3(h1s[0:64, c0:c0 + n], p1[0:64, 0:n], 0, n)
                    p2 = pbank[5]
                    kb.mm(p2[0:64, 0:n], P("hw2")[0:64, :], h1s[0:64, c0:c0 + n])
                    sin3(h2s[0:64, c0:c0 + n], p2[0:64, 0:n], 1, n)
                kb.copy("dve", h2b[0:64, :], h2s[0:64, :])
                wi = [0]
                hcv = P("hconv").rearrange("p (t c) -> p t c", t=3)
                ipi = [0]
                for cc in range(4):
                    kb.dma("sp", win[:], wind[Ls][:, cc * 128:(cc + 1) * 128].rearrange("(t p) c -> p t c", p=128))
                    for qi, dst in enumerate([vF, x1F, x2F]):
                        wt = wch[0]
                        wi[0] += 1
                        c0 = 672 + qi * 512 + cc * 128
                        kb.dma("pool", wt[:], cd_w_in[0, :, c0:c0 + 128].rearrange("(k p) c -> p k c", p=128))
                        for nt in range(2):
                            pp = pbank[4 + nt]
                            for k in range(8):
                                kb.mm(pp[:], wt[:, k, :], Hb[:, k, nt * 512:(nt + 1) * 512], start=(k == 0), stop=(k == 7))
                            kb.copy("act", T1[:, nt * 512:(nt + 1) * 512], pp[:])
                        ci = qi * 4 + cc
                        kb.ts("dve", dst[:], T1[:], hcv[:, 1, ci:ci + 1], ALU.mult)
                        for (s0, Lq) in SEQS[hf]:
                            kb.stt(dst[:, s0 + 1:s0 + Lq], T1[:, s0:s0 + Lq - 1], hcv[:, 0, ci:ci + 1], dst[:, s0 + 1:s0 + Lq], ALU.mult, ALU.add)
                            kb.stt(dst[:, s0:s0 + Lq - 1], T1[:, s0 + 1:s0 + Lq], hcv[:, 2, ci:ci + 1], dst[:, s0:s0 + Lq - 1], ALU.mult, ALU.add)
                    z = vF
                    for n in range(2):
                        gate = x1F if n == 0 else x2F
                        bcol = P("hbias").rearrange("p (n c) -> p n c", n=2)[:, n, cc:cc + 1]
                        for dt in range(nT):
                            pt_ = pbank[4 + dt % 2]
                            for di in range(2):
                                w0 = (n * 2 + di) * 512 + cc * 128
                                kb.mm(pt_[:, di * 128:(di + 1) * 128], h2b[0:64, dt * 128:(dt + 1) * 128], w3[0:64, w0:w0 + 128])
                            kb.tt("dve", ta[:, 0:128], pt_[:, 0:128], win[:, dt, :], ALU.mult)
                            kb.tt("dve", tb[:, 0:128], pt_[:, 128:256], win[:, dt, :], ALU.mult)
                            if dt == 0:
                                kb.ts("dve", tb[:, 0:128], tb[:, 0:128], P("nz0"), ALU.mult)
                            kb.tt("dve", hsum[:, dt, :], ta[:, 0:128], tb[:, 0:128], ALU.add)
                            kb.tt("dve", hdif[:, dt, :], ta[:, 0:128], tb[:, 0:128], ALU.subtract)
                        for ft in range(nT):
                            pc_ = pbank[4 + ft % 2]
                            for dt in range(nT):
                                kb.mm(pc_[:, 0:128], fwd[:, dt, 0, ft * 128:(ft + 1) * 128], hsum[:, dt, :], start=(dt == 0), stop=(dt == nT - 1))
                            kb.copy("act", Hc[:, ft, :], pc_[:, 0:128])
                            for dt in range(nT):
                                kb.mm(pc_[:, 128:256], fwd[:, dt, 1, ft * 128:(ft + 1) * 128], hdif[:, dt, :], start=(dt == 0), stop=(dt == nT - 1))
                            kb.copy("act", Hs[:, ft, :], pc_[:, 128:256])
                        for (s0, Lq) in SEQS[hf]:
                            for st in range(nT):
                                ptr = pbank[4 + st % 2]
                                kb.tr(ptr[:, 0:128], z[:, s0 + st * 128:s0 + (st + 1) * 128], ident)
                                kb.copy("act", zT[:, st, :], ptr[:, 0:128])
                            for ft in range(nT):
                                pzc, pzs = pbank[2], pbank[3]
                                for st in range(nT):
                                    kb.mm(pzc[:, 0:128], fwd[:, st, 0, ft * 128:(ft + 1) * 128], zT[:, st, :], start=(st == 0), stop=(st == nT - 1))
                                for st in range(nT):
                                    kb.mm(pzs[:, 0:128], fwd[:, st, 1, ft * 128:(ft + 1) * 128], zT[:, st, :], start=(st == 0), stop=(st == nT - 1))
                                kb.tt("dve", ta[:, 0:128], pzc[:, 0:128], Hc[:, ft, :], ALU.mult)
                                kb.tt("dve", tb[:, 0:128], pzs[:, 0:128], Hs[:, ft, :], ALU.mult)
                                kb.tt("dve", Yc[:, ft, :], ta[:, 0:128], tb[:, 0:128], ALU.subtract)
                                kb.tt("dve", ta[:, 0:128], pzc[:, 0:128], Hs[:, ft, :], ALU.mult)
                                kb.tt("dve", tb[:, 0:128], pzs[:, 0:128], Hc[:, ft, :], ALU.mult)
                                kb.tt("dve", Ys[:, ft, :], ta[:, 0:128], tb[:, 0:128], ALU.add)
                            blocks = [(b0, min(512, Lq - b0)) for b0 in range(0, Lq, 512)]
                            pys = [pbank[0], pbank[1]]
                            for ft in range(nT):
                                ip = invp[ipi[0] % 2]
                                ipi[0] += 1
                                for i in range(2):
                                    kb.dma("sp", ip[:, i, :], dftd[Ls][2 + i, ft * 128:(ft + 1) * 128, :])
                                for bi, (b0, bn) in enumerate(blocks):
                                    kb.mm(pys[bi][:, 0:bn], Yc[:, ft, :], ip[:, 0, b0:b0 + bn], start=(ft == 0), stop=False)
                                    kb.mm(pys[bi][:, 0:bn], Ys[:, ft, :], ip[:, 1, b0:b0 + bn], start=False, stop=(ft == nT - 1))
                            for bi, (b0, bn) in enumerate(blocks):
                                zs = z[:, s0 + b0:s0 + b0 + bn]
                                kb.ts("dve", tc_[:, 0:bn], zs, bcol, ALU.mult)
                                kb.stt(tc_[:, 0:bn], pys[bi][:, 0:bn], 1.0 / Lq, tc_[:, 0:bn], ALU.mult, ALU.add)
                                kb.tt("dve", zs, tc_[:, 0:bn], gate[:, s0 + b0:s0 + b0 + bn], ALU.mult)
                    kb.copy("dve", mch[:], z[:])
                    wo_apply(4 + cc, mch)
                kb.barrier()

        def mixer_phase(L):
            with contextlib.ExitStack() as ph:
                Hb = sbx(ph, "Hbm", [128, 8, HALF], BF16)
                wo = [sbx(ph, "wo%d" % i, [128, D], BF16) for i in range(2)]
                woi = [0]
                for hf in range(2):
                    rms_stats(hf)
                    make_coefs(L, hf, 1, "nmx_%d" % L)
                    make_H(Hb, hf)

                    def wo_apply(j, mc, hf=hf):
                        w = wo[woi[0] % 2]
                        woi[0] += 1
                        kb.dma("pool", w[:], w_out_d[L, j * 128:(j + 1) * 128, :])
                        for m in range(8):
                            for nt in range(2):
                                po = pbank[6 + (m * 2 + nt) % 2]
                                kb.mm(po[:], w[:, m * 128:(m + 1) * 128], mc[:, nt * 512:(nt + 1) * 512])
                                t0 = hf * HALF + nt * 512
                                kb.stt(X[:, m, t0:t0 + 512], po[:], coef[:, 16 + m:17 + m], X[:, m, t0:t0 + 512], ALU.mult, ALU.add)

                    if L == 0:
                        if test in (None, "rwkv", "l0"):
                            rwkv(hf, Hb, wo_apply)
                        if test in (None, "gdn", "l0"):
                            gdn(hf, Hb, wo_apply)
                    else:
                        if test in (None, "mla", "l1"):
                            mla(hf, Hb, wo_apply)
                        if test in (None, "hy", "l1"):
                            hyena(hf, Hb, wo_apply)
                kb.barrier()

        def final_norm_and_store():
            for hf in range(2):
                rms_stats(hf)
                for j in range(8):
                    for nt in range(2):
                        t0 = hf * HALF + nt * 512
                        tf = tmpf[(j * 2 + nt) % 2]
                        kb.stt(tf[:], X[:, j, t0:t0 + 512], P("fnorm")[:, j:j + 1], rstd[:, nt * 512:(nt + 1) * 512], ALU.mult, ALU.mult)
                        kb.dma("sp", yT[j * 128:(j + 1) * 128, t0:t0 + 512], tf[:])

        if test in ("gdn", "rwkv", "l0"):
            mixer_phase(0)
        elif test in ("mla", "hy", "l1"):
            mixer_phase(1)
        else:
            for L in range(2):
                ffn_phase(L, 0)
                mixer_phase(L)
                ffn_phase(L, 1)
        final_norm_and_store()
        kb.barrier()
        print("instructions:", kb.ninst)
    return nc


def prep_inputs(inp):
    inp = {k: np.asarray(v) for k, v in inp.items()}
    maps = []
    offs = None
    cnames, carr = make_consts()
    ropecs = rope_tables()
    hc = {L: hyena_consts(L) for L in (256, 1024)}
    for c in range(NCORE):
        xp = inp["x_prompt"][4 * c:4 * c + 4].reshape(HALF, D)
        xs = inp["x_sample"][c]
        xT = np.ascontiguousarray(np.concatenate([xp, xs], 0).T)
        Pk = pack_params(inp, c)
        prm = Pk.build()
        offs = Pk.off
        m = {"xT": xT, "prm": prm, "cst": carr, "w_mod": inp["w_mod"],
             "ffn1_w_gu": inp["ffn1_w_gu"], "ffn2_w_gu": inp["ffn2_w_gu"],
             "ffn1_w_down": inp["ffn1_w_down"], "ffn2_w_down": inp["ffn2_w_down"],
             "w_out": inp["w_out"], "ab_w_in": inp["ab_w_in"],
             "sgdn_in": np.ascontiguousarray(inp["state_gdn"][c, 0]),
             "srwkv_in": np.ascontiguousarray(inp["state_rwkv"][c, 0].transpose(0, 1, 3, 2)),
             "rwkv_w_up": inp["rwkv_w_up"], "rwkv_a_up": inp["rwkv_a_up"], "rwkv_g_up": inp["rwkv_g_up"],
             "cd_w_in": inp["cd_w_in"], "mla_w_uq": inp["mla_w_uq"], "mla_w_ukv": inp["mla_w_ukv"],
             "cckvT": np.ascontiguousarray(inp["cache_ckv"][c, 0].T), "ckpeT": np.ascontiguousarray(inp["cache_kpe"][c, 0].T),
             "ropecs": ropecs, "hy_w1": inp["hy_w1"], "hy_w2": inp["hy_w2"], "hy_w3": inp["hy_w3"],
             "dft256": hc[256][0], "dft1024": hc[1024][0], "feat256": hc[256][1], "feat1024": hc[1024][1],
             "win256": hc[256][2], "win1024": hc[1024][2]}
        maps.append(m)
    return maps, offs


def kernel(**inputs):
    maps, offs = prep_inputs(inputs)
    NP = maps[0]["prm"].shape[1]
    nc = build_program(offs, NP)
    res = run_bass_kernel_spmd(nc, maps, core_ids=list(range(NCORE)))
    yp = np.zeros((32, 256, D), np.float32)
    ys = np.zeros((8, 1024, D), np.float32)
    srw = np.zeros((32, 1, 2, 8, 64, 64), np.float32)
    sgd = np.zeros((32, 1, 2, 4, 128, 128), np.float32)
    ckv = np.zeros((32, 1, 256, 256), np.float32)
    kpe = np.zeros((32, 1, 256, 32), np.float32)
    for c in range(NCORE):
        r = res.results[c]
        yT = r["yT"]
        yp[4 * c:4 * c + 4] = yT[:, :HALF].T.reshape(4, 256, D)
        ys[c] = yT[:, HALF:].T
        srw[4 * c:4 * c + 4, 0] = np.asarray(r["srwkv_out"]).transpose(0, 1, 2, 4, 3)
        sgd[4 * c:4 * c + 4, 0] = np.asarray(r["sgdn_out"])
        ckv[4 * c:4 * c + 4, 0] = np.asarray(r["ckv_out"]).T.reshape(4, 256, 256)
        kpe[4 * c:4 * c + 4, 0] = np.asarray(r["kpe_out"]).T.reshape(4, 256, 32)
    return yp, ys, srw, sgd, ckv, kpe
```
